# Optimizing a Trainium2 kernel written in Bass

```python
import math
import jax, jax.numpy as jnp
from jax import lax
import numpy as np

D_MODEL = 1024
BATCH = 8
SEQ = 4096
DEPTH = 1

MIX_WIDTH = D_MODEL
FOURIER_WIDTH = MIX_WIDTH // 2
FOURIER_GROUPS = 8
FOURIER_GROUP_DIM = FOURIER_WIDTH // FOURIER_GROUPS
HYENA_WIDTH = MIX_WIDTH - FOURIER_WIDTH
HYENA_ORDER = 2
SHORT_CONV = 3
POS_BANDS = 16
POS_EMB_DIM = 1 + 2 * POS_BANDS
FILTER_HIDDEN = 64
FILTER_INNER_LAYERS = 2
N_DIRECTIONS = 2
IN_WIDTH = FOURIER_WIDTH + (HYENA_ORDER + 1) * HYENA_WIDTH
D_FF = 4 * D_MODEL
DECAY_TARGET = 1e-2
FAST_DECAY_PCT = 0.3
SLOW_DECAY_PCT = 1.5
MAX_DECAY = math.log(DECAY_TARGET) / FAST_DECAY_PCT
MIN_DECAY = math.log(DECAY_TARGET) / SLOW_DECAY_PCT
EPS = 1e-5

kernel_name = 'hybrid_fnet_hyena_parallel_block'


def rmsnorm(x, g):
    xf = x.astype(jnp.float32)
    y = xf * lax.rsqrt(jnp.mean(xf * xf, axis=-1, keepdims=True) + EPS)
    return (y * g.astype(jnp.float32)).astype(x.dtype)


def fourier_mix(u):
    b, l, _ = u.shape
    ug = u.astype(jnp.float32).reshape(b, l, FOURIER_GROUPS, FOURIER_GROUP_DIM)
    f = jnp.fft.fft2(ug, axes=(1, 3), norm='ortho').real
    return f.reshape(b, l, FOURIER_WIDTH).astype(u.dtype)


def short_conv(u, w, bias):
    l = u.shape[1]
    pad = SHORT_CONV // 2
    up = jnp.pad(u, ((0, 0), (pad, SHORT_CONV - 1 - pad), (0, 0)))
    y = up[:, 0:l] * w[0]
    for j in range(1, SHORT_CONV):
        y = y + up[:, j:j + l] * w[j]
    return y + bias


def positional_features(l):
    pos = jnp.arange(l, dtype=jnp.float32)
    t = pos / jnp.float32(max(l - 1, 1))
    bands = jnp.linspace(1e-4, POS_BANDS - 1, POS_BANDS, dtype=jnp.float32)
    ang = (2.0 * jnp.pi * pos / l)[:, None] * bands[None, :]
    z = jnp.concatenate([t[:, None], jnp.cos(ang), -jnp.sin(ang)], axis=-1)
    return z, t


def implicit_filters(l, w0, b0, w_inner, b_inner, freq, w_out):
    z, t = positional_features(l)
    f32 = jnp.float32
    fr = freq.astype(f32)
    h = jnp.sin(fr * (z @ w0.astype(f32) + b0.astype(f32)))
    for j in range(FILTER_INNER_LAYERS):
        h = jnp.sin(fr * (h @ w_inner[j].astype(f32) + b_inner[j].astype(f32)))
    h = (h @ w_out.astype(f32)).reshape(l, HYENA_ORDER, N_DIRECTIONS, HYENA_WIDTH)
    deltas = jnp.linspace(MIN_DECAY, MAX_DECAY, HYENA_WIDTH, dtype=f32)
    decay = jnp.exp(-t[:, None] * jnp.abs(deltas)[None, :])
    h = h * decay[:, None, None, :]
    h_fwd = h[:, :, 0]
    h_bwd = h[:, :, 1]
    k = jnp.concatenate([h_fwd[:1] + h_bwd[:1], h_fwd[1:], jnp.zeros_like(h_fwd[:1]),
                         h_bwd[1:][::-1]], axis=0)
    return k * lax.rsqrt(jnp.sum(k * k, axis=0, keepdims=True) + EPS)


def fft_long_conv(u, k_freq, d):
    l = u.shape[1]
    uf32 = u.astype(jnp.float32)
    uf = jnp.fft.rfft(uf32, n=2 * l, axis=1)
    y = jnp.fft.irfft(uf * k_freq[None], n=2 * l, axis=1)[:, :l]
    return (y + uf32 * d.astype(jnp.float32)).astype(u.dtype)


def hyena_mix(u, conv_w, conv_b, filt_w0, filt_b0, filt_w_inner, filt_b_inner, filt_freq,
              filt_w_out, long_d):
    l = u.shape[1]
    u = short_conv(u, conv_w, conv_b)
    x1, x2, v = jnp.split(u, HYENA_ORDER + 1, axis=-1)
    k = implicit_filters(l, filt_w0, filt_b0, filt_w_inner, filt_b_inner, filt_freq, filt_w_out)
    k_freq = jnp.fft.rfft(k, axis=0)
    z = v
    for n, gate in enumerate((x1, x2)):
        z = gate * fft_long_conv(z, k_freq[:, n], long_d[n])
    return z


def setup_inputs(seed: int = 0) -> dict:
    key = jax.random.key(seed)
    ks = jax.random.split(key, 20)
    f32 = jnp.float32

    def nrm(k, shape, scale):
        return jax.random.normal(k, shape, f32) * scale

    def gain(k, shape):
        return 1.0 + 0.02 * jax.random.normal(k, shape, f32)

    return {
        'x': jax.random.normal(ks[0], (BATCH, SEQ, D_MODEL), f32),
        'g_mix': gain(ks[1], (DEPTH, D_MODEL)),
        'w_in': nrm(ks[2], (DEPTH, D_MODEL, IN_WIDTH), D_MODEL ** -0.5),
        'conv_w': nrm(ks[3], (DEPTH, SHORT_CONV, (HYENA_ORDER + 1) * HYENA_WIDTH), SHORT_CONV ** -0.5),
        'conv_b': nrm(ks[4], (DEPTH, (HYENA_ORDER + 1) * HYENA_WIDTH), 0.01),
        'filt_w0': nrm(ks[5], (DEPTH, POS_EMB_DIM, FILTER_HIDDEN), POS_EMB_DIM ** -0.5),
        'filt_b0': nrm(ks[6], (DEPTH, FILTER_HIDDEN), 0.1),
        'filt_w_inner': nrm(ks[7], (DEPTH, FILTER_INNER_LAYERS, FILTER_HIDDEN, FILTER_HIDDEN), FILTER_HIDDEN ** -0.5),
        'filt_b_inner': nrm(ks[8], (DEPTH, FILTER_INNER_LAYERS, FILTER_HIDDEN), 0.1),
        'filt_freq': gain(ks[9], (DEPTH, FILTER_HIDDEN)),
        'filt_w_out': nrm(ks[10], (DEPTH, FILTER_HIDDEN, HYENA_ORDER * N_DIRECTIONS * HYENA_WIDTH), FILTER_HIDDEN ** -0.5),
        'long_d': nrm(ks[11], (DEPTH, HYENA_ORDER, HYENA_WIDTH), 0.1),
        'g_fourier': gain(ks[12], (DEPTH, FOURIER_WIDTH)),
        'g_hyena': gain(ks[13], (DEPTH, HYENA_WIDTH)),
        'w_out': nrm(ks[14], (DEPTH, MIX_WIDTH, D_MODEL), MIX_WIDTH ** -0.5),
        'g_mlp': gain(ks[15], (DEPTH, D_MODEL)),
        'w_fc1': nrm(ks[16], (DEPTH, D_MODEL, D_FF), D_MODEL ** -0.5),
        'w_fc2': nrm(ks[17], (DEPTH, D_FF, D_MODEL), D_FF ** -0.5),
        'g_final': gain(ks[18], (D_MODEL,)),
    }


def reference(x, g_mix, w_in, conv_w, conv_b, filt_w0, filt_b0, filt_w_inner, filt_b_inner,
              filt_freq, filt_w_out, long_d, g_fourier, g_hyena, w_out, g_mlp, w_fc1, w_fc2,
              g_final):
    for i in range(DEPTH):
        h = rmsnorm(x, g_mix[i])
        p = jnp.einsum('bld,de->ble', h, w_in[i])
        u_f = p[..., :FOURIER_WIDTH]
        u_h = p[..., FOURIER_WIDTH:]
        y_f = rmsnorm(fourier_mix(u_f), g_fourier[i])
        y_h = rmsnorm(hyena_mix(u_h, conv_w[i], conv_b[i], filt_w0[i], filt_b0[i], filt_w_inner[i],
                                filt_b_inner[i], filt_freq[i], filt_w_out[i], long_d[i]), g_hyena[i])
        y = jnp.concatenate([y_f, y_h], axis=-1)
        x = x + jnp.einsum('ble,ed->bld', y, w_out[i])
        hm = rmsnorm(x, g_mlp[i])
        a = jnp.square(jax.nn.relu(jnp.einsum('bld,df->blf', hm, w_fc1[i])))
        x = x + jnp.einsum('blf,fd->bld', a, w_fc2[i])
    return rmsnorm(x, g_final)
```

```python
import numpy as np
import ml_dtypes
from contextlib import ExitStack
import concourse.bass as bass
import concourse.mybir as mybir
from concourse.bass_utils import run_bass_kernel_spmd

F32 = mybir.dt.float32
BF16 = mybir.dt.bfloat16
I32 = mybir.dt.int32
ALU = mybir.AluOpType
ACTF = mybir.ActivationFunctionType
EPOCH = 8000
L = 4096
NFFT = 8192
EPS = 1e-5
NB = ml_dtypes.bfloat16


class Res:
    __slots__ = ("name", "w", "rd")

    def __init__(self, name):
        self.name = name
        self.w = {}
        self.rd = []


class T:
    __slots__ = ("ap", "r")

    def __init__(self, ap, name):
        self.ap = ap
        self.r = Res(name)


class FW:
    def __init__(self, nc):
        self.nc = nc
        self.eng = {"pe": nc.tensor, "act": nc.scalar, "dve": nc.vector, "pool": nc.gpsimd, "sp": nc.sync}
        self.sems = {}
        self.cnt = {k: 0 for k in ("pe", "act", "dve", "pool")}
        self.waited = {k: {} for k in self.eng}
        self.nsem = 0
        self.recent = {}
        self.gen = 0
        self.free_dma = []
        self._dcnt = {}

    def _sem(self, key):
        s = self.sems.get(key)
        if s is None:
            s = self.nc.alloc_semaphore("s%d" % self.nsem)
            self.nsem += 1
            self.sems[key] = s
        return s

    def _wait(self, engine, deps):
        w = self.waited[engine]
        best = {}
        for key, val in deps:
            if engine == "pe" and key[0] == "pe":
                continue
            if key[0] == "dma" and key[2] < self.gen:
                continue
            if w.get(key, 0) < val and best.get(key, 0) < val:
                best[key] = val
        for key, val in best.items():
            self.eng[engine].wait_ge(self._sem(key), val)
            w[key] = val

    @staticmethod
    def _deps(reads, writes):
        deps = []
        for r in reads:
            deps.extend(r.w.items())
        for r in writes:
            deps.extend(r.w.items())
            deps.extend(r.rd)
        return deps

    @staticmethod
    def _mark(tok, reads, writes, accumulate=False):
        for r in writes:
            if r.w.get(tok[0], 0) < tok[1]:
                r.w[tok[0]] = tok[1]
            r.rd = []
        for r in reads:
            r.rd.append(tok)
            if len(r.rd) > 48:
                d = {}
                for k, v in r.rd:
                    if d.get(k, 0) < v:
                        d[k] = v
                r.rd = list(d.items())

    def _tick(self, engine, inst):
        c = self.cnt[engine]
        self.cnt[engine] = c + 1
        key = (engine, c // EPOCH)
        inst.then_inc(self._sem(key), 1)
        self.recent[key] = c % EPOCH + 1
        return (key, c % EPOCH + 1)

    def barrier(self):
        toks = list(self.recent.items())
        for eng in self.eng:
            self._wait(eng, toks)
        self.recent = {}
        for key in [k for k in self.sems if k[0] == "dma"]:
            self.free_dma.append((self.sems.pop(key), self._dcnt.pop(key)))
        self.gen += 1

    def op(self, engine, fn, reads=(), writes=()):
        reads = [t.r for t in reads]
        writes = [t.r for t in writes]
        self._wait(engine, self._deps(reads, writes))
        tok = self._tick(engine, fn(self.eng[engine]))
        self._mark(tok, reads, writes)

    def mm(self, fns, reads=(), writes=()):
        reads = [t.r for t in reads]
        writes = [t.r for t in writes]
        self._wait("pe", self._deps(reads, writes))
        pe = self.eng["pe"]
        for fn in fns[:-1]:
            fn(pe)
        tok = self._tick("pe", fns[-1](pe))
        self._mark(tok, reads, writes)

    def dma(self, queue, out_ap, in_ap, reads=(), writes=(), sem=None, accumulate=False):
        reads = [t.r for t in reads]
        writes_r = [t.r for t in writes]
        self._wait(queue, self._deps(reads, writes_r))
        s = sem if sem is not None else writes[0]
        key = ("dma", id(s.r), self.gen)
        cnts = self._dcnt
        if key not in self.sems and self.free_dma:
            self.sems[key], cnts[key] = self.free_dma.pop()
        cnts[key] = cnts.get(key, 0) + 16
        self.eng[queue].dma_start(out=out_ap, in_=in_ap).then_inc(self._sem(key), 16)
        tok = (key, cnts[key])
        self.recent[key] = cnts[key]
        self._mark(tok, reads, writes_r, accumulate=accumulate)

    def wait_all(self, engine, ts):
        deps = []
        for t in ts:
            deps.extend(t.r.w.items())
            deps.extend(t.r.rd)
        self._wait(engine, deps)


class Scope:
    def __init__(self, f):
        self.f = f
        self.es = ExitStack()

    def __enter__(self):
        self.es.__enter__()
        return self.es

    def __exit__(self, *a):
        self.f.barrier()
        return self.es.__exit__(*a)


_CONST = None


def host_consts():
    global _CONST
    if _CONST is not None:
        return _CONST
    c = {}
    p = np.arange(128)[:, None, None].astype(np.float64)
    j = np.arange(32)[None, :, None].astype(np.float64)
    cc = np.arange(128)[None, None, :].astype(np.float64)
    ph = 2 * np.pi * (32 * p + j) * (cc + 0.5) / NFFT
    c["HF"] = np.stack([np.cos(ph), -np.sin(ph)], 2).astype(NB)
    pht = ph.transpose(2, 1, 0)
    c["HG"] = np.stack([np.cos(pht), -np.sin(pht)], 2).astype(NB)
    pf = 2 * np.pi * (32 * p + j) * cc / L
    c["FF"] = np.stack([np.cos(pf), np.sin(pf), -np.sin(pf)], 2).astype(NB)
    jd = 2 * np.pi * np.outer(np.arange(32), np.arange(32)) / 32
    C4 = np.kron(np.eye(4), np.cos(jd))
    S4 = np.kron(np.eye(4), np.sin(jd))
    c["CS"] = np.stack([C4, S4, -S4], 1).astype(NB)
    cd = 2 * np.pi * np.outer(np.arange(64), np.arange(64)) / 64
    c["BD"] = np.concatenate([np.kron(np.eye(2), np.cos(cd)), np.kron(np.eye(2), np.sin(cd))], 1).astype(NB)
    c["ident"] = np.eye(128).astype(NB)
    sel = np.zeros((128, 3, 128))
    sel[:, 0, :] = 1.0
    sel[0, 1, :] = 1.0
    sel[0, 2, :] = -1.0
    c["SEL"] = sel.astype(NB)
    pos = np.arange(L, dtype=np.float64)
    t = pos / (L - 1)
    bands = np.linspace(1e-4, 15, 16)
    ang = (2 * np.pi * pos / L)[:, None] * bands[None]
    z = np.concatenate([t[:, None], np.cos(ang), -np.sin(ang)], -1)
    zT = z.T.reshape(33, 128, 32).transpose(0, 2, 1).reshape(33, L)
    c["zT"] = np.ascontiguousarray(zT).astype(np.float32)
    deltas = np.linspace(np.log(1e-2) / 1.5, np.log(1e-2) / 0.3, 512)
    decay = np.exp(-t[:, None] * np.abs(deltas)[None])
    c["decay"] = decay.reshape(128, 32, 512).astype(np.float32)
    _CONST = c
    return c


CONST_SPECS = [("HF", [128, 32, 2, 128], BF16), ("HG", [128, 32, 2, 128], BF16), ("FF", [128, 32, 3, 128], BF16),
               ("CS", [128, 3, 128], BF16), ("BD", [128, 256], BF16), ("ident", [128, 128], BF16),
               ("SEL", [128, 3, 128], BF16), ("zT", [33, L], F32), ("decay", [128, 32, 512], F32)]

IN_SPECS = [("x", [L, 1024], F32), ("gm", [128, 8], F32), ("w_in", [1024, 2048], F32), ("cw", [4, 1536], F32),
            ("f_w0", [33, 64], F32), ("f_b", [64, 3], F32), ("f_wi", [64, 2, 64], F32), ("f_fr", [64, 1], F32),
            ("f_wo", [64, 2048], F32), ("long_d", [1, 2, 512], F32), ("gfh", [128, 8], F32),
            ("w_out", [1024, 1024], F32), ("gml", [128, 8], F32), ("w_fc1", [1024, 4096], F32),
            ("w_fc2", [4096, 1024], F32), ("gfin", [1, 1024], F32)]


def build(upto=99, debug=False):
    nc = bass.Bass("TRN2", target_bir_lowering=False)
    f = FW(nc)
    D = {}
    for name, shape, dt in IN_SPECS + CONST_SPECS:
        D[name] = T(nc.dram_tensor(name, shape, dt, kind="ExternalInput").ap(), name)
    out_d = T(nc.dram_tensor("out", [L, 1024], F32, kind="ExternalOutput").ap(), "out")
    def scratch(name, shape, dt):
        k = "ExternalOutput" if (debug is True or (debug and name in debug)) else "Internal"
        return T(nc.dram_tensor(name, shape, dt, kind=k).ap(), name)

    RAW = scratch("RAW", [128, 34, 1536], BF16)
    U = scratch("U", [128, 32, 1024], BF16)
    XG = scratch("XG", [3, 128, 32, 512], BF16)
    FA = scratch("FA", [4, 128, 32, 512], BF16)
    FB = scratch("FB", [2, 128, 32, 512], BF16)
    KH = scratch("KH", [2, 2, 128, 32, 512], BF16)
    Z1 = scratch("Z1", [128, 32, 512], BF16)
    W1B = scratch("W1B", [128, 8, 4096], BF16)
    W2B = scratch("W2B", [128, 32, 1024], BF16)

    uid = [0]

    def sb(es, name, shape, dt):
        uid[0] += 1
        return T(es.enter_context(nc.sbuf_tensor("sb%d_%s" % (uid[0], name), shape, dt)).ap(), name)

    def ps(es, name, shape, dt=F32):
        uid[0] += 1
        return T(es.enter_context(nc.psum_tensor("ps%d_%s" % (uid[0], name), shape, dt)).ap(), name)

    top = ExitStack()
    ident = sb(top, "ident", [128, 128], BF16)
    CS = sb(top, "CS", [128, 3, 128], BF16)
    epsT = sb(top, "epsT", [128, 1], F32)
    yTbox = []
    scl = sb(top, "scl", [128, 2, 512], F32)
    f.dma("sp", ident.ap, D["ident"].ap, writes=[ident])
    f.dma("sp", CS.ap, D["CS"].ap, writes=[CS])
    f.op("pool", lambda e: e.memset(epsT.ap, EPS), writes=[epsT])

    def rstd_of(ss_ap, ss_t, out_ap, out_t, scale):
        f.op("act", lambda e: e.activation(out=out_ap, in_=ss_ap, func=ACTF.Sqrt, scale=scale, bias=epsT.ap[:, 0:1]),
             reads=[ss_t, epsT], writes=[out_t])
        f.op("dve", lambda e: e.reciprocal(out=out_ap, in_=out_ap), reads=[out_t], writes=[out_t])

    def phase0():
        with Scope(f) as es:
            gml = sb(es, "gml", [128, 8], F32)
            f.dma("sp", gml.ap, D["gml"].ap, writes=[gml])
            st = [sb(es, "p0s%d" % i, [128, 8, 512], F32) for i in range(2)]
            sbf = [sb(es, "p0b%d" % i, [128, 8, 512], BF16) for i in range(2)]
            w1v = D["w_fc1"].ap.rearrange("(k p) f -> p k f", p=128)
            for i in range(8):
                s, b = st[i % 2], sbf[i % 2]
                f.dma("sp", s.ap, w1v[:, :, i * 512:(i + 1) * 512], writes=[s])
                for k in range(8):
                    eng = "dve" if k % 2 == 0 else "pool"
                    f.op(eng, lambda e, k=k: e.tensor_scalar(out=b.ap[:, k, :], in0=s.ap[:, k, :], scalar1=gml.ap[:, k:k + 1],
                                                             scalar2=None, op0=ALU.mult), reads=[s, gml], writes=[b])
                f.dma("pool", W1B.ap[:, :, i * 512:(i + 1) * 512], b.ap, reads=[b], writes=[W1B], sem=b, accumulate=True)
            w2v = D["w_fc2"].ap.rearrange("(c p) d -> p c d", p=128)
            for i in range(8):
                s, b = st[i % 2], sbf[i % 2]
                s4 = s.ap.rearrange("p (a b) n -> p a (b n)", a=4)
                b4 = b.ap.rearrange("p (a b) n -> p a (b n)", a=4)
                f.dma("sp", s4, w2v[:, i * 4:(i + 1) * 4, :], writes=[s])
                f.op("dve", lambda e: e.tensor_copy(out=b4[:, 0:2], in_=s4[:, 0:2]), reads=[s], writes=[b])
                f.op("pool", lambda e: e.tensor_copy(out=b4[:, 2:4], in_=s4[:, 2:4]), reads=[s], writes=[b])
                f.dma("pool", W2B.ap[:, i * 4:(i + 1) * 4, :], b4, reads=[b], writes=[W2B], sem=b, accumulate=True)

    def phase1():
        with Scope(f) as es:
            W1 = sb(es, "W1", [128, 8, 2048], BF16)
            gm = sb(es, "gm", [128, 8], F32)
            BD = sb(es, "BD", [128, 256], BF16)
            f.dma("sp", gm.ap, D["gm"].ap, writes=[gm])
            f.dma("sp", BD.ap, D["BD"].ap, writes=[BD])
            wst = [sb(es, "wst%d" % i, [128, 2048], F32) for i in range(2)]
            wv = D["w_in"].ap.rearrange("(k p) e -> p k e", p=128)
            for k in range(8):
                s = wst[k % 2]
                f.dma("sp", s.ap, wv[:, k, :], writes=[s])
                f.op("dve" if k % 2 == 0 else "pool",
                     lambda e, k=k, s=s: e.tensor_scalar(out=W1.ap[:, k, :], in0=s.ap, scalar1=gm.ap[:, k:k + 1], scalar2=None,
                                                         op0=ALU.mult), reads=[s, gm], writes=[W1])
            xts = [sb(es, "xt%d" % i, [128, 1024], F32) for i in range(3)]
            junk = sb(es, "junk", [128, 1024], BF16)
            ss = [sb(es, "ss%d" % i, [128, 1], F32) for i in range(2)]
            rs = [sb(es, "rs%d" % i, [128, 1], F32) for i in range(2)]
            xs = [sb(es, "xs%d" % i, [128, 1024], BF16) for i in range(2)]
            hT = [sb(es, "hT%d" % i, [128, 8, 128], BF16) for i in range(2)]
            raw = [sb(es, "raw%d" % i, [128, 1536], BF16) for i in range(2)]
            ufT = [sb(es, "ufT%d" % i, [128, 4, 128], BF16) for i in range(2)]
            Ut = [sb(es, "Ut%d" % i, [128, 4, 256], BF16) for i in range(2)]
            pT = ps(es, "pT", [128, 8, 128], BF16)
            pH = [ps(es, "pH%d" % i, [128, 512]) for i in range(3)]
            pF = ps(es, "pF", [128, 4, 128])
            pU = ps(es, "pU", [128, 4, 256])
            xv = D["x"].ap.rearrange("(p j) d -> p j d", j=32)
            for j in range(32):
                xt = xts[j % 3]
                s_, r_, x_, h_, rw, uf, ut = ss[j % 2], rs[j % 2], xs[j % 2], hT[j % 2], raw[j % 2], ufT[j % 2], Ut[j % 2]
                f.dma("sp", xt.ap, xv[:, j, :], writes=[xt])
                f.op("act", lambda e: e.activation(out=junk.ap, in_=xt.ap, func=ACTF.Square, accum_out=s_.ap),
                     reads=[xt], writes=[junk, s_])
                rstd_of(s_.ap, s_, r_.ap, r_, 1.0 / 1024)
                f.op("act", lambda e: e.activation(out=x_.ap, in_=xt.ap, func=ACTF.Copy, scale=r_.ap[:, 0:1]),
                     reads=[xt, r_], writes=[x_])
                f.mm([lambda e, k=k: e.transpose(out=pT.ap[:, k, :], in_=x_.ap[:, k * 128:(k + 1) * 128], identity=ident.ap)
                      for k in range(8)], reads=[x_, ident], writes=[pT])
                f.op("dve", lambda e: e.tensor_copy(out=h_.ap, in_=pT.ap), reads=[pT], writes=[h_])
                for n in range(3):
                    f.mm([lambda e, k=k, n=n: e.matmul(pH[n].ap, lhsT=h_.ap[:, k, :], rhs=W1.ap[:, k, 512 + n * 512:1024 + n * 512],
                                                       start=(k == 0), stop=(k == 7)) for k in range(8)],
                         reads=[h_, W1], writes=[pH[n]])
                    f.op("act", lambda e, n=n: e.activation(out=rw.ap[:, n * 512:(n + 1) * 512], in_=pH[n].ap, func=ACTF.Copy),
                         reads=[pH[n]], writes=[rw])
                f.dma("pool", RAW.ap[:, j + 1, :], rw.ap, reads=[rw], writes=[RAW], sem=rw, accumulate=True)
                fns = []
                for ec in range(4):
                    for k in range(8):
                        fns.append(lambda e, k=k, ec=ec: e.matmul(pF.ap[:, ec, :], lhsT=W1.ap[:, k, ec * 128:(ec + 1) * 128],
                                                                  rhs=h_.ap[:, k, :], start=(k == 0), stop=(k == 7)))
                f.mm(fns, reads=[h_, W1], writes=[pF])
                f.op("dve", lambda e: e.tensor_copy(out=uf.ap, in_=pF.ap), reads=[pF], writes=[uf])
                f.mm([lambda e, ec=ec: e.matmul(pU.ap[:, ec, :], lhsT=uf.ap[:, ec, :], rhs=BD.ap, start=True, stop=True)
                      for ec in range(4)], reads=[uf, BD], writes=[pU])
                f.op("dve", lambda e: e.tensor_copy(out=ut.ap, in_=pU.ap), reads=[pU], writes=[ut])
                f.dma("pool", U.ap[:, j, :], ut.ap.rearrange("p a b -> p (a b)"), reads=[ut], writes=[U], sem=ut, accumulate=True)
            zrow = sb(es, "zrow", [1, 1536], BF16)
            f.op("pool", lambda e: e.memset(zrow.ap, 0.0), writes=[zrow])
            halo = T(None, "halo")
            f.dma("pool", RAW.ap[1:128, 0, :], RAW.ap[0:127, 32, :], reads=[RAW], writes=[halo])
            f.dma("pool", RAW.ap[0:127, 33, :], RAW.ap[1:128, 1, :], reads=[RAW], writes=[halo], accumulate=True)
            f.dma("pool", RAW.ap[0:1, 0, :], zrow.ap, reads=[zrow], writes=[halo], accumulate=True)
            f.dma("pool", RAW.ap[127:128, 33, :], zrow.ap, reads=[zrow], writes=[halo], accumulate=True)
            RAW.r.w.update(halo.r.w)

    def phase1b():
        with Scope(f) as es:
            cw = sb(es, "cw", [128, 4, 1536], F32)
            f.dma("sp", cw.ap, D["cw"].ap.partition_broadcast(128), writes=[cw])
            rg = [sb(es, "rg%d" % i, [128, 4, 1536], BF16) for i in range(2)]
            acc = sb(es, "cacc", [128, 2, 1536], F32)
            t1 = sb(es, "ct1", [128, 2, 1536], F32)
            t2 = sb(es, "ct2", [128, 2, 1536], F32)
            og = [sb(es, "og%d" % i, [128, 2, 1536], BF16) for i in range(2)]

            def wb(i):
                return cw.ap[:, i:i + 1, :].broadcast_to([128, 2, 1536])
            for g in range(16):
                r, o = rg[g % 2], og[g % 2]
                f.dma("sp", r.ap, RAW.ap[:, 2 * g:2 * g + 4, :], reads=[RAW], writes=[r])
                f.op("pool", lambda e: e.tensor_tensor(out=t1.ap, in0=r.ap[:, 0:2, :], in1=wb(0), op=ALU.mult), reads=[r, cw], writes=[t1])
                f.op("dve", lambda e: e.tensor_tensor(out=acc.ap, in0=r.ap[:, 1:3, :], in1=wb(1), op=ALU.mult), reads=[r, cw], writes=[acc])
                f.op("pool", lambda e: e.tensor_tensor(out=t2.ap, in0=r.ap[:, 2:4, :], in1=wb(2), op=ALU.mult), reads=[r, cw], writes=[t2])
                f.op("dve", lambda e: e.tensor_tensor(out=acc.ap, in0=acc.ap, in1=t1.ap, op=ALU.add), reads=[acc, t1], writes=[acc])
                f.op("dve", lambda e: e.tensor_tensor(out=acc.ap, in0=acc.ap, in1=t2.ap, op=ALU.add), reads=[acc, t2], writes=[acc])
                f.op("dve", lambda e: e.tensor_tensor(out=o.ap, in0=acc.ap, in1=wb(3), op=ALU.add), reads=[acc, cw], writes=[o])
                for t in range(3):
                    f.dma("pool", XG.ap[t, :, 2 * g:2 * g + 2, :], o.ap[:, :, t * 512:(t + 1) * 512], reads=[o], writes=[XG],
                          sem=o, accumulate=True)

    def relayout_view(dram_ap):
        return dram_ap.rearrange("(cp q) j n -> (q j) cp n", q=4)

    def load_split(queue, dst, dst_ap, src_ap, reads, n):
        tot = dst_ap.shape[1]
        step = tot // n
        for i in range(n):
            f.dma(queue, dst_ap[:, i * step:(i + 1) * step], src_ap[:, i * step:(i + 1) * step], reads=reads, writes=[dst],
                  sem=T(None, "ls"))

    def stage1(es, src, mats, combos, dst_slots, tagp):
        nslot = len(combos)
        pA = [ps(es, "%spA%d" % (tagp, i), [128, 512]) for i in range(nslot)]
        stg = [sb(es, "%sstg%d" % (tagp, i), [128, nslot, 4, 512], BF16) for i in range(2)]
        for j in range(32):
            sg = stg[(j // 4) % 2]
            for si, terms in enumerate(combos):
                f.mm([lambda e, mi=mi, rf=rf, ti=ti, si=si: e.matmul(pA[si].ap, lhsT=mats.ap[:, j, mi, :], rhs=rf(j),
                                                                       start=(ti == 0), stop=(ti == len(terms) - 1))
                      for ti, (mi, rf) in enumerate(terms)], reads=[src, mats], writes=[pA[si]])
                eng = "act" if (si + j) % 2 == 0 else "dve"
                if eng == "act":
                    f.op("act", lambda e, si=si: e.activation(out=sg.ap[:, si, j % 4, :], in_=pA[si].ap, func=ACTF.Copy),
                         reads=[pA[si]], writes=[sg])
                else:
                    f.op("dve", lambda e, si=si: e.tensor_copy(out=sg.ap[:, si, j % 4, :], in_=pA[si].ap), reads=[pA[si]], writes=[sg])
            if j % 4 == 3:
                for si in range(nslot):
                    f.dma("pool", FA.ap[dst_slots[si], :, j - 3:j + 1, :], sg.ap[:, si], reads=[sg], writes=[FA], sem=sg, accumulate=True)

    def phase2():
        with Scope(f) as es:
            FF = sb(es, "FF", [128, 32, 3, 128], BF16)
            f.dma("sp", FF.ap, D["FF"].ap, writes=[FF])
            Us = sb(es, "Us", [128, 32, 4, 2, 128], BF16)
            load_split("sp", Us, Us.ap.rearrange("p j a b n -> p j (a b n)"), U.ap, [U], 4)
            uc = lambda j: Us.ap[:, j, :, 0, :]
            us = lambda j: Us.ap[:, j, :, 1, :]
            with Scope(f) as es2:
                stage1(es2, Us, FF, [[(0, uc), (2, us)], [(1, uc), (0, us)]], [0, 1], "f")
        with Scope(f) as es:
            A2 = [sb(es, "fA2%d" % i, [128, 32, 512], BF16) for i in range(2)]
            for i in range(2):
                load_split("sp", A2[i], A2[i].ap, relayout_view(FA.ap[i]), [FA], 4)
            pY = [ps(es, "fpY%d" % i, [128, 512]) for i in range(2)]
            pT2 = [ps(es, "fpT%d" % i, [128, 4, 128], BF16) for i in range(2)]
            junk = sb(es, "fjunk", [128, 512], BF16)
            ssq = [sb(es, "fss%d" % i, [128, 1], F32) for i in range(2)]
            rsq = [sb(es, "frs%d" % i, [128, 1], F32) for i in range(2)]
            yn = [sb(es, "fyn%d" % i, [128, 512], BF16) for i in range(2)]
            yT = yTbox[0]
            yTv = yT.ap.rearrange("p k (d cp q) -> p k cp q d", d=32, cp=32, q=4)
            for cp in range(32):
                py, pt, s_, r_, y_ = pY[cp % 2], pT2[cp % 2], ssq[cp % 2], rsq[cp % 2], yn[cp % 2]
                f.mm([lambda e: e.matmul(py.ap, lhsT=CS.ap[:, 0, :], rhs=A2[0].ap[:, cp, :], start=True, stop=False),
                      lambda e: e.matmul(py.ap, lhsT=CS.ap[:, 2, :], rhs=A2[1].ap[:, cp, :], start=False, stop=True)],
                     reads=[CS, A2[0], A2[1]], writes=[py])
                f.op("act", lambda e: e.activation(out=junk.ap, in_=py.ap, func=ACTF.Square, accum_out=s_.ap), reads=[py], writes=[junk, s_])
                rstd_of(s_.ap, s_, r_.ap, r_, 1.0 / (512.0 * 512.0 * 512.0))
                f.op("dve", lambda e: e.tensor_scalar(out=y_.ap, in0=py.ap, scalar1=r_.ap[:, 0:1], scalar2=1.0 / 512.0, op0=ALU.mult,
                                                      op1=ALU.mult), reads=[py, r_], writes=[y_])
                f.mm([lambda e, k=k: e.transpose(out=pt.ap[:, k, :], in_=y_.ap[:, k * 128:(k + 1) * 128], identity=ident.ap)
                      for k in range(4)], reads=[y_, ident], writes=[pt])
                f.op("act", lambda e: e.activation(out=yTv[:, 0:4, cp], in_=pt.ap.rearrange("p k (q d) -> p k q d", q=4),
                                                   func=ACTF.Copy), reads=[pt], writes=[yT])

    def phase3():
        with Scope(f) as es:
            hcur = [sb(es, "fh%d" % i, [64, L], F32) for i in range(2)]
            wo = sb(es, "fwo", [64, 2, 2, 512], F32)
            wsd = sb(es, "fwsd", [64, 2, 2, 512], F32)
            HF = sb(es, "HF3", [128, 32, 2, 128], BF16)
            SEL = sb(es, "SEL", [128, 3, 128], BF16)
            dl = sb(es, "dl", [1, 2, 512], F32)
            f.dma("sp", wo.ap.rearrange("p a b n -> p (a b n)"), D["f_wo"].ap, writes=[wo])
            f.dma("sp", HF.ap, D["HF"].ap, writes=[HF])
            f.dma("sp", SEL.ap, D["SEL"].ap, writes=[SEL])
            f.dma("sp", dl.ap, D["long_d"].ap, writes=[dl])
            for o in range(2):
                f.op("dve", lambda e, o=o: e.tensor_tensor(out=wsd.ap[:, o, 0, :], in0=wo.ap[:, o, 0, :], in1=wo.ap[:, o, 1, :], op=ALU.add),
                     reads=[wo], writes=[wsd])
                f.op("dve", lambda e, o=o: e.tensor_tensor(out=wsd.ap[:, o, 1, :], in0=wo.ap[:, o, 0, :], in1=wo.ap[:, o, 1, :], op=ALU.subtract),
                     reads=[wo], writes=[wsd])
            with Scope(f) as em:
                zT = sb(em, "zT", [33, L], F32)
                w0 = sb(em, "fw0", [33, 64], F32)
                fb = sb(em, "fb", [64, 3], F32)
                wi = sb(em, "fwi", [64, 2, 64], F32)
                fr = sb(em, "ffr", [64, 1], F32)
                s1 = sb(em, "fs1", [64, 1], F32)
                s2 = sb(em, "fs2", [64, 3], F32)
                f.dma("sp", zT.ap, D["zT"].ap, writes=[zT])
                f.dma("sp", w0.ap, D["f_w0"].ap, writes=[w0])
                f.dma("sp", fb.ap, D["f_b"].ap, writes=[fb])
                f.dma("sp", wi.ap, D["f_wi"].ap, writes=[wi])
                f.dma("sp", fr.ap, D["f_fr"].ap, writes=[fr])
                inv2pi = float(1.0 / (2 * np.pi))
                f.op("dve", lambda e: e.tensor_scalar(out=s1.ap, in0=fr.ap, scalar1=inv2pi, scalar2=None, op0=ALU.mult), reads=[fr], writes=[s1])
                f.op("dve", lambda e: e.tensor_scalar(out=s2.ap, in0=fb.ap, scalar1=s1.ap[:, 0:1], scalar2=16.0, op0=ALU.mult, op1=ALU.add),
                     reads=[fb, s1], writes=[s2])
                pm = [ps(em, "fpm%d" % i, [64, 512]) for i in range(2)]
                rr = [sb(em, "frr%d" % i, [64, 512], F32) for i in range(2)]
                ki = [sb(em, "fki%d" % i, [64, 512], I32) for i in range(2)]
                kf = [sb(em, "fkf%d" % i, [64, 512], F32) for i in range(2)]
                for layer in range(3):
                    src = zT if layer == 0 else hcur[(layer - 1) % 2]
                    dst = hcur[layer % 2]
                    lw = w0.ap if layer == 0 else wi.ap[:, layer - 1, :]
                    lwt = w0 if layer == 0 else wi
                    kdim = 33 if layer == 0 else 64
                    for c8 in range(8):
                        i = c8 % 2
                        cs = slice(c8 * 512, (c8 + 1) * 512)
                        f.mm([lambda e: e.matmul(pm[i].ap, lhsT=lw, rhs=src.ap[0:kdim, cs], start=True, stop=True)], reads=[src, lwt], writes=[pm[i]])
                        f.op("dve", lambda e: e.tensor_scalar(out=rr[i].ap, in0=pm[i].ap, scalar1=s1.ap[:, 0:1], scalar2=s2.ap[:, layer:layer + 1],
                                                              op0=ALU.mult, op1=ALU.add), reads=[pm[i], s1, s2], writes=[rr[i]])
                        f.op("dve", lambda e: e.tensor_copy(out=ki[i].ap, in_=rr[i].ap), reads=[rr[i]], writes=[ki[i]])
                        f.op("dve", lambda e: e.tensor_copy(out=kf[i].ap, in_=ki[i].ap), reads=[ki[i]], writes=[kf[i]])
                        f.op("dve", lambda e: e.tensor_tensor(out=rr[i].ap, in0=rr[i].ap, in1=kf[i].ap, op=ALU.subtract), reads=[rr[i], kf[i]], writes=[rr[i]])
                        f.op("dve", lambda e: e.tensor_single_scalar(out=kf[i].ap, in_=rr[i].ap, scalar=0.5, op=ALU.is_gt), reads=[rr[i]], writes=[kf[i]])
                        f.op("dve", lambda e: e.tensor_tensor(out=rr[i].ap, in0=rr[i].ap, in1=kf[i].ap, op=ALU.subtract), reads=[rr[i], kf[i]], writes=[rr[i]])
                        f.op("act", lambda e: e.activation(out=dst.ap[:, cs], in_=rr[i].ap, func=ACTF.Sin, scale=float(2 * np.pi)),
                             reads=[rr[i]], writes=[dst])
            h3 = hcur[0]
            for o in range(2):
                with Scope(f) as eo:
                    sd = [sb(eo, "fsd%d" % i, [128, 32, 512], BF16) for i in range(2)]
                    sq = [sb(eo, "fsq%d" % i, [128, 512], BF16) for i in range(2)]
                    dec = [sb(eo, "fdec%d" % i, [128, 512], F32) for i in range(2)]
                    pk = [ps(eo, "fpk%d" % i, [128, 512]) for i in range(2)]
                    pn = ps(eo, "fpn", [128, 512])
                    sqt = sb(eo, "fsqt", [128, 512], F32)
                    tmp0 = sb(eo, "ftmp0", [1, 512], F32)
                    nsq = 0
                    for j in range(32):
                        dc = dec[j % 2]
                        f.dma("sp", dc.ap, D["decay"].ap[:, j, :], writes=[dc])
                        for v in range(2):
                            f.mm([lambda e, v=v: e.matmul(pk[v].ap, lhsT=h3.ap[:, j * 128:(j + 1) * 128], rhs=wsd.ap[:, o, v, :], start=True, stop=True)],
                                 reads=[h3, wsd], writes=[pk[v]])
                            f.op("dve", lambda e, v=v: e.tensor_tensor(out=sd[v].ap[:, j, :], in0=pk[v].ap, in1=dc.ap, op=ALU.mult),
                                 reads=[pk[v], dc], writes=[sd[v]])
                            q_ = sq[nsq % 2]
                            nsq += 1
                            f.op("act", lambda e, v=v: e.activation(out=q_.ap, in_=sd[v].ap[:, j, :], func=ACTF.Square), reads=[sd[v]], writes=[q_])
                            first = (j == 0 and v == 0)
                            fns = [lambda e: e.matmul(pn.ap, lhsT=SEL.ap[:, 0, :], rhs=q_.ap, start=first, stop=False)]
                            if j == 0:
                                fns.append(lambda e, v=v: e.matmul(pn.ap, lhsT=SEL.ap[:, 1 + v, :], rhs=q_.ap, start=False, stop=False))
                            if j == 31 and v == 1:
                                fns = [lambda e: e.matmul(pn.ap, lhsT=SEL.ap[:, 0, :], rhs=q_.ap, start=False, stop=True)]
                            f.mm(fns, reads=[q_, SEL], writes=[pn])
                    f.op("act", lambda e: e.activation(out=sqt.ap, in_=pn.ap, func=ACTF.Sqrt, scale=0.5, bias=epsT.ap[:, 0:1]),
                         reads=[pn, epsT], writes=[sqt])
                    f.op("dve", lambda e, o=o: e.reciprocal(out=scl.ap[:, o, :], in_=sqt.ap), reads=[sqt], writes=[scl])
                    f.op("dve", lambda e, o=o: e.tensor_tensor(out=tmp0.ap, in0=dl.ap[0:1, o, :], in1=sqt.ap[0:1, :], op=ALU.mult),
                         reads=[dl, sqt], writes=[tmp0])
                    f.op("dve", lambda e: e.tensor_tensor(out=sd[0].ap[0:1, 0, :], in0=sd[0].ap[0:1, 0, :], in1=tmp0.ap, op=ALU.add),
                         reads=[sd[0], tmp0], writes=[sd[0]])
                    f.op("dve", lambda e, o=o: e.tensor_scalar(out=scl.ap[:, o, :], in0=scl.ap[:, o, :], scalar1=2.0 / NFFT, scalar2=None, op0=ALU.mult),
                         reads=[scl], writes=[scl])
                    with Scope(f) as es2:
                        stage1(es2, sd[0], HF, [[(0, lambda j: sd[0].ap[:, j, :])], [(1, lambda j: sd[0].ap[:, j, :])]], [0, 1], "k%da" % o)
                    with Scope(f) as es2:
                        stage1(es2, sd[1], HF, [[(0, lambda j: sd[1].ap[:, j, :])], [(1, lambda j: sd[1].ap[:, j, :])]], [2, 3], "k%db" % o)
                with Scope(f) as es2:
                    A2 = [sb(es2, "kA2%d" % i, [128, 16, 512], BF16) for i in range(4)]
                    kst = [sb(es2, "kst%d" % i, [128, 2, 4, 512], BF16) for i in range(2)]
                    pkk = [ps(es2, "kpk%d" % i, [128, 512]) for i in range(2)]
                    for half in range(2):
                        for i in range(4):
                            load_split("sp", A2[i], A2[i].ap, relayout_view(FA.ap[i])[:, half * 16:(half + 1) * 16, :], [FA], 2)
                        for c16 in range(16):
                            cp = half * 16 + c16
                            st_ = kst[(cp // 4) % 2]
                            f.mm([lambda e: e.matmul(pkk[0].ap, lhsT=CS.ap[:, 0, :], rhs=A2[0].ap[:, c16, :], start=True, stop=False),
                                  lambda e: e.matmul(pkk[0].ap, lhsT=CS.ap[:, 1, :], rhs=A2[1].ap[:, c16, :], start=False, stop=True)],
                                 reads=[CS, A2[0], A2[1]], writes=[pkk[0]])
                            f.mm([lambda e: e.matmul(pkk[1].ap, lhsT=CS.ap[:, 0, :], rhs=A2[3].ap[:, c16, :], start=True, stop=False),
                                  lambda e: e.matmul(pkk[1].ap, lhsT=CS.ap[:, 2, :], rhs=A2[2].ap[:, c16, :], start=False, stop=True)],
                                 reads=[CS, A2[2], A2[3]], writes=[pkk[1]])
                            f.op("act", lambda e: e.activation(out=st_.ap[:, 0, cp % 4, :], in_=pkk[0].ap, func=ACTF.Copy), reads=[pkk[0]], writes=[st_])
                            f.op("dve", lambda e: e.tensor_copy(out=st_.ap[:, 1, cp % 4, :], in_=pkk[1].ap), reads=[pkk[1]], writes=[st_])
                            if cp % 4 == 3:
                                for r in range(2):
                                    f.dma("pool", KH.ap[o, r, :, cp - 3:cp + 1, :], st_.ap[:, r], reads=[st_], writes=[KH], sem=st_, accumulate=True)

    def phase_h(o):
        with Scope(f) as es:
            HF = sb(es, "HFh", [128, 32, 2, 128], BF16)
            z = sb(es, "hz", [128, 32, 512], BF16)
            f.dma("sp", HF.ap, D["HF"].ap, writes=[HF])
            if o == 0:
                load_split("sp", z, z.ap, XG.ap[2], [XG], 2)
            else:
                load_split("sp", z, z.ap, Z1.ap, [Z1], 2)
            stage1(es, z, HF, [[(0, lambda j: z.ap[:, j, :])], [(1, lambda j: z.ap[:, j, :])]], [0, 1], "h%d" % o)
        with Scope(f) as es:
            A2 = [sb(es, "hA2%d" % i, [128, 32, 512], BF16) for i in range(2)]
            for i in range(2):
                load_split("sp", A2[i], A2[i].ap, relayout_view(FA.ap[i]), [FA], 4)
            kk = [[sb(es, "hk%d%d" % (i, r), [128, 4, 512], BF16) for r in range(2)] for i in range(2)]
            pX = [ps(es, "hpX%d" % i, [128, 512]) for i in range(2)]
            pB = [ps(es, "hpB%d" % i, [128, 512]) for i in range(2)]
            tt = [sb(es, "htt%d" % i, [128, 512], F32) for i in range(4)]
            Y = [[sb(es, "hY%d%d" % (i, r), [128, 512], BF16) for r in range(2)] for i in range(2)]
            bst = [sb(es, "hbst%d" % i, [128, 2, 4, 512], BF16) for i in range(2)]
            FBv = [relayout_view(FB.ap[r]) for r in range(2)]
            for cp in range(32):
                g = cp // 4
                kb = kk[g % 2]
                if cp % 4 == 0:
                    for r in range(2):
                        f.dma("sp", kb[r].ap, KH.ap[o, r, :, cp:cp + 4, :], reads=[KH], writes=[kb[r]])
                kr = kb[0].ap[:, cp % 4, :]
                kim = kb[1].ap[:, cp % 4, :]
                f.mm([lambda e: e.matmul(pX[0].ap, lhsT=CS.ap[:, 0, :], rhs=A2[0].ap[:, cp, :], start=True, stop=False),
                      lambda e: e.matmul(pX[0].ap, lhsT=CS.ap[:, 1, :], rhs=A2[1].ap[:, cp, :], start=False, stop=True)],
                     reads=[CS, A2[0], A2[1]], writes=[pX[0]])
                f.mm([lambda e: e.matmul(pX[1].ap, lhsT=CS.ap[:, 0, :], rhs=A2[1].ap[:, cp, :], start=True, stop=False),
                      lambda e: e.matmul(pX[1].ap, lhsT=CS.ap[:, 2, :], rhs=A2[0].ap[:, cp, :], start=False, stop=True)],
                     reads=[CS, A2[0], A2[1]], writes=[pX[1]])
                f.op("dve", lambda e: e.tensor_tensor(out=tt[0].ap, in0=pX[0].ap, in1=kr, op=ALU.mult), reads=[pX[0], kb[0]], writes=[tt[0]])
                f.op("dve", lambda e: e.tensor_tensor(out=tt[1].ap, in0=pX[1].ap, in1=kim, op=ALU.mult), reads=[pX[1], kb[1]], writes=[tt[1]])
                f.op("dve", lambda e: e.tensor_tensor(out=tt[2].ap, in0=pX[0].ap, in1=kim, op=ALU.mult), reads=[pX[0], kb[1]], writes=[tt[2]])
                f.op("dve", lambda e: e.tensor_tensor(out=tt[3].ap, in0=pX[1].ap, in1=kr, op=ALU.mult), reads=[pX[1], kb[0]], writes=[tt[3]])
                yy = Y[cp % 2]
                f.op("pool", lambda e: e.tensor_tensor(out=yy[0].ap, in0=tt[0].ap, in1=tt[1].ap, op=ALU.subtract), reads=[tt[0], tt[1]], writes=[yy[0]])
                f.op("pool", lambda e: e.tensor_tensor(out=yy[1].ap, in0=tt[2].ap, in1=tt[3].ap, op=ALU.add), reads=[tt[2], tt[3]], writes=[yy[1]])
                f.mm([lambda e: e.matmul(pB[0].ap, lhsT=CS.ap[:, 0, :], rhs=yy[0].ap, start=True, stop=False),
                      lambda e: e.matmul(pB[0].ap, lhsT=CS.ap[:, 2, :], rhs=yy[1].ap, start=False, stop=True)],
                     reads=[CS, yy[0], yy[1]], writes=[pB[0]])
                f.mm([lambda e: e.matmul(pB[1].ap, lhsT=CS.ap[:, 1, :], rhs=yy[0].ap, start=True, stop=False),
                      lambda e: e.matmul(pB[1].ap, lhsT=CS.ap[:, 0, :], rhs=yy[1].ap, start=False, stop=True)],
                     reads=[CS, yy[0], yy[1]], writes=[pB[1]])
                bs = bst[g % 2]
                f.op("act", lambda e: e.activation(out=bs.ap[:, 0, cp % 4, :], in_=pB[0].ap, func=ACTF.Copy), reads=[pB[0]], writes=[bs])
                f.op("act", lambda e: e.activation(out=bs.ap[:, 1, cp % 4, :], in_=pB[1].ap, func=ACTF.Copy), reads=[pB[1]], writes=[bs])
                if cp % 4 == 3:
                    for r in range(2):
                        f.dma("pool", FBv[r][:, cp - 3:cp + 1, :], bs.ap[:, r], reads=[bs], writes=[FB], sem=bs, accumulate=True)
        with Scope(f) as es:
            HG = sb(es, "HGh", [128, 32, 2, 128], BF16)
            Bc = [sb(es, "hBc%d" % i, [128, 16, 512], BF16) for i in range(2)]
            gt = sb(es, "hgt", [128, 32, 512], BF16)
            f.dma("sp", HG.ap, D["HG"].ap, writes=[HG])
            load_split("sp", gt, gt.ap, XG.ap[o], [XG], 2)
            for h in range(4):
                eng = "dve" if h % 2 == 0 else "pool"
                f.op(eng, lambda e, h=h: e.tensor_tensor(out=gt.ap[:, h * 8:(h + 1) * 8, :], in0=gt.ap[:, h * 8:(h + 1) * 8, :],
                                                         in1=scl.ap[:, o:o + 1, :].broadcast_to([128, 8, 512]), op=ALU.mult),
                     reads=[gt, scl], writes=[gt])
            gs = gt
            pY = [ps(es, "hpY%d" % i, [128, 512]) for i in range(2)]
            if o == 0:
                zst = [sb(es, "hzst%d" % i, [128, 4, 512], BF16) for i in range(2)]
            else:
                yf = [sb(es, "hyf%d" % i, [128, 512], F32) for i in range(2)]
                junk = sb(es, "hjunk", [128, 512], BF16)
                ssq = [sb(es, "hss%d" % i, [128, 1], F32) for i in range(2)]
                rsq = [sb(es, "hrs%d" % i, [128, 1], F32) for i in range(2)]
                yn = [sb(es, "hyn%d" % i, [128, 512], BF16) for i in range(2)]
                pT2 = [ps(es, "hpT%d" % i, [128, 4, 128], BF16) for i in range(2)]
                yT = yTbox[0]
                yTv = yT.ap.rearrange("p k (pp j) -> p k j pp", j=32)
            for j in range(32):
                if j % 16 == 0:
                    for r in range(2):
                        load_split("sp", Bc[r], Bc[r].ap, FB.ap[r][:, j:j + 16, :], [FB], 2)
                jl = j % 16
                py = pY[j % 2]
                f.mm([lambda e: e.matmul(py.ap, lhsT=HG.ap[:, j, 0, :], rhs=Bc[0].ap[:, jl, :], start=True, stop=False),
                      lambda e: e.matmul(py.ap, lhsT=HG.ap[:, j, 1, :], rhs=Bc[1].ap[:, jl, :], start=False, stop=True)],
                     reads=[HG, Bc[0], Bc[1]], writes=[py])
                if o == 0:
                    zs = zst[(j // 4) % 2]
                    f.op("dve", lambda e: e.tensor_tensor(out=zs.ap[:, j % 4, :], in0=py.ap, in1=gs.ap[:, j, :], op=ALU.mult),
                         reads=[py, gs], writes=[zs])
                    if j % 4 == 3:
                        f.dma("pool", Z1.ap[:, j - 3:j + 1, :], zs.ap, reads=[zs], writes=[Z1], sem=zs, accumulate=True)
                else:
                    y_, s_, r_, n_, pt = yf[j % 2], ssq[j % 2], rsq[j % 2], yn[j % 2], pT2[j % 2]
                    f.op("dve", lambda e: e.tensor_tensor(out=y_.ap, in0=py.ap, in1=gs.ap[:, j, :], op=ALU.mult), reads=[py, gs], writes=[y_])
                    f.op("act", lambda e: e.activation(out=junk.ap, in_=y_.ap, func=ACTF.Square, accum_out=s_.ap), reads=[y_], writes=[junk, s_])
                    rstd_of(s_.ap, s_, r_.ap, r_, 1.0 / 512.0)
                    f.op("act", lambda e: e.activation(out=n_.ap, in_=y_.ap, func=ACTF.Copy, scale=r_.ap[:, 0:1]), reads=[y_, r_], writes=[n_])
                    f.mm([lambda e, k=k: e.transpose(out=pt.ap[:, k, :], in_=n_.ap[:, k * 128:(k + 1) * 128], identity=ident.ap)
                          for k in range(4)], reads=[n_, ident], writes=[pt])
                    f.op("dve", lambda e: e.tensor_copy(out=yTv[:, 4:8, j, :], in_=pt.ap), reads=[pt], writes=[yT])

    def phase6():
        with Scope(f) as es:
            yT = yTbox[0]
            Wo = sb(es, "Wo", [128, 8, 1024], BF16)
            gfh = sb(es, "gfh", [128, 8], F32)
            gfin = sb(es, "gfin", [128, 1024], F32)
            f.dma("sp", gfh.ap, D["gfh"].ap, writes=[gfh])
            f.dma("sp", gfin.ap, D["gfin"].ap.partition_broadcast(128).rearrange("p a n -> p (a n)"), writes=[gfin])
            with Scope(f) as es2:
                wst = [sb(es2, "wost%d" % i, [128, 1024], F32) for i in range(2)]
                wv = D["w_out"].ap.rearrange("(k p) e -> p k e", p=128)
                for k in range(8):
                    s = wst[k % 2]
                    f.dma("sp", s.ap, wv[:, k, :], writes=[s])
                    f.op("dve", lambda e, k=k, s=s: e.tensor_scalar(out=Wo.ap[:, k, :], in0=s.ap, scalar1=gfh.ap[:, k:k + 1], scalar2=None,
                                                                    op0=ALU.mult), reads=[s, gfh], writes=[Wo])
            aT = sb(es, "aT", [128, 32, 512], BF16)
            hmT = sb(es, "hmT", [128, 8, 512], BF16)
            x1s = [sb(es, "x1s%d" % i, [128, 4, 1024], F32) for i in range(2)]
            W1s = [sb(es, "W1s%d" % i, [128, 8, 256], BF16) for i in range(2)]
            W2s = [sb(es, "W2s%d" % i, [128, 4, 512], BF16) for i in range(2)]
            rl = [sb(es, "rl%d" % i, [128, 512], BF16) for i in range(2)]
            hm = [sb(es, "hm%d" % i, [128, 1024], BF16) for i in range(2)]
            junk = sb(es, "mjunk", [128, 1024], BF16)
            ssq = [sb(es, "mss%d" % i, [128, 1], F32) for i in range(2)]
            rsq = [sb(es, "mrs%d" % i, [128, 1], F32) for i in range(2)]
            pw = [ps(es, "mpw%d" % i, [128, 512]) for i in range(2)]
            pT = ps(es, "mpT", [128, 8, 128], BF16)
            p2 = [ps(es, "mp2%d" % i, [128, 512]) for i in range(4)]
            xls = [[T(None, "xl") for _ in range(4)] for _ in range(2)]
            xss = [[T(None, "xs") for _ in range(4)] for _ in range(2)]
            xrows = D["x"].ap.rearrange("(n p) d -> n p d", p=128)
            orows = out_d.ap.rearrange("(n p) d -> n p d", p=128)
            n1 = 0
            n2 = 0
            for tb in range(8):
                xb = x1s[tb % 2]
                for s in range(4):
                    tt_ = tb * 4 + s
                    f.dma("sp", xb.ap[:, s, :], xrows[tt_], writes=[xb], sem=xls[tb % 2][s])
                    for dh in range(2):
                        f.mm([lambda e, k=k, dh=dh: e.matmul(pw[dh].ap, lhsT=yT.ap[:, k, tt_ * 128:(tt_ + 1) * 128],
                                                             rhs=Wo.ap[:, k, dh * 512:(dh + 1) * 512], start=(k == 0), stop=(k == 7))
                              for k in range(8)], reads=[yT, Wo], writes=[pw[dh]])
                        f.op("dve", lambda e, dh=dh: e.tensor_tensor(out=xb.ap[:, s, dh * 512:(dh + 1) * 512], in0=pw[dh].ap,
                                                                     in1=xb.ap[:, s, dh * 512:(dh + 1) * 512], op=ALU.add),
                             reads=[pw[dh], xb], writes=[xb])
                    s_, r_, h_ = ssq[s % 2], rsq[s % 2], hm[s % 2]
                    f.op("act", lambda e: e.activation(out=junk.ap, in_=xb.ap[:, s, :], func=ACTF.Square, accum_out=s_.ap), reads=[xb], writes=[junk, s_])
                    rstd_of(s_.ap, s_, r_.ap, r_, 1.0 / 1024)
                    f.op("act", lambda e: e.activation(out=h_.ap, in_=xb.ap[:, s, :], func=ACTF.Copy, scale=r_.ap[:, 0:1]), reads=[xb, r_], writes=[h_])
                    f.mm([lambda e, k=k: e.transpose(out=pT.ap[:, k, :], in_=h_.ap[:, k * 128:(k + 1) * 128], identity=ident.ap)
                          for k in range(8)], reads=[h_, ident], writes=[pT])
                    f.op("dve", lambda e: e.tensor_copy(out=hmT.ap[:, :, s * 128:(s + 1) * 128], in_=pT.ap), reads=[pT], writes=[hmT])
                for fg in range(16):
                    ws = W1s[n1 % 2]
                    n1 += 1
                    f.dma("sp", ws.ap, W1B.ap[:, :, fg * 256:(fg + 1) * 256], reads=[W1B], writes=[ws])
                    for fl in range(2):
                        fc = fg * 2 + fl
                        pp = pw[fc % 2]
                        f.mm([lambda e, k=k: e.matmul(pp.ap, lhsT=ws.ap[:, k, fl * 128:(fl + 1) * 128], rhs=hmT.ap[:, k, :],
                                                      start=(k == 0), stop=(k == 7)) for k in range(8)], reads=[ws, hmT], writes=[pp])
                        rr_ = rl[fc % 2]
                        f.op("act", lambda e: e.activation(out=rr_.ap, in_=pp.ap, func=ACTF.Relu), reads=[pp], writes=[rr_])
                        f.op("pool", lambda e: e.tensor_tensor(out=aT.ap[:, fc, :], in0=rr_.ap, in1=rr_.ap, op=ALU.mult), reads=[rr_], writes=[aT])
                for dh in range(2):
                    for fg in range(8):
                        ws = W2s[n2 % 2]
                        n2 += 1
                        f.dma("sp", ws.ap, W2B.ap[:, fg * 4:(fg + 1) * 4, dh * 512:(dh + 1) * 512], reads=[W2B], writes=[ws])
                        for s in range(4):
                            f.mm([lambda e, fl=fl: e.matmul(p2[s].ap, lhsT=aT.ap[:, fg * 4 + fl, s * 128:(s + 1) * 128], rhs=ws.ap[:, fl, :],
                                                            start=(fg == 0 and fl == 0), stop=(fg == 7 and fl == 3)) for fl in range(4)],
                                 reads=[aT, ws], writes=[p2[s]])
                    for s in range(4):
                        f.op("dve", lambda e: e.tensor_tensor(out=xb.ap[:, s, dh * 512:(dh + 1) * 512], in0=p2[s].ap,
                                                              in1=xb.ap[:, s, dh * 512:(dh + 1) * 512], op=ALU.add), reads=[p2[s], xb], writes=[xb])
                for s in range(4):
                    s_, r_ = ssq[s % 2], rsq[s % 2]
                    f.op("act", lambda e: e.activation(out=junk.ap, in_=xb.ap[:, s, :], func=ACTF.Square, accum_out=s_.ap), reads=[xb], writes=[junk, s_])
                    rstd_of(s_.ap, s_, r_.ap, r_, 1.0 / 1024)
                    f.op("act", lambda e: e.activation(out=xb.ap[:, s, :], in_=xb.ap[:, s, :], func=ACTF.Copy, scale=r_.ap[:, 0:1]), reads=[xb, r_], writes=[xb])
                    f.op("pool", lambda e: e.tensor_tensor(out=xb.ap[:, s, :], in0=xb.ap[:, s, :], in1=gfin.ap, op=ALU.mult), reads=[xb, gfin], writes=[xb])
                    f.dma("pool", orows[tb * 4 + s], xb.ap[:, s, :], reads=[xb], writes=[out_d], sem=xss[tb % 2][s], accumulate=True)

    def alloc_yT():
        yTbox.append(sb(top, "yT", [128, 8, L], BF16))
    phases = [("p0", phase0), ("p1", phase1), ("p1b", phase1b), ("p3", phase3), ("yT", alloc_yT), ("p2", phase2),
              ("p4", lambda: phase_h(0)), ("p5", lambda: phase_h(1)), ("p6", phase6)]
    for i, (nm, fn) in enumerate(phases):
        if i <= upto:
            fn()
    f.wait_all("pool", [out_d, RAW, U, XG, FA, FB, KH, Z1, W1B, W2B])
    top.close()
    return nc


def make_in_maps(inputs):
    C = host_consts()
    g = {k: np.asarray(v, dtype=np.float32) for k, v in inputs.items()}

    def pk(v):
        return np.ascontiguousarray(v.reshape(8, 128).T)
    shared = {
        "gm": pk(g["g_mix"][0]), "w_in": g["w_in"][0], "cw": np.concatenate([g["conv_w"][0], g["conv_b"]], 0),
        "f_w0": g["filt_w0"][0],
        "f_b": np.ascontiguousarray(np.stack([g["filt_b0"][0], g["filt_b_inner"][0, 0], g["filt_b_inner"][0, 1]], 1)),
        "f_wi": np.ascontiguousarray(g["filt_w_inner"][0].transpose(1, 0, 2)),
        "f_fr": np.ascontiguousarray(g["filt_freq"][0][:, None]),
        "f_wo": g["filt_w_out"][0], "long_d": g["long_d"],
        "gfh": pk(np.concatenate([g["g_fourier"][0], g["g_hyena"][0]])),
        "w_out": g["w_out"][0], "gml": pk(g["g_mlp"][0]), "w_fc1": g["w_fc1"][0], "w_fc2": g["w_fc2"][0],
        "gfin": g["g_final"][None, :],
    }
    shared.update({k: C[k] for k, _, _ in CONST_SPECS})
    shared = {k: np.ascontiguousarray(v) for k, v in shared.items()}
    maps = []
    for b in range(8):
        m = dict(shared)
        m["x"] = np.ascontiguousarray(g["x"][b])
        maps.append(m)
    return maps


def kernel(**inputs):
    nc = build()
    in_maps = make_in_maps(inputs)
    res = run_bass_kernel_spmd(nc, in_maps, core_ids=list(range(8)))
    return np.stack([np.asarray(r["out"], dtype=np.float32) for r in res.results], 0)
```

```python
import numpy as np
import ml_dtypes
from contextlib import ExitStack
import concourse.bass as bass
import concourse.mybir as mybir
from concourse.bass_utils import run_bass_kernel_spmd

F32 = mybir.dt.float32
BF16 = mybir.dt.bfloat16
I32 = mybir.dt.int32
ALU = mybir.AluOpType
ACTF = mybir.ActivationFunctionType
EPOCH = 8000
L = 4096
NFFT = 8192
EPS = 1e-5
NB = ml_dtypes.bfloat16


class Res:
    __slots__ = ("name", "w", "rd")

    def __init__(self, name):
        self.name = name
        self.w = {}
        self.rd = []


class T:
    __slots__ = ("ap", "r")

    def __init__(self, ap, name):
        self.ap = ap
        self.r = Res(name)


class FW:
    def __init__(self, nc):
        self.nc = nc
        self.eng = {"pe": nc.tensor, "act": nc.scalar, "dve": nc.vector, "pool": nc.gpsimd, "sp": nc.sync}
        self.sems = {}
        self.cnt = {k: 0 for k in ("pe", "act", "dve", "pool")}
        self.waited = {k: {} for k in self.eng}
        self.nsem = 0
        self.recent = {}
        self.gen = 0
        self.free_dma = []
        self._dcnt = {}

    def _sem(self, key):
        s = self.sems.get(key)
        if s is None:
            s = self.nc.alloc_semaphore("s%d" % self.nsem)
            self.nsem += 1
            self.sems[key] = s
        return s

    def _wait(self, engine, deps):
        w = self.waited[engine]
        best = {}
        for key, val in deps:
            if engine == "pe" and key[0] == "pe":
                continue
            if key[0] == "dma" and key[2] < self.gen:
                continue
            if w.get(key, 0) < val and best.get(key, 0) < val:
                best[key] = val
        for key, val in best.items():
            self.eng[engine].wait_ge(self._sem(key), val)
            w[key] = val

    @staticmethod
    def _deps(reads, writes):
        deps = []
        for r in reads:
            deps.extend(r.w.items())
        for r in writes:
            deps.extend(r.w.items())
            deps.extend(r.rd)
        return deps

    @staticmethod
    def _mark(tok, reads, writes, accumulate=False):
        for r in writes:
            if r.w.get(tok[0], 0) < tok[1]:
                r.w[tok[0]] = tok[1]
            r.rd = []
        for r in reads:
            r.rd.append(tok)
            if len(r.rd) > 48:
                d = {}
                for k, v in r.rd:
                    if d.get(k, 0) < v:
                        d[k] = v
                r.rd = list(d.items())

    def _tick(self, engine, inst):
        c = self.cnt[engine]
        self.cnt[engine] = c + 1
        key = (engine, c // EPOCH)
        inst.then_inc(self._sem(key), 1)
        self.recent[key] = c % EPOCH + 1
        return (key, c % EPOCH + 1)

    def barrier(self):
        toks = list(self.recent.items())
        for eng in self.eng:
            self._wait(eng, toks)
        self.recent = {}
        for key in [k for k in self.sems if k[0] == "dma"]:
            self.free_dma.append((self.sems.pop(key), self._dcnt.pop(key)))
        self.gen += 1

    def op(self, engine, fn, reads=(), writes=()):
        reads = [t.r for t in reads]
        writes = [t.r for t in writes]
        self._wait(engine, self._deps(reads, writes))
        tok = self._tick(engine, fn(self.eng[engine]))
        self._mark(tok, reads, writes)

    def mm(self, fns, reads=(), writes=()):
        reads = [t.r for t in reads]
        writes = [t.r for t in writes]
        self._wait("pe", self._deps(reads, writes))
        pe = self.eng["pe"]
        for fn in fns[:-1]:
            fn(pe)
        tok = self._tick("pe", fns[-1](pe))
        self._mark(tok, reads, writes)

    def dma(self, queue, out_ap, in_ap, reads=(), writes=(), sem=None, accumulate=False):
        reads = [t.r for t in reads]
        writes_r = [t.r for t in writes]
        self._wait(queue, self._deps(reads, writes_r))
        s = sem if sem is not None else writes[0]
        key = ("dma", id(s.r), self.gen)
        cnts = self._dcnt
        if key not in self.sems and self.free_dma:
            self.sems[key], cnts[key] = self.free_dma.pop()
        cnts[key] = cnts.get(key, 0) + 16
        self.eng[queue].dma_start(out=out_ap, in_=in_ap).then_inc(self._sem(key), 16)
        tok = (key, cnts[key])
        self.recent[key] = cnts[key]
        self._mark(tok, reads, writes_r, accumulate=accumulate)

    def wait_all(self, engine, ts):
        deps = []
        for t in ts:
            deps.extend(t.r.w.items())
            deps.extend(t.r.rd)
        self._wait(engine, deps)


class Scope:
    def __init__(self, f):
        self.f = f
        self.es = ExitStack()

    def __enter__(self):
        self.es.__enter__()
        return self.es

    def __exit__(self, *a):
        self.f.barrier()
        return self.es.__exit__(*a)


_CONST = None


def host_consts():
    global _CONST
    if _CONST is not None:
        return _CONST
    c = {}
    p = np.arange(128)[:, None, None].astype(np.float64)
    j = np.arange(32)[None, :, None].astype(np.float64)
    cc = np.arange(128)[None, None, :].astype(np.float64)
    ph = 2 * np.pi * (32 * p + j) * (cc + 0.5) / NFFT
    c["HF"] = np.stack([np.cos(ph), -np.sin(ph)], 2).astype(NB)
    pht = ph.transpose(2, 1, 0)
    c["HG"] = np.stack([np.cos(pht), -np.sin(pht)], 2).astype(NB)
    pf = 2 * np.pi * (32 * p + j) * cc / L
    c["FF"] = np.stack([np.cos(pf), np.sin(pf), -np.sin(pf)], 2).astype(NB)
    jd = 2 * np.pi * np.outer(np.arange(32), np.arange(32)) / 32
    C4 = np.kron(np.eye(4), np.cos(jd))
    S4 = np.kron(np.eye(4), np.sin(jd))
    c["CS"] = np.stack([C4, S4, -S4], 1).astype(NB)
    cd = 2 * np.pi * np.outer(np.arange(64), np.arange(64)) / 64
    c["BD"] = np.concatenate([np.kron(np.eye(2), np.cos(cd)), np.kron(np.eye(2), np.sin(cd))], 1).astype(NB)
    c["ident"] = np.eye(128).astype(NB)
    sel = np.zeros((128, 3, 128))
    sel[:, 0, :] = 1.0
    sel[0, 1, :] = 1.0
    sel[0, 2, :] = -1.0
    c["SEL"] = sel.astype(NB)
    pos = np.arange(L, dtype=np.float64)
    t = pos / (L - 1)
    bands = np.linspace(1e-4, 15, 16)
    ang = (2 * np.pi * pos / L)[:, None] * bands[None]
    z = np.concatenate([t[:, None], np.cos(ang), -np.sin(ang)], -1)
    zT = z.T.reshape(33, 128, 32).transpose(0, 2, 1).reshape(33, L)
    c["zT"] = np.ascontiguousarray(zT).astype(np.float32)
    deltas = np.linspace(np.log(1e-2) / 1.5, np.log(1e-2) / 0.3, 512)
    decay = np.exp(-t[:, None] * np.abs(deltas)[None])
    c["decay"] = decay.reshape(128, 32, 512).astype(np.float32)
    _CONST = c
    return c


CONST_SPECS = [("HF", [128, 32, 2, 128], BF16), ("HG", [128, 32, 2, 128], BF16), ("FF", [128, 32, 3, 128], BF16),
               ("CS", [128, 3, 128], BF16), ("BD", [128, 256], BF16), ("ident", [128, 128], BF16),
               ("SEL", [128, 3, 128], BF16), ("zT", [33, L], F32), ("decay", [128, 32, 512], F32)]

IN_SPECS = [("x", [L, 1024], F32), ("gm", [128, 8], F32), ("w_in", [1024, 2048], F32), ("cw", [4, 1536], F32),
            ("f_w0", [33, 64], F32), ("f_b", [64, 3], F32), ("f_wi", [64, 2, 64], F32), ("f_fr", [64, 1], F32),
            ("f_wo", [64, 2048], F32), ("long_d", [1, 2, 512], F32), ("gfh", [128, 8], F32),
            ("w_out", [1024, 1024], F32), ("gml", [128, 8], F32), ("w_fc1", [1024, 4096], F32),
            ("w_fc2", [4096, 1024], F32), ("gfin", [1, 1024], F32)]


def build(upto=99, debug=False):
    nc = bass.Bass("TRN2", target_bir_lowering=False)
    f = FW(nc)
    D = {}
    for name, shape, dt in IN_SPECS + CONST_SPECS:
        D[name] = T(nc.dram_tensor(name, shape, dt, kind="ExternalInput").ap(), name)
    out_d = T(nc.dram_tensor("out", [L, 1024], F32, kind="ExternalOutput").ap(), "out")
    def scratch(name, shape, dt):
        k = "ExternalOutput" if (debug is True or (debug and name in debug)) else "Internal"
        return T(nc.dram_tensor(name, shape, dt, kind=k).ap(), name)

    RAW = scratch("RAW", [128, 34, 1536], BF16)
    U = scratch("U", [128, 32, 1024], BF16)
    XG = scratch("XG", [3, 128, 32, 512], BF16)
    FA = scratch("FA", [4, 128, 32, 512], BF16)
    FB = scratch("FB", [2, 128, 32, 512], BF16)
    KH = scratch("KH", [2, 2, 128, 32, 512], BF16)
    Z1 = scratch("Z1", [128, 32, 512], BF16)
    W1B = scratch("W1B", [128, 8, 4096], BF16)
    W2B = scratch("W2B", [128, 32, 1024], BF16)

    uid = [0]

    def sb(es, name, shape, dt):
        uid[0] += 1
        return T(es.enter_context(nc.sbuf_tensor("sb%d_%s" % (uid[0], name), shape, dt)).ap(), name)

    def ps(es, name, shape, dt=F32):
        uid[0] += 1
        return T(es.enter_context(nc.psum_tensor("ps%d_%s" % (uid[0], name), shape, dt)).ap(), name)

    top = ExitStack()
    ident = sb(top, "ident", [128, 128], BF16)
    CS = sb(top, "CS", [128, 3, 128], BF16)
    epsT = sb(top, "epsT", [128, 1], F32)
    yTbox = []
    scl = sb(top, "scl", [128, 2, 512], F32)
    f.dma("sp", ident.ap, D["ident"].ap, writes=[ident])
    f.dma("sp", CS.ap, D["CS"].ap, writes=[CS])
    f.op("pool", lambda e: e.memset(epsT.ap, EPS), writes=[epsT])

    def rstd_of(ss_ap, ss_t, out_ap, out_t, scale):
        f.op("act", lambda e: e.activation(out=out_ap, in_=ss_ap, func=ACTF.Sqrt, scale=scale, bias=epsT.ap[:, 0:1]),
             reads=[ss_t, epsT], writes=[out_t])
        f.op("dve", lambda e: e.reciprocal(out=out_ap, in_=out_ap), reads=[out_t], writes=[out_t])

    def phase0():
        with Scope(f) as es:
            gml = sb(es, "gml", [128, 8], F32)
            f.dma("sp", gml.ap, D["gml"].ap, writes=[gml])
            st = [sb(es, "p0s%d" % i, [128, 8, 512], F32) for i in range(2)]
            sbf = [sb(es, "p0b%d" % i, [128, 8, 512], BF16) for i in range(2)]
            w1v = D["w_fc1"].ap.rearrange("(k p) f -> p k f", p=128)
            for i in range(8):
                s, b = st[i % 2], sbf[i % 2]
                f.dma("sp", s.ap, w1v[:, :, i * 512:(i + 1) * 512], writes=[s])
                for k in range(8):
                    if k % 2 == 0:
                        f.op("dve", lambda e, k=k: e.tensor_scalar(out=b.ap[:, k, :], in0=s.ap[:, k, :], scalar1=gml.ap[:, k:k + 1],
                                                                   scalar2=None, op0=ALU.mult), reads=[s, gml], writes=[b])
                    else:
                        f.op("act", lambda e, k=k: e.activation(out=b.ap[:, k, :], in_=s.ap[:, k, :], func=ACTF.Copy,
                                                                scale=gml.ap[:, k:k + 1]), reads=[s, gml], writes=[b])
                f.dma("pool", W1B.ap[:, :, i * 512:(i + 1) * 512], b.ap, reads=[b], writes=[W1B], sem=b, accumulate=True)
            w2v = D["w_fc2"].ap.rearrange("(c p) d -> p c d", p=128)
            for i in range(8):
                s, b = st[i % 2], sbf[i % 2]
                s4 = s.ap.rearrange("p (a b) n -> p a (b n)", a=4)
                b4 = b.ap.rearrange("p (a b) n -> p a (b n)", a=4)
                f.dma("sp", s4, w2v[:, i * 4:(i + 1) * 4, :], writes=[s])
                f.op("dve", lambda e: e.tensor_copy(out=b4[:, 0:2], in_=s4[:, 0:2]), reads=[s], writes=[b])
                f.op("act", lambda e: e.activation(out=b4[:, 2:4], in_=s4[:, 2:4], func=ACTF.Copy), reads=[s], writes=[b])
                f.dma("pool", W2B.ap[:, i * 4:(i + 1) * 4, :], b4, reads=[b], writes=[W2B], sem=b, accumulate=True)

    def phase1():
        with Scope(f) as es:
            W1 = sb(es, "W1", [128, 8, 2048], BF16)
            gm = sb(es, "gm", [128, 8], F32)
            BD = sb(es, "BD", [128, 256], BF16)
            f.dma("sp", gm.ap, D["gm"].ap, writes=[gm])
            f.dma("sp", BD.ap, D["BD"].ap, writes=[BD])
            wst = [sb(es, "wst%d" % i, [128, 2048], F32) for i in range(2)]
            wv = D["w_in"].ap.rearrange("(k p) e -> p k e", p=128)
            for k in range(8):
                s = wst[k % 2]
                f.dma("sp", s.ap, wv[:, k, :], writes=[s])
                if k % 2 == 0:
                    f.op("dve", lambda e, k=k, s=s: e.tensor_scalar(out=W1.ap[:, k, :], in0=s.ap, scalar1=gm.ap[:, k:k + 1], scalar2=None,
                                                                    op0=ALU.mult), reads=[s, gm], writes=[W1])
                else:
                    f.op("act", lambda e, k=k, s=s: e.activation(out=W1.ap[:, k, :], in_=s.ap, func=ACTF.Copy, scale=gm.ap[:, k:k + 1]),
                         reads=[s, gm], writes=[W1])
            xts = [sb(es, "xt%d" % i, [128, 1024], F32) for i in range(3)]
            junk = sb(es, "junk", [128, 1024], BF16)
            ss = [sb(es, "ss%d" % i, [128, 1], F32) for i in range(2)]
            rs = [sb(es, "rs%d" % i, [128, 1], F32) for i in range(2)]
            xs = [sb(es, "xs%d" % i, [128, 1024], BF16) for i in range(2)]
            hT = [sb(es, "hT%d" % i, [128, 8, 128], BF16) for i in range(2)]
            raw = [sb(es, "raw%d" % i, [128, 1536], BF16) for i in range(2)]
            ufT = [sb(es, "ufT%d" % i, [128, 4, 128], BF16) for i in range(2)]
            Ut = [sb(es, "Ut%d" % i, [128, 4, 256], BF16) for i in range(2)]
            pT = ps(es, "pT", [128, 8, 128], BF16)
            pH = [ps(es, "pH%d" % i, [128, 512]) for i in range(3)]
            pF = ps(es, "pF", [128, 4, 128])
            pU = ps(es, "pU", [128, 4, 256])
            xv = D["x"].ap.rearrange("(p j) d -> p j d", j=32)
            for j in range(32):
                xt = xts[j % 3]
                s_, r_, x_, h_, rw, uf, ut = ss[j % 2], rs[j % 2], xs[j % 2], hT[j % 2], raw[j % 2], ufT[j % 2], Ut[j % 2]
                f.dma("sp", xt.ap, xv[:, j, :], writes=[xt])
                f.op("act", lambda e: e.activation(out=junk.ap, in_=xt.ap, func=ACTF.Square, accum_out=s_.ap),
                     reads=[xt], writes=[junk, s_])
                rstd_of(s_.ap, s_, r_.ap, r_, 1.0 / 1024)
                f.op("act", lambda e: e.activation(out=x_.ap, in_=xt.ap, func=ACTF.Copy, scale=r_.ap[:, 0:1]),
                     reads=[xt, r_], writes=[x_])
                f.mm([lambda e, k=k: e.transpose(out=pT.ap[:, k, :], in_=x_.ap[:, k * 128:(k + 1) * 128], identity=ident.ap)
                      for k in range(8)], reads=[x_, ident], writes=[pT])
                f.op("dve", lambda e: e.tensor_copy(out=h_.ap, in_=pT.ap), reads=[pT], writes=[h_])
                for n in range(3):
                    f.mm([lambda e, k=k, n=n: e.matmul(pH[n].ap, lhsT=h_.ap[:, k, :], rhs=W1.ap[:, k, 512 + n * 512:1024 + n * 512],
                                                       start=(k == 0), stop=(k == 7)) for k in range(8)],
                         reads=[h_, W1], writes=[pH[n]])
                    f.op("act", lambda e, n=n: e.activation(out=rw.ap[:, n * 512:(n + 1) * 512], in_=pH[n].ap, func=ACTF.Copy),
                         reads=[pH[n]], writes=[rw])
                f.dma("pool", RAW.ap[:, j + 1, :], rw.ap, reads=[rw], writes=[RAW], sem=rw, accumulate=True)
                fns = []
                for ec in range(4):
                    for k in range(8):
                        fns.append(lambda e, k=k, ec=ec: e.matmul(pF.ap[:, ec, :], lhsT=W1.ap[:, k, ec * 128:(ec + 1) * 128],
                                                                  rhs=h_.ap[:, k, :], start=(k == 0), stop=(k == 7)))
                f.mm(fns, reads=[h_, W1], writes=[pF])
                f.op("dve", lambda e: e.tensor_copy(out=uf.ap, in_=pF.ap), reads=[pF], writes=[uf])
                f.mm([lambda e, ec=ec: e.matmul(pU.ap[:, ec, :], lhsT=uf.ap[:, ec, :], rhs=BD.ap, start=True, stop=True)
                      for ec in range(4)], reads=[uf, BD], writes=[pU])
                f.op("dve", lambda e: e.tensor_copy(out=ut.ap, in_=pU.ap), reads=[pU], writes=[ut])
                f.dma("pool", U.ap[:, j, :], ut.ap.rearrange("p a b -> p (a b)"), reads=[ut], writes=[U], sem=ut, accumulate=True)
            zrow = sb(es, "zrow", [1, 1536], BF16)
            f.op("pool", lambda e: e.memset(zrow.ap, 0.0), writes=[zrow])
            halo = T(None, "halo")
            f.dma("pool", RAW.ap[1:128, 0, :], RAW.ap[0:127, 32, :], reads=[RAW], writes=[halo])
            f.dma("pool", RAW.ap[0:127, 33, :], RAW.ap[1:128, 1, :], reads=[RAW], writes=[halo], accumulate=True)
            f.dma("pool", RAW.ap[0:1, 0, :], zrow.ap, reads=[zrow], writes=[halo], accumulate=True)
            f.dma("pool", RAW.ap[127:128, 33, :], zrow.ap, reads=[zrow], writes=[halo], accumulate=True)
            RAW.r.w.update(halo.r.w)

    def phase1b():
        with Scope(f) as es:
            cw = sb(es, "cw", [128, 4, 1536], F32)
            f.dma("sp", cw.ap, D["cw"].ap.partition_broadcast(128), writes=[cw])
            rg = [sb(es, "rg%d" % i, [128, 4, 1536], BF16) for i in range(2)]
            acc = sb(es, "cacc", [128, 2, 1536], F32)
            t1 = sb(es, "ct1", [128, 2, 1536], F32)
            t2 = sb(es, "ct2", [128, 2, 1536], F32)
            og = [sb(es, "og%d" % i, [128, 2, 1536], BF16) for i in range(2)]

            def wb(i):
                return cw.ap[:, i:i + 1, :].broadcast_to([128, 2, 1536])
            for g in range(16):
                r, o = rg[g % 2], og[g % 2]
                f.dma("sp", r.ap, RAW.ap[:, 2 * g:2 * g + 4, :], reads=[RAW], writes=[r])
                f.op("pool", lambda e: e.tensor_tensor(out=t1.ap, in0=r.ap[:, 0:2, :], in1=wb(0), op=ALU.mult), reads=[r, cw], writes=[t1])
                f.op("dve", lambda e: e.tensor_tensor(out=acc.ap, in0=r.ap[:, 1:3, :], in1=wb(1), op=ALU.mult), reads=[r, cw], writes=[acc])
                f.op("pool", lambda e: e.tensor_tensor(out=t2.ap, in0=r.ap[:, 2:4, :], in1=wb(2), op=ALU.mult), reads=[r, cw], writes=[t2])
                f.op("dve", lambda e: e.tensor_tensor(out=acc.ap, in0=acc.ap, in1=t1.ap, op=ALU.add), reads=[acc, t1], writes=[acc])
                f.op("dve", lambda e: e.tensor_tensor(out=acc.ap, in0=acc.ap, in1=t2.ap, op=ALU.add), reads=[acc, t2], writes=[acc])
                f.op("dve", lambda e: e.tensor_tensor(out=o.ap, in0=acc.ap, in1=wb(3), op=ALU.add), reads=[acc, cw], writes=[o])
                for t in range(3):
                    f.dma("pool", XG.ap[t, :, 2 * g:2 * g + 2, :], o.ap[:, :, t * 512:(t + 1) * 512], reads=[o], writes=[XG],
                          sem=o, accumulate=True)

    def relayout_view(dram_ap):
        return dram_ap.rearrange("(cp q) j n -> (q j) cp n", q=4)

    def load_split(queue, dst, dst_ap, src_ap, reads, n):
        tot = dst_ap.shape[1]
        step = tot // n
        for i in range(n):
            f.dma(queue, dst_ap[:, i * step:(i + 1) * step], src_ap[:, i * step:(i + 1) * step], reads=reads, writes=[dst],
                  sem=T(None, "ls"))

    def stage1(es, src, mats, combos, dst_slots, tagp):
        nslot = len(combos)
        pA = [ps(es, "%spA%d" % (tagp, i), [128, 512]) for i in range(nslot)]
        stg = [sb(es, "%sstg%d" % (tagp, i), [128, nslot, 4, 512], BF16) for i in range(2)]
        for j in range(32):
            sg = stg[(j // 4) % 2]
            for si, terms in enumerate(combos):
                f.mm([lambda e, mi=mi, rf=rf, ti=ti, si=si: e.matmul(pA[si].ap, lhsT=mats.ap[:, j, mi, :], rhs=rf(j),
                                                                       start=(ti == 0), stop=(ti == len(terms) - 1))
                      for ti, (mi, rf) in enumerate(terms)], reads=[src, mats], writes=[pA[si]])
                eng = "act" if (si + j) % 2 == 0 else "dve"
                if eng == "act":
                    f.op("act", lambda e, si=si: e.activation(out=sg.ap[:, si, j % 4, :], in_=pA[si].ap, func=ACTF.Copy),
                         reads=[pA[si]], writes=[sg])
                else:
                    f.op("dve", lambda e, si=si: e.tensor_copy(out=sg.ap[:, si, j % 4, :], in_=pA[si].ap), reads=[pA[si]], writes=[sg])
            if j % 4 == 3:
                for si in range(nslot):
                    f.dma("pool", FA.ap[dst_slots[si], :, j - 3:j + 1, :], sg.ap[:, si], reads=[sg], writes=[FA], sem=sg, accumulate=True)

    def phase2():
        with Scope(f) as es:
            FF = sb(es, "FF", [128, 32, 3, 128], BF16)
            f.dma("sp", FF.ap, D["FF"].ap, writes=[FF])
            Us = sb(es, "Us", [128, 32, 4, 2, 128], BF16)
            load_split("sp", Us, Us.ap.rearrange("p j a b n -> p j (a b n)"), U.ap, [U], 4)
            uc = lambda j: Us.ap[:, j, :, 0, :]
            us = lambda j: Us.ap[:, j, :, 1, :]
            with Scope(f) as es2:
                stage1(es2, Us, FF, [[(0, uc), (2, us)], [(1, uc), (0, us)]], [0, 1], "f")
        with Scope(f) as es:
            A2 = [sb(es, "fA2%d" % i, [128, 32, 512], BF16) for i in range(2)]
            for i in range(2):
                load_split("sp", A2[i], A2[i].ap, relayout_view(FA.ap[i]), [FA], 4)
            pY = [ps(es, "fpY%d" % i, [128, 512]) for i in range(2)]
            pT2 = [ps(es, "fpT%d" % i, [128, 4, 128], BF16) for i in range(2)]
            junk = sb(es, "fjunk", [128, 512], BF16)
            ssq = [sb(es, "fss%d" % i, [128, 1], F32) for i in range(2)]
            rsq = [sb(es, "frs%d" % i, [128, 1], F32) for i in range(2)]
            yn = [sb(es, "fyn%d" % i, [128, 512], BF16) for i in range(2)]
            yT = yTbox[0]
            yTv = yT.ap.rearrange("p k (d cp q) -> p k cp q d", d=32, cp=32, q=4)
            for cp in range(32):
                py, pt, s_, r_, y_ = pY[cp % 2], pT2[cp % 2], ssq[cp % 2], rsq[cp % 2], yn[cp % 2]
                f.mm([lambda e: e.matmul(py.ap, lhsT=CS.ap[:, 0, :], rhs=A2[0].ap[:, cp, :], start=True, stop=False),
                      lambda e: e.matmul(py.ap, lhsT=CS.ap[:, 2, :], rhs=A2[1].ap[:, cp, :], start=False, stop=True)],
                     reads=[CS, A2[0], A2[1]], writes=[py])
                f.op("act", lambda e: e.activation(out=junk.ap, in_=py.ap, func=ACTF.Square, accum_out=s_.ap), reads=[py], writes=[junk, s_])
                rstd_of(s_.ap, s_, r_.ap, r_, 1.0 / (512.0 * 512.0 * 512.0))
                f.op("dve", lambda e: e.tensor_scalar(out=y_.ap, in0=py.ap, scalar1=r_.ap[:, 0:1], scalar2=1.0 / 512.0, op0=ALU.mult,
                                                      op1=ALU.mult), reads=[py, r_], writes=[y_])
                f.mm([lambda e, k=k: e.transpose(out=pt.ap[:, k, :], in_=y_.ap[:, k * 128:(k + 1) * 128], identity=ident.ap)
                      for k in range(4)], reads=[y_, ident], writes=[pt])
                f.op("act", lambda e: e.activation(out=yTv[:, 0:4, cp], in_=pt.ap.rearrange("p k (q d) -> p k q d", q=4),
                                                   func=ACTF.Copy), reads=[pt], writes=[yT])

    def phase3():
        with Scope(f) as es:
            hcur = [sb(es, "fh%d" % i, [64, L], F32) for i in range(2)]
            wo = sb(es, "fwo", [64, 2, 2, 512], F32)
            wsd = sb(es, "fwsd", [64, 2, 2, 512], F32)
            HF = sb(es, "HF3", [128, 32, 2, 128], BF16)
            SEL = sb(es, "SEL", [128, 3, 128], BF16)
            dl = sb(es, "dl", [1, 2, 512], F32)
            f.dma("sp", wo.ap.rearrange("p a b n -> p (a b n)"), D["f_wo"].ap, writes=[wo])
            f.dma("sp", HF.ap, D["HF"].ap, writes=[HF])
            f.dma("sp", SEL.ap, D["SEL"].ap, writes=[SEL])
            f.dma("sp", dl.ap, D["long_d"].ap, writes=[dl])
            for o in range(2):
                f.op("dve", lambda e, o=o: e.tensor_tensor(out=wsd.ap[:, o, 0, :], in0=wo.ap[:, o, 0, :], in1=wo.ap[:, o, 1, :], op=ALU.add),
                     reads=[wo], writes=[wsd])
                f.op("dve", lambda e, o=o: e.tensor_tensor(out=wsd.ap[:, o, 1, :], in0=wo.ap[:, o, 0, :], in1=wo.ap[:, o, 1, :], op=ALU.subtract),
                     reads=[wo], writes=[wsd])
            with Scope(f) as em:
                zT = sb(em, "zT", [33, L], F32)
                w0 = sb(em, "fw0", [33, 64], F32)
                fb = sb(em, "fb", [64, 3], F32)
                wi = sb(em, "fwi", [64, 2, 64], F32)
                fr = sb(em, "ffr", [64, 1], F32)
                s1 = sb(em, "fs1", [64, 1], F32)
                s2 = sb(em, "fs2", [64, 3], F32)
                f.dma("sp", zT.ap, D["zT"].ap, writes=[zT])
                f.dma("sp", w0.ap, D["f_w0"].ap, writes=[w0])
                f.dma("sp", fb.ap, D["f_b"].ap, writes=[fb])
                f.dma("sp", wi.ap, D["f_wi"].ap, writes=[wi])
                f.dma("sp", fr.ap, D["f_fr"].ap, writes=[fr])
                inv2pi = float(1.0 / (2 * np.pi))
                f.op("dve", lambda e: e.tensor_scalar(out=s1.ap, in0=fr.ap, scalar1=inv2pi, scalar2=None, op0=ALU.mult), reads=[fr], writes=[s1])
                f.op("dve", lambda e: e.tensor_scalar(out=s2.ap, in0=fb.ap, scalar1=s1.ap[:, 0:1], scalar2=16.0, op0=ALU.mult, op1=ALU.add),
                     reads=[fb, s1], writes=[s2])
                pm = [ps(em, "fpm%d" % i, [64, 512]) for i in range(2)]
                rr = [sb(em, "frr%d" % i, [64, 512], F32) for i in range(2)]
                ki = [sb(em, "fki%d" % i, [64, 512], I32) for i in range(2)]
                kf = [sb(em, "fkf%d" % i, [64, 512], F32) for i in range(2)]
                for layer in range(3):
                    src = zT if layer == 0 else hcur[(layer - 1) % 2]
                    dst = hcur[layer % 2]
                    lw = w0.ap if layer == 0 else wi.ap[:, layer - 1, :]
                    lwt = w0 if layer == 0 else wi
                    kdim = 33 if layer == 0 else 64
                    for c8 in range(8):
                        i = c8 % 2
                        cs = slice(c8 * 512, (c8 + 1) * 512)
                        f.mm([lambda e: e.matmul(pm[i].ap, lhsT=lw, rhs=src.ap[0:kdim, cs], start=True, stop=True)], reads=[src, lwt], writes=[pm[i]])
                        f.op("dve", lambda e: e.tensor_scalar(out=rr[i].ap, in0=pm[i].ap, scalar1=s1.ap[:, 0:1], scalar2=s2.ap[:, layer:layer + 1],
                                                              op0=ALU.mult, op1=ALU.add), reads=[pm[i], s1, s2], writes=[rr[i]])
                        f.op("dve", lambda e: e.tensor_copy(out=ki[i].ap, in_=rr[i].ap), reads=[rr[i]], writes=[ki[i]])
                        f.op("dve", lambda e: e.tensor_copy(out=kf[i].ap, in_=ki[i].ap), reads=[ki[i]], writes=[kf[i]])
                        f.op("dve", lambda e: e.tensor_tensor(out=rr[i].ap, in0=rr[i].ap, in1=kf[i].ap, op=ALU.subtract), reads=[rr[i], kf[i]], writes=[rr[i]])
                        f.op("dve", lambda e: e.tensor_single_scalar(out=kf[i].ap, in_=rr[i].ap, scalar=0.5, op=ALU.is_gt), reads=[rr[i]], writes=[kf[i]])
                        f.op("dve", lambda e: e.tensor_tensor(out=rr[i].ap, in0=rr[i].ap, in1=kf[i].ap, op=ALU.subtract), reads=[rr[i], kf[i]], writes=[rr[i]])
                        f.op("act", lambda e: e.activation(out=dst.ap[:, cs], in_=rr[i].ap, func=ACTF.Sin, scale=float(2 * np.pi)),
                             reads=[rr[i]], writes=[dst])
            h3 = sb(es, "fh3b", [64, L], BF16)
            wsdb = sb(es, "fwsdb", [64, 2, 2, 512], BF16)
            f.op("dve", lambda e: e.tensor_copy(out=h3.ap[:, 0:2048], in_=hcur[0].ap[:, 0:2048]), reads=[hcur[0]], writes=[h3])
            f.op("act", lambda e: e.activation(out=h3.ap[:, 2048:4096], in_=hcur[0].ap[:, 2048:4096], func=ACTF.Copy), reads=[hcur[0]], writes=[h3])
            f.op("dve", lambda e: e.tensor_copy(out=wsdb.ap, in_=wsd.ap), reads=[wsd], writes=[wsdb])
            for o in range(2):
                with Scope(f) as eo:
                    sd = [sb(eo, "fsd%d" % i, [128, 32, 512], BF16) for i in range(2)]
                    sq = [sb(eo, "fsq%d" % i, [128, 512], BF16) for i in range(2)]
                    dec = [sb(eo, "fdec%d" % i, [128, 512], F32) for i in range(2)]
                    pk = [ps(eo, "fpk%d" % i, [128, 512]) for i in range(2)]
                    pn = ps(eo, "fpn", [128, 512])
                    sqt = sb(eo, "fsqt", [128, 512], F32)
                    tmp0 = sb(eo, "ftmp0", [1, 512], F32)
                    nsq = 0
                    for j in range(32):
                        dc = dec[j % 2]
                        f.dma("sp", dc.ap, D["decay"].ap[:, j, :], writes=[dc])
                        for v in range(2):
                            f.mm([lambda e, v=v: e.matmul(pk[v].ap, lhsT=h3.ap[:, j * 128:(j + 1) * 128], rhs=wsdb.ap[:, o, v, :], start=True, stop=True)],
                                 reads=[h3, wsdb], writes=[pk[v]])
                            f.op("dve", lambda e, v=v: e.tensor_tensor(out=sd[v].ap[:, j, :], in0=pk[v].ap, in1=dc.ap, op=ALU.mult),
                                 reads=[pk[v], dc], writes=[sd[v]])
                            q_ = sq[nsq % 2]
                            nsq += 1
                            f.op("act", lambda e, v=v: e.activation(out=q_.ap, in_=sd[v].ap[:, j, :], func=ACTF.Square), reads=[sd[v]], writes=[q_])
                            first = (j == 0 and v == 0)
                            fns = [lambda e: e.matmul(pn.ap, lhsT=SEL.ap[:, 0, :], rhs=q_.ap, start=first, stop=False)]
                            if j == 0:
                                fns.append(lambda e, v=v: e.matmul(pn.ap, lhsT=SEL.ap[:, 1 + v, :], rhs=q_.ap, start=False, stop=False))
                            if j == 31 and v == 1:
                                fns = [lambda e: e.matmul(pn.ap, lhsT=SEL.ap[:, 0, :], rhs=q_.ap, start=False, stop=True)]
                            f.mm(fns, reads=[q_, SEL], writes=[pn])
                    f.op("act", lambda e: e.activation(out=sqt.ap, in_=pn.ap, func=ACTF.Sqrt, scale=0.5, bias=epsT.ap[:, 0:1]),
                         reads=[pn, epsT], writes=[sqt])
                    f.op("dve", lambda e, o=o: e.reciprocal(out=scl.ap[:, o, :], in_=sqt.ap), reads=[sqt], writes=[scl])
                    f.op("dve", lambda e, o=o: e.tensor_tensor(out=tmp0.ap, in0=dl.ap[0:1, o, :], in1=sqt.ap[0:1, :], op=ALU.mult),
                         reads=[dl, sqt], writes=[tmp0])
                    f.op("dve", lambda e: e.tensor_tensor(out=sd[0].ap[0:1, 0, :], in0=sd[0].ap[0:1, 0, :], in1=tmp0.ap, op=ALU.add),
                         reads=[sd[0], tmp0], writes=[sd[0]])
                    f.op("dve", lambda e, o=o: e.tensor_scalar(out=scl.ap[:, o, :], in0=scl.ap[:, o, :], scalar1=2.0 / NFFT, scalar2=None, op0=ALU.mult),
                         reads=[scl], writes=[scl])
                    with Scope(f) as es2:
                        stage1(es2, sd[0], HF, [[(0, lambda j: sd[0].ap[:, j, :])], [(1, lambda j: sd[0].ap[:, j, :])]], [0, 1], "k%da" % o)
                    with Scope(f) as es2:
                        stage1(es2, sd[1], HF, [[(0, lambda j: sd[1].ap[:, j, :])], [(1, lambda j: sd[1].ap[:, j, :])]], [2, 3], "k%db" % o)
                with Scope(f) as es2:
                    A2 = [sb(es2, "kA2%d" % i, [128, 16, 512], BF16) for i in range(4)]
                    kst = [sb(es2, "kst%d" % i, [128, 2, 4, 512], BF16) for i in range(2)]
                    pkk = [ps(es2, "kpk%d" % i, [128, 512]) for i in range(2)]
                    for half in range(2):
                        for i in range(4):
                            load_split("sp", A2[i], A2[i].ap, relayout_view(FA.ap[i])[:, half * 16:(half + 1) * 16, :], [FA], 2)
                        for c16 in range(16):
                            cp = half * 16 + c16
                            st_ = kst[(cp // 4) % 2]
                            f.mm([lambda e: e.matmul(pkk[0].ap, lhsT=CS.ap[:, 0, :], rhs=A2[0].ap[:, c16, :], start=True, stop=False),
                                  lambda e: e.matmul(pkk[0].ap, lhsT=CS.ap[:, 1, :], rhs=A2[1].ap[:, c16, :], start=False, stop=True)],
                                 reads=[CS, A2[0], A2[1]], writes=[pkk[0]])
                            f.mm([lambda e: e.matmul(pkk[1].ap, lhsT=CS.ap[:, 0, :], rhs=A2[3].ap[:, c16, :], start=True, stop=False),
                                  lambda e: e.matmul(pkk[1].ap, lhsT=CS.ap[:, 2, :], rhs=A2[2].ap[:, c16, :], start=False, stop=True)],
                                 reads=[CS, A2[2], A2[3]], writes=[pkk[1]])
                            f.op("act", lambda e: e.activation(out=st_.ap[:, 0, cp % 4, :], in_=pkk[0].ap, func=ACTF.Copy), reads=[pkk[0]], writes=[st_])
                            f.op("dve", lambda e: e.tensor_copy(out=st_.ap[:, 1, cp % 4, :], in_=pkk[1].ap), reads=[pkk[1]], writes=[st_])
                            if cp % 4 == 3:
                                for r in range(2):
                                    f.dma("pool", KH.ap[o, r, :, cp - 3:cp + 1, :], st_.ap[:, r], reads=[st_], writes=[KH], sem=st_, accumulate=True)

    def phase_h(o):
        with Scope(f) as es:
            HF = sb(es, "HFh", [128, 32, 2, 128], BF16)
            z = sb(es, "hz", [128, 32, 512], BF16)
            f.dma("sp", HF.ap, D["HF"].ap, writes=[HF])
            if o == 0:
                load_split("sp", z, z.ap, XG.ap[2], [XG], 2)
            else:
                load_split("sp", z, z.ap, Z1.ap, [Z1], 2)
            stage1(es, z, HF, [[(0, lambda j: z.ap[:, j, :])], [(1, lambda j: z.ap[:, j, :])]], [0, 1], "h%d" % o)
        with Scope(f) as es:
            A2 = [sb(es, "hA2%d" % i, [128, 32, 512], BF16) for i in range(2)]
            for i in range(2):
                load_split("sp", A2[i], A2[i].ap, relayout_view(FA.ap[i]), [FA], 4)
            kk = [[sb(es, "hk%d%d" % (i, r), [128, 4, 512], BF16) for r in range(2)] for i in range(2)]
            pX = [ps(es, "hpX%d" % i, [128, 512]) for i in range(2)]
            pB = [ps(es, "hpB%d" % i, [128, 512]) for i in range(2)]
            tt = [sb(es, "htt%d" % i, [128, 512], F32) for i in range(4)]
            Y = [[sb(es, "hY%d%d" % (i, r), [128, 512], BF16) for r in range(2)] for i in range(2)]
            bst = [sb(es, "hbst%d" % i, [128, 2, 4, 512], BF16) for i in range(2)]
            FBv = [relayout_view(FB.ap[r]) for r in range(2)]
            for cp in range(32):
                g = cp // 4
                kb = kk[g % 2]
                if cp % 4 == 0:
                    for r in range(2):
                        f.dma("sp", kb[r].ap, KH.ap[o, r, :, cp:cp + 4, :], reads=[KH], writes=[kb[r]])
                kr = kb[0].ap[:, cp % 4, :]
                kim = kb[1].ap[:, cp % 4, :]
                f.mm([lambda e: e.matmul(pX[0].ap, lhsT=CS.ap[:, 0, :], rhs=A2[0].ap[:, cp, :], start=True, stop=False),
                      lambda e: e.matmul(pX[0].ap, lhsT=CS.ap[:, 1, :], rhs=A2[1].ap[:, cp, :], start=False, stop=True)],
                     reads=[CS, A2[0], A2[1]], writes=[pX[0]])
                f.mm([lambda e: e.matmul(pX[1].ap, lhsT=CS.ap[:, 0, :], rhs=A2[1].ap[:, cp, :], start=True, stop=False),
                      lambda e: e.matmul(pX[1].ap, lhsT=CS.ap[:, 2, :], rhs=A2[0].ap[:, cp, :], start=False, stop=True)],
                     reads=[CS, A2[0], A2[1]], writes=[pX[1]])
                f.op("dve", lambda e: e.tensor_tensor(out=tt[0].ap, in0=pX[0].ap, in1=kr, op=ALU.mult), reads=[pX[0], kb[0]], writes=[tt[0]])
                f.op("dve", lambda e: e.tensor_tensor(out=tt[1].ap, in0=pX[1].ap, in1=kim, op=ALU.mult), reads=[pX[1], kb[1]], writes=[tt[1]])
                f.op("dve", lambda e: e.tensor_tensor(out=tt[2].ap, in0=pX[0].ap, in1=kim, op=ALU.mult), reads=[pX[0], kb[1]], writes=[tt[2]])
                f.op("dve", lambda e: e.tensor_tensor(out=tt[3].ap, in0=pX[1].ap, in1=kr, op=ALU.mult), reads=[pX[1], kb[0]], writes=[tt[3]])
                yy = Y[cp % 2]
                f.op("pool", lambda e: e.tensor_tensor(out=yy[0].ap, in0=tt[0].ap, in1=tt[1].ap, op=ALU.subtract), reads=[tt[0], tt[1]], writes=[yy[0]])
                f.op("pool", lambda e: e.tensor_tensor(out=yy[1].ap, in0=tt[2].ap, in1=tt[3].ap, op=ALU.add), reads=[tt[2], tt[3]], writes=[yy[1]])
                f.mm([lambda e: e.matmul(pB[0].ap, lhsT=CS.ap[:, 0, :], rhs=yy[0].ap, start=True, stop=False),
                      lambda e: e.matmul(pB[0].ap, lhsT=CS.ap[:, 2, :], rhs=yy[1].ap, start=False, stop=True)],
                     reads=[CS, yy[0], yy[1]], writes=[pB[0]])
                f.mm([lambda e: e.matmul(pB[1].ap, lhsT=CS.ap[:, 1, :], rhs=yy[0].ap, start=True, stop=False),
                      lambda e: e.matmul(pB[1].ap, lhsT=CS.ap[:, 0, :], rhs=yy[1].ap, start=False, stop=True)],
                     reads=[CS, yy[0], yy[1]], writes=[pB[1]])
                bs = bst[g % 2]
                f.op("act", lambda e: e.activation(out=bs.ap[:, 0, cp % 4, :], in_=pB[0].ap, func=ACTF.Copy), reads=[pB[0]], writes=[bs])
                f.op("act", lambda e: e.activation(out=bs.ap[:, 1, cp % 4, :], in_=pB[1].ap, func=ACTF.Copy), reads=[pB[1]], writes=[bs])
                if cp % 4 == 3:
                    for r in range(2):
                        f.dma("pool", FBv[r][:, cp - 3:cp + 1, :], bs.ap[:, r], reads=[bs], writes=[FB], sem=bs, accumulate=True)
        with Scope(f) as es:
            HG = sb(es, "HGh", [128, 32, 2, 128], BF16)
            Bc = [sb(es, "hBc%d" % i, [128, 16, 512], BF16) for i in range(2)]
            gt = sb(es, "hgt", [128, 32, 512], BF16)
            f.dma("sp", HG.ap, D["HG"].ap, writes=[HG])
            load_split("sp", gt, gt.ap, XG.ap[o], [XG], 2)
            for h in range(4):
                eng = "dve" if h % 2 == 0 else "pool"
                f.op(eng, lambda e, h=h: e.tensor_tensor(out=gt.ap[:, h * 8:(h + 1) * 8, :], in0=gt.ap[:, h * 8:(h + 1) * 8, :],
                                                         in1=scl.ap[:, o:o + 1, :].broadcast_to([128, 8, 512]), op=ALU.mult),
                     reads=[gt, scl], writes=[gt])
            gs = gt
            pY = [ps(es, "hpY%d" % i, [128, 512]) for i in range(2)]
            if o == 0:
                zst = [sb(es, "hzst%d" % i, [128, 4, 512], BF16) for i in range(2)]
            else:
                yf = [sb(es, "hyf%d" % i, [128, 512], F32) for i in range(2)]
                junk = sb(es, "hjunk", [128, 512], BF16)
                ssq = [sb(es, "hss%d" % i, [128, 1], F32) for i in range(2)]
                rsq = [sb(es, "hrs%d" % i, [128, 1], F32) for i in range(2)]
                yn = [sb(es, "hyn%d" % i, [128, 512], BF16) for i in range(2)]
                pT2 = [ps(es, "hpT%d" % i, [128, 4, 128], BF16) for i in range(2)]
                yT = yTbox[0]
                yTv = yT.ap.rearrange("p k (pp j) -> p k j pp", j=32)
            for j in range(32):
                if j % 16 == 0:
                    for r in range(2):
                        load_split("sp", Bc[r], Bc[r].ap, FB.ap[r][:, j:j + 16, :], [FB], 2)
                jl = j % 16
                py = pY[j % 2]
                f.mm([lambda e: e.matmul(py.ap, lhsT=HG.ap[:, j, 0, :], rhs=Bc[0].ap[:, jl, :], start=True, stop=False),
                      lambda e: e.matmul(py.ap, lhsT=HG.ap[:, j, 1, :], rhs=Bc[1].ap[:, jl, :], start=False, stop=True)],
                     reads=[HG, Bc[0], Bc[1]], writes=[py])
                if o == 0:
                    zs = zst[(j // 4) % 2]
                    f.op("dve", lambda e: e.tensor_tensor(out=zs.ap[:, j % 4, :], in0=py.ap, in1=gs.ap[:, j, :], op=ALU.mult),
                         reads=[py, gs], writes=[zs])
                    if j % 4 == 3:
                        f.dma("pool", Z1.ap[:, j - 3:j + 1, :], zs.ap, reads=[zs], writes=[Z1], sem=zs, accumulate=True)
                else:
                    y_, s_, r_, n_, pt = yf[j % 2], ssq[j % 2], rsq[j % 2], yn[j % 2], pT2[j % 2]
                    f.op("dve", lambda e: e.tensor_tensor(out=y_.ap, in0=py.ap, in1=gs.ap[:, j, :], op=ALU.mult), reads=[py, gs], writes=[y_])
                    f.op("act", lambda e: e.activation(out=junk.ap, in_=y_.ap, func=ACTF.Square, accum_out=s_.ap), reads=[y_], writes=[junk, s_])
                    rstd_of(s_.ap, s_, r_.ap, r_, 1.0 / 512.0)
                    f.op("act", lambda e: e.activation(out=n_.ap, in_=y_.ap, func=ACTF.Copy, scale=r_.ap[:, 0:1]), reads=[y_, r_], writes=[n_])
                    f.mm([lambda e, k=k: e.transpose(out=pt.ap[:, k, :], in_=n_.ap[:, k * 128:(k + 1) * 128], identity=ident.ap)
                          for k in range(4)], reads=[n_, ident], writes=[pt])
                    f.op("dve", lambda e: e.tensor_copy(out=yTv[:, 4:8, j, :], in_=pt.ap), reads=[pt], writes=[yT])

    def phase6():
        with Scope(f) as es:
            yT = yTbox[0]
            Wo = sb(es, "Wo", [128, 8, 1024], BF16)
            gfh = sb(es, "gfh", [128, 8], F32)
            gfin = sb(es, "gfin", [128, 1024], F32)
            f.dma("sp", gfh.ap, D["gfh"].ap, writes=[gfh])
            f.dma("sp", gfin.ap, D["gfin"].ap.partition_broadcast(128).rearrange("p a n -> p (a n)"), writes=[gfin])
            with Scope(f) as es2:
                wst = [sb(es2, "wost%d" % i, [128, 1024], F32) for i in range(2)]
                wv = D["w_out"].ap.rearrange("(k p) e -> p k e", p=128)
                for k in range(8):
                    s = wst[k % 2]
                    f.dma("sp", s.ap, wv[:, k, :], writes=[s])
                    f.op("dve", lambda e, k=k, s=s: e.tensor_scalar(out=Wo.ap[:, k, :], in0=s.ap, scalar1=gfh.ap[:, k:k + 1], scalar2=None,
                                                                    op0=ALU.mult), reads=[s, gfh], writes=[Wo])
            aT = sb(es, "aT", [128, 32, 512], BF16)
            hmT = [sb(es, "hmT%d" % i, [128, 8, 512], BF16) for i in range(2)]
            x1s = [sb(es, "x1s%d" % i, [128, 4, 1024], F32) for i in range(2)]
            W1s = [sb(es, "W1s%d" % i, [128, 8, 256], BF16) for i in range(2)]
            W2s = [sb(es, "W2s%d" % i, [128, 4, 512], BF16) for i in range(2)]
            rl = [sb(es, "rl%d" % i, [128, 512], BF16) for i in range(2)]
            hm = [sb(es, "hm%d" % i, [128, 1024], BF16) for i in range(2)]
            junk = sb(es, "mjunk", [128, 1024], BF16)
            ssq = [sb(es, "mss%d" % i, [128, 1], F32) for i in range(4)]
            rsq = [sb(es, "mrs%d" % i, [128, 1], F32) for i in range(4)]
            pw = [ps(es, "mpw%d" % i, [128, 512]) for i in range(2)]
            po = ps(es, "mpo", [128, 512])
            pT = ps(es, "mpT", [128, 8, 128], BF16)
            p2 = [ps(es, "mp2%d" % i, [128, 512]) for i in range(4)]
            xls = [[T(None, "xl") for _ in range(4)] for _ in range(2)]
            xss = [[T(None, "xs") for _ in range(4)] for _ in range(2)]
            xrows = D["x"].ap.rearrange("(n p) d -> n p d", p=128)
            orows = out_d.ap.rearrange("(n p) d -> n p d", p=128)
            cnt = {"n1": 0, "n2": 0}

            def pro_a(tb, s):
                xb = x1s[tb % 2]
                tt_ = tb * 4 + s
                f.dma("sp", xb.ap[:, s, :], xrows[tt_], writes=[xb], sem=xls[tb % 2][s])
                for dh in range(2):
                    f.mm([lambda e, k=k, dh=dh: e.matmul(po.ap, lhsT=yT.ap[:, k, tt_ * 128:(tt_ + 1) * 128],
                                                         rhs=Wo.ap[:, k, dh * 512:(dh + 1) * 512], start=(k == 0), stop=(k == 7))
                          for k in range(8)], reads=[yT, Wo], writes=[po])
                    f.op("dve", lambda e, dh=dh: e.tensor_tensor(out=xb.ap[:, s, dh * 512:(dh + 1) * 512], in0=po.ap,
                                                                 in1=xb.ap[:, s, dh * 512:(dh + 1) * 512], op=ALU.add),
                         reads=[po, xb], writes=[xb])

            def pro_b(tb, s):
                xb = x1s[tb % 2]
                s_, r_, h_ = ssq[s % 2], rsq[s % 2], hm[s % 2]
                f.op("act", lambda e: e.activation(out=junk.ap, in_=xb.ap[:, s, :], func=ACTF.Square, accum_out=s_.ap), reads=[xb], writes=[junk, s_])
                rstd_of(s_.ap, s_, r_.ap, r_, 1.0 / 1024)
                f.op("act", lambda e: e.activation(out=h_.ap, in_=xb.ap[:, s, :], func=ACTF.Copy, scale=r_.ap[:, 0:1]), reads=[xb, r_], writes=[h_])
                f.mm([lambda e, k=k: e.transpose(out=pT.ap[:, k, :], in_=h_.ap[:, k * 128:(k + 1) * 128], identity=ident.ap)
                      for k in range(8)], reads=[h_, ident], writes=[pT])
                f.op("dve", lambda e: e.tensor_copy(out=hmT[tb % 2].ap[:, :, s * 128:(s + 1) * 128], in_=pT.ap), reads=[pT], writes=[hmT[tb % 2]])

            def fc1_group(tb, fg):
                ws = W1s[cnt["n1"] % 2]
                cnt["n1"] += 1
                f.dma("sp", ws.ap, W1B.ap[:, :, fg * 256:(fg + 1) * 256], reads=[W1B], writes=[ws])
                for fl in range(2):
                    fc = fg * 2 + fl
                    pp = pw[fc % 2]
                    f.mm([lambda e, k=k: e.matmul(pp.ap, lhsT=ws.ap[:, k, fl * 128:(fl + 1) * 128], rhs=hmT[tb % 2].ap[:, k, :],
                                                  start=(k == 0), stop=(k == 7)) for k in range(8)], reads=[ws, hmT[tb % 2]], writes=[pp])
                    rr_ = rl[fc % 2]
                    f.op("act", lambda e: e.activation(out=rr_.ap, in_=pp.ap, func=ACTF.Relu), reads=[pp], writes=[rr_])
                    f.op("pool", lambda e: e.tensor_tensor(out=aT.ap[:, fc, :], in0=rr_.ap, in1=rr_.ap, op=ALU.mult), reads=[rr_], writes=[aT])

            def fc2_all(tb):
                xb = x1s[tb % 2]
                for dh in range(2):
                    for fg in range(8):
                        ws = W2s[cnt["n2"] % 2]
                        cnt["n2"] += 1
                        f.dma("sp", ws.ap, W2B.ap[:, fg * 4:(fg + 1) * 4, dh * 512:(dh + 1) * 512], reads=[W2B], writes=[ws])
                        for s in range(4):
                            f.mm([lambda e, fl=fl: e.matmul(p2[s].ap, lhsT=aT.ap[:, fg * 4 + fl, s * 128:(s + 1) * 128], rhs=ws.ap[:, fl, :],
                                                            start=(fg == 0 and fl == 0), stop=(fg == 7 and fl == 3)) for fl in range(4)],
                                 reads=[aT, ws], writes=[p2[s]])
                    for s in range(4):
                        f.op("dve", lambda e: e.tensor_tensor(out=xb.ap[:, s, dh * 512:(dh + 1) * 512], in0=p2[s].ap,
                                                              in1=xb.ap[:, s, dh * 512:(dh + 1) * 512], op=ALU.add), reads=[p2[s], xb], writes=[xb])

            def epilogue(tb):
                xb = x1s[tb % 2]
                for s in range(4):
                    s_, r_ = ssq[2 + s % 2], rsq[2 + s % 2]
                    f.op("act", lambda e: e.activation(out=junk.ap, in_=xb.ap[:, s, :], func=ACTF.Square, accum_out=s_.ap), reads=[xb], writes=[junk, s_])
                    rstd_of(s_.ap, s_, r_.ap, r_, 1.0 / 1024)
                    f.op("act", lambda e: e.activation(out=xb.ap[:, s, :], in_=xb.ap[:, s, :], func=ACTF.Copy, scale=r_.ap[:, 0:1]), reads=[xb, r_], writes=[xb])
                    f.op("pool", lambda e: e.tensor_tensor(out=xb.ap[:, s, :], in0=xb.ap[:, s, :], in1=gfin.ap, op=ALU.mult), reads=[xb, gfin], writes=[xb])
                    f.dma("pool", orows[tb * 4 + s], xb.ap[:, s, :], reads=[xb], writes=[out_d], sem=xss[tb % 2][s], accumulate=True)

            for s in range(4):
                pro_a(0, s)
                pro_b(0, s)
            for tb in range(8):
                for fg in range(16):
                    fc1_group(tb, fg)
                    if tb + 1 < 8:
                        if fg % 4 == 0:
                            pro_a(tb + 1, fg // 4)
                        if fg % 4 == 3:
                            pro_b(tb + 1, fg // 4)
                fc2_all(tb)
                epilogue(tb)

    def alloc_yT():
        yTbox.append(sb(top, "yT", [128, 8, L], BF16))
    phases = [("p0", phase0), ("p1", phase1), ("p1b", phase1b), ("p3", phase3), ("yT", alloc_yT), ("p2", phase2),
              ("p4", lambda: phase_h(0)), ("p5", lambda: phase_h(1)), ("p6", phase6)]
    for i, (nm, fn) in enumerate(phases):
        if i <= upto:
            fn()
    f.wait_all("pool", [out_d, RAW, U, XG, FA, FB, KH, Z1, W1B, W2B])
    top.close()
    return nc


def make_in_maps(inputs):
    C = host_consts()
    g = {k: np.asarray(v, dtype=np.float32) for k, v in inputs.items()}

    def pk(v):
        return np.ascontiguousarray(v.reshape(8, 128).T)
    shared = {
        "gm": pk(g["g_mix"][0]), "w_in": g["w_in"][0], "cw": np.concatenate([g["conv_w"][0], g["conv_b"]], 0),
        "f_w0": g["filt_w0"][0],
        "f_b": np.ascontiguousarray(np.stack([g["filt_b0"][0], g["filt_b_inner"][0, 0], g["filt_b_inner"][0, 1]], 1)),
        "f_wi": np.ascontiguousarray(g["filt_w_inner"][0].transpose(1, 0, 2)),
        "f_fr": np.ascontiguousarray(g["filt_freq"][0][:, None]),
        "f_wo": g["filt_w_out"][0], "long_d": g["long_d"],
        "gfh": pk(np.concatenate([g["g_fourier"][0], g["g_hyena"][0]])),
        "w_out": g["w_out"][0], "gml": pk(g["g_mlp"][0]), "w_fc1": g["w_fc1"][0], "w_fc2": g["w_fc2"][0],
        "gfin": g["g_final"][None, :],
    }
    shared.update({k: C[k] for k, _, _ in CONST_SPECS})
    shared = {k: np.ascontiguousarray(v) for k, v in shared.items()}
    maps = []
    for b in range(8):
        m = dict(shared)
        m["x"] = np.ascontiguousarray(g["x"][b])
        maps.append(m)
    return maps


def kernel(**inputs):
    nc = build()
    in_maps = make_in_maps(inputs)
    res = run_bass_kernel_spmd(nc, in_maps, core_ids=list(range(8)))
    return np.stack([np.asarray(r["out"], dtype=np.float32) for r in res.results], 0)
```

```python
import numpy as np
import ml_dtypes
from contextlib import ExitStack
import concourse.bass as bass
import concourse.mybir as mybir
from concourse.bass_utils import run_bass_kernel_spmd

F32 = mybir.dt.float32
BF16 = mybir.dt.bfloat16
I32 = mybir.dt.int32
ALU = mybir.AluOpType
ACTF = mybir.ActivationFunctionType
EPOCH = 8000
L = 4096
NFFT = 8192
EPS = 1e-5
NB = ml_dtypes.bfloat16


class Res:
    __slots__ = ("name", "w", "rd")

    def __init__(self, name):
        self.name = name
        self.w = {}
        self.rd = []


class T:
    __slots__ = ("ap", "r")

    def __init__(self, ap, name):
        self.ap = ap
        self.r = Res(name)


class FW:
    def __init__(self, nc):
        self.nc = nc
        self.eng = {"pe": nc.tensor, "act": nc.scalar, "dve": nc.vector, "pool": nc.gpsimd, "sp": nc.sync}
        self.sems = {}
        self.cnt = {k: 0 for k in ("pe", "act", "dve", "pool")}
        self.waited = {k: {} for k in self.eng}
        self.nsem = 0
        self.recent = {}
        self.gen = 0
        self.free_dma = []
        self._dcnt = {}

    def _sem(self, key):
        s = self.sems.get(key)
        if s is None:
            s = self.nc.alloc_semaphore("s%d" % self.nsem)
            self.nsem += 1
            self.sems[key] = s
        return s

    def _wait(self, engine, deps):
        w = self.waited[engine]
        best = {}
        for key, val in deps:
            if engine == "pe" and key[0] == "pe":
                continue
            if key[0] == "dma" and key[2] < self.gen:
                continue
            if w.get(key, 0) < val and best.get(key, 0) < val:
                best[key] = val
        for key, val in best.items():
            self.eng[engine].wait_ge(self._sem(key), val)
            w[key] = val

    @staticmethod
    def _deps(reads, writes):
        deps = []
        for r in reads:
            deps.extend(r.w.items())
        for r in writes:
            deps.extend(r.w.items())
            deps.extend(r.rd)
        return deps

    @staticmethod
    def _mark(tok, reads, writes, accumulate=False):
        for r in writes:
            if r.w.get(tok[0], 0) < tok[1]:
                r.w[tok[0]] = tok[1]
            r.rd = []
        for r in reads:
            r.rd.append(tok)
            if len(r.rd) > 48:
                d = {}
                for k, v in r.rd:
                    if d.get(k, 0) < v:
                        d[k] = v
                r.rd = list(d.items())

    def _tick(self, engine, inst):
        c = self.cnt[engine]
        self.cnt[engine] = c + 1
        key = (engine, c // EPOCH)
        inst.then_inc(self._sem(key), 1)
        self.recent[key] = c % EPOCH + 1
        return (key, c % EPOCH + 1)

    def barrier(self):
        toks = list(self.recent.items())
        for eng in self.eng:
            self._wait(eng, toks)
        self.recent = {}
        for key in [k for k in self.sems if k[0] == "dma"]:
            self.free_dma.append((self.sems.pop(key), self._dcnt.pop(key)))
        self.gen += 1

    def op(self, engine, fn, reads=(), writes=()):
        reads = [t.r for t in reads]
        writes = [t.r for t in writes]
        self._wait(engine, self._deps(reads, writes))
        tok = self._tick(engine, fn(self.eng[engine]))
        self._mark(tok, reads, writes)

    def mm(self, fns, reads=(), writes=()):
        reads = [t.r for t in reads]
        writes = [t.r for t in writes]
        self._wait("pe", self._deps(reads, writes))
        pe = self.eng["pe"]
        for fn in fns[:-1]:
            fn(pe)
        tok = self._tick("pe", fns[-1](pe))
        self._mark(tok, reads, writes)

    def dma(self, queue, out_ap, in_ap, reads=(), writes=(), sem=None, accumulate=False):
        reads = [t.r for t in reads]
        writes_r = [t.r for t in writes]
        self._wait(queue, self._deps(reads, writes_r))
        s = sem if sem is not None else writes[0]
        key = ("dma", id(s.r), self.gen)
        cnts = self._dcnt
        if key not in self.sems and self.free_dma:
            self.sems[key], cnts[key] = self.free_dma.pop()
        cnts[key] = cnts.get(key, 0) + 16
        self.eng[queue].dma_start(out=out_ap, in_=in_ap).then_inc(self._sem(key), 16)
        tok = (key, cnts[key])
        self.recent[key] = cnts[key]
        self._mark(tok, reads, writes_r, accumulate=accumulate)

    def wait_all(self, engine, ts):
        deps = []
        for t in ts:
            deps.extend(t.r.w.items())
            deps.extend(t.r.rd)
        self._wait(engine, deps)


class Scope:
    def __init__(self, f):
        self.f = f
        self.es = ExitStack()

    def __enter__(self):
        self.es.__enter__()
        return self.es

    def __exit__(self, *a):
        self.f.barrier()
        return self.es.__exit__(*a)


_CONST = None


def host_consts():
    global _CONST
    if _CONST is not None:
        return _CONST
    c = {}
    p = np.arange(128)[:, None, None].astype(np.float64)
    j = np.arange(32)[None, :, None].astype(np.float64)
    cc = np.arange(128)[None, None, :].astype(np.float64)
    ph = 2 * np.pi * (32 * p + j) * (cc + 0.5) / NFFT
    c["HF"] = np.stack([np.cos(ph), -np.sin(ph)], 2).astype(NB)
    pht = ph.transpose(2, 1, 0)
    c["HG"] = np.stack([np.cos(pht), -np.sin(pht)], 2).astype(NB)
    pf = 2 * np.pi * (32 * p + j) * cc / L
    c["FF"] = np.stack([np.cos(pf), np.sin(pf), -np.sin(pf)], 2).astype(NB)
    jd = 2 * np.pi * np.outer(np.arange(32), np.arange(32)) / 32
    C4 = np.kron(np.eye(4), np.cos(jd))
    S4 = np.kron(np.eye(4), np.sin(jd))
    c["CS"] = np.stack([C4, S4, -S4], 1).astype(NB)
    cd = 2 * np.pi * np.outer(np.arange(64), np.arange(64)) / 64
    c["BD"] = np.concatenate([np.kron(np.eye(2), np.cos(cd)), np.kron(np.eye(2), np.sin(cd))], 1).astype(NB)
    c["ident"] = np.eye(128).astype(NB)
    sel = np.zeros((128, 3, 128))
    sel[:, 0, :] = 1.0
    sel[0, 1, :] = 1.0
    sel[0, 2, :] = -1.0
    c["SEL"] = sel.astype(NB)
    pos = np.arange(L, dtype=np.float64)
    t = pos / (L - 1)
    bands = np.linspace(1e-4, 15, 16)
    ang = (2 * np.pi * pos / L)[:, None] * bands[None]
    z = np.concatenate([t[:, None], np.cos(ang), -np.sin(ang)], -1)
    zT = z.T.reshape(33, 128, 32).transpose(0, 2, 1).reshape(33, L)
    c["zT"] = np.ascontiguousarray(zT).astype(np.float32)
    deltas = np.linspace(np.log(1e-2) / 1.5, np.log(1e-2) / 0.3, 512)
    decay = np.exp(-t[:, None] * np.abs(deltas)[None])
    c["decay"] = decay.reshape(128, 32, 512).astype(np.float32)
    _CONST = c
    return c


CONST_SPECS = [("HF", [128, 32, 2, 128], BF16), ("HG", [128, 32, 2, 128], BF16), ("FF", [128, 32, 3, 128], BF16),
               ("CS", [128, 3, 128], BF16), ("BD", [128, 256], BF16), ("ident", [128, 128], BF16),
               ("SEL", [128, 3, 128], BF16), ("zT", [33, L], F32), ("decay", [128, 32, 512], F32)]

IN_SPECS = [("x", [L, 1024], F32), ("gm", [128, 8], F32), ("w_in", [1024, 2048], F32), ("cw", [4, 1536], F32),
            ("f_w0", [33, 64], F32), ("f_b", [64, 3], F32), ("f_wi", [64, 2, 64], F32), ("f_fr", [64, 1], F32),
            ("f_wo", [64, 2048], F32), ("long_d", [1, 2, 512], F32), ("gfh", [128, 8], F32),
            ("w_out", [1024, 1024], F32), ("gml", [128, 8], F32), ("w_fc1", [1024, 4096], F32),
            ("w_fc2", [4096, 1024], F32), ("gfin", [1, 1024], F32)]


def build(upto=99, debug=False):
    nc = bass.Bass("TRN2", target_bir_lowering=False)
    f = FW(nc)
    D = {}
    for name, shape, dt in IN_SPECS + CONST_SPECS:
        D[name] = T(nc.dram_tensor(name, shape, dt, kind="ExternalInput").ap(), name)
    out_d = T(nc.dram_tensor("out", [L, 1024], F32, kind="ExternalOutput").ap(), "out")
    def scratch(name, shape, dt):
        k = "ExternalOutput" if (debug is True or (debug and name in debug)) else "Internal"
        return T(nc.dram_tensor(name, shape, dt, kind=k).ap(), name)

    RAW = scratch("RAW", [128, 34, 1536], BF16)
    U = scratch("U", [128, 32, 1024], BF16)
    XG = scratch("XG", [3, 128, 32, 512], BF16)
    FA = scratch("FA", [4, 128, 32, 512], BF16)
    FB = scratch("FB", [2, 128, 32, 512], BF16)
    KH = scratch("KH", [2, 2, 128, 32, 512], BF16)
    Z1 = scratch("Z1", [128, 32, 512], BF16)
    W1B = scratch("W1B", [128, 8, 4096], BF16)
    W2B = scratch("W2B", [128, 32, 1024], BF16)

    uid = [0]

    def sb(es, name, shape, dt):
        uid[0] += 1
        return T(es.enter_context(nc.sbuf_tensor("sb%d_%s" % (uid[0], name), shape, dt)).ap(), name)

    def ps(es, name, shape, dt=F32):
        uid[0] += 1
        return T(es.enter_context(nc.psum_tensor("ps%d_%s" % (uid[0], name), shape, dt)).ap(), name)

    top = ExitStack()
    ident = sb(top, "ident", [128, 128], BF16)
    CS = sb(top, "CS", [128, 3, 128], BF16)
    epsT = sb(top, "epsT", [128, 1], F32)
    yTbox = []
    scl = sb(top, "scl", [128, 2, 512], F32)
    f.dma("sp", ident.ap, D["ident"].ap, writes=[ident])
    f.dma("sp", CS.ap, D["CS"].ap, writes=[CS])
    f.op("pool", lambda e: e.memset(epsT.ap, EPS), writes=[epsT])

    def rstd_of(ss_ap, ss_t, out_ap, out_t, scale):
        f.op("act", lambda e: e.activation(out=out_ap, in_=ss_ap, func=ACTF.Sqrt, scale=scale, bias=epsT.ap[:, 0:1]),
             reads=[ss_t, epsT], writes=[out_t])
        f.op("dve", lambda e: e.reciprocal(out=out_ap, in_=out_ap), reads=[out_t], writes=[out_t])

    def phase0():
        with Scope(f) as es:
            gml = sb(es, "gml", [128, 8], F32)
            f.dma("sp", gml.ap, D["gml"].ap, writes=[gml])
            st = [sb(es, "p0s%d" % i, [128, 8, 512], F32) for i in range(2)]
            sbf = [sb(es, "p0b%d" % i, [128, 8, 512], BF16) for i in range(2)]
            w1v = D["w_fc1"].ap.rearrange("(k p) f -> p k f", p=128)
            for i in range(8):
                s, b = st[i % 2], sbf[i % 2]
                f.dma("sp", s.ap, w1v[:, :, i * 512:(i + 1) * 512], writes=[s])
                for k in range(8):
                    if k % 2 == 0:
                        f.op("dve", lambda e, k=k: e.tensor_scalar(out=b.ap[:, k, :], in0=s.ap[:, k, :], scalar1=gml.ap[:, k:k + 1],
                                                                   scalar2=None, op0=ALU.mult), reads=[s, gml], writes=[b])
                    else:
                        f.op("act", lambda e, k=k: e.activation(out=b.ap[:, k, :], in_=s.ap[:, k, :], func=ACTF.Copy,
                                                                scale=gml.ap[:, k:k + 1]), reads=[s, gml], writes=[b])
                f.dma("pool", W1B.ap[:, :, i * 512:(i + 1) * 512], b.ap, reads=[b], writes=[W1B], sem=b, accumulate=True)
            w2v = D["w_fc2"].ap.rearrange("(c p) d -> p c d", p=128)
            for i in range(8):
                s, b = st[i % 2], sbf[i % 2]
                s4 = s.ap.rearrange("p (a b) n -> p a (b n)", a=4)
                b4 = b.ap.rearrange("p (a b) n -> p a (b n)", a=4)
                f.dma("sp", s4, w2v[:, i * 4:(i + 1) * 4, :], writes=[s])
                f.op("dve", lambda e: e.tensor_copy(out=b4[:, 0:2], in_=s4[:, 0:2]), reads=[s], writes=[b])
                f.op("act", lambda e: e.activation(out=b4[:, 2:4], in_=s4[:, 2:4], func=ACTF.Copy), reads=[s], writes=[b])
                f.dma("pool", W2B.ap[:, i * 4:(i + 1) * 4, :], b4, reads=[b], writes=[W2B], sem=b, accumulate=True)

    def phase1():
        with Scope(f) as es:
            W1 = sb(es, "W1", [128, 8, 2048], BF16)
            gm = sb(es, "gm", [128, 8], F32)
            BD = sb(es, "BD", [128, 256], BF16)
            f.dma("sp", gm.ap, D["gm"].ap, writes=[gm])
            f.dma("sp", BD.ap, D["BD"].ap, writes=[BD])
            wst = [sb(es, "wst%d" % i, [128, 2048], F32) for i in range(2)]
            wv = D["w_in"].ap.rearrange("(k p) e -> p k e", p=128)
            for k in range(8):
                s = wst[k % 2]
                f.dma("sp", s.ap, wv[:, k, :], writes=[s])
                if k % 2 == 0:
                    f.op("dve", lambda e, k=k, s=s: e.tensor_scalar(out=W1.ap[:, k, :], in0=s.ap, scalar1=gm.ap[:, k:k + 1], scalar2=None,
                                                                    op0=ALU.mult), reads=[s, gm], writes=[W1])
                else:
                    f.op("act", lambda e, k=k, s=s: e.activation(out=W1.ap[:, k, :], in_=s.ap, func=ACTF.Copy, scale=gm.ap[:, k:k + 1]),
                         reads=[s, gm], writes=[W1])
            xts = [sb(es, "xt%d" % i, [128, 1024], F32) for i in range(3)]
            junk = sb(es, "junk", [128, 1024], BF16)
            ss = [sb(es, "ss%d" % i, [128, 1], F32) for i in range(2)]
            rs = [sb(es, "rs%d" % i, [128, 1], F32) for i in range(2)]
            xs = [sb(es, "xs%d" % i, [128, 1024], BF16) for i in range(2)]
            hT = [sb(es, "hT%d" % i, [128, 8, 128], BF16) for i in range(2)]
            raw = [sb(es, "raw%d" % i, [128, 1536], BF16) for i in range(2)]
            ufT = [sb(es, "ufT%d" % i, [128, 4, 128], BF16) for i in range(2)]
            Ut = [sb(es, "Ut%d" % i, [128, 4, 256], BF16) for i in range(2)]
            pT = ps(es, "pT", [128, 8, 128], BF16)
            pH = [ps(es, "pH%d" % i, [128, 512]) for i in range(3)]
            pF = ps(es, "pF", [128, 4, 128])
            pU = ps(es, "pU", [128, 4, 256])
            xv = D["x"].ap.rearrange("(p j) d -> p j d", j=32)
            for j in range(32):
                xt = xts[j % 3]
                s_, r_, x_, h_, rw, uf, ut = ss[j % 2], rs[j % 2], xs[j % 2], hT[j % 2], raw[j % 2], ufT[j % 2], Ut[j % 2]
                f.dma("sp", xt.ap, xv[:, j, :], writes=[xt])
                f.op("act", lambda e: e.activation(out=junk.ap, in_=xt.ap, func=ACTF.Square, accum_out=s_.ap),
                     reads=[xt], writes=[junk, s_])
                rstd_of(s_.ap, s_, r_.ap, r_, 1.0 / 1024)
                f.op("act", lambda e: e.activation(out=x_.ap, in_=xt.ap, func=ACTF.Copy, scale=r_.ap[:, 0:1]),
                     reads=[xt, r_], writes=[x_])
                f.mm([lambda e, k=k: e.transpose(out=pT.ap[:, k, :], in_=x_.ap[:, k * 128:(k + 1) * 128], identity=ident.ap)
                      for k in range(8)], reads=[x_, ident], writes=[pT])
                f.op("dve", lambda e: e.tensor_copy(out=h_.ap, in_=pT.ap), reads=[pT], writes=[h_])
                for n in range(3):
                    f.mm([lambda e, k=k, n=n: e.matmul(pH[n].ap, lhsT=h_.ap[:, k, :], rhs=W1.ap[:, k, 512 + n * 512:1024 + n * 512],
                                                       start=(k == 0), stop=(k == 7)) for k in range(8)],
                         reads=[h_, W1], writes=[pH[n]])
                    f.op("act", lambda e, n=n: e.activation(out=rw.ap[:, n * 512:(n + 1) * 512], in_=pH[n].ap, func=ACTF.Copy),
                         reads=[pH[n]], writes=[rw])
                f.dma("pool", RAW.ap[:, j + 1, :], rw.ap, reads=[rw], writes=[RAW], sem=rw, accumulate=True)
                fns = []
                for ec in range(4):
                    for k in range(8):
                        fns.append(lambda e, k=k, ec=ec: e.matmul(pF.ap[:, ec, :], lhsT=W1.ap[:, k, ec * 128:(ec + 1) * 128],
                                                                  rhs=h_.ap[:, k, :], start=(k == 0), stop=(k == 7)))
                f.mm(fns, reads=[h_, W1], writes=[pF])
                f.op("dve", lambda e: e.tensor_copy(out=uf.ap, in_=pF.ap), reads=[pF], writes=[uf])
                f.mm([lambda e, ec=ec: e.matmul(pU.ap[:, ec, :], lhsT=uf.ap[:, ec, :], rhs=BD.ap, start=True, stop=True)
                      for ec in range(4)], reads=[uf, BD], writes=[pU])
                f.op("dve", lambda e: e.tensor_copy(out=ut.ap, in_=pU.ap), reads=[pU], writes=[ut])
                f.dma("pool", U.ap[:, j, :], ut.ap.rearrange("p a b -> p (a b)"), reads=[ut], writes=[U], sem=ut, accumulate=True)
            zrow = sb(es, "zrow", [1, 1536], BF16)
            f.op("pool", lambda e: e.memset(zrow.ap, 0.0), writes=[zrow])
            halo = T(None, "halo")
            f.dma("pool", RAW.ap[1:128, 0, :], RAW.ap[0:127, 32, :], reads=[RAW], writes=[halo])
            f.dma("pool", RAW.ap[0:127, 33, :], RAW.ap[1:128, 1, :], reads=[RAW], writes=[halo], accumulate=True)
            f.dma("pool", RAW.ap[0:1, 0, :], zrow.ap, reads=[zrow], writes=[halo], accumulate=True)
            f.dma("pool", RAW.ap[127:128, 33, :], zrow.ap, reads=[zrow], writes=[halo], accumulate=True)
            RAW.r.w.update(halo.r.w)

    def phase1b():
        with Scope(f) as es:
            cw = sb(es, "cw", [128, 4, 1536], F32)
            f.dma("sp", cw.ap, D["cw"].ap.partition_broadcast(128), writes=[cw])
            rg = [sb(es, "rg%d" % i, [128, 4, 1536], BF16) for i in range(2)]
            acc = sb(es, "cacc", [128, 2, 1536], F32)
            t1 = sb(es, "ct1", [128, 2, 1536], F32)
            t2 = sb(es, "ct2", [128, 2, 1536], F32)
            og = [sb(es, "og%d" % i, [128, 2, 1536], BF16) for i in range(2)]

            def wb(i):
                return cw.ap[:, i:i + 1, :].broadcast_to([128, 2, 1536])
            for g in range(16):
                r, o = rg[g % 2], og[g % 2]
                f.dma("sp", r.ap, RAW.ap[:, 2 * g:2 * g + 4, :], reads=[RAW], writes=[r])
                f.op("pool", lambda e: e.tensor_tensor(out=t1.ap, in0=r.ap[:, 0:2, :], in1=wb(0), op=ALU.mult), reads=[r, cw], writes=[t1])
                f.op("dve", lambda e: e.tensor_tensor(out=acc.ap, in0=r.ap[:, 1:3, :], in1=wb(1), op=ALU.mult), reads=[r, cw], writes=[acc])
                f.op("pool", lambda e: e.tensor_tensor(out=t2.ap, in0=r.ap[:, 2:4, :], in1=wb(2), op=ALU.mult), reads=[r, cw], writes=[t2])
                f.op("dve", lambda e: e.tensor_tensor(out=acc.ap, in0=acc.ap, in1=t1.ap, op=ALU.add), reads=[acc, t1], writes=[acc])
                f.op("dve", lambda e: e.tensor_tensor(out=acc.ap, in0=acc.ap, in1=t2.ap, op=ALU.add), reads=[acc, t2], writes=[acc])
                f.op("dve", lambda e: e.tensor_tensor(out=o.ap, in0=acc.ap, in1=wb(3), op=ALU.add), reads=[acc, cw], writes=[o])
                for t in range(3):
                    f.dma("pool", XG.ap[t, :, 2 * g:2 * g + 2, :], o.ap[:, :, t * 512:(t + 1) * 512], reads=[o], writes=[XG],
                          sem=o, accumulate=True)

    def relayout_view(dram_ap):
        return dram_ap.rearrange("(cp q) j n -> (q j) cp n", q=4)

    def load_split(queue, dst, dst_ap, src_ap, reads, n):
        tot = dst_ap.shape[1]
        step = tot // n
        for i in range(n):
            f.dma(queue, dst_ap[:, i * step:(i + 1) * step], src_ap[:, i * step:(i + 1) * step], reads=reads, writes=[dst],
                  sem=T(None, "ls"))

    def stage1(es, src, mats, combos, dst_slots, tagp):
        nslot = len(combos)
        pAA = [[ps(es, "%spA%d_%d" % (tagp, i, b), [128, 512]) for i in range(nslot)] for b in range(2)]
        stg = [sb(es, "%sstg%d" % (tagp, i), [128, nslot, 4, 512], BF16) for i in range(2)]
        for j in range(32):
            sg = stg[(j // 4) % 2]
            pA = pAA[j % 2]
            for si, terms in enumerate(combos):
                f.mm([lambda e, mi=mi, rf=rf, ti=ti, si=si: e.matmul(pA[si].ap, lhsT=mats.ap[:, j, mi, :], rhs=rf(j),
                                                                       start=(ti == 0), stop=(ti == len(terms) - 1))
                      for ti, (mi, rf) in enumerate(terms)], reads=[src, mats], writes=[pA[si]])
                eng = "act" if (si + j) % 2 == 0 else "dve"
                if eng == "act":
                    f.op("act", lambda e, si=si: e.activation(out=sg.ap[:, si, j % 4, :], in_=pA[si].ap, func=ACTF.Copy),
                         reads=[pA[si]], writes=[sg])
                else:
                    f.op("dve", lambda e, si=si: e.tensor_copy(out=sg.ap[:, si, j % 4, :], in_=pA[si].ap), reads=[pA[si]], writes=[sg])
            if j % 4 == 3:
                for si in range(nslot):
                    f.dma("pool", FA.ap[dst_slots[si], :, j - 3:j + 1, :], sg.ap[:, si], reads=[sg], writes=[FA], sem=sg, accumulate=True)

    def phase2():
        with Scope(f) as es:
            FF = sb(es, "FF", [128, 32, 3, 128], BF16)
            f.dma("sp", FF.ap, D["FF"].ap, writes=[FF])
            Us = sb(es, "Us", [128, 32, 4, 2, 128], BF16)
            load_split("sp", Us, Us.ap.rearrange("p j a b n -> p j (a b n)"), U.ap, [U], 4)
            uc = lambda j: Us.ap[:, j, :, 0, :]
            us = lambda j: Us.ap[:, j, :, 1, :]
            with Scope(f) as es2:
                stage1(es2, Us, FF, [[(0, uc), (2, us)], [(1, uc), (0, us)]], [0, 1], "f")
        with Scope(f) as es:
            A2 = [sb(es, "fA2%d" % i, [128, 32, 512], BF16) for i in range(2)]
            for i in range(2):
                load_split("sp", A2[i], A2[i].ap, relayout_view(FA.ap[i]), [FA], 4)
            pY = [ps(es, "fpY%d" % i, [128, 512]) for i in range(2)]
            pT2 = [ps(es, "fpT%d" % i, [128, 4, 128], BF16) for i in range(2)]
            junk = sb(es, "fjunk", [128, 512], BF16)
            ssq = [sb(es, "fss%d" % i, [128, 1], F32) for i in range(2)]
            rsq = [sb(es, "frs%d" % i, [128, 1], F32) for i in range(2)]
            yn = [sb(es, "fyn%d" % i, [128, 512], BF16) for i in range(2)]
            yT = yTbox[0]
            yTv = yT.ap.rearrange("p k (d cp q) -> p k cp q d", d=32, cp=32, q=4)
            for cp in range(32):
                py, pt, s_, r_, y_ = pY[cp % 2], pT2[cp % 2], ssq[cp % 2], rsq[cp % 2], yn[cp % 2]
                f.mm([lambda e: e.matmul(py.ap, lhsT=CS.ap[:, 0, :], rhs=A2[0].ap[:, cp, :], start=True, stop=False),
                      lambda e: e.matmul(py.ap, lhsT=CS.ap[:, 2, :], rhs=A2[1].ap[:, cp, :], start=False, stop=True)],
                     reads=[CS, A2[0], A2[1]], writes=[py])
                f.op("act", lambda e: e.activation(out=junk.ap, in_=py.ap, func=ACTF.Square, accum_out=s_.ap), reads=[py], writes=[junk, s_])
                rstd_of(s_.ap, s_, r_.ap, r_, 1.0 / (512.0 * 512.0 * 512.0))
                f.op("dve", lambda e: e.tensor_scalar(out=y_.ap, in0=py.ap, scalar1=r_.ap[:, 0:1], scalar2=1.0 / 512.0, op0=ALU.mult,
                                                      op1=ALU.mult), reads=[py, r_], writes=[y_])
                f.mm([lambda e, k=k: e.transpose(out=pt.ap[:, k, :], in_=y_.ap[:, k * 128:(k + 1) * 128], identity=ident.ap)
                      for k in range(4)], reads=[y_, ident], writes=[pt])
                f.op("act", lambda e: e.activation(out=yTv[:, 0:4, cp], in_=pt.ap.rearrange("p k (q d) -> p k q d", q=4),
                                                   func=ACTF.Copy), reads=[pt], writes=[yT])

    def phase3():
        with Scope(f) as es:
            hcur = [sb(es, "fh%d" % i, [64, L], F32) for i in range(2)]
            wo = sb(es, "fwo", [64, 2, 2, 512], F32)
            wsd = sb(es, "fwsd", [64, 2, 2, 512], F32)
            HF = sb(es, "HF3", [128, 32, 2, 128], BF16)
            SEL = sb(es, "SEL", [128, 3, 128], BF16)
            dl = sb(es, "dl", [1, 2, 512], F32)
            f.dma("sp", wo.ap.rearrange("p a b n -> p (a b n)"), D["f_wo"].ap, writes=[wo])
            f.dma("sp", HF.ap, D["HF"].ap, writes=[HF])
            f.dma("sp", SEL.ap, D["SEL"].ap, writes=[SEL])
            f.dma("sp", dl.ap, D["long_d"].ap, writes=[dl])
            for o in range(2):
                f.op("dve", lambda e, o=o: e.tensor_tensor(out=wsd.ap[:, o, 0, :], in0=wo.ap[:, o, 0, :], in1=wo.ap[:, o, 1, :], op=ALU.add),
                     reads=[wo], writes=[wsd])
                f.op("dve", lambda e, o=o: e.tensor_tensor(out=wsd.ap[:, o, 1, :], in0=wo.ap[:, o, 0, :], in1=wo.ap[:, o, 1, :], op=ALU.subtract),
                     reads=[wo], writes=[wsd])
            with Scope(f) as em:
                zT = sb(em, "zT", [33, L], F32)
                w0 = sb(em, "fw0", [33, 64], F32)
                fb = sb(em, "fb", [64, 3], F32)
                wi = sb(em, "fwi", [64, 2, 64], F32)
                fr = sb(em, "ffr", [64, 1], F32)
                s1 = sb(em, "fs1", [64, 1], F32)
                s2 = sb(em, "fs2", [64, 3], F32)
                f.dma("sp", zT.ap, D["zT"].ap, writes=[zT])
                f.dma("sp", w0.ap, D["f_w0"].ap, writes=[w0])
                f.dma("sp", fb.ap, D["f_b"].ap, writes=[fb])
                f.dma("sp", wi.ap, D["f_wi"].ap, writes=[wi])
                f.dma("sp", fr.ap, D["f_fr"].ap, writes=[fr])
                inv2pi = float(1.0 / (2 * np.pi))
                f.op("dve", lambda e: e.tensor_scalar(out=s1.ap, in0=fr.ap, scalar1=inv2pi, scalar2=None, op0=ALU.mult), reads=[fr], writes=[s1])
                f.op("dve", lambda e: e.tensor_scalar(out=s2.ap, in0=fb.ap, scalar1=s1.ap[:, 0:1], scalar2=16.0, op0=ALU.mult, op1=ALU.add),
                     reads=[fb, s1], writes=[s2])
                pm = [ps(em, "fpm%d" % i, [64, 512]) for i in range(2)]
                rr = [sb(em, "frr%d" % i, [64, 512], F32) for i in range(2)]
                ki = [sb(em, "fki%d" % i, [64, 512], I32) for i in range(2)]
                kf = [sb(em, "fkf%d" % i, [64, 512], F32) for i in range(2)]
                for layer in range(3):
                    src = zT if layer == 0 else hcur[(layer - 1) % 2]
                    dst = hcur[layer % 2]
                    lw = w0.ap if layer == 0 else wi.ap[:, layer - 1, :]
                    lwt = w0 if layer == 0 else wi
                    kdim = 33 if layer == 0 else 64
                    for c8 in range(8):
                        i = c8 % 2
                        cs = slice(c8 * 512, (c8 + 1) * 512)
                        f.mm([lambda e: e.matmul(pm[i].ap, lhsT=lw, rhs=src.ap[0:kdim, cs], start=True, stop=True)], reads=[src, lwt], writes=[pm[i]])
                        f.op("dve", lambda e: e.tensor_scalar(out=rr[i].ap, in0=pm[i].ap, scalar1=s1.ap[:, 0:1], scalar2=s2.ap[:, layer:layer + 1],
                                                              op0=ALU.mult, op1=ALU.add), reads=[pm[i], s1, s2], writes=[rr[i]])
                        f.op("dve", lambda e: e.tensor_copy(out=ki[i].ap, in_=rr[i].ap), reads=[rr[i]], writes=[ki[i]])
                        f.op("dve", lambda e: e.tensor_copy(out=kf[i].ap, in_=ki[i].ap), reads=[ki[i]], writes=[kf[i]])
                        f.op("dve", lambda e: e.tensor_tensor(out=rr[i].ap, in0=rr[i].ap, in1=kf[i].ap, op=ALU.subtract), reads=[rr[i], kf[i]], writes=[rr[i]])
                        f.op("dve", lambda e: e.tensor_single_scalar(out=kf[i].ap, in_=rr[i].ap, scalar=0.5, op=ALU.is_gt), reads=[rr[i]], writes=[kf[i]])
                        f.op("dve", lambda e: e.tensor_tensor(out=rr[i].ap, in0=rr[i].ap, in1=kf[i].ap, op=ALU.subtract), reads=[rr[i], kf[i]], writes=[rr[i]])
                        f.op("act", lambda e: e.activation(out=dst.ap[:, cs], in_=rr[i].ap, func=ACTF.Sin, scale=float(2 * np.pi)),
                             reads=[rr[i]], writes=[dst])
            h3 = sb(es, "fh3b", [64, L], BF16)
            wsdb = sb(es, "fwsdb", [64, 2, 2, 512], BF16)
            f.op("dve", lambda e: e.tensor_copy(out=h3.ap[:, 0:2048], in_=hcur[0].ap[:, 0:2048]), reads=[hcur[0]], writes=[h3])
            f.op("act", lambda e: e.activation(out=h3.ap[:, 2048:4096], in_=hcur[0].ap[:, 2048:4096], func=ACTF.Copy), reads=[hcur[0]], writes=[h3])
            f.op("dve", lambda e: e.tensor_copy(out=wsdb.ap, in_=wsd.ap), reads=[wsd], writes=[wsdb])
            for o in range(2):
                with Scope(f) as eo:
                    sd = [sb(eo, "fsd%d" % i, [128, 32, 512], BF16) for i in range(2)]
                    sq = [sb(eo, "fsq%d" % i, [128, 512], BF16) for i in range(2)]
                    dec = [sb(eo, "fdec%d" % i, [128, 512], F32) for i in range(2)]
                    pk = [ps(eo, "fpk%d" % i, [128, 512]) for i in range(2)]
                    pn = ps(eo, "fpn", [128, 512])
                    sqt = sb(eo, "fsqt", [128, 512], F32)
                    tmp0 = sb(eo, "ftmp0", [1, 512], F32)
                    nsq = 0
                    for j in range(32):
                        dc = dec[j % 2]
                        f.dma("sp", dc.ap, D["decay"].ap[:, j, :], writes=[dc])
                        for v in range(2):
                            f.mm([lambda e, v=v: e.matmul(pk[v].ap, lhsT=h3.ap[:, j * 128:(j + 1) * 128], rhs=wsdb.ap[:, o, v, :], start=True, stop=True)],
                                 reads=[h3, wsdb], writes=[pk[v]])
                            f.op("dve", lambda e, v=v: e.tensor_tensor(out=sd[v].ap[:, j, :], in0=pk[v].ap, in1=dc.ap, op=ALU.mult),
                                 reads=[pk[v], dc], writes=[sd[v]])
                            q_ = sq[nsq % 2]
                            nsq += 1
                            f.op("act", lambda e, v=v: e.activation(out=q_.ap, in_=sd[v].ap[:, j, :], func=ACTF.Square), reads=[sd[v]], writes=[q_])
                            first = (j == 0 and v == 0)
                            fns = [lambda e: e.matmul(pn.ap, lhsT=SEL.ap[:, 0, :], rhs=q_.ap, start=first, stop=False)]
                            if j == 0:
                                fns.append(lambda e, v=v: e.matmul(pn.ap, lhsT=SEL.ap[:, 1 + v, :], rhs=q_.ap, start=False, stop=False))
                            if j == 31 and v == 1:
                                fns = [lambda e: e.matmul(pn.ap, lhsT=SEL.ap[:, 0, :], rhs=q_.ap, start=False, stop=True)]
                            f.mm(fns, reads=[q_, SEL], writes=[pn])
                    f.op("act", lambda e: e.activation(out=sqt.ap, in_=pn.ap, func=ACTF.Sqrt, scale=0.5, bias=epsT.ap[:, 0:1]),
                         reads=[pn, epsT], writes=[sqt])
                    f.op("dve", lambda e, o=o: e.reciprocal(out=scl.ap[:, o, :], in_=sqt.ap), reads=[sqt], writes=[scl])
                    f.op("dve", lambda e, o=o: e.tensor_tensor(out=tmp0.ap, in0=dl.ap[0:1, o, :], in1=sqt.ap[0:1, :], op=ALU.mult),
                         reads=[dl, sqt], writes=[tmp0])
                    f.op("dve", lambda e: e.tensor_tensor(out=sd[0].ap[0:1, 0, :], in0=sd[0].ap[0:1, 0, :], in1=tmp0.ap, op=ALU.add),
                         reads=[sd[0], tmp0], writes=[sd[0]])
                    f.op("dve", lambda e, o=o: e.tensor_scalar(out=scl.ap[:, o, :], in0=scl.ap[:, o, :], scalar1=2.0 / NFFT, scalar2=None, op0=ALU.mult),
                         reads=[scl], writes=[scl])
                    with Scope(f) as es2:
                        stage1(es2, sd[0], HF, [[(0, lambda j: sd[0].ap[:, j, :])], [(1, lambda j: sd[0].ap[:, j, :])]], [0, 1], "k%da" % o)
                    with Scope(f) as es2:
                        stage1(es2, sd[1], HF, [[(0, lambda j: sd[1].ap[:, j, :])], [(1, lambda j: sd[1].ap[:, j, :])]], [2, 3], "k%db" % o)
                with Scope(f) as es2:
                    A2 = [sb(es2, "kA2%d" % i, [128, 16, 512], BF16) for i in range(4)]
                    kst = [sb(es2, "kst%d" % i, [128, 2, 4, 512], BF16) for i in range(2)]
                    pkk = [ps(es2, "kpk%d" % i, [128, 512]) for i in range(2)]
                    for half in range(2):
                        for i in range(4):
                            load_split("sp", A2[i], A2[i].ap, relayout_view(FA.ap[i])[:, half * 16:(half + 1) * 16, :], [FA], 2)
                        for c16 in range(16):
                            cp = half * 16 + c16
                            st_ = kst[(cp // 4) % 2]
                            f.mm([lambda e: e.matmul(pkk[0].ap, lhsT=CS.ap[:, 0, :], rhs=A2[0].ap[:, c16, :], start=True, stop=False),
                                  lambda e: e.matmul(pkk[0].ap, lhsT=CS.ap[:, 1, :], rhs=A2[1].ap[:, c16, :], start=False, stop=True)],
                                 reads=[CS, A2[0], A2[1]], writes=[pkk[0]])
                            f.mm([lambda e: e.matmul(pkk[1].ap, lhsT=CS.ap[:, 0, :], rhs=A2[3].ap[:, c16, :], start=True, stop=False),
                                  lambda e: e.matmul(pkk[1].ap, lhsT=CS.ap[:, 2, :], rhs=A2[2].ap[:, c16, :], start=False, stop=True)],
                                 reads=[CS, A2[2], A2[3]], writes=[pkk[1]])
                            f.op("act", lambda e: e.activation(out=st_.ap[:, 0, cp % 4, :], in_=pkk[0].ap, func=ACTF.Copy), reads=[pkk[0]], writes=[st_])
                            f.op("dve", lambda e: e.tensor_copy(out=st_.ap[:, 1, cp % 4, :], in_=pkk[1].ap), reads=[pkk[1]], writes=[st_])
                            if cp % 4 == 3:
                                for r in range(2):
                                    f.dma("pool", KH.ap[o, r, :, cp - 3:cp + 1, :], st_.ap[:, r], reads=[st_], writes=[KH], sem=st_, accumulate=True)

    def phase_h(o):
        with Scope(f) as es:
            HF = sb(es, "HFh", [128, 32, 2, 128], BF16)
            z = sb(es, "hz", [128, 32, 512], BF16)
            f.dma("sp", HF.ap, D["HF"].ap, writes=[HF])
            if o == 0:
                load_split("sp", z, z.ap, XG.ap[2], [XG], 2)
            else:
                load_split("sp", z, z.ap, Z1.ap, [Z1], 2)
            stage1(es, z, HF, [[(0, lambda j: z.ap[:, j, :])], [(1, lambda j: z.ap[:, j, :])]], [0, 1], "h%d" % o)
        with Scope(f) as es:
            A2 = [sb(es, "hA2%d" % i, [128, 32, 512], BF16) for i in range(2)]
            for i in range(2):
                load_split("sp", A2[i], A2[i].ap, relayout_view(FA.ap[i]), [FA], 4)
            kk = [[sb(es, "hk%d%d" % (i, r), [128, 4, 512], BF16) for r in range(2)] for i in range(2)]
            pXX = [[ps(es, "hpX%d_%d" % (i, b), [128, 512]) for i in range(2)] for b in range(2)]
            pBB = [[ps(es, "hpB%d_%d" % (i, b), [128, 512]) for i in range(2)] for b in range(2)]
            ttt = [[sb(es, "htt%d_%d" % (i, b), [128, 512], F32) for i in range(4)] for b in range(2)]
            Y = [[sb(es, "hY%d%d" % (i, r), [128, 512], BF16) for r in range(2)] for i in range(2)]
            bst = [sb(es, "hbst%d" % i, [128, 2, 4, 512], BF16) for i in range(2)]
            FBv = [relayout_view(FB.ap[r]) for r in range(2)]
            for cp in range(32):
                g = cp // 4
                pX, pB, tt = pXX[cp % 2], pBB[cp % 2], ttt[cp % 2]
                kb = kk[g % 2]
                if cp % 4 == 0:
                    for r in range(2):
                        f.dma("sp", kb[r].ap, KH.ap[o, r, :, cp:cp + 4, :], reads=[KH], writes=[kb[r]])
                kr = kb[0].ap[:, cp % 4, :]
                kim = kb[1].ap[:, cp % 4, :]
                f.mm([lambda e: e.matmul(pX[0].ap, lhsT=CS.ap[:, 0, :], rhs=A2[0].ap[:, cp, :], start=True, stop=False),
                      lambda e: e.matmul(pX[0].ap, lhsT=CS.ap[:, 1, :], rhs=A2[1].ap[:, cp, :], start=False, stop=True)],
                     reads=[CS, A2[0], A2[1]], writes=[pX[0]])
                f.mm([lambda e: e.matmul(pX[1].ap, lhsT=CS.ap[:, 0, :], rhs=A2[1].ap[:, cp, :], start=True, stop=False),
                      lambda e: e.matmul(pX[1].ap, lhsT=CS.ap[:, 2, :], rhs=A2[0].ap[:, cp, :], start=False, stop=True)],
                     reads=[CS, A2[0], A2[1]], writes=[pX[1]])
                f.op("dve", lambda e: e.tensor_tensor(out=tt[0].ap, in0=pX[0].ap, in1=kr, op=ALU.mult), reads=[pX[0], kb[0]], writes=[tt[0]])
                f.op("dve", lambda e: e.tensor_tensor(out=tt[1].ap, in0=pX[1].ap, in1=kim, op=ALU.mult), reads=[pX[1], kb[1]], writes=[tt[1]])
                f.op("dve", lambda e: e.tensor_tensor(out=tt[2].ap, in0=pX[0].ap, in1=kim, op=ALU.mult), reads=[pX[0], kb[1]], writes=[tt[2]])
                f.op("dve", lambda e: e.tensor_tensor(out=tt[3].ap, in0=pX[1].ap, in1=kr, op=ALU.mult), reads=[pX[1], kb[0]], writes=[tt[3]])
                yy = Y[cp % 2]
                f.op("pool", lambda e: e.tensor_tensor(out=yy[0].ap, in0=tt[0].ap, in1=tt[1].ap, op=ALU.subtract), reads=[tt[0], tt[1]], writes=[yy[0]])
                f.op("pool", lambda e: e.tensor_tensor(out=yy[1].ap, in0=tt[2].ap, in1=tt[3].ap, op=ALU.add), reads=[tt[2], tt[3]], writes=[yy[1]])
                f.mm([lambda e: e.matmul(pB[0].ap, lhsT=CS.ap[:, 0, :], rhs=yy[0].ap, start=True, stop=False),
                      lambda e: e.matmul(pB[0].ap, lhsT=CS.ap[:, 2, :], rhs=yy[1].ap, start=False, stop=True)],
                     reads=[CS, yy[0], yy[1]], writes=[pB[0]])
                f.mm([lambda e: e.matmul(pB[1].ap, lhsT=CS.ap[:, 1, :], rhs=yy[0].ap, start=True, stop=False),
                      lambda e: e.matmul(pB[1].ap, lhsT=CS.ap[:, 0, :], rhs=yy[1].ap, start=False, stop=True)],
                     reads=[CS, yy[0], yy[1]], writes=[pB[1]])
                bs = bst[g % 2]
                f.op("act", lambda e: e.activation(out=bs.ap[:, 0, cp % 4, :], in_=pB[0].ap, func=ACTF.Copy), reads=[pB[0]], writes=[bs])
                f.op("act", lambda e: e.activation(out=bs.ap[:, 1, cp % 4, :], in_=pB[1].ap, func=ACTF.Copy), reads=[pB[1]], writes=[bs])
                if cp % 4 == 3:
                    for r in range(2):
                        f.dma("pool", FBv[r][:, cp - 3:cp + 1, :], bs.ap[:, r], reads=[bs], writes=[FB], sem=bs, accumulate=True)
        with Scope(f) as es:
            HG = sb(es, "HGh", [128, 32, 2, 128], BF16)
            Bc = [sb(es, "hBc%d" % i, [128, 16, 512], BF16) for i in range(2)]
            gt = sb(es, "hgt", [128, 32, 512], BF16)
            f.dma("sp", HG.ap, D["HG"].ap, writes=[HG])
            load_split("sp", gt, gt.ap, XG.ap[o], [XG], 2)
            for h in range(4):
                eng = "dve" if h % 2 == 0 else "pool"
                f.op(eng, lambda e, h=h: e.tensor_tensor(out=gt.ap[:, h * 8:(h + 1) * 8, :], in0=gt.ap[:, h * 8:(h + 1) * 8, :],
                                                         in1=scl.ap[:, o:o + 1, :].broadcast_to([128, 8, 512]), op=ALU.mult),
                     reads=[gt, scl], writes=[gt])
            gs = gt
            pY = [ps(es, "hpY%d" % i, [128, 512]) for i in range(2)]
            if o == 0:
                zst = [sb(es, "hzst%d" % i, [128, 4, 512], BF16) for i in range(2)]
            else:
                yf = [sb(es, "hyf%d" % i, [128, 512], F32) for i in range(2)]
                junk = sb(es, "hjunk", [128, 512], BF16)
                ssq = [sb(es, "hss%d" % i, [128, 1], F32) for i in range(2)]
                rsq = [sb(es, "hrs%d" % i, [128, 1], F32) for i in range(2)]
                yn = [sb(es, "hyn%d" % i, [128, 512], BF16) for i in range(2)]
                pT2 = [ps(es, "hpT%d" % i, [128, 4, 128], BF16) for i in range(2)]
                yT = yTbox[0]
                yTv = yT.ap.rearrange("p k (pp j) -> p k j pp", j=32)
            for j in range(32):
                if j % 16 == 0:
                    for r in range(2):
                        load_split("sp", Bc[r], Bc[r].ap, FB.ap[r][:, j:j + 16, :], [FB], 2)
                jl = j % 16
                py = pY[j % 2]
                f.mm([lambda e: e.matmul(py.ap, lhsT=HG.ap[:, j, 0, :], rhs=Bc[0].ap[:, jl, :], start=True, stop=False),
                      lambda e: e.matmul(py.ap, lhsT=HG.ap[:, j, 1, :], rhs=Bc[1].ap[:, jl, :], start=False, stop=True)],
                     reads=[HG, Bc[0], Bc[1]], writes=[py])
                if o == 0:
                    zs = zst[(j // 4) % 2]
                    f.op("dve", lambda e: e.tensor_tensor(out=zs.ap[:, j % 4, :], in0=py.ap, in1=gs.ap[:, j, :], op=ALU.mult),
                         reads=[py, gs], writes=[zs])
                    if j % 4 == 3:
                        f.dma("pool", Z1.ap[:, j - 3:j + 1, :], zs.ap, reads=[zs], writes=[Z1], sem=zs, accumulate=True)
                else:
                    y_, s_, r_, n_, pt = yf[j % 2], ssq[j % 2], rsq[j % 2], yn[j % 2], pT2[j % 2]
                    f.op("dve", lambda e: e.tensor_tensor(out=y_.ap, in0=py.ap, in1=gs.ap[:, j, :], op=ALU.mult), reads=[py, gs], writes=[y_])
                    f.op("act", lambda e: e.activation(out=junk.ap, in_=y_.ap, func=ACTF.Square, accum_out=s_.ap), reads=[y_], writes=[junk, s_])
                    rstd_of(s_.ap, s_, r_.ap, r_, 1.0 / 512.0)
                    f.op("act", lambda e: e.activation(out=n_.ap, in_=y_.ap, func=ACTF.Copy, scale=r_.ap[:, 0:1]), reads=[y_, r_], writes=[n_])
                    f.mm([lambda e, k=k: e.transpose(out=pt.ap[:, k, :], in_=n_.ap[:, k * 128:(k + 1) * 128], identity=ident.ap)
                          for k in range(4)], reads=[n_, ident], writes=[pt])
                    f.op("dve", lambda e: e.tensor_copy(out=yTv[:, 4:8, j, :], in_=pt.ap), reads=[pt], writes=[yT])

    def phase6():
        with Scope(f) as es:
            yT = yTbox[0]
            Wo = sb(es, "Wo", [128, 8, 1024], BF16)
            gfh = sb(es, "gfh", [128, 8], F32)
            gfin = sb(es, "gfin", [128, 1024], F32)
            f.dma("sp", gfh.ap, D["gfh"].ap, writes=[gfh])
            f.dma("sp", gfin.ap, D["gfin"].ap.partition_broadcast(128).rearrange("p a n -> p (a n)"), writes=[gfin])
            with Scope(f) as es2:
                wst = [sb(es2, "wost%d" % i, [128, 1024], F32) for i in range(2)]
                wv = D["w_out"].ap.rearrange("(k p) e -> p k e", p=128)
                for k in range(8):
                    s = wst[k % 2]
                    f.dma("sp", s.ap, wv[:, k, :], writes=[s])
                    f.op("dve", lambda e, k=k, s=s: e.tensor_scalar(out=Wo.ap[:, k, :], in0=s.ap, scalar1=gfh.ap[:, k:k + 1], scalar2=None,
                                                                    op0=ALU.mult), reads=[s, gfh], writes=[Wo])
            aT = sb(es, "aT", [128, 32, 512], BF16)
            hmT = [sb(es, "hmT%d" % i, [128, 8, 512], BF16) for i in range(2)]
            x1s = [sb(es, "x1s%d" % i, [128, 4, 1024], F32) for i in range(2)]
            W1s = [sb(es, "W1s%d" % i, [128, 8, 256], BF16) for i in range(2)]
            W2s = [sb(es, "W2s%d" % i, [128, 4, 512], BF16) for i in range(2)]
            rl = [sb(es, "rl%d" % i, [128, 512], BF16) for i in range(2)]
            hm = [sb(es, "hm%d" % i, [128, 1024], BF16) for i in range(2)]
            junk = sb(es, "mjunk", [128, 1024], BF16)
            ssq = [sb(es, "mss%d" % i, [128, 1], F32) for i in range(4)]
            rsq = [sb(es, "mrs%d" % i, [128, 1], F32) for i in range(4)]
            pw = [ps(es, "mpw%d" % i, [128, 512]) for i in range(2)]
            po = ps(es, "mpo", [128, 512])
            pT = ps(es, "mpT", [128, 8, 128], BF16)
            p2 = [ps(es, "mp2%d" % i, [128, 512]) for i in range(4)]
            xls = [[T(None, "xl") for _ in range(4)] for _ in range(2)]
            xss = [[T(None, "xs") for _ in range(4)] for _ in range(2)]
            xrows = D["x"].ap.rearrange("(n p) d -> n p d", p=128)
            orows = out_d.ap.rearrange("(n p) d -> n p d", p=128)
            cnt = {"n1": 0, "n2": 0}

            def pro_a(tb, s):
                xb = x1s[tb % 2]
                tt_ = tb * 4 + s
                f.dma("sp", xb.ap[:, s, :], xrows[tt_], writes=[xb], sem=xls[tb % 2][s])
                for dh in range(2):
                    f.mm([lambda e, k=k, dh=dh: e.matmul(po.ap, lhsT=yT.ap[:, k, tt_ * 128:(tt_ + 1) * 128],
                                                         rhs=Wo.ap[:, k, dh * 512:(dh + 1) * 512], start=(k == 0), stop=(k == 7))
                          for k in range(8)], reads=[yT, Wo], writes=[po])
                    f.op("dve", lambda e, dh=dh: e.tensor_tensor(out=xb.ap[:, s, dh * 512:(dh + 1) * 512], in0=po.ap,
                                                                 in1=xb.ap[:, s, dh * 512:(dh + 1) * 512], op=ALU.add),
                         reads=[po, xb], writes=[xb])

            def pro_b(tb, s):
                xb = x1s[tb % 2]
                s_, r_, h_ = ssq[s % 2], rsq[s % 2], hm[s % 2]
                f.op("act", lambda e: e.activation(out=junk.ap, in_=xb.ap[:, s, :], func=ACTF.Square, accum_out=s_.ap), reads=[xb], writes=[junk, s_])
                rstd_of(s_.ap, s_, r_.ap, r_, 1.0 / 1024)
                f.op("act", lambda e: e.activation(out=h_.ap, in_=xb.ap[:, s, :], func=ACTF.Copy, scale=r_.ap[:, 0:1]), reads=[xb, r_], writes=[h_])
                f.mm([lambda e, k=k: e.transpose(out=pT.ap[:, k, :], in_=h_.ap[:, k * 128:(k + 1) * 128], identity=ident.ap)
                      for k in range(8)], reads=[h_, ident], writes=[pT])
                f.op("dve", lambda e: e.tensor_copy(out=hmT[tb % 2].ap[:, :, s * 128:(s + 1) * 128], in_=pT.ap), reads=[pT], writes=[hmT[tb % 2]])

            def fc1_group(tb, fg):
                ws = W1s[cnt["n1"] % 2]
                cnt["n1"] += 1
                f.dma("sp", ws.ap, W1B.ap[:, :, fg * 256:(fg + 1) * 256], reads=[W1B], writes=[ws])
                for fl in range(2):
                    fc = fg * 2 + fl
                    pp = pw[fc % 2]
                    f.mm([lambda e, k=k: e.matmul(pp.ap, lhsT=ws.ap[:, k, fl * 128:(fl + 1) * 128], rhs=hmT[tb % 2].ap[:, k, :],
                                                  start=(k == 0), stop=(k == 7)) for k in range(8)], reads=[ws, hmT[tb % 2]], writes=[pp])
                    rr_ = rl[fc % 2]
                    f.op("act", lambda e: e.activation(out=rr_.ap, in_=pp.ap, func=ACTF.Relu), reads=[pp], writes=[rr_])
                    f.op("pool", lambda e: e.tensor_tensor(out=aT.ap[:, fc, :], in0=rr_.ap, in1=rr_.ap, op=ALU.mult), reads=[rr_], writes=[aT])

            def fc2_all(tb):
                xb = x1s[tb % 2]
                for dh in range(2):
                    for fg in range(8):
                        ws = W2s[cnt["n2"] % 2]
                        cnt["n2"] += 1
                        f.dma("sp", ws.ap, W2B.ap[:, fg * 4:(fg + 1) * 4, dh * 512:(dh + 1) * 512], reads=[W2B], writes=[ws])
                        for s in range(4):
                            f.mm([lambda e, fl=fl: e.matmul(p2[s].ap, lhsT=aT.ap[:, fg * 4 + fl, s * 128:(s + 1) * 128], rhs=ws.ap[:, fl, :],
                                                            start=(fg == 0 and fl == 0), stop=(fg == 7 and fl == 3)) for fl in range(4)],
                                 reads=[aT, ws], writes=[p2[s]])
                    for s in range(4):
                        f.op("dve", lambda e: e.tensor_tensor(out=xb.ap[:, s, dh * 512:(dh + 1) * 512], in0=p2[s].ap,
                                                              in1=xb.ap[:, s, dh * 512:(dh + 1) * 512], op=ALU.add), reads=[p2[s], xb], writes=[xb])

            def epilogue(tb):
                xb = x1s[tb % 2]
                for s in range(4):
                    s_, r_ = ssq[2 + s % 2], rsq[2 + s % 2]
                    f.op("act", lambda e: e.activation(out=junk.ap, in_=xb.ap[:, s, :], func=ACTF.Square, accum_out=s_.ap), reads=[xb], writes=[junk, s_])
                    rstd_of(s_.ap, s_, r_.ap, r_, 1.0 / 1024)
                    f.op("act", lambda e: e.activation(out=xb.ap[:, s, :], in_=xb.ap[:, s, :], func=ACTF.Copy, scale=r_.ap[:, 0:1]), reads=[xb, r_], writes=[xb])
                    f.op("pool", lambda e: e.tensor_tensor(out=xb.ap[:, s, :], in0=xb.ap[:, s, :], in1=gfin.ap, op=ALU.mult), reads=[xb, gfin], writes=[xb])
                    f.dma("pool", orows[tb * 4 + s], xb.ap[:, s, :], reads=[xb], writes=[out_d], sem=xss[tb % 2][s], accumulate=True)

            for s in range(4):
                pro_a(0, s)
                pro_b(0, s)
            for tb in range(8):
                for fg in range(16):
                    fc1_group(tb, fg)
                    if tb + 1 < 8:
                        if fg % 4 == 0:
                            pro_a(tb + 1, fg // 4)
                        if fg % 4 == 3:
                            pro_b(tb + 1, fg // 4)
                fc2_all(tb)
                epilogue(tb)

    def alloc_yT():
        yTbox.append(sb(top, "yT", [128, 8, L], BF16))
    phases = [("p0", phase0), ("p1", phase1), ("p1b", phase1b), ("p3", phase3), ("yT", alloc_yT), ("p2", phase2),
              ("p4", lambda: phase_h(0)), ("p5", lambda: phase_h(1)), ("p6", phase6)]
    for i, (nm, fn) in enumerate(phases):
        if i <= upto:
            fn()
    f.wait_all("pool", [out_d, RAW, U, XG, FA, FB, KH, Z1, W1B, W2B])
    top.close()
    return nc


def make_in_maps(inputs):
    C = host_consts()
    g = {k: np.asarray(v, dtype=np.float32) for k, v in inputs.items()}

    def pk(v):
        return np.ascontiguousarray(v.reshape(8, 128).T)
    shared = {
        "gm": pk(g["g_mix"][0]), "w_in": g["w_in"][0], "cw": np.concatenate([g["conv_w"][0], g["conv_b"]], 0),
        "f_w0": g["filt_w0"][0],
        "f_b": np.ascontiguousarray(np.stack([g["filt_b0"][0], g["filt_b_inner"][0, 0], g["filt_b_inner"][0, 1]], 1)),
        "f_wi": np.ascontiguousarray(g["filt_w_inner"][0].transpose(1, 0, 2)),
        "f_fr": np.ascontiguousarray(g["filt_freq"][0][:, None]),
        "f_wo": g["filt_w_out"][0], "long_d": g["long_d"],
        "gfh": pk(np.concatenate([g["g_fourier"][0], g["g_hyena"][0]])),
        "w_out": g["w_out"][0], "gml": pk(g["g_mlp"][0]), "w_fc1": g["w_fc1"][0], "w_fc2": g["w_fc2"][0],
        "gfin": g["g_final"][None, :],
    }
    shared.update({k: C[k] for k, _, _ in CONST_SPECS})
    shared = {k: np.ascontiguousarray(v) for k, v in shared.items()}
    maps = []
    for b in range(8):
        m = dict(shared)
        m["x"] = np.ascontiguousarray(g["x"][b])
        maps.append(m)
    return maps


def kernel(**inputs):
    nc = build()
    in_maps = make_in_maps(inputs)
    res = run_bass_kernel_spmd(nc, in_maps, core_ids=list(range(8)))
    return np.stack([np.asarray(r["out"], dtype=np.float32) for r in res.results], 0)
```

```python
import numpy as np
import ml_dtypes
from contextlib import ExitStack
import concourse.bass as bass
import concourse.mybir as mybir
from concourse.bass_utils import run_bass_kernel_spmd

F32 = mybir.dt.float32
BF16 = mybir.dt.bfloat16
I32 = mybir.dt.int32
ALU = mybir.AluOpType
ACTF = mybir.ActivationFunctionType
EPOCH = 8000
L = 4096
NFFT = 8192
EPS = 1e-5
NB = ml_dtypes.bfloat16


class Res:
    __slots__ = ("name", "w", "rd")

    def __init__(self, name):
        self.name = name
        self.w = {}
        self.rd = []


class T:
    __slots__ = ("ap", "r")

    def __init__(self, ap, name):
        self.ap = ap
        self.r = Res(name)


class FW:
    def __init__(self, nc):
        self.nc = nc
        self.eng = {"pe": nc.tensor, "act": nc.scalar, "dve": nc.vector, "pool": nc.gpsimd, "sp": nc.sync}
        self.sems = {}
        self.cnt = {k: 0 for k in ("pe", "act", "dve", "pool")}
        self.waited = {k: {} for k in self.eng}
        self.nsem = 0
        self.recent = {}
        self.gen = 0
        self.free_dma = []
        self._dcnt = {}

    def _sem(self, key):
        s = self.sems.get(key)
        if s is None:
            s = self.nc.alloc_semaphore("s%d" % self.nsem)
            self.nsem += 1
            self.sems[key] = s
        return s

    def _wait(self, engine, deps):
        w = self.waited[engine]
        best = {}
        for key, val in deps:
            if engine == "pe" and key[0] == "pe":
                continue
            if key[0] == "dma" and key[2] < self.gen:
                continue
            if w.get(key, 0) < val and best.get(key, 0) < val:
                best[key] = val
        for key, val in best.items():
            self.eng[engine].wait_ge(self._sem(key), val)
            w[key] = val

    @staticmethod
    def _deps(reads, writes):
        deps = []
        for r in reads:
            deps.extend(r.w.items())
        for r in writes:
            deps.extend(r.w.items())
            deps.extend(r.rd)
        return deps

    @staticmethod
    def _mark(tok, reads, writes, accumulate=False):
        for r in writes:
            if r.w.get(tok[0], 0) < tok[1]:
                r.w[tok[0]] = tok[1]
            r.rd = []
        for r in reads:
            r.rd.append(tok)
            if len(r.rd) > 48:
                d = {}
                for k, v in r.rd:
                    if d.get(k, 0) < v:
                        d[k] = v
                r.rd = list(d.items())

    def _tick(self, engine, inst):
        c = self.cnt[engine]
        self.cnt[engine] = c + 1
        key = (engine, c // EPOCH)
        inst.then_inc(self._sem(key), 1)
        self.recent[key] = c % EPOCH + 1
        return (key, c % EPOCH + 1)

    def barrier(self):
        toks = list(self.recent.items())
        for eng in self.eng:
            self._wait(eng, toks)
        self.recent = {}
        for key in [k for k in self.sems if k[0] == "dma"]:
            self.free_dma.append((self.sems.pop(key), self._dcnt.pop(key)))
        self.gen += 1

    def op(self, engine, fn, reads=(), writes=()):
        reads = [t.r for t in reads]
        writes = [t.r for t in writes]
        self._wait(engine, self._deps(reads, writes))
        tok = self._tick(engine, fn(self.eng[engine]))
        self._mark(tok, reads, writes)

    def mm(self, fns, reads=(), writes=()):
        reads = [t.r for t in reads]
        writes = [t.r for t in writes]
        self._wait("pe", self._deps(reads, writes))
        pe = self.eng["pe"]
        for fn in fns[:-1]:
            fn(pe)
        tok = self._tick("pe", fns[-1](pe))
        self._mark(tok, reads, writes)

    def dma(self, queue, out_ap, in_ap, reads=(), writes=(), sem=None, accumulate=False):
        reads = [t.r for t in reads]
        writes_r = [t.r for t in writes]
        self._wait(queue, self._deps(reads, writes_r))
        s = sem if sem is not None else writes[0]
        key = ("dma", id(s.r), self.gen)
        cnts = self._dcnt
        if key not in self.sems and self.free_dma:
            self.sems[key], cnts[key] = self.free_dma.pop()
        cnts[key] = cnts.get(key, 0) + 16
        self.eng[queue].dma_start(out=out_ap, in_=in_ap).then_inc(self._sem(key), 16)
        tok = (key, cnts[key])
        self.recent[key] = cnts[key]
        self._mark(tok, reads, writes_r, accumulate=accumulate)

    def wait_all(self, engine, ts):
        deps = []
        for t in ts:
            deps.extend(t.r.w.items())
            deps.extend(t.r.rd)
        self._wait(engine, deps)


class Scope:
    def __init__(self, f):
        self.f = f
        self.es = ExitStack()

    def __enter__(self):
        self.es.__enter__()
        return self.es

    def __exit__(self, *a):
        self.f.barrier()
        return self.es.__exit__(*a)


_CONST = None


def host_consts():
    global _CONST
    if _CONST is not None:
        return _CONST
    c = {}
    p = np.arange(128)[:, None, None].astype(np.float64)
    j = np.arange(32)[None, :, None].astype(np.float64)
    cc = np.arange(128)[None, None, :].astype(np.float64)
    ph = 2 * np.pi * (32 * p + j) * (cc + 0.5) / NFFT
    c["HF"] = np.stack([np.cos(ph), -np.sin(ph)], 2).astype(NB)
    pht = ph.transpose(2, 1, 0)
    c["HG"] = np.stack([np.cos(pht), -np.sin(pht)], 2).astype(NB)
    pf = 2 * np.pi * (32 * p + j) * cc / L
    c["FF"] = np.stack([np.cos(pf), np.sin(pf), -np.sin(pf)], 2).astype(NB)
    jd = 2 * np.pi * np.outer(np.arange(32), np.arange(32)) / 32
    C4 = np.kron(np.eye(4), np.cos(jd))
    S4 = np.kron(np.eye(4), np.sin(jd))
    c["CS"] = np.stack([C4, S4, -S4], 1).astype(NB)
    cd = 2 * np.pi * np.outer(np.arange(64), np.arange(64)) / 64
    c["BD"] = np.concatenate([np.kron(np.eye(2), np.cos(cd)), np.kron(np.eye(2), np.sin(cd))], 1).astype(NB)
    c["ident"] = np.eye(128).astype(NB)
    sel = np.zeros((128, 3, 128))
    sel[:, 0, :] = 1.0
    sel[0, 1, :] = 1.0
    sel[0, 2, :] = -1.0
    c["SEL"] = sel.astype(NB)
    pos = np.arange(L, dtype=np.float64)
    t = pos / (L - 1)
    bands = np.linspace(1e-4, 15, 16)
    ang = (2 * np.pi * pos / L)[:, None] * bands[None]
    z = np.concatenate([t[:, None], np.cos(ang), -np.sin(ang)], -1)
    zT = z.T.reshape(33, 128, 32).transpose(0, 2, 1).reshape(33, L)
    c["zT"] = np.ascontiguousarray(zT).astype(np.float32)
    deltas = np.linspace(np.log(1e-2) / 1.5, np.log(1e-2) / 0.3, 512)
    decay = np.exp(-t[:, None] * np.abs(deltas)[None])
    c["decay"] = decay.reshape(128, 32, 512).astype(np.float32)
    _CONST = c
    return c


CONST_SPECS = [("HF", [128, 32, 2, 128], BF16), ("HG", [128, 32, 2, 128], BF16), ("FF", [128, 32, 3, 128], BF16),
               ("CS", [128, 3, 128], BF16), ("BD", [128, 256], BF16), ("ident", [128, 128], BF16),
               ("SEL", [128, 3, 128], BF16), ("zT", [33, L], F32), ("decay", [128, 32, 512], F32)]

IN_SPECS = [("x", [L, 1024], F32), ("gm", [128, 8], F32), ("w_in", [1024, 2048], F32), ("cw", [4, 1536], F32),
            ("f_w0", [33, 64], F32), ("f_b", [64, 3], F32), ("f_wi", [64, 2, 64], F32), ("f_fr", [64, 1], F32),
            ("f_wo", [64, 2048], F32), ("long_d", [1, 2, 512], F32), ("gfh", [128, 8], F32),
            ("w_out", [1024, 1024], F32), ("gml", [128, 8], F32), ("w_fc1", [1024, 4096], F32),
            ("w_fc2", [4096, 1024], F32), ("gfin", [1, 1024], F32)]


def build(upto=99, debug=False):
    nc = bass.Bass("TRN2", target_bir_lowering=False)
    f = FW(nc)
    D = {}
    for name, shape, dt in IN_SPECS + CONST_SPECS:
        D[name] = T(nc.dram_tensor(name, shape, dt, kind="ExternalInput").ap(), name)
    out_d = T(nc.dram_tensor("out", [L, 1024], F32, kind="ExternalOutput").ap(), "out")
    def scratch(name, shape, dt):
        k = "ExternalOutput" if (debug is True or (debug and name in debug)) else "Internal"
        return T(nc.dram_tensor(name, shape, dt, kind=k).ap(), name)

    RAW = scratch("RAW", [128, 34, 1536], BF16)
    U = scratch("U", [128, 32, 1024], BF16)
    XG = scratch("XG", [3, 128, 32, 512], BF16)
    FA = scratch("FA", [4, 128, 32, 512], BF16)
    FB = scratch("FB", [2, 128, 32, 512], BF16)
    KH = scratch("KH", [2, 2, 128, 32, 512], BF16)
    Z1 = scratch("Z1", [128, 32, 512], BF16)
    W1B = scratch("W1B", [128, 8, 4096], BF16)
    W2B = scratch("W2B", [128, 32, 1024], BF16)

    uid = [0]

    def sb(es, name, shape, dt):
        uid[0] += 1
        return T(es.enter_context(nc.sbuf_tensor("sb%d_%s" % (uid[0], name), shape, dt)).ap(), name)

    def ps(es, name, shape, dt=F32):
        uid[0] += 1
        return T(es.enter_context(nc.psum_tensor("ps%d_%s" % (uid[0], name), shape, dt)).ap(), name)

    top = ExitStack()
    ident = sb(top, "ident", [128, 128], BF16)
    CS = sb(top, "CS", [128, 3, 128], BF16)
    epsT = sb(top, "epsT", [128, 1], F32)
    yTbox = []
    scl = sb(top, "scl", [128, 2, 512], F32)
    f.dma("sp", ident.ap, D["ident"].ap, writes=[ident])
    f.dma("sp", CS.ap, D["CS"].ap, writes=[CS])
    f.op("pool", lambda e: e.memset(epsT.ap, EPS), writes=[epsT])

    def rstd_of(ss_ap, ss_t, out_ap, out_t, scale):
        f.op("act", lambda e: e.activation(out=out_ap, in_=ss_ap, func=ACTF.Sqrt, scale=scale, bias=epsT.ap[:, 0:1]),
             reads=[ss_t, epsT], writes=[out_t])
        f.op("dve", lambda e: e.reciprocal(out=out_ap, in_=out_ap), reads=[out_t], writes=[out_t])

    def phase0():
        with Scope(f) as es:
            gml = sb(es, "gml", [128, 8], F32)
            f.dma("sp", gml.ap, D["gml"].ap, writes=[gml])
            st = [sb(es, "p0s%d" % i, [128, 8, 512], F32) for i in range(2)]
            sbf = [sb(es, "p0b%d" % i, [128, 8, 512], BF16) for i in range(2)]
            w1v = D["w_fc1"].ap.rearrange("(k p) f -> p k f", p=128)
            for i in range(8):
                s, b = st[i % 2], sbf[i % 2]
                f.dma("sp", s.ap, w1v[:, :, i * 512:(i + 1) * 512], writes=[s])
                for k in range(8):
                    if k % 2 == 0:
                        f.op("dve", lambda e, k=k: e.tensor_scalar(out=b.ap[:, k, :], in0=s.ap[:, k, :], scalar1=gml.ap[:, k:k + 1],
                                                                   scalar2=None, op0=ALU.mult), reads=[s, gml], writes=[b])
                    else:
                        f.op("act", lambda e, k=k: e.activation(out=b.ap[:, k, :], in_=s.ap[:, k, :], func=ACTF.Copy,
                                                                scale=gml.ap[:, k:k + 1]), reads=[s, gml], writes=[b])
                f.dma("pool", W1B.ap[:, :, i * 512:(i + 1) * 512], b.ap, reads=[b], writes=[W1B], sem=b, accumulate=True)
            w2v = D["w_fc2"].ap.rearrange("(c p) d -> p c d", p=128)
            for i in range(8):
                s, b = st[i % 2], sbf[i % 2]
                s4 = s.ap.rearrange("p (a b) n -> p a (b n)", a=4)
                b4 = b.ap.rearrange("p (a b) n -> p a (b n)", a=4)
                f.dma("sp", s4, w2v[:, i * 4:(i + 1) * 4, :], writes=[s])
                f.op("dve", lambda e: e.tensor_copy(out=b4[:, 0:2], in_=s4[:, 0:2]), reads=[s], writes=[b])
                f.op("act", lambda e: e.activation(out=b4[:, 2:4], in_=s4[:, 2:4], func=ACTF.Copy), reads=[s], writes=[b])
                f.dma("pool", W2B.ap[:, i * 4:(i + 1) * 4, :], b4, reads=[b], writes=[W2B], sem=b, accumulate=True)

    def phase1():
        with Scope(f) as es:
            W1 = sb(es, "W1", [128, 8, 2048], BF16)
            gm = sb(es, "gm", [128, 8], F32)
            BD = sb(es, "BD", [128, 256], BF16)
            cw = sb(es, "cw", [128, 4, 1536], F32)
            f.dma("sp", gm.ap, D["gm"].ap, writes=[gm])
            f.dma("sp", BD.ap, D["BD"].ap, writes=[BD])
            f.dma("sp", cw.ap, D["cw"].ap.partition_broadcast(128), writes=[cw])
            with Scope(f) as es0:
                wst = [sb(es0, "wst%d" % i, [128, 2048], F32) for i in range(2)]
                wv = D["w_in"].ap.rearrange("(k p) e -> p k e", p=128)
                for k in range(8):
                    s = wst[k % 2]
                    f.dma("sp", s.ap, wv[:, k, :], writes=[s])
                    if k % 2 == 0:
                        f.op("dve", lambda e, k=k, s=s: e.tensor_scalar(out=W1.ap[:, k, :], in0=s.ap, scalar1=gm.ap[:, k:k + 1], scalar2=None,
                                                                        op0=ALU.mult), reads=[s, gm], writes=[W1])
                    else:
                        f.op("act", lambda e, k=k, s=s: e.activation(out=W1.ap[:, k, :], in_=s.ap, func=ACTF.Copy, scale=gm.ap[:, k:k + 1]),
                             reads=[s, gm], writes=[W1])
            xts = [sb(es, "xt%d" % i, [128, 1024], F32) for i in range(3)]
            junk = sb(es, "junk", [128, 1024], BF16)
            ss = [sb(es, "ss%d" % i, [128, 1], F32) for i in range(2)]
            rs = [sb(es, "rs%d" % i, [128, 1], F32) for i in range(2)]
            xs = [sb(es, "xs%d" % i, [128, 1024], BF16) for i in range(2)]
            hT = [sb(es, "hT%d" % i, [128, 8, 128], BF16) for i in range(2)]
            rring = [sb(es, "rawr%d" % i, [128, 1536], BF16) for i in range(4)]
            rded = {j: sb(es, "rawd%d" % j, [128, 1536], BF16) for j in (0, 1, 30, 31)}
            hlo = sb(es, "hlo", [128, 1536], BF16)
            hhi = sb(es, "hhi", [128, 1536], BF16)
            acc = sb(es, "cacc", [128, 1536], F32)
            t1 = sb(es, "ct1", [128, 1536], F32)
            t2 = sb(es, "ct2", [128, 1536], F32)
            og = [sb(es, "og%d" % i, [128, 1536], BF16) for i in range(2)]
            ufT = [sb(es, "ufT%d" % i, [128, 4, 128], BF16) for i in range(2)]
            Ut = [sb(es, "Ut%d" % i, [128, 4, 256], BF16) for i in range(2)]
            pT = ps(es, "pT", [128, 8, 128], BF16)
            pH = [ps(es, "pH%d" % i, [128, 512]) for i in range(3)]
            pF = ps(es, "pF", [128, 4, 128])
            pU = ps(es, "pU", [128, 4, 256])
            xv = D["x"].ap.rearrange("(p j) d -> p j d", j=32)
            f.op("pool", lambda e: e.memset(hlo.ap, 0.0), writes=[hlo])
            f.op("pool", lambda e: e.memset(hhi.ap, 0.0), writes=[hhi])

            def rawbuf(j):
                return rded[j] if j in rded else rring[(j - 2) % 4]
            nconv = [0]

            def conv(jo, left, center, right):
                o = og[nconv[0] % 2]
                nconv[0] += 1
                f.op("pool", lambda e: e.tensor_tensor(out=t1.ap, in0=left.ap, in1=cw.ap[:, 0, :], op=ALU.mult), reads=[left, cw], writes=[t1])
                f.op("dve", lambda e: e.tensor_tensor(out=acc.ap, in0=center.ap, in1=cw.ap[:, 1, :], op=ALU.mult), reads=[center, cw], writes=[acc])
                f.op("pool", lambda e: e.tensor_tensor(out=t2.ap, in0=right.ap, in1=cw.ap[:, 2, :], op=ALU.mult), reads=[right, cw], writes=[t2])
                f.op("dve", lambda e: e.tensor_tensor(out=acc.ap, in0=acc.ap, in1=t1.ap, op=ALU.add), reads=[acc, t1], writes=[acc])
                f.op("dve", lambda e: e.tensor_tensor(out=acc.ap, in0=acc.ap, in1=t2.ap, op=ALU.add), reads=[acc, t2], writes=[acc])
                f.op("dve", lambda e: e.tensor_tensor(out=o.ap, in0=acc.ap, in1=cw.ap[:, 3, :], op=ALU.add), reads=[acc, cw], writes=[o])
                for t in range(3):
                    f.dma("pool", XG.ap[t, :, jo, :], o.ap[:, t * 512:(t + 1) * 512], reads=[o], writes=[XG], sem=o, accumulate=True)

            for j in range(2):
                f.dma("sp", xts[j % 3].ap, xv[:, j, :], writes=[xts[j % 3]])
            for j in range(32):
                if j + 2 < 32:
                    f.dma("sp", xts[(j + 2) % 3].ap, xv[:, j + 2, :], writes=[xts[(j + 2) % 3]])
                xt = xts[j % 3]
                s_, r_, x_, h_, uf, ut = ss[j % 2], rs[j % 2], xs[j % 2], hT[j % 2], ufT[j % 2], Ut[j % 2]
                rw = rawbuf(j)
                f.op("act", lambda e: e.activation(out=junk.ap, in_=xt.ap, func=ACTF.Square, accum_out=s_.ap),
                     reads=[xt], writes=[junk, s_])
                rstd_of(s_.ap, s_, r_.ap, r_, 1.0 / 1024)
                f.op("act", lambda e: e.activation(out=x_.ap, in_=xt.ap, func=ACTF.Copy, scale=r_.ap[:, 0:1]),
                     reads=[xt, r_], writes=[x_])
                f.mm([lambda e, k=k: e.transpose(out=pT.ap[:, k, :], in_=x_.ap[:, k * 128:(k + 1) * 128], identity=ident.ap)
                      for k in range(8)], reads=[x_, ident], writes=[pT])
                f.op("dve", lambda e: e.tensor_copy(out=h_.ap, in_=pT.ap), reads=[pT], writes=[h_])
                for n in range(3):
                    f.mm([lambda e, k=k, n=n: e.matmul(pH[n].ap, lhsT=h_.ap[:, k, :], rhs=W1.ap[:, k, 512 + n * 512:1024 + n * 512],
                                                       start=(k == 0), stop=(k == 7)) for k in range(8)],
                         reads=[h_, W1], writes=[pH[n]])
                    f.op("act", lambda e, n=n: e.activation(out=rw.ap[:, n * 512:(n + 1) * 512], in_=pH[n].ap, func=ACTF.Copy),
                         reads=[pH[n]], writes=[rw])
                fns = []
                for ec in range(4):
                    for k in range(8):
                        fns.append(lambda e, k=k, ec=ec: e.matmul(pF.ap[:, ec, :], lhsT=W1.ap[:, k, ec * 128:(ec + 1) * 128],
                                                                  rhs=h_.ap[:, k, :], start=(k == 0), stop=(k == 7)))
                f.mm(fns, reads=[h_, W1], writes=[pF])
                f.op("dve", lambda e: e.tensor_copy(out=uf.ap, in_=pF.ap), reads=[pF], writes=[uf])
                f.mm([lambda e, ec=ec: e.matmul(pU.ap[:, ec, :], lhsT=uf.ap[:, ec, :], rhs=BD.ap, start=True, stop=True)
                      for ec in range(4)], reads=[uf, BD], writes=[pU])
                f.op("dve", lambda e: e.tensor_copy(out=ut.ap, in_=pU.ap), reads=[pU], writes=[ut])
                f.dma("pool", U.ap[:, j, :], ut.ap.rearrange("p a b -> p (a b)"), reads=[ut], writes=[U], sem=ut, accumulate=True)
                if j >= 2:
                    conv(j - 1, rawbuf(j - 2), rawbuf(j - 1), rawbuf(j))
            f.dma("sp", hlo.ap[1:128, :], rded[31].ap[0:127, :], reads=[rded[31]], writes=[hlo])
            f.dma("sp", hhi.ap[0:127, :], rded[0].ap[1:128, :], reads=[rded[0]], writes=[hhi])
            conv(0, hlo, rded[0], rded[1])
            conv(31, rded[30], rded[31], hhi)

    def relayout_view(dram_ap):
        return dram_ap.rearrange("(cp q) j n -> (q j) cp n", q=4)

    def load_split(queue, dst, dst_ap, src_ap, reads, n):
        tot = dst_ap.shape[1]
        step = tot // n
        for i in range(n):
            f.dma(queue, dst_ap[:, i * step:(i + 1) * step], src_ap[:, i * step:(i + 1) * step], reads=reads, writes=[dst],
                  sem=T(None, "ls"))

    def stage1(es, src, mats, combos, dst_slots, tagp):
        nslot = len(combos)
        pAA = [[ps(es, "%spA%d_%d" % (tagp, i, b), [128, 512]) for i in range(nslot)] for b in range(2)]
        stg = [sb(es, "%sstg%d" % (tagp, i), [128, nslot, 4, 512], BF16) for i in range(2)]
        for j in range(32):
            sg = stg[(j // 4) % 2]
            pA = pAA[j % 2]
            for si, terms in enumerate(combos):
                f.mm([lambda e, mi=mi, rf=rf, ti=ti, si=si: e.matmul(pA[si].ap, lhsT=mats.ap[:, j, mi, :], rhs=rf(j),
                                                                       start=(ti == 0), stop=(ti == len(terms) - 1))
                      for ti, (mi, rf) in enumerate(terms)], reads=[src, mats], writes=[pA[si]])
                eng = "act" if (si + j) % 2 == 0 else "dve"
                if eng == "act":
                    f.op("act", lambda e, si=si: e.activation(out=sg.ap[:, si, j % 4, :], in_=pA[si].ap, func=ACTF.Copy),
                         reads=[pA[si]], writes=[sg])
                else:
                    f.op("dve", lambda e, si=si: e.tensor_copy(out=sg.ap[:, si, j % 4, :], in_=pA[si].ap), reads=[pA[si]], writes=[sg])
            if j % 4 == 3:
                for si in range(nslot):
                    f.dma("sp", FA.ap[dst_slots[si], :, j - 3:j + 1, :], sg.ap[:, si], reads=[sg], writes=[FA], sem=sg, accumulate=True)

    def phase2():
        with Scope(f) as es:
            FF = sb(es, "FF", [128, 32, 3, 128], BF16)
            f.dma("sp", FF.ap, D["FF"].ap, writes=[FF])
            Us = sb(es, "Us", [128, 32, 4, 2, 128], BF16)
            load_split("sp", Us, Us.ap.rearrange("p j a b n -> p j (a b n)"), U.ap, [U], 4)
            uc = lambda j: Us.ap[:, j, :, 0, :]
            us = lambda j: Us.ap[:, j, :, 1, :]
            with Scope(f) as es2:
                stage1(es2, Us, FF, [[(0, uc), (2, us)], [(1, uc), (0, us)]], [0, 1], "f")
        with Scope(f) as es:
            A2 = [sb(es, "fA2%d" % i, [128, 32, 512], BF16) for i in range(2)]
            for i in range(2):
                load_split("sp", A2[i], A2[i].ap, relayout_view(FA.ap[i]), [FA], 4)
            pY = [ps(es, "fpY%d" % i, [128, 512]) for i in range(2)]
            pT2 = [ps(es, "fpT%d" % i, [128, 4, 128], BF16) for i in range(2)]
            junk = sb(es, "fjunk", [128, 512], BF16)
            ssq = [sb(es, "fss%d" % i, [128, 1], F32) for i in range(2)]
            rsq = [sb(es, "frs%d" % i, [128, 1], F32) for i in range(2)]
            yn = [sb(es, "fyn%d" % i, [128, 512], BF16) for i in range(2)]
            yT = yTbox[0]
            yTv = yT.ap.rearrange("p k (d cp q) -> p k cp q d", d=32, cp=32, q=4)
            for cp in range(32):
                py, pt, s_, r_, y_ = pY[cp % 2], pT2[cp % 2], ssq[cp % 2], rsq[cp % 2], yn[cp % 2]
                f.mm([lambda e: e.matmul(py.ap, lhsT=CS.ap[:, 0, :], rhs=A2[0].ap[:, cp, :], start=True, stop=False),
                      lambda e: e.matmul(py.ap, lhsT=CS.ap[:, 2, :], rhs=A2[1].ap[:, cp, :], start=False, stop=True)],
                     reads=[CS, A2[0], A2[1]], writes=[py])
                f.op("act", lambda e: e.activation(out=junk.ap, in_=py.ap, func=ACTF.Square, accum_out=s_.ap), reads=[py], writes=[junk, s_])
                rstd_of(s_.ap, s_, r_.ap, r_, 1.0 / (512.0 * 512.0 * 512.0))
                f.op("dve", lambda e: e.tensor_scalar(out=y_.ap, in0=py.ap, scalar1=r_.ap[:, 0:1], scalar2=1.0 / 512.0, op0=ALU.mult,
                                                      op1=ALU.mult), reads=[py, r_], writes=[y_])
                f.mm([lambda e, k=k: e.transpose(out=pt.ap[:, k, :], in_=y_.ap[:, k * 128:(k + 1) * 128], identity=ident.ap)
                      for k in range(4)], reads=[y_, ident], writes=[pt])
                f.op("act", lambda e: e.activation(out=yTv[:, 0:4, cp], in_=pt.ap.rearrange("p k (q d) -> p k q d", q=4),
                                                   func=ACTF.Copy), reads=[pt], writes=[yT])

    def phase3():
        with Scope(f) as es:
            hcur = [sb(es, "fh%d" % i, [64, L], F32) for i in range(2)]
            wo = sb(es, "fwo", [64, 2, 2, 512], F32)
            wsd = sb(es, "fwsd", [64, 2, 2, 512], F32)
            HF = sb(es, "HF3", [128, 32, 2, 128], BF16)
            SEL = sb(es, "SEL", [128, 3, 128], BF16)
            dl = sb(es, "dl", [1, 2, 512], F32)
            f.dma("sp", wo.ap.rearrange("p a b n -> p (a b n)"), D["f_wo"].ap, writes=[wo])
            f.dma("sp", HF.ap, D["HF"].ap, writes=[HF])
            f.dma("sp", SEL.ap, D["SEL"].ap, writes=[SEL])
            f.dma("sp", dl.ap, D["long_d"].ap, writes=[dl])
            for o in range(2):
                f.op("dve", lambda e, o=o: e.tensor_tensor(out=wsd.ap[:, o, 0, :], in0=wo.ap[:, o, 0, :], in1=wo.ap[:, o, 1, :], op=ALU.add),
                     reads=[wo], writes=[wsd])
                f.op("dve", lambda e, o=o: e.tensor_tensor(out=wsd.ap[:, o, 1, :], in0=wo.ap[:, o, 0, :], in1=wo.ap[:, o, 1, :], op=ALU.subtract),
                     reads=[wo], writes=[wsd])
            with Scope(f) as em:
                zT = sb(em, "zT", [33, L], F32)
                w0 = sb(em, "fw0", [33, 64], F32)
                fb = sb(em, "fb", [64, 3], F32)
                wi = sb(em, "fwi", [64, 2, 64], F32)
                fr = sb(em, "ffr", [64, 1], F32)
                s1 = sb(em, "fs1", [64, 1], F32)
                s2 = sb(em, "fs2", [64, 3], F32)
                f.dma("sp", zT.ap, D["zT"].ap, writes=[zT])
                f.dma("sp", w0.ap, D["f_w0"].ap, writes=[w0])
                f.dma("sp", fb.ap, D["f_b"].ap, writes=[fb])
                f.dma("sp", wi.ap, D["f_wi"].ap, writes=[wi])
                f.dma("sp", fr.ap, D["f_fr"].ap, writes=[fr])
                inv2pi = float(1.0 / (2 * np.pi))
                f.op("dve", lambda e: e.tensor_scalar(out=s1.ap, in0=fr.ap, scalar1=inv2pi, scalar2=None, op0=ALU.mult), reads=[fr], writes=[s1])
                f.op("dve", lambda e: e.tensor_scalar(out=s2.ap, in0=fb.ap, scalar1=s1.ap[:, 0:1], scalar2=16.0, op0=ALU.mult, op1=ALU.add),
                     reads=[fb, s1], writes=[s2])
                pm = [ps(em, "fpm%d" % i, [64, 512]) for i in range(2)]
                rr = [sb(em, "frr%d" % i, [64, 512], F32) for i in range(2)]
                ki = [sb(em, "fki%d" % i, [64, 512], I32) for i in range(2)]
                kf = [sb(em, "fkf%d" % i, [64, 512], F32) for i in range(2)]
                for layer in range(3):
                    src = zT if layer == 0 else hcur[(layer - 1) % 2]
                    dst = hcur[layer % 2]
                    lw = w0.ap if layer == 0 else wi.ap[:, layer - 1, :]
                    lwt = w0 if layer == 0 else wi
                    kdim = 33 if layer == 0 else 64
                    for c8 in range(8):
                        i = c8 % 2
                        cs = slice(c8 * 512, (c8 + 1) * 512)
                        f.mm([lambda e: e.matmul(pm[i].ap, lhsT=lw, rhs=src.ap[0:kdim, cs], start=True, stop=True)], reads=[src, lwt], writes=[pm[i]])
                        f.op("dve", lambda e: e.tensor_scalar(out=rr[i].ap, in0=pm[i].ap, scalar1=s1.ap[:, 0:1], scalar2=s2.ap[:, layer:layer + 1],
                                                              op0=ALU.mult, op1=ALU.add), reads=[pm[i], s1, s2], writes=[rr[i]])
                        f.op("dve", lambda e: e.tensor_copy(out=ki[i].ap, in_=rr[i].ap), reads=[rr[i]], writes=[ki[i]])
                        f.op("dve", lambda e: e.tensor_copy(out=kf[i].ap, in_=ki[i].ap), reads=[ki[i]], writes=[kf[i]])
                        f.op("dve", lambda e: e.tensor_tensor(out=rr[i].ap, in0=rr[i].ap, in1=kf[i].ap, op=ALU.subtract), reads=[rr[i], kf[i]], writes=[rr[i]])
                        f.op("dve", lambda e: e.tensor_single_scalar(out=kf[i].ap, in_=rr[i].ap, scalar=0.5, op=ALU.is_gt), reads=[rr[i]], writes=[kf[i]])
                        f.op("dve", lambda e: e.tensor_tensor(out=rr[i].ap, in0=rr[i].ap, in1=kf[i].ap, op=ALU.subtract), reads=[rr[i], kf[i]], writes=[rr[i]])
                        f.op("act", lambda e: e.activation(out=dst.ap[:, cs], in_=rr[i].ap, func=ACTF.Sin, scale=float(2 * np.pi)),
                             reads=[rr[i]], writes=[dst])
            h3 = sb(es, "fh3b", [64, L], BF16)
            wsdb = sb(es, "fwsdb", [64, 2, 2, 512], BF16)
            f.op("dve", lambda e: e.tensor_copy(out=h3.ap[:, 0:2048], in_=hcur[0].ap[:, 0:2048]), reads=[hcur[0]], writes=[h3])
            f.op("act", lambda e: e.activation(out=h3.ap[:, 2048:4096], in_=hcur[0].ap[:, 2048:4096], func=ACTF.Copy), reads=[hcur[0]], writes=[h3])
            f.op("dve", lambda e: e.tensor_copy(out=wsdb.ap, in_=wsd.ap), reads=[wsd], writes=[wsdb])
            for o in range(2):
                with Scope(f) as eo:
                    sd = [sb(eo, "fsd%d" % i, [128, 32, 512], BF16) for i in range(2)]
                    sq = [sb(eo, "fsq%d" % i, [128, 512], BF16) for i in range(2)]
                    dec = [sb(eo, "fdec%d" % i, [128, 512], F32) for i in range(2)]
                    pk = [ps(eo, "fpk%d" % i, [128, 512]) for i in range(2)]
                    pn = ps(eo, "fpn", [128, 512])
                    sqt = sb(eo, "fsqt", [128, 512], F32)
                    tmp0 = sb(eo, "ftmp0", [1, 512], F32)
                    nsq = 0
                    for j in range(32):
                        dc = dec[j % 2]
                        f.dma("sp", dc.ap, D["decay"].ap[:, j, :], writes=[dc])
                        for v in range(2):
                            f.mm([lambda e, v=v: e.matmul(pk[v].ap, lhsT=h3.ap[:, j * 128:(j + 1) * 128], rhs=wsdb.ap[:, o, v, :], start=True, stop=True)],
                                 reads=[h3, wsdb], writes=[pk[v]])
                            f.op("dve", lambda e, v=v: e.tensor_tensor(out=sd[v].ap[:, j, :], in0=pk[v].ap, in1=dc.ap, op=ALU.mult),
                                 reads=[pk[v], dc], writes=[sd[v]])
                            q_ = sq[nsq % 2]
                            nsq += 1
                            f.op("act", lambda e, v=v: e.activation(out=q_.ap, in_=sd[v].ap[:, j, :], func=ACTF.Square), reads=[sd[v]], writes=[q_])
                            first = (j == 0 and v == 0)
                            fns = [lambda e: e.matmul(pn.ap, lhsT=SEL.ap[:, 0, :], rhs=q_.ap, start=first, stop=False)]
                            if j == 0:
                                fns.append(lambda e, v=v: e.matmul(pn.ap, lhsT=SEL.ap[:, 1 + v, :], rhs=q_.ap, start=False, stop=False))
                            if j == 31 and v == 1:
                                fns = [lambda e: e.matmul(pn.ap, lhsT=SEL.ap[:, 0, :], rhs=q_.ap, start=False, stop=True)]
                            f.mm(fns, reads=[q_, SEL], writes=[pn])
                    f.op("act", lambda e: e.activation(out=sqt.ap, in_=pn.ap, func=ACTF.Sqrt, scale=0.5, bias=epsT.ap[:, 0:1]),
                         reads=[pn, epsT], writes=[sqt])
                    f.op("dve", lambda e, o=o: e.reciprocal(out=scl.ap[:, o, :], in_=sqt.ap), reads=[sqt], writes=[scl])
                    f.op("dve", lambda e, o=o: e.tensor_tensor(out=tmp0.ap, in0=dl.ap[0:1, o, :], in1=sqt.ap[0:1, :], op=ALU.mult),
                         reads=[dl, sqt], writes=[tmp0])
                    f.op("dve", lambda e: e.tensor_tensor(out=sd[0].ap[0:1, 0, :], in0=sd[0].ap[0:1, 0, :], in1=tmp0.ap, op=ALU.add),
                         reads=[sd[0], tmp0], writes=[sd[0]])
                    f.op("dve", lambda e, o=o: e.tensor_scalar(out=scl.ap[:, o, :], in0=scl.ap[:, o, :], scalar1=2.0 / NFFT, scalar2=None, op0=ALU.mult),
                         reads=[scl], writes=[scl])
                    with Scope(f) as es2:
                        stage1(es2, sd[0], HF, [[(0, lambda j: sd[0].ap[:, j, :])], [(1, lambda j: sd[0].ap[:, j, :])]], [0, 1], "k%da" % o)
                    with Scope(f) as es2:
                        stage1(es2, sd[1], HF, [[(0, lambda j: sd[1].ap[:, j, :])], [(1, lambda j: sd[1].ap[:, j, :])]], [2, 3], "k%db" % o)
                with Scope(f) as es2:
                    A2 = [sb(es2, "kA2%d" % i, [128, 16, 512], BF16) for i in range(4)]
                    kst = [sb(es2, "kst%d" % i, [128, 2, 4, 512], BF16) for i in range(2)]
                    pkk = [ps(es2, "kpk%d" % i, [128, 512]) for i in range(2)]
                    for half in range(2):
                        for i in range(4):
                            load_split("sp", A2[i], A2[i].ap, relayout_view(FA.ap[i])[:, half * 16:(half + 1) * 16, :], [FA], 2)
                        for c16 in range(16):
                            cp = half * 16 + c16
                            st_ = kst[(cp // 4) % 2]
                            f.mm([lambda e: e.matmul(pkk[0].ap, lhsT=CS.ap[:, 0, :], rhs=A2[0].ap[:, c16, :], start=True, stop=False),
                                  lambda e: e.matmul(pkk[0].ap, lhsT=CS.ap[:, 1, :], rhs=A2[1].ap[:, c16, :], start=False, stop=True)],
                                 reads=[CS, A2[0], A2[1]], writes=[pkk[0]])
                            f.mm([lambda e: e.matmul(pkk[1].ap, lhsT=CS.ap[:, 0, :], rhs=A2[3].ap[:, c16, :], start=True, stop=False),
                                  lambda e: e.matmul(pkk[1].ap, lhsT=CS.ap[:, 2, :], rhs=A2[2].ap[:, c16, :], start=False, stop=True)],
                                 reads=[CS, A2[2], A2[3]], writes=[pkk[1]])
                            f.op("act", lambda e: e.activation(out=st_.ap[:, 0, cp % 4, :], in_=pkk[0].ap, func=ACTF.Copy), reads=[pkk[0]], writes=[st_])
                            f.op("dve", lambda e: e.tensor_copy(out=st_.ap[:, 1, cp % 4, :], in_=pkk[1].ap), reads=[pkk[1]], writes=[st_])
                            if cp % 4 == 3:
                                for r in range(2):
                                    f.dma("pool", KH.ap[o, r, :, cp - 3:cp + 1, :], st_.ap[:, r], reads=[st_], writes=[KH], sem=st_, accumulate=True)

    def phase_h(o):
        with Scope(f) as es:
            HF = sb(es, "HFh", [128, 32, 2, 128], BF16)
            z = sb(es, "hz", [128, 32, 512], BF16)
            f.dma("sp", HF.ap, D["HF"].ap, writes=[HF])
            if o == 0:
                load_split("sp", z, z.ap, XG.ap[2], [XG], 2)
            else:
                load_split("sp", z, z.ap, Z1.ap, [Z1], 2)
            stage1(es, z, HF, [[(0, lambda j: z.ap[:, j, :])], [(1, lambda j: z.ap[:, j, :])]], [0, 1], "h%d" % o)
        with Scope(f) as es:
            A2 = [sb(es, "hA2%d" % i, [128, 32, 512], BF16) for i in range(2)]
            for i in range(2):
                load_split("sp", A2[i], A2[i].ap, relayout_view(FA.ap[i]), [FA], 4)
            kk = [[sb(es, "hk%d%d" % (i, r), [128, 4, 512], BF16) for r in range(2)] for i in range(2)]
            pXX = [[ps(es, "hpX%d_%d" % (i, b), [128, 512]) for i in range(2)] for b in range(2)]
            pBB = [[ps(es, "hpB%d_%d" % (i, b), [128, 512]) for i in range(2)] for b in range(2)]
            ttt = [[sb(es, "htt%d_%d" % (i, b), [128, 512], F32) for i in range(4)] for b in range(2)]
            Y = [[sb(es, "hY%d%d" % (i, r), [128, 512], BF16) for r in range(2)] for i in range(2)]
            bst = [sb(es, "hbst%d" % i, [128, 2, 4, 512], BF16) for i in range(2)]
            FBv = [relayout_view(FB.ap[r]) for r in range(2)]
            def emit_x(cp):
                pX = pXX[cp % 2]
                f.mm([lambda e: e.matmul(pX[0].ap, lhsT=CS.ap[:, 0, :], rhs=A2[0].ap[:, cp, :], start=True, stop=False),
                      lambda e: e.matmul(pX[0].ap, lhsT=CS.ap[:, 1, :], rhs=A2[1].ap[:, cp, :], start=False, stop=True)],
                     reads=[CS, A2[0], A2[1]], writes=[pX[0]])
                f.mm([lambda e: e.matmul(pX[1].ap, lhsT=CS.ap[:, 0, :], rhs=A2[1].ap[:, cp, :], start=True, stop=False),
                      lambda e: e.matmul(pX[1].ap, lhsT=CS.ap[:, 2, :], rhs=A2[0].ap[:, cp, :], start=False, stop=True)],
                     reads=[CS, A2[0], A2[1]], writes=[pX[1]])

            emit_x(0)
            for cp in range(32):
                g = cp // 4
                pX, pB, tt = pXX[cp % 2], pBB[cp % 2], ttt[cp % 2]
                kb = kk[g % 2]
                if cp % 4 == 0:
                    for r in range(2):
                        f.dma("sp", kb[r].ap, KH.ap[o, r, :, cp:cp + 4, :], reads=[KH], writes=[kb[r]])
                kr = kb[0].ap[:, cp % 4, :]
                kim = kb[1].ap[:, cp % 4, :]
                f.op("dve", lambda e: e.tensor_tensor(out=tt[0].ap, in0=pX[0].ap, in1=kr, op=ALU.mult), reads=[pX[0], kb[0]], writes=[tt[0]])
                f.op("dve", lambda e: e.tensor_tensor(out=tt[1].ap, in0=pX[1].ap, in1=kim, op=ALU.mult), reads=[pX[1], kb[1]], writes=[tt[1]])
                yy = Y[cp % 2]
                f.op("pool", lambda e: e.tensor_tensor(out=yy[0].ap, in0=tt[0].ap, in1=tt[1].ap, op=ALU.subtract), reads=[tt[0], tt[1]], writes=[yy[0]])
                f.op("dve", lambda e: e.tensor_tensor(out=tt[2].ap, in0=pX[0].ap, in1=kim, op=ALU.mult), reads=[pX[0], kb[1]], writes=[tt[2]])
                f.op("dve", lambda e: e.tensor_tensor(out=tt[3].ap, in0=pX[1].ap, in1=kr, op=ALU.mult), reads=[pX[1], kb[0]], writes=[tt[3]])
                f.op("pool", lambda e: e.tensor_tensor(out=yy[1].ap, in0=tt[2].ap, in1=tt[3].ap, op=ALU.add), reads=[tt[2], tt[3]], writes=[yy[1]])
                if cp + 1 < 32:
                    emit_x(cp + 1)
                f.mm([lambda e: e.matmul(pB[0].ap, lhsT=CS.ap[:, 0, :], rhs=yy[0].ap, start=True, stop=False),
                      lambda e: e.matmul(pB[0].ap, lhsT=CS.ap[:, 2, :], rhs=yy[1].ap, start=False, stop=True)],
                     reads=[CS, yy[0], yy[1]], writes=[pB[0]])
                f.mm([lambda e: e.matmul(pB[1].ap, lhsT=CS.ap[:, 1, :], rhs=yy[0].ap, start=True, stop=False),
                      lambda e: e.matmul(pB[1].ap, lhsT=CS.ap[:, 0, :], rhs=yy[1].ap, start=False, stop=True)],
                     reads=[CS, yy[0], yy[1]], writes=[pB[1]])
                bs = bst[g % 2]
                f.op("act", lambda e: e.activation(out=bs.ap[:, 0, cp % 4, :], in_=pB[0].ap, func=ACTF.Copy), reads=[pB[0]], writes=[bs])
                f.op("act", lambda e: e.activation(out=bs.ap[:, 1, cp % 4, :], in_=pB[1].ap, func=ACTF.Copy), reads=[pB[1]], writes=[bs])
                if cp % 4 == 3:
                    for r in range(2):
                        f.dma("sp", FBv[r][:, cp - 3:cp + 1, :], bs.ap[:, r], reads=[bs], writes=[FB], sem=bs, accumulate=True)
        with Scope(f) as es:
            HG = sb(es, "HGh", [128, 32, 2, 128], BF16)
            Bc = [sb(es, "hBc%d" % i, [128, 16, 512], BF16) for i in range(2)]
            gt = sb(es, "hgt", [128, 32, 512], BF16)
            f.dma("sp", HG.ap, D["HG"].ap, writes=[HG])
            load_split("sp", gt, gt.ap, XG.ap[o], [XG], 2)
            for h in range(4):
                eng = "dve" if h % 2 == 0 else "pool"
                f.op(eng, lambda e, h=h: e.tensor_tensor(out=gt.ap[:, h * 8:(h + 1) * 8, :], in0=gt.ap[:, h * 8:(h + 1) * 8, :],
                                                         in1=scl.ap[:, o:o + 1, :].broadcast_to([128, 8, 512]), op=ALU.mult),
                     reads=[gt, scl], writes=[gt])
            gs = gt
            pY = [ps(es, "hpY%d" % i, [128, 512]) for i in range(2)]
            if o == 0:
                zst = [sb(es, "hzst%d" % i, [128, 4, 512], BF16) for i in range(2)]
            else:
                yf = [sb(es, "hyf%d" % i, [128, 512], F32) for i in range(2)]
                junk = sb(es, "hjunk", [128, 512], BF16)
                ssq = [sb(es, "hss%d" % i, [128, 1], F32) for i in range(2)]
                rsq = [sb(es, "hrs%d" % i, [128, 1], F32) for i in range(2)]
                yn = [sb(es, "hyn%d" % i, [128, 512], BF16) for i in range(2)]
                pT2 = [ps(es, "hpT%d" % i, [128, 4, 128], BF16) for i in range(2)]
                yT = yTbox[0]
                yTv = yT.ap.rearrange("p k (pp j) -> p k j pp", j=32)
            for j in range(32):
                if j % 16 == 0:
                    for r in range(2):
                        load_split("sp", Bc[r], Bc[r].ap, FB.ap[r][:, j:j + 16, :], [FB], 2)
                jl = j % 16
                py = pY[j % 2]
                f.mm([lambda e: e.matmul(py.ap, lhsT=HG.ap[:, j, 0, :], rhs=Bc[0].ap[:, jl, :], start=True, stop=False),
                      lambda e: e.matmul(py.ap, lhsT=HG.ap[:, j, 1, :], rhs=Bc[1].ap[:, jl, :], start=False, stop=True)],
                     reads=[HG, Bc[0], Bc[1]], writes=[py])
                if o == 0:
                    zs = zst[(j // 4) % 2]
                    f.op("dve", lambda e: e.tensor_tensor(out=zs.ap[:, j % 4, :], in0=py.ap, in1=gs.ap[:, j, :], op=ALU.mult),
                         reads=[py, gs], writes=[zs])
                    if j % 4 == 3:
                        f.dma("pool", Z1.ap[:, j - 3:j + 1, :], zs.ap, reads=[zs], writes=[Z1], sem=zs, accumulate=True)
                else:
                    y_, s_, r_, n_, pt = yf[j % 2], ssq[j % 2], rsq[j % 2], yn[j % 2], pT2[j % 2]
                    f.op("dve", lambda e: e.tensor_tensor(out=y_.ap, in0=py.ap, in1=gs.ap[:, j, :], op=ALU.mult), reads=[py, gs], writes=[y_])
                    f.op("act", lambda e: e.activation(out=junk.ap, in_=y_.ap, func=ACTF.Square, accum_out=s_.ap), reads=[y_], writes=[junk, s_])
                    rstd_of(s_.ap, s_, r_.ap, r_, 1.0 / 512.0)
                    f.op("act", lambda e: e.activation(out=n_.ap, in_=y_.ap, func=ACTF.Copy, scale=r_.ap[:, 0:1]), reads=[y_, r_], writes=[n_])
                    f.mm([lambda e, k=k: e.transpose(out=pt.ap[:, k, :], in_=n_.ap[:, k * 128:(k + 1) * 128], identity=ident.ap)
                          for k in range(4)], reads=[n_, ident], writes=[pt])
                    f.op("dve", lambda e: e.tensor_copy(out=yTv[:, 4:8, j, :], in_=pt.ap), reads=[pt], writes=[yT])

    def phase6():
        with Scope(f) as es:
            yT = yTbox[0]
            Wo = sb(es, "Wo", [128, 8, 1024], BF16)
            gfh = sb(es, "gfh", [128, 8], F32)
            gfin = sb(es, "gfin", [128, 1024], F32)
            f.dma("sp", gfh.ap, D["gfh"].ap, writes=[gfh])
            f.dma("sp", gfin.ap, D["gfin"].ap.partition_broadcast(128).rearrange("p a n -> p (a n)"), writes=[gfin])
            with Scope(f) as es2:
                wst = [sb(es2, "wost%d" % i, [128, 1024], F32) for i in range(2)]
                wv = D["w_out"].ap.rearrange("(k p) e -> p k e", p=128)
                for k in range(8):
                    s = wst[k % 2]
                    f.dma("sp", s.ap, wv[:, k, :], writes=[s])
                    f.op("dve", lambda e, k=k, s=s: e.tensor_scalar(out=Wo.ap[:, k, :], in0=s.ap, scalar1=gfh.ap[:, k:k + 1], scalar2=None,
                                                                    op0=ALU.mult), reads=[s, gfh], writes=[Wo])
            aT = sb(es, "aT", [128, 32, 512], BF16)
            hmT = [sb(es, "hmT%d" % i, [128, 8, 512], BF16) for i in range(2)]
            x1s = [sb(es, "x1s%d" % i, [128, 4, 1024], F32) for i in range(2)]
            W1s = [sb(es, "W1s%d" % i, [128, 8, 256], BF16) for i in range(2)]
            W2s = [sb(es, "W2s%d" % i, [128, 4, 512], BF16) for i in range(2)]
            rl = [sb(es, "rl%d" % i, [128, 512], BF16) for i in range(2)]
            hm = [sb(es, "hm%d" % i, [128, 1024], BF16) for i in range(2)]
            junk = sb(es, "mjunk", [128, 1024], BF16)
            ssq = [sb(es, "mss%d" % i, [128, 1], F32) for i in range(4)]
            rsq = [sb(es, "mrs%d" % i, [128, 1], F32) for i in range(4)]
            pw = [ps(es, "mpw%d" % i, [128, 512]) for i in range(2)]
            po = ps(es, "mpo", [128, 512])
            pT = ps(es, "mpT", [128, 8, 128], BF16)
            p2 = [ps(es, "mp2%d" % i, [128, 512]) for i in range(4)]
            xls = [[T(None, "xl") for _ in range(4)] for _ in range(2)]
            xss = [[T(None, "xs") for _ in range(4)] for _ in range(2)]
            xrows = D["x"].ap.rearrange("(n p) d -> n p d", p=128)
            orows = out_d.ap.rearrange("(n p) d -> n p d", p=128)
            cnt = {"n1": 0, "n2": 0}

            def pro_a(tb, s):
                xb = x1s[tb % 2]
                tt_ = tb * 4 + s
                f.dma("sp", xb.ap[:, s, :], xrows[tt_], writes=[xb], sem=xls[tb % 2][s])
                for dh in range(2):
                    f.mm([lambda e, k=k, dh=dh: e.matmul(po.ap, lhsT=yT.ap[:, k, tt_ * 128:(tt_ + 1) * 128],
                                                         rhs=Wo.ap[:, k, dh * 512:(dh + 1) * 512], start=(k == 0), stop=(k == 7))
                          for k in range(8)], reads=[yT, Wo], writes=[po])
                    f.op("dve", lambda e, dh=dh: e.tensor_tensor(out=xb.ap[:, s, dh * 512:(dh + 1) * 512], in0=po.ap,
                                                                 in1=xb.ap[:, s, dh * 512:(dh + 1) * 512], op=ALU.add),
                         reads=[po, xb], writes=[xb])

            def pro_b(tb, s):
                xb = x1s[tb % 2]
                s_, r_, h_ = ssq[s % 2], rsq[s % 2], hm[s % 2]
                f.op("act", lambda e: e.activation(out=junk.ap, in_=xb.ap[:, s, :], func=ACTF.Square, accum_out=s_.ap), reads=[xb], writes=[junk, s_])
                rstd_of(s_.ap, s_, r_.ap, r_, 1.0 / 1024)
                f.op("act", lambda e: e.activation(out=h_.ap, in_=xb.ap[:, s, :], func=ACTF.Copy, scale=r_.ap[:, 0:1]), reads=[xb, r_], writes=[h_])
                f.mm([lambda e, k=k: e.transpose(out=pT.ap[:, k, :], in_=h_.ap[:, k * 128:(k + 1) * 128], identity=ident.ap)
                      for k in range(8)], reads=[h_, ident], writes=[pT])
                f.op("dve", lambda e: e.tensor_copy(out=hmT[tb % 2].ap[:, :, s * 128:(s + 1) * 128], in_=pT.ap), reads=[pT], writes=[hmT[tb % 2]])

            def fc1_group(tb, fg):
                ws = W1s[cnt["n1"] % 2]
                cnt["n1"] += 1
                f.dma("sp", ws.ap, W1B.ap[:, :, fg * 256:(fg + 1) * 256], reads=[W1B], writes=[ws])
                for fl in range(2):
                    fc = fg * 2 + fl
                    pp = pw[fc % 2]
                    f.mm([lambda e, k=k: e.matmul(pp.ap, lhsT=ws.ap[:, k, fl * 128:(fl + 1) * 128], rhs=hmT[tb % 2].ap[:, k, :],
                                                  start=(k == 0), stop=(k == 7)) for k in range(8)], reads=[ws, hmT[tb % 2]], writes=[pp])
                    rr_ = rl[fc % 2]
                    f.op("act", lambda e: e.activation(out=rr_.ap, in_=pp.ap, func=ACTF.Relu), reads=[pp], writes=[rr_])
                    f.op("pool", lambda e: e.tensor_tensor(out=aT.ap[:, fc, :], in0=rr_.ap, in1=rr_.ap, op=ALU.mult), reads=[rr_], writes=[aT])

            def fc2_all(tb):
                xb = x1s[tb % 2]
                for dh in range(2):
                    for fg in range(8):
                        ws = W2s[cnt["n2"] % 2]
                        cnt["n2"] += 1
                        f.dma("sp", ws.ap, W2B.ap[:, fg * 4:(fg + 1) * 4, dh * 512:(dh + 1) * 512], reads=[W2B], writes=[ws])
                        for s in range(4):
                            f.mm([lambda e, fl=fl: e.matmul(p2[s].ap, lhsT=aT.ap[:, fg * 4 + fl, s * 128:(s + 1) * 128], rhs=ws.ap[:, fl, :],
                                                            start=(fg == 0 and fl == 0), stop=(fg == 7 and fl == 3)) for fl in range(4)],
                                 reads=[aT, ws], writes=[p2[s]])
                    for s in range(4):
                        f.op("dve", lambda e: e.tensor_tensor(out=xb.ap[:, s, dh * 512:(dh + 1) * 512], in0=p2[s].ap,
                                                              in1=xb.ap[:, s, dh * 512:(dh + 1) * 512], op=ALU.add), reads=[p2[s], xb], writes=[xb])

            def epilogue(tb):
                xb = x1s[tb % 2]
                for s in range(4):
                    s_, r_ = ssq[2 + s % 2], rsq[2 + s % 2]
                    f.op("act", lambda e: e.activation(out=junk.ap, in_=xb.ap[:, s, :], func=ACTF.Square, accum_out=s_.ap), reads=[xb], writes=[junk, s_])
                    rstd_of(s_.ap, s_, r_.ap, r_, 1.0 / 1024)
                    f.op("act", lambda e: e.activation(out=xb.ap[:, s, :], in_=xb.ap[:, s, :], func=ACTF.Copy, scale=r_.ap[:, 0:1]), reads=[xb, r_], writes=[xb])
                    f.op("pool", lambda e: e.tensor_tensor(out=xb.ap[:, s, :], in0=xb.ap[:, s, :], in1=gfin.ap, op=ALU.mult), reads=[xb, gfin], writes=[xb])
                    f.dma("pool", orows[tb * 4 + s], xb.ap[:, s, :], reads=[xb], writes=[out_d], sem=xss[tb % 2][s], accumulate=True)

            for s in range(4):
                pro_a(0, s)
                pro_b(0, s)
            for tb in range(8):
                for fg in range(16):
                    fc1_group(tb, fg)
                    if tb + 1 < 8:
                        if fg % 4 == 0:
                            pro_a(tb + 1, fg // 4)
                        if fg % 4 == 3:
                            pro_b(tb + 1, fg // 4)
                fc2_all(tb)
                epilogue(tb)

    def alloc_yT():
        yTbox.append(sb(top, "yT", [128, 8, L], BF16))
    phases = [("p0", phase0), ("p1", phase1), ("p3", phase3), ("yT", alloc_yT), ("p2", phase2),
              ("p4", lambda: phase_h(0)), ("p5", lambda: phase_h(1)), ("p6", phase6)]
    for i, (nm, fn) in enumerate(phases):
        if i <= upto:
            fn()
    f.wait_all("pool", [out_d, RAW, U, XG, FA, FB, KH, Z1, W1B, W2B])
    top.close()
    return nc


def make_in_maps(inputs):
    C = host_consts()
    g = {k: np.asarray(v, dtype=np.float32) for k, v in inputs.items()}

    def pk(v):
        return np.ascontiguousarray(v.reshape(8, 128).T)
    shared = {
        "gm": pk(g["g_mix"][0]), "w_in": g["w_in"][0], "cw": np.concatenate([g["conv_w"][0], g["conv_b"]], 0),
        "f_w0": g["filt_w0"][0],
        "f_b": np.ascontiguousarray(np.stack([g["filt_b0"][0], g["filt_b_inner"][0, 0], g["filt_b_inner"][0, 1]], 1)),
        "f_wi": np.ascontiguousarray(g["filt_w_inner"][0].transpose(1, 0, 2)),
        "f_fr": np.ascontiguousarray(g["filt_freq"][0][:, None]),
        "f_wo": g["filt_w_out"][0], "long_d": g["long_d"],
        "gfh": pk(np.concatenate([g["g_fourier"][0], g["g_hyena"][0]])),
        "w_out": g["w_out"][0], "gml": pk(g["g_mlp"][0]), "w_fc1": g["w_fc1"][0], "w_fc2": g["w_fc2"][0],
        "gfin": g["g_final"][None, :],
    }
    shared.update({k: C[k] for k, _, _ in CONST_SPECS})
    shared = {k: np.ascontiguousarray(v) for k, v in shared.items()}
    maps = []
    for b in range(8):
        m = dict(shared)
        m["x"] = np.ascontiguousarray(g["x"][b])
        maps.append(m)
    return maps


def kernel(**inputs):
    nc = build()
    in_maps = make_in_maps(inputs)
    res = run_bass_kernel_spmd(nc, in_maps, core_ids=list(range(8)))
    return np.stack([np.asarray(r["out"], dtype=np.float32) for r in res.results], 0)
```

```python
import numpy as np
import ml_dtypes
from contextlib import ExitStack
import concourse.bass as bass
import concourse.mybir as mybir
from concourse.bass_utils import run_bass_kernel_spmd

F32 = mybir.dt.float32
BF16 = mybir.dt.bfloat16
I32 = mybir.dt.int32
ALU = mybir.AluOpType
ACTF = mybir.ActivationFunctionType
EPOCH = 8000
L = 4096
NFFT = 8192
EPS = 1e-5
NB = ml_dtypes.bfloat16


class Res:
    __slots__ = ("name", "w", "rd")

    def __init__(self, name):
        self.name = name
        self.w = {}
        self.rd = []


class T:
    __slots__ = ("ap", "r")

    def __init__(self, ap, name):
        self.ap = ap
        self.r = Res(name)


class FW:
    def __init__(self, nc):
        self.nc = nc
        self.eng = {"pe": nc.tensor, "act": nc.scalar, "dve": nc.vector, "pool": nc.gpsimd, "sp": nc.sync}
        self.sems = {}
        self.cnt = {k: 0 for k in ("pe", "act", "dve", "pool")}
        self.waited = {k: {} for k in self.eng}
        self.nsem = 0
        self.recent = {}
        self.gen = 0
        self.free_dma = []
        self._dcnt = {}

    def _sem(self, key):
        s = self.sems.get(key)
        if s is None:
            s = self.nc.alloc_semaphore("s%d" % self.nsem)
            self.nsem += 1
            self.sems[key] = s
        return s

    def _wait(self, engine, deps):
        w = self.waited[engine]
        best = {}
        for key, val in deps:
            if engine == "pe" and key[0] == "pe":
                continue
            if key[0] == "dma" and key[2] < self.gen:
                continue
            if w.get(key, 0) < val and best.get(key, 0) < val:
                best[key] = val
        for key, val in best.items():
            self.eng[engine].wait_ge(self._sem(key), val)
            w[key] = val

    @staticmethod
    def _deps(reads, writes):
        deps = []
        for r in reads:
            deps.extend(r.w.items())
        for r in writes:
            deps.extend(r.w.items())
            deps.extend(r.rd)
        return deps

    @staticmethod
    def _mark(tok, reads, writes, accumulate=False):
        for r in writes:
            if r.w.get(tok[0], 0) < tok[1]:
                r.w[tok[0]] = tok[1]
            r.rd = []
        for r in reads:
            r.rd.append(tok)
            if len(r.rd) > 48:
                d = {}
                for k, v in r.rd:
                    if d.get(k, 0) < v:
                        d[k] = v
                r.rd = list(d.items())

    def _tick(self, engine, inst):
        c = self.cnt[engine]
        self.cnt[engine] = c + 1
        key = (engine, c // EPOCH)
        inst.then_inc(self._sem(key), 1)
        self.recent[key] = c % EPOCH + 1
        return (key, c % EPOCH + 1)

    def barrier(self):
        toks = list(self.recent.items())
        for eng in self.eng:
            self._wait(eng, toks)
        self.recent = {}
        for key in [k for k in self.sems if k[0] == "dma"]:
            self.free_dma.append((self.sems.pop(key), self._dcnt.pop(key)))
        self.gen += 1

    def op(self, engine, fn, reads=(), writes=()):
        reads = [t.r for t in reads]
        writes = [t.r for t in writes]
        self._wait(engine, self._deps(reads, writes))
        tok = self._tick(engine, fn(self.eng[engine]))
        self._mark(tok, reads, writes)

    def mm(self, fns, reads=(), writes=()):
        reads = [t.r for t in reads]
        writes = [t.r for t in writes]
        self._wait("pe", self._deps(reads, writes))
        pe = self.eng["pe"]
        for fn in fns[:-1]:
            fn(pe)
        tok = self._tick("pe", fns[-1](pe))
        self._mark(tok, reads, writes)

    def dma(self, queue, out_ap, in_ap, reads=(), writes=(), sem=None, accumulate=False):
        reads = [t.r for t in reads]
        writes_r = [t.r for t in writes]
        self._wait(queue, self._deps(reads, writes_r))
        s = sem if sem is not None else writes[0]
        key = ("dma", id(s.r), self.gen)
        cnts = self._dcnt
        if key not in self.sems and self.free_dma:
            self.sems[key], cnts[key] = self.free_dma.pop()
        cnts[key] = cnts.get(key, 0) + 16
        self.eng[queue].dma_start(out=out_ap, in_=in_ap).then_inc(self._sem(key), 16)
        tok = (key, cnts[key])
        self.recent[key] = cnts[key]
        self._mark(tok, reads, writes_r, accumulate=accumulate)

    def wait_all(self, engine, ts):
        deps = []
        for t in ts:
            deps.extend(t.r.w.items())
            deps.extend(t.r.rd)
        self._wait(engine, deps)


class Scope:
    def __init__(self, f):
        self.f = f
        self.es = ExitStack()

    def __enter__(self):
        self.es.__enter__()
        return self.es

    def __exit__(self, *a):
        self.f.barrier()
        return self.es.__exit__(*a)


_CONST = None


def host_consts():
    global _CONST
    if _CONST is not None:
        return _CONST
    c = {}
    p = np.arange(128)[:, None, None].astype(np.float64)
    j = np.arange(32)[None, :, None].astype(np.float64)
    cc = np.arange(128)[None, None, :].astype(np.float64)
    ph = 2 * np.pi * (32 * p + j) * (cc + 0.5) / NFFT
    c["HF"] = np.stack([np.cos(ph), -np.sin(ph)], 2).astype(NB)
    pht = ph.transpose(2, 1, 0)
    c["HG"] = np.stack([np.cos(pht), -np.sin(pht)], 2).astype(NB)
    pf = 2 * np.pi * (32 * p + j) * cc / L
    c["FF"] = np.stack([np.cos(pf), np.sin(pf), -np.sin(pf)], 2).astype(NB)
    jd = 2 * np.pi * np.outer(np.arange(32), np.arange(32)) / 32
    C4 = np.kron(np.eye(4), np.cos(jd))
    S4 = np.kron(np.eye(4), np.sin(jd))
    c["CS"] = np.stack([C4, S4, -S4], 1).astype(NB)
    cd = 2 * np.pi * np.outer(np.arange(64), np.arange(64)) / 64
    c["BD"] = np.concatenate([np.kron(np.eye(2), np.cos(cd)), np.kron(np.eye(2), np.sin(cd))], 1).astype(NB)
    c["ident"] = np.eye(128).astype(NB)
    sel = np.zeros((128, 3, 128))
    sel[:, 0, :] = 1.0
    sel[0, 1, :] = 1.0
    sel[0, 2, :] = -1.0
    c["SEL"] = sel.astype(NB)
    pos = np.arange(L, dtype=np.float64)
    t = pos / (L - 1)
    bands = np.linspace(1e-4, 15, 16)
    ang = (2 * np.pi * pos / L)[:, None] * bands[None]
    z = np.concatenate([t[:, None], np.cos(ang), -np.sin(ang)], -1)
    zT = z.T.reshape(33, 128, 32).transpose(0, 2, 1).reshape(33, L)
    c["zT"] = np.ascontiguousarray(zT).astype(np.float32)
    deltas = np.linspace(np.log(1e-2) / 1.5, np.log(1e-2) / 0.3, 512)
    decay = np.exp(-t[:, None] * np.abs(deltas)[None])
    c["decay"] = decay.reshape(128, 32, 512).astype(np.float32)
    _CONST = c
    return c


CONST_SPECS = [("HF", [128, 32, 2, 128], BF16), ("HG", [128, 32, 2, 128], BF16), ("FF", [128, 32, 3, 128], BF16),
               ("CS", [128, 3, 128], BF16), ("BD", [128, 256], BF16), ("ident", [128, 128], BF16),
               ("SEL", [128, 3, 128], BF16), ("zT", [33, L], F32), ("decay", [128, 32, 512], F32)]

IN_SPECS = [("x", [L, 1024], F32), ("gm", [128, 8], F32), ("w_in", [1024, 2048], F32), ("cw", [4, 1536], F32),
            ("f_w0", [33, 64], F32), ("f_b", [64, 3], F32), ("f_wi", [64, 2, 64], F32), ("f_fr", [64, 1], F32),
            ("f_wo", [64, 2048], F32), ("long_d", [1, 2, 512], F32), ("gfh", [128, 8], F32),
            ("w_out", [1024, 1024], F32), ("gml", [128, 8], F32), ("w_fc1", [1024, 4096], F32),
            ("w_fc2", [4096, 1024], F32), ("gfin", [1, 1024], F32)]


def build(upto=99, debug=False):
    nc = bass.Bass("TRN2", target_bir_lowering=False)
    f = FW(nc)
    D = {}
    for name, shape, dt in IN_SPECS + CONST_SPECS:
        D[name] = T(nc.dram_tensor(name, shape, dt, kind="ExternalInput").ap(), name)
    out_d = T(nc.dram_tensor("out", [L, 1024], F32, kind="ExternalOutput").ap(), "out")
    def scratch(name, shape, dt):
        k = "ExternalOutput" if (debug is True or (debug and name in debug)) else "Internal"
        return T(nc.dram_tensor(name, shape, dt, kind=k).ap(), name)

    RAW = scratch("RAW", [128, 34, 1536], BF16)
    U = scratch("U", [128, 32, 1024], BF16)
    XG = scratch("XG", [3, 128, 32, 512], BF16)
    FA = scratch("FA", [4, 128, 32, 512], BF16)
    FB = scratch("FB", [2, 128, 32, 512], BF16)
    KH = scratch("KH", [2, 2, 128, 32, 512], BF16)
    Z1 = scratch("Z1", [128, 32, 512], BF16)
    W1B = scratch("W1B", [128, 8, 4096], BF16)
    W2B = scratch("W2B", [128, 32, 1024], BF16)

    uid = [0]

    def sb(es, name, shape, dt):
        uid[0] += 1
        return T(es.enter_context(nc.sbuf_tensor("sb%d_%s" % (uid[0], name), shape, dt)).ap(), name)

    def ps(es, name, shape, dt=F32):
        uid[0] += 1
        return T(es.enter_context(nc.psum_tensor("ps%d_%s" % (uid[0], name), shape, dt)).ap(), name)

    top = ExitStack()
    ident = sb(top, "ident", [128, 128], BF16)
    CS = sb(top, "CS", [128, 3, 128], BF16)
    epsT = sb(top, "epsT", [128, 1], F32)
    yTbox = []
    scl = sb(top, "scl", [128, 2, 512], F32)
    f.dma("sp", ident.ap, D["ident"].ap, writes=[ident])
    f.dma("sp", CS.ap, D["CS"].ap, writes=[CS])
    f.op("pool", lambda e: e.memset(epsT.ap, EPS), writes=[epsT])

    def rstd_of(ss_ap, ss_t, out_ap, out_t, scale):
        f.op("act", lambda e: e.activation(out=out_ap, in_=ss_ap, func=ACTF.Sqrt, scale=scale, bias=epsT.ap[:, 0:1]),
             reads=[ss_t, epsT], writes=[out_t])
        f.op("dve", lambda e: e.reciprocal(out=out_ap, in_=out_ap), reads=[out_t], writes=[out_t])

    def phase0():
        with Scope(f) as es:
            gml = sb(es, "gml", [128, 8], F32)
            f.dma("sp", gml.ap, D["gml"].ap, writes=[gml])
            st = [sb(es, "p0s%d" % i, [128, 8, 512], F32) for i in range(2)]
            sbf = [sb(es, "p0b%d" % i, [128, 8, 512], BF16) for i in range(2)]
            w1v = D["w_fc1"].ap.rearrange("(k p) f -> p k f", p=128)
            for i in range(8):
                s, b = st[i % 2], sbf[i % 2]
                f.dma("sp", s.ap, w1v[:, :, i * 512:(i + 1) * 512], writes=[s])
                for k in range(8):
                    if k % 2 == 0:
                        f.op("dve", lambda e, k=k: e.tensor_scalar(out=b.ap[:, k, :], in0=s.ap[:, k, :], scalar1=gml.ap[:, k:k + 1],
                                                                   scalar2=None, op0=ALU.mult), reads=[s, gml], writes=[b])
                    else:
                        f.op("act", lambda e, k=k: e.activation(out=b.ap[:, k, :], in_=s.ap[:, k, :], func=ACTF.Copy,
                                                                scale=gml.ap[:, k:k + 1]), reads=[s, gml], writes=[b])
                f.dma("pool", W1B.ap[:, :, i * 512:(i + 1) * 512], b.ap, reads=[b], writes=[W1B], sem=b, accumulate=True)
            w2v = D["w_fc2"].ap.rearrange("(c p) d -> p c d", p=128)
            for i in range(8):
                s, b = st[i % 2], sbf[i % 2]
                s4 = s.ap.rearrange("p (a b) n -> p a (b n)", a=4)
                b4 = b.ap.rearrange("p (a b) n -> p a (b n)", a=4)
                f.dma("sp", s4, w2v[:, i * 4:(i + 1) * 4, :], writes=[s])
                f.op("dve", lambda e: e.tensor_copy(out=b4[:, 0:2], in_=s4[:, 0:2]), reads=[s], writes=[b])
                f.op("act", lambda e: e.activation(out=b4[:, 2:4], in_=s4[:, 2:4], func=ACTF.Copy), reads=[s], writes=[b])
                f.dma("pool", W2B.ap[:, i * 4:(i + 1) * 4, :], b4, reads=[b], writes=[W2B], sem=b, accumulate=True)

    def phase1():
        with Scope(f) as es:
            W1 = sb(es, "W1", [128, 8, 2048], BF16)
            gm = sb(es, "gm", [128, 8], F32)
            BD = sb(es, "BD", [128, 256], BF16)
            cw = sb(es, "cw", [128, 4, 1536], F32)
            f.dma("sp", gm.ap, D["gm"].ap, writes=[gm])
            f.dma("sp", BD.ap, D["BD"].ap, writes=[BD])
            f.dma("sp", cw.ap, D["cw"].ap.partition_broadcast(128), writes=[cw])
            with Scope(f) as es0:
                wst = [sb(es0, "wst%d" % i, [128, 2048], F32) for i in range(2)]
                wv = D["w_in"].ap.rearrange("(k p) e -> p k e", p=128)
                for k in range(8):
                    s = wst[k % 2]
                    f.dma("sp", s.ap, wv[:, k, :], writes=[s])
                    if k % 2 == 0:
                        f.op("dve", lambda e, k=k, s=s: e.tensor_scalar(out=W1.ap[:, k, :], in0=s.ap, scalar1=gm.ap[:, k:k + 1], scalar2=None,
                                                                        op0=ALU.mult), reads=[s, gm], writes=[W1])
                    else:
                        f.op("act", lambda e, k=k, s=s: e.activation(out=W1.ap[:, k, :], in_=s.ap, func=ACTF.Copy, scale=gm.ap[:, k:k + 1]),
                             reads=[s, gm], writes=[W1])
            xts = [sb(es, "xt%d" % i, [128, 1024], F32) for i in range(3)]
            junk = sb(es, "junk", [128, 1024], BF16)
            ss = [sb(es, "ss%d" % i, [128, 1], F32) for i in range(2)]
            rs = [sb(es, "rs%d" % i, [128, 1], F32) for i in range(2)]
            xs = [sb(es, "xs%d" % i, [128, 1024], BF16) for i in range(2)]
            hT = [sb(es, "hT%d" % i, [128, 8, 128], BF16) for i in range(2)]
            rring = [sb(es, "rawr%d" % i, [128, 1536], BF16) for i in range(4)]
            rded = {j: sb(es, "rawd%d" % j, [128, 1536], BF16) for j in (0, 1, 30, 31)}
            hlo = sb(es, "hlo", [128, 1536], BF16)
            hhi = sb(es, "hhi", [128, 1536], BF16)
            acc = sb(es, "cacc", [128, 1536], F32)
            t1 = sb(es, "ct1", [128, 1536], F32)
            t2 = sb(es, "ct2", [128, 1536], F32)
            t3 = sb(es, "ct3", [128, 1536], F32)
            og = [sb(es, "og%d" % i, [128, 1536], BF16) for i in range(2)]
            ufT = [sb(es, "ufT%d" % i, [128, 4, 128], BF16) for i in range(2)]
            Ut = [sb(es, "Ut%d" % i, [128, 4, 256], BF16) for i in range(2)]
            pT = ps(es, "pT", [128, 8, 128], BF16)
            pH = [ps(es, "pH%d" % i, [128, 512]) for i in range(3)]
            pF = ps(es, "pF", [128, 4, 128])
            pU = ps(es, "pU", [128, 4, 256])
            xv = D["x"].ap.rearrange("(p j) d -> p j d", j=32)
            f.op("pool", lambda e: e.memset(hlo.ap, 0.0), writes=[hlo])
            f.op("pool", lambda e: e.memset(hhi.ap, 0.0), writes=[hhi])

            def rawbuf(j):
                return rded[j] if j in rded else rring[(j - 2) % 4]
            nconv = [0]

            def conv(jo, left, center, right):
                o = og[nconv[0] % 2]
                eng = "pool" if nconv[0] % 3 == 2 else "dve"
                a_, t_ = (acc, t1) if eng == "dve" else (t2, t3)
                nconv[0] += 1
                f.op(eng, lambda e: e.tensor_tensor(out=a_.ap, in0=center.ap, in1=cw.ap[:, 1, :], op=ALU.mult), reads=[center, cw], writes=[a_])
                f.op(eng, lambda e: e.tensor_tensor(out=t_.ap, in0=left.ap, in1=cw.ap[:, 0, :], op=ALU.mult), reads=[left, cw], writes=[t_])
                f.op(eng, lambda e: e.tensor_tensor(out=a_.ap, in0=a_.ap, in1=t_.ap, op=ALU.add), reads=[a_, t_], writes=[a_])
                f.op(eng, lambda e: e.tensor_tensor(out=t_.ap, in0=right.ap, in1=cw.ap[:, 2, :], op=ALU.mult), reads=[right, cw], writes=[t_])
                f.op(eng, lambda e: e.tensor_tensor(out=a_.ap, in0=a_.ap, in1=t_.ap, op=ALU.add), reads=[a_, t_], writes=[a_])
                f.op(eng, lambda e: e.tensor_tensor(out=o.ap, in0=a_.ap, in1=cw.ap[:, 3, :], op=ALU.add), reads=[a_, cw], writes=[o])
                for t in range(3):
                    f.dma("pool", XG.ap[t, :, jo, :], o.ap[:, t * 512:(t + 1) * 512], reads=[o], writes=[XG], sem=o, accumulate=True)

            for j in range(2):
                f.dma("sp", xts[j % 3].ap, xv[:, j, :], writes=[xts[j % 3]])
            for j in range(32):
                if j + 2 < 32:
                    f.dma("sp", xts[(j + 2) % 3].ap, xv[:, j + 2, :], writes=[xts[(j + 2) % 3]])
                xt = xts[j % 3]
                s_, r_, x_, h_, uf, ut = ss[j % 2], rs[j % 2], xs[j % 2], hT[j % 2], ufT[j % 2], Ut[j % 2]
                rw = rawbuf(j)
                f.op("act", lambda e: e.activation(out=junk.ap, in_=xt.ap, func=ACTF.Square, accum_out=s_.ap),
                     reads=[xt], writes=[junk, s_])
                rstd_of(s_.ap, s_, r_.ap, r_, 1.0 / 1024)
                f.op("act", lambda e: e.activation(out=x_.ap, in_=xt.ap, func=ACTF.Copy, scale=r_.ap[:, 0:1]),
                     reads=[xt, r_], writes=[x_])
                f.mm([lambda e, k=k: e.transpose(out=pT.ap[:, k, :], in_=x_.ap[:, k * 128:(k + 1) * 128], identity=ident.ap)
                      for k in range(8)], reads=[x_, ident], writes=[pT])
                f.op("dve", lambda e: e.tensor_copy(out=h_.ap, in_=pT.ap), reads=[pT], writes=[h_])
                for n in range(3):
                    f.mm([lambda e, k=k, n=n: e.matmul(pH[n].ap, lhsT=h_.ap[:, k, :], rhs=W1.ap[:, k, 512 + n * 512:1024 + n * 512],
                                                       start=(k == 0), stop=(k == 7)) for k in range(8)],
                         reads=[h_, W1], writes=[pH[n]])
                    f.op("act", lambda e, n=n: e.activation(out=rw.ap[:, n * 512:(n + 1) * 512], in_=pH[n].ap, func=ACTF.Copy),
                         reads=[pH[n]], writes=[rw])
                fns = []
                for ec in range(4):
                    for k in range(8):
                        fns.append(lambda e, k=k, ec=ec: e.matmul(pF.ap[:, ec, :], lhsT=W1.ap[:, k, ec * 128:(ec + 1) * 128],
                                                                  rhs=h_.ap[:, k, :], start=(k == 0), stop=(k == 7)))
                f.mm(fns, reads=[h_, W1], writes=[pF])
                f.op("dve", lambda e: e.tensor_copy(out=uf.ap, in_=pF.ap), reads=[pF], writes=[uf])
                f.mm([lambda e, ec=ec: e.matmul(pU.ap[:, ec, :], lhsT=uf.ap[:, ec, :], rhs=BD.ap, start=True, stop=True)
                      for ec in range(4)], reads=[uf, BD], writes=[pU])
                f.op("dve", lambda e: e.tensor_copy(out=ut.ap, in_=pU.ap), reads=[pU], writes=[ut])
                f.dma("pool", U.ap[:, j, :], ut.ap.rearrange("p a b -> p (a b)"), reads=[ut], writes=[U], sem=ut, accumulate=True)
                if j >= 2:
                    conv(j - 1, rawbuf(j - 2), rawbuf(j - 1), rawbuf(j))
            f.dma("sp", hlo.ap[1:128, :], rded[31].ap[0:127, :], reads=[rded[31]], writes=[hlo])
            f.dma("sp", hhi.ap[0:127, :], rded[0].ap[1:128, :], reads=[rded[0]], writes=[hhi])
            conv(0, hlo, rded[0], rded[1])
            conv(31, rded[30], rded[31], hhi)

    def relayout_view(dram_ap):
        return dram_ap.rearrange("(cp q) j n -> (q j) cp n", q=4)

    def load_split(queue, dst, dst_ap, src_ap, reads, n):
        tot = dst_ap.shape[1]
        step = tot // n
        for i in range(n):
            f.dma(queue, dst_ap[:, i * step:(i + 1) * step], src_ap[:, i * step:(i + 1) * step], reads=reads, writes=[dst],
                  sem=T(None, "ls"))

    def stage1(es, src, mats, combos, dst_slots, tagp):
        nslot = len(combos)
        pAA = [[ps(es, "%spA%d_%d" % (tagp, i, b), [128, 512]) for i in range(nslot)] for b in range(2)]
        stg = [sb(es, "%sstg%d" % (tagp, i), [128, nslot, 4, 512], BF16) for i in range(2)]
        for j in range(32):
            sg = stg[(j // 4) % 2]
            pA = pAA[j % 2]
            for si, terms in enumerate(combos):
                f.mm([lambda e, mi=mi, rf=rf, ti=ti, si=si: e.matmul(pA[si].ap, lhsT=mats.ap[:, j, mi, :], rhs=rf(j),
                                                                       start=(ti == 0), stop=(ti == len(terms) - 1))
                      for ti, (mi, rf) in enumerate(terms)], reads=[src, mats], writes=[pA[si]])
                eng = "act" if (si + j) % 2 == 0 else "dve"
                if eng == "act":
                    f.op("act", lambda e, si=si: e.activation(out=sg.ap[:, si, j % 4, :], in_=pA[si].ap, func=ACTF.Copy),
                         reads=[pA[si]], writes=[sg])
                else:
                    f.op("dve", lambda e, si=si: e.tensor_copy(out=sg.ap[:, si, j % 4, :], in_=pA[si].ap), reads=[pA[si]], writes=[sg])
            if j % 4 == 3:
                for si in range(nslot):
                    f.dma("sp", FA.ap[dst_slots[si], :, j - 3:j + 1, :], sg.ap[:, si], reads=[sg], writes=[FA], sem=sg, accumulate=True)

    def phase2():
        with Scope(f) as es:
            FF = sb(es, "FF", [128, 32, 3, 128], BF16)
            f.dma("sp", FF.ap, D["FF"].ap, writes=[FF])
            Us = sb(es, "Us", [128, 32, 4, 2, 128], BF16)
            load_split("sp", Us, Us.ap.rearrange("p j a b n -> p j (a b n)"), U.ap, [U], 4)
            uc = lambda j: Us.ap[:, j, :, 0, :]
            us = lambda j: Us.ap[:, j, :, 1, :]
            with Scope(f) as es2:
                stage1(es2, Us, FF, [[(0, uc), (2, us)], [(1, uc), (0, us)]], [0, 1], "f")
        with Scope(f) as es:
            A2 = [sb(es, "fA2%d" % i, [128, 32, 512], BF16) for i in range(2)]
            for i in range(2):
                load_split("sp", A2[i], A2[i].ap, relayout_view(FA.ap[i]), [FA], 4)
            pY = [ps(es, "fpY%d" % i, [128, 512]) for i in range(2)]
            pT2 = [ps(es, "fpT%d" % i, [128, 4, 128], BF16) for i in range(2)]
            junk = sb(es, "fjunk", [128, 512], BF16)
            ssq = [sb(es, "fss%d" % i, [128, 1], F32) for i in range(2)]
            rsq = [sb(es, "frs%d" % i, [128, 1], F32) for i in range(2)]
            yn = [sb(es, "fyn%d" % i, [128, 512], BF16) for i in range(2)]
            yT = yTbox[0]
            yTv = yT.ap.rearrange("p k (d cp q) -> p k cp q d", d=32, cp=32, q=4)
            for cp in range(32):
                py, pt, s_, r_, y_ = pY[cp % 2], pT2[cp % 2], ssq[cp % 2], rsq[cp % 2], yn[cp % 2]
                f.mm([lambda e: e.matmul(py.ap, lhsT=CS.ap[:, 0, :], rhs=A2[0].ap[:, cp, :], start=True, stop=False),
                      lambda e: e.matmul(py.ap, lhsT=CS.ap[:, 2, :], rhs=A2[1].ap[:, cp, :], start=False, stop=True)],
                     reads=[CS, A2[0], A2[1]], writes=[py])
                f.op("act", lambda e: e.activation(out=junk.ap, in_=py.ap, func=ACTF.Square, accum_out=s_.ap), reads=[py], writes=[junk, s_])
                rstd_of(s_.ap, s_, r_.ap, r_, 1.0 / (512.0 * 512.0 * 512.0))
                f.op("dve", lambda e: e.tensor_scalar(out=y_.ap, in0=py.ap, scalar1=r_.ap[:, 0:1], scalar2=1.0 / 512.0, op0=ALU.mult,
                                                      op1=ALU.mult), reads=[py, r_], writes=[y_])
                f.mm([lambda e, k=k: e.transpose(out=pt.ap[:, k, :], in_=y_.ap[:, k * 128:(k + 1) * 128], identity=ident.ap)
                      for k in range(4)], reads=[y_, ident], writes=[pt])
                f.op("act", lambda e: e.activation(out=yTv[:, 0:4, cp], in_=pt.ap.rearrange("p k (q d) -> p k q d", q=4),
                                                   func=ACTF.Copy), reads=[pt], writes=[yT])

    def phase3():
        with Scope(f) as es:
            hcur = [sb(es, "fh%d" % i, [64, L], F32) for i in range(2)]
            wo = sb(es, "fwo", [64, 2, 2, 512], F32)
            wsd = sb(es, "fwsd", [64, 2, 2, 512], F32)
            HF = sb(es, "HF3", [128, 32, 2, 128], BF16)
            SEL = sb(es, "SEL", [128, 3, 128], BF16)
            dl = sb(es, "dl", [1, 2, 512], F32)
            f.dma("sp", wo.ap.rearrange("p a b n -> p (a b n)"), D["f_wo"].ap, writes=[wo])
            f.dma("sp", HF.ap, D["HF"].ap, writes=[HF])
            f.dma("sp", SEL.ap, D["SEL"].ap, writes=[SEL])
            f.dma("sp", dl.ap, D["long_d"].ap, writes=[dl])
            for o in range(2):
                f.op("dve", lambda e, o=o: e.tensor_tensor(out=wsd.ap[:, o, 0, :], in0=wo.ap[:, o, 0, :], in1=wo.ap[:, o, 1, :], op=ALU.add),
                     reads=[wo], writes=[wsd])
                f.op("dve", lambda e, o=o: e.tensor_tensor(out=wsd.ap[:, o, 1, :], in0=wo.ap[:, o, 0, :], in1=wo.ap[:, o, 1, :], op=ALU.subtract),
                     reads=[wo], writes=[wsd])
            with Scope(f) as em:
                zT = sb(em, "zT", [33, L], F32)
                w0 = sb(em, "fw0", [33, 64], F32)
                fb = sb(em, "fb", [64, 3], F32)
                wi = sb(em, "fwi", [64, 2, 64], F32)
                fr = sb(em, "ffr", [64, 1], F32)
                s1 = sb(em, "fs1", [64, 1], F32)
                s2 = sb(em, "fs2", [64, 3], F32)
                f.dma("sp", zT.ap, D["zT"].ap, writes=[zT])
                f.dma("sp", w0.ap, D["f_w0"].ap, writes=[w0])
                f.dma("sp", fb.ap, D["f_b"].ap, writes=[fb])
                f.dma("sp", wi.ap, D["f_wi"].ap, writes=[wi])
                f.dma("sp", fr.ap, D["f_fr"].ap, writes=[fr])
                inv2pi = float(1.0 / (2 * np.pi))
                f.op("dve", lambda e: e.tensor_scalar(out=s1.ap, in0=fr.ap, scalar1=inv2pi, scalar2=None, op0=ALU.mult), reads=[fr], writes=[s1])
                f.op("dve", lambda e: e.tensor_scalar(out=s2.ap, in0=fb.ap, scalar1=s1.ap[:, 0:1], scalar2=16.0, op0=ALU.mult, op1=ALU.add),
                     reads=[fb, s1], writes=[s2])
                pm = [ps(em, "fpm%d" % i, [64, 512]) for i in range(2)]
                rr = [sb(em, "frr%d" % i, [64, 512], F32) for i in range(2)]
                ki = [sb(em, "fki%d" % i, [64, 512], I32) for i in range(2)]
                kf = [sb(em, "fkf%d" % i, [64, 512], F32) for i in range(2)]
                for layer in range(3):
                    src = zT if layer == 0 else hcur[(layer - 1) % 2]
                    dst = hcur[layer % 2]
                    lw = w0.ap if layer == 0 else wi.ap[:, layer - 1, :]
                    lwt = w0 if layer == 0 else wi
                    kdim = 33 if layer == 0 else 64
                    for c8 in range(8):
                        i = c8 % 2
                        cs = slice(c8 * 512, (c8 + 1) * 512)
                        f.mm([lambda e: e.matmul(pm[i].ap, lhsT=lw, rhs=src.ap[0:kdim, cs], start=True, stop=True)], reads=[src, lwt], writes=[pm[i]])
                        f.op("dve", lambda e: e.tensor_scalar(out=rr[i].ap, in0=pm[i].ap, scalar1=s1.ap[:, 0:1], scalar2=s2.ap[:, layer:layer + 1],
                                                              op0=ALU.mult, op1=ALU.add), reads=[pm[i], s1, s2], writes=[rr[i]])
                        f.op("dve", lambda e: e.tensor_copy(out=ki[i].ap, in_=rr[i].ap), reads=[rr[i]], writes=[ki[i]])
                        f.op("dve", lambda e: e.tensor_copy(out=kf[i].ap, in_=ki[i].ap), reads=[ki[i]], writes=[kf[i]])
                        f.op("dve", lambda e: e.tensor_tensor(out=rr[i].ap, in0=rr[i].ap, in1=kf[i].ap, op=ALU.subtract), reads=[rr[i], kf[i]], writes=[rr[i]])
                        f.op("dve", lambda e: e.tensor_single_scalar(out=kf[i].ap, in_=rr[i].ap, scalar=0.5, op=ALU.is_gt), reads=[rr[i]], writes=[kf[i]])
                        f.op("dve", lambda e: e.tensor_tensor(out=rr[i].ap, in0=rr[i].ap, in1=kf[i].ap, op=ALU.subtract), reads=[rr[i], kf[i]], writes=[rr[i]])
                        f.op("act", lambda e: e.activation(out=dst.ap[:, cs], in_=rr[i].ap, func=ACTF.Sin, scale=float(2 * np.pi)),
                             reads=[rr[i]], writes=[dst])
            h3 = sb(es, "fh3b", [64, L], BF16)
            wsdb = sb(es, "fwsdb", [64, 2, 2, 512], BF16)
            f.op("dve", lambda e: e.tensor_copy(out=h3.ap[:, 0:2048], in_=hcur[0].ap[:, 0:2048]), reads=[hcur[0]], writes=[h3])
            f.op("act", lambda e: e.activation(out=h3.ap[:, 2048:4096], in_=hcur[0].ap[:, 2048:4096], func=ACTF.Copy), reads=[hcur[0]], writes=[h3])
            f.op("dve", lambda e: e.tensor_copy(out=wsdb.ap, in_=wsd.ap), reads=[wsd], writes=[wsdb])
            for o in range(2):
                with Scope(f) as eo:
                    sd = [sb(eo, "fsd%d" % i, [128, 32, 512], BF16) for i in range(2)]
                    sq = [sb(eo, "fsq%d" % i, [128, 512], BF16) for i in range(2)]
                    dec = [sb(eo, "fdec%d" % i, [128, 512], F32) for i in range(2)]
                    pk = [ps(eo, "fpk%d" % i, [128, 512]) for i in range(2)]
                    pn = ps(eo, "fpn", [128, 512])
                    sqt = sb(eo, "fsqt", [128, 512], F32)
                    tmp0 = sb(eo, "ftmp0", [1, 512], F32)
                    nsq = 0
                    for j in range(32):
                        dc = dec[j % 2]
                        f.dma("sp", dc.ap, D["decay"].ap[:, j, :], writes=[dc])
                        for v in range(2):
                            f.mm([lambda e, v=v: e.matmul(pk[v].ap, lhsT=h3.ap[:, j * 128:(j + 1) * 128], rhs=wsdb.ap[:, o, v, :], start=True, stop=True)],
                                 reads=[h3, wsdb], writes=[pk[v]])
                            f.op("dve", lambda e, v=v: e.tensor_tensor(out=sd[v].ap[:, j, :], in0=pk[v].ap, in1=dc.ap, op=ALU.mult),
                                 reads=[pk[v], dc], writes=[sd[v]])
                            q_ = sq[nsq % 2]
                            nsq += 1
                            f.op("act", lambda e, v=v: e.activation(out=q_.ap, in_=sd[v].ap[:, j, :], func=ACTF.Square), reads=[sd[v]], writes=[q_])
                            first = (j == 0 and v == 0)
                            fns = [lambda e: e.matmul(pn.ap, lhsT=SEL.ap[:, 0, :], rhs=q_.ap, start=first, stop=False)]
                            if j == 0:
                                fns.append(lambda e, v=v: e.matmul(pn.ap, lhsT=SEL.ap[:, 1 + v, :], rhs=q_.ap, start=False, stop=False))
                            if j == 31 and v == 1:
                                fns = [lambda e: e.matmul(pn.ap, lhsT=SEL.ap[:, 0, :], rhs=q_.ap, start=False, stop=True)]
                            f.mm(fns, reads=[q_, SEL], writes=[pn])
                    f.op("act", lambda e: e.activation(out=sqt.ap, in_=pn.ap, func=ACTF.Sqrt, scale=0.5, bias=epsT.ap[:, 0:1]),
                         reads=[pn, epsT], writes=[sqt])
                    f.op("dve", lambda e, o=o: e.reciprocal(out=scl.ap[:, o, :], in_=sqt.ap), reads=[sqt], writes=[scl])
                    f.op("dve", lambda e, o=o: e.tensor_tensor(out=tmp0.ap, in0=dl.ap[0:1, o, :], in1=sqt.ap[0:1, :], op=ALU.mult),
                         reads=[dl, sqt], writes=[tmp0])
                    f.op("dve", lambda e: e.tensor_tensor(out=sd[0].ap[0:1, 0, :], in0=sd[0].ap[0:1, 0, :], in1=tmp0.ap, op=ALU.add),
                         reads=[sd[0], tmp0], writes=[sd[0]])
                    f.op("dve", lambda e, o=o: e.tensor_scalar(out=scl.ap[:, o, :], in0=scl.ap[:, o, :], scalar1=2.0 / NFFT, scalar2=None, op0=ALU.mult),
                         reads=[scl], writes=[scl])
                    with Scope(f) as es2:
                        stage1(es2, sd[0], HF, [[(0, lambda j: sd[0].ap[:, j, :])], [(1, lambda j: sd[0].ap[:, j, :])]], [0, 1], "k%da" % o)
                    with Scope(f) as es2:
                        stage1(es2, sd[1], HF, [[(0, lambda j: sd[1].ap[:, j, :])], [(1, lambda j: sd[1].ap[:, j, :])]], [2, 3], "k%db" % o)
                with Scope(f) as es2:
                    A2 = [sb(es2, "kA2%d" % i, [128, 16, 512], BF16) for i in range(4)]
                    kst = [sb(es2, "kst%d" % i, [128, 2, 4, 512], BF16) for i in range(2)]
                    pkk = [ps(es2, "kpk%d" % i, [128, 512]) for i in range(2)]
                    for half in range(2):
                        for i in range(4):
                            load_split("sp", A2[i], A2[i].ap, relayout_view(FA.ap[i])[:, half * 16:(half + 1) * 16, :], [FA], 2)
                        for c16 in range(16):
                            cp = half * 16 + c16
                            st_ = kst[(cp // 4) % 2]
                            f.mm([lambda e: e.matmul(pkk[0].ap, lhsT=CS.ap[:, 0, :], rhs=A2[0].ap[:, c16, :], start=True, stop=False),
                                  lambda e: e.matmul(pkk[0].ap, lhsT=CS.ap[:, 1, :], rhs=A2[1].ap[:, c16, :], start=False, stop=True)],
                                 reads=[CS, A2[0], A2[1]], writes=[pkk[0]])
                            f.mm([lambda e: e.matmul(pkk[1].ap, lhsT=CS.ap[:, 0, :], rhs=A2[3].ap[:, c16, :], start=True, stop=False),
                                  lambda e: e.matmul(pkk[1].ap, lhsT=CS.ap[:, 2, :], rhs=A2[2].ap[:, c16, :], start=False, stop=True)],
                                 reads=[CS, A2[2], A2[3]], writes=[pkk[1]])
                            f.op("act", lambda e: e.activation(out=st_.ap[:, 0, cp % 4, :], in_=pkk[0].ap, func=ACTF.Copy), reads=[pkk[0]], writes=[st_])
                            f.op("dve", lambda e: e.tensor_copy(out=st_.ap[:, 1, cp % 4, :], in_=pkk[1].ap), reads=[pkk[1]], writes=[st_])
                            if cp % 4 == 3:
                                for r in range(2):
                                    f.dma("pool", KH.ap[o, r, :, cp - 3:cp + 1, :], st_.ap[:, r], reads=[st_], writes=[KH], sem=st_, accumulate=True)

    def phase_h(o):
        with Scope(f) as es:
            HF = sb(es, "HFh", [128, 32, 2, 128], BF16)
            z = sb(es, "hz", [128, 32, 512], BF16)
            f.dma("sp", HF.ap, D["HF"].ap, writes=[HF])
            if o == 0:
                load_split("sp", z, z.ap, XG.ap[2], [XG], 2)
            else:
                load_split("sp", z, z.ap, Z1.ap, [Z1], 2)
            stage1(es, z, HF, [[(0, lambda j: z.ap[:, j, :])], [(1, lambda j: z.ap[:, j, :])]], [0, 1], "h%d" % o)
        with Scope(f) as es:
            A2 = [sb(es, "hA2%d" % i, [128, 32, 512], BF16) for i in range(2)]
            for i in range(2):
                load_split("sp", A2[i], A2[i].ap, relayout_view(FA.ap[i]), [FA], 4)
            kk = [[sb(es, "hk%d%d" % (i, r), [128, 4, 512], BF16) for r in range(2)] for i in range(2)]
            pXX = [[ps(es, "hpX%d_%d" % (i, b), [128, 512]) for i in range(2)] for b in range(2)]
            pBB = [[ps(es, "hpB%d_%d" % (i, b), [128, 512]) for i in range(2)] for b in range(2)]
            ttt = [[sb(es, "htt%d_%d" % (i, b), [128, 512], F32) for i in range(4)] for b in range(2)]
            Y = [[sb(es, "hY%d%d" % (i, r), [128, 512], BF16) for r in range(2)] for i in range(2)]
            bst = [sb(es, "hbst%d" % i, [128, 2, 4, 512], BF16) for i in range(2)]
            FBv = [relayout_view(FB.ap[r]) for r in range(2)]
            def emit_x(cp):
                pX = pXX[cp % 2]
                f.mm([lambda e: e.matmul(pX[0].ap, lhsT=CS.ap[:, 0, :], rhs=A2[0].ap[:, cp, :], start=True, stop=False),
                      lambda e: e.matmul(pX[0].ap, lhsT=CS.ap[:, 1, :], rhs=A2[1].ap[:, cp, :], start=False, stop=True)],
                     reads=[CS, A2[0], A2[1]], writes=[pX[0]])
                f.mm([lambda e: e.matmul(pX[1].ap, lhsT=CS.ap[:, 0, :], rhs=A2[1].ap[:, cp, :], start=True, stop=False),
                      lambda e: e.matmul(pX[1].ap, lhsT=CS.ap[:, 2, :], rhs=A2[0].ap[:, cp, :], start=False, stop=True)],
                     reads=[CS, A2[0], A2[1]], writes=[pX[1]])

            emit_x(0)
            for cp in range(32):
                g = cp // 4
                pX, pB, tt = pXX[cp % 2], pBB[cp % 2], ttt[cp % 2]
                kb = kk[g % 2]
                if cp % 4 == 0:
                    for r in range(2):
                        f.dma("sp", kb[r].ap, KH.ap[o, r, :, cp:cp + 4, :], reads=[KH], writes=[kb[r]])
                kr = kb[0].ap[:, cp % 4, :]
                kim = kb[1].ap[:, cp % 4, :]
                f.op("dve", lambda e: e.tensor_tensor(out=tt[0].ap, in0=pX[0].ap, in1=kr, op=ALU.mult), reads=[pX[0], kb[0]], writes=[tt[0]])
                f.op("dve", lambda e: e.tensor_tensor(out=tt[1].ap, in0=pX[1].ap, in1=kim, op=ALU.mult), reads=[pX[1], kb[1]], writes=[tt[1]])
                yy = Y[cp % 2]
                f.op("pool", lambda e: e.tensor_tensor(out=yy[0].ap, in0=tt[0].ap, in1=tt[1].ap, op=ALU.subtract), reads=[tt[0], tt[1]], writes=[yy[0]])
                f.op("dve", lambda e: e.tensor_tensor(out=tt[2].ap, in0=pX[0].ap, in1=kim, op=ALU.mult), reads=[pX[0], kb[1]], writes=[tt[2]])
                f.op("dve", lambda e: e.tensor_tensor(out=tt[3].ap, in0=pX[1].ap, in1=kr, op=ALU.mult), reads=[pX[1], kb[0]], writes=[tt[3]])
                f.op("pool", lambda e: e.tensor_tensor(out=yy[1].ap, in0=tt[2].ap, in1=tt[3].ap, op=ALU.add), reads=[tt[2], tt[3]], writes=[yy[1]])
                if cp + 1 < 32:
                    emit_x(cp + 1)
                f.mm([lambda e: e.matmul(pB[0].ap, lhsT=CS.ap[:, 0, :], rhs=yy[0].ap, start=True, stop=False),
                      lambda e: e.matmul(pB[0].ap, lhsT=CS.ap[:, 2, :], rhs=yy[1].ap, start=False, stop=True)],
                     reads=[CS, yy[0], yy[1]], writes=[pB[0]])
                f.mm([lambda e: e.matmul(pB[1].ap, lhsT=CS.ap[:, 1, :], rhs=yy[0].ap, start=True, stop=False),
                      lambda e: e.matmul(pB[1].ap, lhsT=CS.ap[:, 0, :], rhs=yy[1].ap, start=False, stop=True)],
                     reads=[CS, yy[0], yy[1]], writes=[pB[1]])
                bs = bst[g % 2]
                f.op("act", lambda e: e.activation(out=bs.ap[:, 0, cp % 4, :], in_=pB[0].ap, func=ACTF.Copy), reads=[pB[0]], writes=[bs])
                f.op("act", lambda e: e.activation(out=bs.ap[:, 1, cp % 4, :], in_=pB[1].ap, func=ACTF.Copy), reads=[pB[1]], writes=[bs])
                if cp % 4 == 3:
                    for r in range(2):
                        f.dma("sp", FBv[r][:, cp - 3:cp + 1, :], bs.ap[:, r], reads=[bs], writes=[FB], sem=bs, accumulate=True)
        with Scope(f) as es:
            HG = sb(es, "HGh", [128, 32, 2, 128], BF16)
            Bc = [sb(es, "hBc%d" % i, [128, 16, 512], BF16) for i in range(2)]
            gt = sb(es, "hgt", [128, 32, 512], BF16)
            f.dma("sp", HG.ap, D["HG"].ap, writes=[HG])
            load_split("sp", gt, gt.ap, XG.ap[o], [XG], 2)
            for h in range(4):
                eng = "dve" if h % 2 == 0 else "pool"
                f.op(eng, lambda e, h=h: e.tensor_tensor(out=gt.ap[:, h * 8:(h + 1) * 8, :], in0=gt.ap[:, h * 8:(h + 1) * 8, :],
                                                         in1=scl.ap[:, o:o + 1, :].broadcast_to([128, 8, 512]), op=ALU.mult),
                     reads=[gt, scl], writes=[gt])
            gs = gt
            pY = [ps(es, "hpY%d" % i, [128, 512]) for i in range(2)]
            if o == 0:
                zst = [sb(es, "hzst%d" % i, [128, 4, 512], BF16) for i in range(2)]
            else:
                yf = [sb(es, "hyf%d" % i, [128, 512], F32) for i in range(2)]
                junk = sb(es, "hjunk", [128, 512], BF16)
                ssq = [sb(es, "hss%d" % i, [128, 1], F32) for i in range(2)]
                rsq = [sb(es, "hrs%d" % i, [128, 1], F32) for i in range(2)]
                yn = [sb(es, "hyn%d" % i, [128, 512], BF16) for i in range(2)]
                pT2 = [ps(es, "hpT%d" % i, [128, 4, 128], BF16) for i in range(2)]
                yT = yTbox[0]
                yTv = yT.ap.rearrange("p k (pp j) -> p k j pp", j=32)
            for j in range(32):
                if j % 16 == 0:
                    for r in range(2):
                        load_split("sp", Bc[r], Bc[r].ap, FB.ap[r][:, j:j + 16, :], [FB], 2)
                jl = j % 16
                py = pY[j % 2]
                f.mm([lambda e: e.matmul(py.ap, lhsT=HG.ap[:, j, 0, :], rhs=Bc[0].ap[:, jl, :], start=True, stop=False),
                      lambda e: e.matmul(py.ap, lhsT=HG.ap[:, j, 1, :], rhs=Bc[1].ap[:, jl, :], start=False, stop=True)],
                     reads=[HG, Bc[0], Bc[1]], writes=[py])
                if o == 0:
                    zs = zst[(j // 4) % 2]
                    f.op("dve", lambda e: e.tensor_tensor(out=zs.ap[:, j % 4, :], in0=py.ap, in1=gs.ap[:, j, :], op=ALU.mult),
                         reads=[py, gs], writes=[zs])
                    if j % 4 == 3:
                        f.dma("pool", Z1.ap[:, j - 3:j + 1, :], zs.ap, reads=[zs], writes=[Z1], sem=zs, accumulate=True)
                else:
                    y_, s_, r_, n_, pt = yf[j % 2], ssq[j % 2], rsq[j % 2], yn[j % 2], pT2[j % 2]
                    f.op("dve", lambda e: e.tensor_tensor(out=y_.ap, in0=py.ap, in1=gs.ap[:, j, :], op=ALU.mult), reads=[py, gs], writes=[y_])
                    f.op("act", lambda e: e.activation(out=junk.ap, in_=y_.ap, func=ACTF.Square, accum_out=s_.ap), reads=[y_], writes=[junk, s_])
                    rstd_of(s_.ap, s_, r_.ap, r_, 1.0 / 512.0)
                    f.op("act", lambda e: e.activation(out=n_.ap, in_=y_.ap, func=ACTF.Copy, scale=r_.ap[:, 0:1]), reads=[y_, r_], writes=[n_])
                    f.mm([lambda e, k=k: e.transpose(out=pt.ap[:, k, :], in_=n_.ap[:, k * 128:(k + 1) * 128], identity=ident.ap)
                          for k in range(4)], reads=[n_, ident], writes=[pt])
                    f.op("dve", lambda e: e.tensor_copy(out=yTv[:, 4:8, j, :], in_=pt.ap), reads=[pt], writes=[yT])

    def phase6():
        with Scope(f) as es:
            yT = yTbox[0]
            Wo = sb(es, "Wo", [128, 8, 1024], BF16)
            gfh = sb(es, "gfh", [128, 8], F32)
            gfin = sb(es, "gfin", [128, 1024], F32)
            f.dma("sp", gfh.ap, D["gfh"].ap, writes=[gfh])
            f.dma("sp", gfin.ap, D["gfin"].ap.partition_broadcast(128).rearrange("p a n -> p (a n)"), writes=[gfin])
            with Scope(f) as es2:
                wst = [sb(es2, "wost%d" % i, [128, 1024], F32) for i in range(2)]
                wv = D["w_out"].ap.rearrange("(k p) e -> p k e", p=128)
                for k in range(8):
                    s = wst[k % 2]
                    f.dma("sp", s.ap, wv[:, k, :], writes=[s])
                    f.op("dve", lambda e, k=k, s=s: e.tensor_scalar(out=Wo.ap[:, k, :], in0=s.ap, scalar1=gfh.ap[:, k:k + 1], scalar2=None,
                                                                    op0=ALU.mult), reads=[s, gfh], writes=[Wo])
            aT = sb(es, "aT", [128, 32, 512], BF16)
            hmT = [sb(es, "hmT%d" % i, [128, 8, 512], BF16) for i in range(2)]
            x1s = [sb(es, "x1s%d" % i, [128, 4, 1024], F32) for i in range(2)]
            W1s = [sb(es, "W1s%d" % i, [128, 8, 256], BF16) for i in range(2)]
            W2s = [sb(es, "W2s%d" % i, [128, 4, 512], BF16) for i in range(2)]
            rl = [sb(es, "rl%d" % i, [128, 512], BF16) for i in range(2)]
            hm = [sb(es, "hm%d" % i, [128, 1024], BF16) for i in range(2)]
            junk = sb(es, "mjunk", [128, 1024], BF16)
            ssq = [sb(es, "mss%d" % i, [128, 1], F32) for i in range(4)]
            rsq = [sb(es, "mrs%d" % i, [128, 1], F32) for i in range(4)]
            pw = [ps(es, "mpw%d" % i, [128, 512]) for i in range(2)]
            po = ps(es, "mpo", [128, 512])
            pT = ps(es, "mpT", [128, 8, 128], BF16)
            p2 = [ps(es, "mp2%d" % i, [128, 512]) for i in range(4)]
            xls = [[T(None, "xl") for _ in range(4)] for _ in range(2)]
            xss = [[T(None, "xs") for _ in range(4)] for _ in range(2)]
            xrows = D["x"].ap.rearrange("(n p) d -> n p d", p=128)
            orows = out_d.ap.rearrange("(n p) d -> n p d", p=128)
            cnt = {"n1": 0, "n2": 0}

            xq = [[T(x1s[b].ap[:, s, :], "xq%d%d" % (b, s)) for s in range(4)] for b in range(2)]

            def pro_a(tb, s):
                xb = xq[tb % 2][s]
                tt_ = tb * 4 + s
                f.dma("sp", xb.ap, xrows[tt_], writes=[xb], sem=xls[tb % 2][s])
                for dh in range(2):
                    f.mm([lambda e, k=k, dh=dh: e.matmul(po.ap, lhsT=yT.ap[:, k, tt_ * 128:(tt_ + 1) * 128],
                                                         rhs=Wo.ap[:, k, dh * 512:(dh + 1) * 512], start=(k == 0), stop=(k == 7))
                          for k in range(8)], reads=[yT, Wo], writes=[po])
                    f.op("dve", lambda e, dh=dh: e.tensor_tensor(out=xb.ap[:, dh * 512:(dh + 1) * 512], in0=po.ap,
                                                                 in1=xb.ap[:, dh * 512:(dh + 1) * 512], op=ALU.add),
                         reads=[po, xb], writes=[xb])

            def pro_b1(tb, s):
                xb = xq[tb % 2][s]
                s_, r_, h_ = ssq[s % 2], rsq[s % 2], hm[s % 2]
                f.op("act", lambda e: e.activation(out=junk.ap, in_=xb.ap, func=ACTF.Square, accum_out=s_.ap), reads=[xb], writes=[junk, s_])
                rstd_of(s_.ap, s_, r_.ap, r_, 1.0 / 1024)
                f.op("act", lambda e: e.activation(out=h_.ap, in_=xb.ap, func=ACTF.Copy, scale=r_.ap[:, 0:1]), reads=[xb, r_], writes=[h_])

            def pro_b2(tb, s):
                h_ = hm[s % 2]
                f.mm([lambda e, k=k: e.transpose(out=pT.ap[:, k, :], in_=h_.ap[:, k * 128:(k + 1) * 128], identity=ident.ap)
                      for k in range(8)], reads=[h_, ident], writes=[pT])
                f.op("dve", lambda e: e.tensor_copy(out=hmT[tb % 2].ap[:, :, s * 128:(s + 1) * 128], in_=pT.ap), reads=[pT], writes=[hmT[tb % 2]])

            def fc1_group(tb, fg):
                ws = W1s[cnt["n1"] % 2]
                cnt["n1"] += 1
                f.dma("sp", ws.ap, W1B.ap[:, :, fg * 256:(fg + 1) * 256], reads=[W1B], writes=[ws])
                for fl in range(2):
                    fc = fg * 2 + fl
                    pp = pw[fc % 2]
                    f.mm([lambda e, k=k: e.matmul(pp.ap, lhsT=ws.ap[:, k, fl * 128:(fl + 1) * 128], rhs=hmT[tb % 2].ap[:, k, :],
                                                  start=(k == 0), stop=(k == 7)) for k in range(8)], reads=[ws, hmT[tb % 2]], writes=[pp])
                    rr_ = rl[fc % 2]
                    f.op("act", lambda e: e.activation(out=rr_.ap, in_=pp.ap, func=ACTF.Relu), reads=[pp], writes=[rr_])
                    f.op("pool", lambda e: e.tensor_tensor(out=aT.ap[:, fc, :], in0=rr_.ap, in1=rr_.ap, op=ALU.mult), reads=[rr_], writes=[aT])

            def fc2_all(tb):
                for dh in range(2):
                    for fg in range(8):
                        ws = W2s[cnt["n2"] % 2]
                        cnt["n2"] += 1
                        f.dma("sp", ws.ap, W2B.ap[:, fg * 4:(fg + 1) * 4, dh * 512:(dh + 1) * 512], reads=[W2B], writes=[ws])
                        for s in range(4):
                            f.mm([lambda e, fl=fl: e.matmul(p2[s].ap, lhsT=aT.ap[:, fg * 4 + fl, s * 128:(s + 1) * 128], rhs=ws.ap[:, fl, :],
                                                            start=(fg == 0 and fl == 0), stop=(fg == 7 and fl == 3)) for fl in range(4)],
                                 reads=[aT, ws], writes=[p2[s]])
                    for s in range(4):
                        xb = xq[tb % 2][s]
                        f.op("dve", lambda e: e.tensor_tensor(out=xb.ap[:, dh * 512:(dh + 1) * 512], in0=p2[s].ap,
                                                              in1=xb.ap[:, dh * 512:(dh + 1) * 512], op=ALU.add), reads=[p2[s], xb], writes=[xb])

            def epilogue(tb, s):
                xb = xq[tb % 2][s]
                s_, r_ = ssq[2 + s % 2], rsq[2 + s % 2]
                f.op("act", lambda e: e.activation(out=junk.ap, in_=xb.ap, func=ACTF.Square, accum_out=s_.ap), reads=[xb], writes=[junk, s_])
                rstd_of(s_.ap, s_, r_.ap, r_, 1.0 / 1024)
                f.op("dve", lambda e: e.scalar_tensor_tensor(out=xb.ap, in0=xb.ap, scalar=r_.ap[:, 0:1], in1=gfin.ap, op0=ALU.mult, op1=ALU.mult),
                     reads=[xb, r_, gfin], writes=[xb])
                f.dma("pool", orows[tb * 4 + s], xb.ap, reads=[xb], writes=[out_d], sem=xss[tb % 2][s], accumulate=True)

            for s in range(4):
                pro_a(0, s)
                pro_b1(0, s)
                pro_b2(0, s)
            for tb in range(8):
                for fg in range(16):
                    fc1_group(tb, fg)
                    if tb >= 1 and fg < 4:
                        epilogue(tb - 1, fg)
                    if tb + 1 < 8 and fg >= 4:
                        q, r = divmod(fg - 4, 3)
                        if r == 0:
                            pro_a(tb + 1, q)
                        elif r == 1:
                            pro_b1(tb + 1, q)
                        else:
                            pro_b2(tb + 1, q)
                fc2_all(tb)
            for s in range(4):
                epilogue(7, s)

    def alloc_yT():
        yTbox.append(sb(top, "yT", [128, 8, L], BF16))
    phases = [("p0", phase0), ("p1", phase1), ("p3", phase3), ("yT", alloc_yT), ("p2", phase2),
              ("p4", lambda: phase_h(0)), ("p5", lambda: phase_h(1)), ("p6", phase6)]
    for i, (nm, fn) in enumerate(phases):
        if i <= upto:
            fn()
    f.wait_all("pool", [out_d, RAW, U, XG, FA, FB, KH, Z1, W1B, W2B])
    top.close()
    return nc


def make_in_maps(inputs):
    C = host_consts()
    g = {k: np.asarray(v, dtype=np.float32) for k, v in inputs.items()}

    def pk(v):
        return np.ascontiguousarray(v.reshape(8, 128).T)
    shared = {
        "gm": pk(g["g_mix"][0]), "w_in": g["w_in"][0], "cw": np.concatenate([g["conv_w"][0], g["conv_b"]], 0),
        "f_w0": g["filt_w0"][0],
        "f_b": np.ascontiguousarray(np.stack([g["filt_b0"][0], g["filt_b_inner"][0, 0], g["filt_b_inner"][0, 1]], 1)),
        "f_wi": np.ascontiguousarray(g["filt_w_inner"][0].transpose(1, 0, 2)),
        "f_fr": np.ascontiguousarray(g["filt_freq"][0][:, None]),
        "f_wo": g["filt_w_out"][0], "long_d": g["long_d"],
        "gfh": pk(np.concatenate([g["g_fourier"][0], g["g_hyena"][0]])),
        "w_out": g["w_out"][0], "gml": pk(g["g_mlp"][0]), "w_fc1": g["w_fc1"][0], "w_fc2": g["w_fc2"][0],
        "gfin": g["g_final"][None, :],
    }
    shared.update({k: C[k] for k, _, _ in CONST_SPECS})
    shared = {k: np.ascontiguousarray(v) for k, v in shared.items()}
    maps = []
    for b in range(8):
        m = dict(shared)
        m["x"] = np.ascontiguousarray(g["x"][b])
        maps.append(m)
    return maps


def kernel(**inputs):
    nc = build()
    in_maps = make_in_maps(inputs)
    res = run_bass_kernel_spmd(nc, in_maps, core_ids=list(range(8)))
    return np.stack([np.asarray(r["out"], dtype=np.float32) for r in res.results], 0)
```

```python
import numpy as np
import ml_dtypes
from contextlib import ExitStack
import concourse.bass as bass
import concourse.mybir as mybir
from concourse.bass_utils import run_bass_kernel_spmd

F32 = mybir.dt.float32
BF16 = mybir.dt.bfloat16
I32 = mybir.dt.int32
ALU = mybir.AluOpType
ACTF = mybir.ActivationFunctionType
EPOCH = 8000
L = 4096
NFFT = 8192
EPS = 1e-5
NB = ml_dtypes.bfloat16


class Res:
    __slots__ = ("name", "w", "rd")

    def __init__(self, name):
        self.name = name
        self.w = {}
        self.rd = []


class T:
    __slots__ = ("ap", "r")

    def __init__(self, ap, name):
        self.ap = ap
        self.r = Res(name)


class FW:
    def __init__(self, nc):
        self.nc = nc
        self.eng = {"pe": nc.tensor, "act": nc.scalar, "dve": nc.vector, "pool": nc.gpsimd, "sp": nc.sync}
        self.sems = {}
        self.cnt = {k: 0 for k in ("pe", "act", "dve", "pool")}
        self.waited = {k: {} for k in self.eng}
        self.nsem = 0
        self.recent = {}
        self.gen = 0
        self.free_dma = []
        self._dcnt = {}

    def _sem(self, key):
        s = self.sems.get(key)
        if s is None:
            s = self.nc.alloc_semaphore("s%d" % self.nsem)
            self.nsem += 1
            self.sems[key] = s
        return s

    def _wait(self, engine, deps):
        w = self.waited[engine]
        best = {}
        for key, val in deps:
            if engine == "pe" and key[0] == "pe":
                continue
            if key[0] == "dma" and key[2] < self.gen:
                continue
            if w.get(key, 0) < val and best.get(key, 0) < val:
                best[key] = val
        for key, val in best.items():
            self.eng[engine].wait_ge(self._sem(key), val)
            w[key] = val

    @staticmethod
    def _deps(reads, writes):
        deps = []
        for r in reads:
            deps.extend(r.w.items())
        for r in writes:
            deps.extend(r.w.items())
            deps.extend(r.rd)
        return deps

    @staticmethod
    def _mark(tok, reads, writes, accumulate=False):
        for r in writes:
            if r.w.get(tok[0], 0) < tok[1]:
                r.w[tok[0]] = tok[1]
            r.rd = []
        for r in reads:
            r.rd.append(tok)
            if len(r.rd) > 48:
                d = {}
                for k, v in r.rd:
                    if d.get(k, 0) < v:
                        d[k] = v
                r.rd = list(d.items())

    def _tick(self, engine, inst):
        c = self.cnt[engine]
        self.cnt[engine] = c + 1
        key = (engine, c // EPOCH)
        inst.then_inc(self._sem(key), 1)
        self.recent[key] = c % EPOCH + 1
        return (key, c % EPOCH + 1)

    def barrier(self):
        toks = list(self.recent.items())
        for eng in self.eng:
            self._wait(eng, toks)
        self.recent = {}
        for key in [k for k in self.sems if k[0] == "dma"]:
            self.free_dma.append((self.sems.pop(key), self._dcnt.pop(key)))
        self.gen += 1

    def op(self, engine, fn, reads=(), writes=()):
        reads = [t.r for t in reads]
        writes = [t.r for t in writes]
        self._wait(engine, self._deps(reads, writes))
        tok = self._tick(engine, fn(self.eng[engine]))
        self._mark(tok, reads, writes)

    def mm(self, fns, reads=(), writes=()):
        reads = [t.r for t in reads]
        writes = [t.r for t in writes]
        self._wait("pe", self._deps(reads, writes))
        pe = self.eng["pe"]
        for fn in fns[:-1]:
            fn(pe)
        tok = self._tick("pe", fns[-1](pe))
        self._mark(tok, reads, writes)

    def dma(self, queue, out_ap, in_ap, reads=(), writes=(), sem=None, accumulate=False):
        reads = [t.r for t in reads]
        writes_r = [t.r for t in writes]
        self._wait(queue, self._deps(reads, writes_r))
        s = sem if sem is not None else writes[0]
        key = ("dma", id(s.r), self.gen)
        cnts = self._dcnt
        if key not in self.sems and self.free_dma:
            self.sems[key], cnts[key] = self.free_dma.pop()
        cnts[key] = cnts.get(key, 0) + 16
        self.eng[queue].dma_start(out=out_ap, in_=in_ap).then_inc(self._sem(key), 16)
        tok = (key, cnts[key])
        self.recent[key] = cnts[key]
        self._mark(tok, reads, writes_r, accumulate=accumulate)

    def wait_all(self, engine, ts):
        deps = []
        for t in ts:
            deps.extend(t.r.w.items())
            deps.extend(t.r.rd)
        self._wait(engine, deps)


class Scope:
    def __init__(self, f):
        self.f = f
        self.es = ExitStack()

    def __enter__(self):
        self.es.__enter__()
        return self.es

    def __exit__(self, *a):
        self.f.barrier()
        return self.es.__exit__(*a)


_CONST = None


def host_consts():
    global _CONST
    if _CONST is not None:
        return _CONST
    c = {}
    p = np.arange(128)[:, None, None].astype(np.float64)
    j = np.arange(32)[None, :, None].astype(np.float64)
    cc = np.arange(128)[None, None, :].astype(np.float64)
    ph = 2 * np.pi * (32 * p + j) * (cc + 0.5) / NFFT
    c["HF"] = np.stack([np.cos(ph), -np.sin(ph)], 2).astype(NB)
    pht = ph.transpose(2, 1, 0)
    c["HG"] = np.stack([np.cos(pht), -np.sin(pht)], 2).astype(NB)
    pf = 2 * np.pi * (32 * p + j) * cc / L
    c["FF"] = np.stack([np.cos(pf), np.sin(pf), -np.sin(pf)], 2).astype(NB)
    jd = 2 * np.pi * np.outer(np.arange(32), np.arange(32)) / 32
    C4 = np.kron(np.eye(4), np.cos(jd))
    S4 = np.kron(np.eye(4), np.sin(jd))
    c["CS"] = np.stack([C4, S4, -S4], 1).astype(NB)
    cd = 2 * np.pi * np.outer(np.arange(64), np.arange(64)) / 64
    c["BD"] = np.concatenate([np.kron(np.eye(2), np.cos(cd)), np.kron(np.eye(2), np.sin(cd))], 1).astype(NB)
    c["ident"] = np.eye(128).astype(NB)
    sel = np.zeros((128, 3, 128))
    sel[:, 0, :] = 1.0
    sel[0, 1, :] = 1.0
    sel[0, 2, :] = -1.0
    c["SEL"] = sel.astype(NB)
    pos = np.arange(L, dtype=np.float64)
    t = pos / (L - 1)
    bands = np.linspace(1e-4, 15, 16)
    ang = (2 * np.pi * pos / L)[:, None] * bands[None]
    z = np.concatenate([t[:, None], np.cos(ang), -np.sin(ang)], -1)
    zT = z.T.reshape(33, 128, 32).transpose(0, 2, 1).reshape(33, L)
    c["zT"] = np.ascontiguousarray(zT).astype(np.float32)
    deltas = np.linspace(np.log(1e-2) / 1.5, np.log(1e-2) / 0.3, 512)
    decay = np.exp(-t[:, None] * np.abs(deltas)[None])
    c["decay"] = decay.reshape(128, 32, 512).astype(np.float32)
    _CONST = c
    return c


CONST_SPECS = [("HF", [128, 32, 2, 128], BF16), ("HG", [128, 32, 2, 128], BF16), ("FF", [128, 32, 3, 128], BF16),
               ("CS", [128, 3, 128], BF16), ("BD", [128, 256], BF16), ("ident", [128, 128], BF16),
               ("SEL", [128, 3, 128], BF16), ("zT", [33, L], F32), ("decay", [128, 32, 512], F32)]

IN_SPECS = [("x", [L, 1024], F32), ("gm", [128, 8], F32), ("w_in", [1024, 2048], F32), ("cw", [4, 1536], F32),
            ("f_w0", [33, 64], F32), ("f_b", [64, 3], F32), ("f_wi", [64, 2, 64], F32), ("f_fr", [64, 1], F32),
            ("f_wo", [64, 2048], F32), ("long_d", [1, 2, 512], F32), ("gfh", [128, 8], F32),
            ("w_out", [1024, 1024], F32), ("gml", [128, 8], F32), ("w_fc1", [1024, 4096], F32),
            ("w_fc2", [4096, 1024], F32), ("gfin", [1, 1024], F32)]


def build(upto=99, debug=False):
    nc = bass.Bass("TRN2", target_bir_lowering=False)
    f = FW(nc)
    D = {}
    for name, shape, dt in IN_SPECS + CONST_SPECS:
        D[name] = T(nc.dram_tensor(name, shape, dt, kind="ExternalInput").ap(), name)
    out_d = T(nc.dram_tensor("out", [L, 1024], F32, kind="ExternalOutput").ap(), "out")
    def scratch(name, shape, dt):
        k = "ExternalOutput" if (debug is True or (debug and name in debug)) else "Internal"
        return T(nc.dram_tensor(name, shape, dt, kind=k).ap(), name)

    RAW = scratch("RAW", [128, 34, 1536], BF16)
    U = scratch("U", [128, 32, 1024], BF16)
    XG = scratch("XG", [3, 128, 32, 512], BF16)
    FA = scratch("FA", [4, 128, 32, 512], BF16)
    FB = scratch("FB", [2, 128, 32, 512], BF16)
    KH = scratch("KH", [2, 2, 128, 32, 512], BF16)
    Z1 = scratch("Z1", [128, 32, 512], BF16)
    W1B = scratch("W1B", [128, 8, 4096], BF16)
    W2B = scratch("W2B", [128, 32, 1024], BF16)

    uid = [0]

    def sb(es, name, shape, dt):
        uid[0] += 1
        return T(es.enter_context(nc.sbuf_tensor("sb%d_%s" % (uid[0], name), shape, dt)).ap(), name)

    def ps(es, name, shape, dt=F32):
        uid[0] += 1
        return T(es.enter_context(nc.psum_tensor("ps%d_%s" % (uid[0], name), shape, dt)).ap(), name)

    top = ExitStack()
    ident = sb(top, "ident", [128, 128], BF16)
    CS = sb(top, "CS", [128, 3, 128], BF16)
    epsT = sb(top, "epsT", [128, 1], F32)
    yTbox = []
    scl = sb(top, "scl", [128, 2, 512], F32)
    f.dma("sp", ident.ap, D["ident"].ap, writes=[ident])
    f.dma("sp", CS.ap, D["CS"].ap, writes=[CS])
    f.op("pool", lambda e: e.memset(epsT.ap, EPS), writes=[epsT])

    def rstd_of(ss_ap, ss_t, out_ap, out_t, scale):
        f.op("act", lambda e: e.activation(out=out_ap, in_=ss_ap, func=ACTF.Sqrt, scale=scale, bias=epsT.ap[:, 0:1]),
             reads=[ss_t, epsT], writes=[out_t])
        f.op("dve", lambda e: e.reciprocal(out=out_ap, in_=out_ap), reads=[out_t], writes=[out_t])

    def phase0():
        with Scope(f) as es:
            gml = sb(es, "gml", [128, 8], F32)
            f.dma("sp", gml.ap, D["gml"].ap, writes=[gml])
            st = [sb(es, "p0s%d" % i, [128, 8, 512], F32) for i in range(2)]
            sbf = [sb(es, "p0b%d" % i, [128, 8, 512], BF16) for i in range(2)]
            w1v = D["w_fc1"].ap.rearrange("(k p) f -> p k f", p=128)
            for i in range(8):
                s, b = st[i % 2], sbf[i % 2]
                f.dma("sp", s.ap, w1v[:, :, i * 512:(i + 1) * 512], writes=[s])
                for k in range(8):
                    if k % 2 == 0:
                        f.op("dve", lambda e, k=k: e.tensor_scalar(out=b.ap[:, k, :], in0=s.ap[:, k, :], scalar1=gml.ap[:, k:k + 1],
                                                                   scalar2=None, op0=ALU.mult), reads=[s, gml], writes=[b])
                    else:
                        f.op("act", lambda e, k=k: e.activation(out=b.ap[:, k, :], in_=s.ap[:, k, :], func=ACTF.Copy,
                                                                scale=gml.ap[:, k:k + 1]), reads=[s, gml], writes=[b])
                f.dma("pool", W1B.ap[:, :, i * 512:(i + 1) * 512], b.ap, reads=[b], writes=[W1B], sem=b, accumulate=True)
            w2v = D["w_fc2"].ap.rearrange("(c p) d -> p c d", p=128)
            for i in range(8):
                s, b = st[i % 2], sbf[i % 2]
                s4 = s.ap.rearrange("p (a b) n -> p a (b n)", a=4)
                b4 = b.ap.rearrange("p (a b) n -> p a (b n)", a=4)
                f.dma("sp", s4, w2v[:, i * 4:(i + 1) * 4, :], writes=[s])
                f.op("dve", lambda e: e.tensor_copy(out=b4[:, 0:2], in_=s4[:, 0:2]), reads=[s], writes=[b])
                f.op("act", lambda e: e.activation(out=b4[:, 2:4], in_=s4[:, 2:4], func=ACTF.Copy), reads=[s], writes=[b])
                f.dma("pool", W2B.ap[:, i * 4:(i + 1) * 4, :], b4, reads=[b], writes=[W2B], sem=b, accumulate=True)

    def phase1():
        with Scope(f) as es:
            W1 = sb(es, "W1", [128, 8, 2048], BF16)
            gm = sb(es, "gm", [128, 8], F32)
            BD = sb(es, "BD", [128, 256], BF16)
            cw = sb(es, "cw", [128, 4, 1536], F32)
            f.dma("sp", gm.ap, D["gm"].ap, writes=[gm])
            f.dma("sp", BD.ap, D["BD"].ap, writes=[BD])
            f.dma("sp", cw.ap, D["cw"].ap.partition_broadcast(128), writes=[cw])
            with Scope(f) as es0:
                wst = [sb(es0, "wst%d" % i, [128, 2048], F32) for i in range(2)]
                wv = D["w_in"].ap.rearrange("(k p) e -> p k e", p=128)
                for k in range(8):
                    s = wst[k % 2]
                    f.dma("sp", s.ap, wv[:, k, :], writes=[s])
                    if k % 2 == 0:
                        f.op("dve", lambda e, k=k, s=s: e.tensor_scalar(out=W1.ap[:, k, :], in0=s.ap, scalar1=gm.ap[:, k:k + 1], scalar2=None,
                                                                        op0=ALU.mult), reads=[s, gm], writes=[W1])
                    else:
                        f.op("act", lambda e, k=k, s=s: e.activation(out=W1.ap[:, k, :], in_=s.ap, func=ACTF.Copy, scale=gm.ap[:, k:k + 1]),
                             reads=[s, gm], writes=[W1])
            xts = [sb(es, "xt%d" % i, [128, 1024], F32) for i in range(3)]
            junk = sb(es, "junk", [128, 1024], BF16)
            ss = [sb(es, "ss%d" % i, [128, 1], F32) for i in range(2)]
            rs = [sb(es, "rs%d" % i, [128, 1], F32) for i in range(2)]
            xs = [sb(es, "xs%d" % i, [128, 1024], BF16) for i in range(2)]
            hT = [sb(es, "hT%d" % i, [128, 8, 128], BF16) for i in range(2)]
            rring = [sb(es, "rawr%d" % i, [128, 1536], BF16) for i in range(4)]
            rded = {j: sb(es, "rawd%d" % j, [128, 1536], BF16) for j in (0, 1, 30, 31)}
            hlo = sb(es, "hlo", [128, 1536], BF16)
            hhi = sb(es, "hhi", [128, 1536], BF16)
            acc = sb(es, "cacc", [128, 1536], F32)
            t1 = sb(es, "ct1", [128, 1536], F32)
            t2 = sb(es, "ct2", [128, 1536], F32)
            t3 = sb(es, "ct3", [128, 1536], F32)
            og = [sb(es, "og%d" % i, [128, 1536], BF16) for i in range(2)]
            ufT = [sb(es, "ufT%d" % i, [128, 4, 128], BF16) for i in range(2)]
            Ut = [sb(es, "Ut%d" % i, [128, 4, 256], BF16) for i in range(2)]
            pT = ps(es, "pT", [128, 8, 128], BF16)
            pH = [ps(es, "pH%d" % i, [128, 512]) for i in range(3)]
            pF = ps(es, "pF", [128, 4, 128])
            pU = ps(es, "pU", [128, 4, 256])
            xv = D["x"].ap.rearrange("(p j) d -> p j d", j=32)
            f.op("pool", lambda e: e.memset(hlo.ap, 0.0), writes=[hlo])
            f.op("pool", lambda e: e.memset(hhi.ap, 0.0), writes=[hhi])

            def rawbuf(j):
                return rded[j] if j in rded else rring[(j - 2) % 4]
            nconv = [0]

            def conv(jo, left, center, right):
                o = og[nconv[0] % 2]
                eng = "dve"
                a_, t_ = (acc, t1) if eng == "dve" else (t2, t3)
                nconv[0] += 1
                f.op(eng, lambda e: e.tensor_tensor(out=a_.ap, in0=center.ap, in1=cw.ap[:, 1, :], op=ALU.mult), reads=[center, cw], writes=[a_])
                f.op(eng, lambda e: e.tensor_tensor(out=t_.ap, in0=left.ap, in1=cw.ap[:, 0, :], op=ALU.mult), reads=[left, cw], writes=[t_])
                f.op(eng, lambda e: e.tensor_tensor(out=a_.ap, in0=a_.ap, in1=t_.ap, op=ALU.add), reads=[a_, t_], writes=[a_])
                f.op(eng, lambda e: e.tensor_tensor(out=t_.ap, in0=right.ap, in1=cw.ap[:, 2, :], op=ALU.mult), reads=[right, cw], writes=[t_])
                f.op(eng, lambda e: e.tensor_tensor(out=a_.ap, in0=a_.ap, in1=t_.ap, op=ALU.add), reads=[a_, t_], writes=[a_])
                f.op(eng, lambda e: e.tensor_tensor(out=o.ap, in0=a_.ap, in1=cw.ap[:, 3, :], op=ALU.add), reads=[a_, cw], writes=[o])
                for t in range(3):
                    f.dma("pool", XG.ap[t, :, jo, :], o.ap[:, t * 512:(t + 1) * 512], reads=[o], writes=[XG], sem=o, accumulate=True)

            for j in range(2):
                f.dma("sp", xts[j % 3].ap, xv[:, j, :], writes=[xts[j % 3]])
            for j in range(32):
                if j + 2 < 32:
                    f.dma("sp", xts[(j + 2) % 3].ap, xv[:, j + 2, :], writes=[xts[(j + 2) % 3]])
                xt = xts[j % 3]
                s_, r_, x_, h_, uf, ut = ss[j % 2], rs[j % 2], xs[j % 2], hT[j % 2], ufT[j % 2], Ut[j % 2]
                rw = rawbuf(j)
                f.op("act", lambda e: e.activation(out=junk.ap, in_=xt.ap, func=ACTF.Square, accum_out=s_.ap),
                     reads=[xt], writes=[junk, s_])
                rstd_of(s_.ap, s_, r_.ap, r_, 1.0 / 1024)
                f.op("act", lambda e: e.activation(out=x_.ap, in_=xt.ap, func=ACTF.Copy, scale=r_.ap[:, 0:1]),
                     reads=[xt, r_], writes=[x_])
                f.mm([lambda e, k=k: e.transpose(out=pT.ap[:, k, :], in_=x_.ap[:, k * 128:(k + 1) * 128], identity=ident.ap)
                      for k in range(8)], reads=[x_, ident], writes=[pT])
                f.op("dve", lambda e: e.tensor_copy(out=h_.ap, in_=pT.ap), reads=[pT], writes=[h_])
                for n in range(3):
                    f.mm([lambda e, k=k, n=n: e.matmul(pH[n].ap, lhsT=h_.ap[:, k, :], rhs=W1.ap[:, k, 512 + n * 512:1024 + n * 512],
                                                       start=(k == 0), stop=(k == 7)) for k in range(8)],
                         reads=[h_, W1], writes=[pH[n]])
                    f.op("act", lambda e, n=n: e.activation(out=rw.ap[:, n * 512:(n + 1) * 512], in_=pH[n].ap, func=ACTF.Copy),
                         reads=[pH[n]], writes=[rw])
                fns = []
                for ec in range(4):
                    for k in range(8):
                        fns.append(lambda e, k=k, ec=ec: e.matmul(pF.ap[:, ec, :], lhsT=W1.ap[:, k, ec * 128:(ec + 1) * 128],
                                                                  rhs=h_.ap[:, k, :], start=(k == 0), stop=(k == 7)))
                f.mm(fns, reads=[h_, W1], writes=[pF])
                f.op("dve", lambda e: e.tensor_copy(out=uf.ap, in_=pF.ap), reads=[pF], writes=[uf])
                f.mm([lambda e, ec=ec: e.matmul(pU.ap[:, ec, :], lhsT=uf.ap[:, ec, :], rhs=BD.ap, start=True, stop=True)
                      for ec in range(4)], reads=[uf, BD], writes=[pU])
                f.op("dve", lambda e: e.tensor_copy(out=ut.ap, in_=pU.ap), reads=[pU], writes=[ut])
                f.dma("pool", U.ap[:, j, :], ut.ap.rearrange("p a b -> p (a b)"), reads=[ut], writes=[U], sem=ut, accumulate=True)
                if j >= 2:
                    conv(j - 1, rawbuf(j - 2), rawbuf(j - 1), rawbuf(j))
            f.dma("sp", hlo.ap[1:128, :], rded[31].ap[0:127, :], reads=[rded[31]], writes=[hlo])
            f.dma("sp", hhi.ap[0:127, :], rded[0].ap[1:128, :], reads=[rded[0]], writes=[hhi])
            conv(0, hlo, rded[0], rded[1])
            conv(31, rded[30], rded[31], hhi)

    def relayout_view(dram_ap):
        return dram_ap.rearrange("(cp q) j n -> (q j) cp n", q=4)

    def load_split(queue, dst, dst_ap, src_ap, reads, n):
        tot = dst_ap.shape[1]
        step = tot // n
        for i in range(n):
            f.dma(queue, dst_ap[:, i * step:(i + 1) * step], src_ap[:, i * step:(i + 1) * step], reads=reads, writes=[dst],
                  sem=T(None, "ls"))

    def stage1(es, src, mats, combos, dst_slots, tagp):
        nslot = len(combos)
        pAA = [[ps(es, "%spA%d_%d" % (tagp, i, b), [128, 512]) for i in range(nslot)] for b in range(2)]
        stg = [sb(es, "%sstg%d" % (tagp, i), [128, nslot, 4, 512], BF16) for i in range(2)]
        for j in range(32):
            sg = stg[(j // 4) % 2]
            pA = pAA[j % 2]
            for si, terms in enumerate(combos):
                f.mm([lambda e, mi=mi, rf=rf, ti=ti, si=si: e.matmul(pA[si].ap, lhsT=mats.ap[:, j, mi, :], rhs=rf(j),
                                                                       start=(ti == 0), stop=(ti == len(terms) - 1))
                      for ti, (mi, rf) in enumerate(terms)], reads=[src, mats], writes=[pA[si]])
                eng = "act" if (si + j) % 2 == 0 else "dve"
                if eng == "act":
                    f.op("act", lambda e, si=si: e.activation(out=sg.ap[:, si, j % 4, :], in_=pA[si].ap, func=ACTF.Copy),
                         reads=[pA[si]], writes=[sg])
                else:
                    f.op("dve", lambda e, si=si: e.tensor_copy(out=sg.ap[:, si, j % 4, :], in_=pA[si].ap), reads=[pA[si]], writes=[sg])
            if j % 4 == 3:
                for si in range(nslot):
                    f.dma("sp", FA.ap[dst_slots[si], :, j - 3:j + 1, :], sg.ap[:, si], reads=[sg], writes=[FA], sem=sg, accumulate=True)

    def phase2():
        with Scope(f) as es:
            FF = sb(es, "FF", [128, 32, 3, 128], BF16)
            f.dma("sp", FF.ap, D["FF"].ap, writes=[FF])
            Us = sb(es, "Us", [128, 32, 4, 2, 128], BF16)
            load_split("sp", Us, Us.ap.rearrange("p j a b n -> p j (a b n)"), U.ap, [U], 4)
            uc = lambda j: Us.ap[:, j, :, 0, :]
            us = lambda j: Us.ap[:, j, :, 1, :]
            with Scope(f) as es2:
                stage1(es2, Us, FF, [[(0, uc), (2, us)], [(1, uc), (0, us)]], [0, 1], "f")
        with Scope(f) as es:
            A2 = [sb(es, "fA2%d" % i, [128, 32, 512], BF16) for i in range(2)]
            for i in range(2):
                load_split("sp", A2[i], A2[i].ap, relayout_view(FA.ap[i]), [FA], 4)
            pY = [ps(es, "fpY%d" % i, [128, 512]) for i in range(2)]
            pT2 = [ps(es, "fpT%d" % i, [128, 4, 128], BF16) for i in range(2)]
            junk = sb(es, "fjunk", [128, 512], BF16)
            ssq = [sb(es, "fss%d" % i, [128, 1], F32) for i in range(2)]
            rsq = [sb(es, "frs%d" % i, [128, 1], F32) for i in range(2)]
            yn = [sb(es, "fyn%d" % i, [128, 512], BF16) for i in range(2)]
            yT = yTbox[0]
            yTv = yT.ap.rearrange("p k (d cp q) -> p k cp q d", d=32, cp=32, q=4)
            for cp in range(32):
                py, pt, s_, r_, y_ = pY[cp % 2], pT2[cp % 2], ssq[cp % 2], rsq[cp % 2], yn[cp % 2]
                f.mm([lambda e: e.matmul(py.ap, lhsT=CS.ap[:, 0, :], rhs=A2[0].ap[:, cp, :], start=True, stop=False),
                      lambda e: e.matmul(py.ap, lhsT=CS.ap[:, 2, :], rhs=A2[1].ap[:, cp, :], start=False, stop=True)],
                     reads=[CS, A2[0], A2[1]], writes=[py])
                f.op("act", lambda e: e.activation(out=junk.ap, in_=py.ap, func=ACTF.Square, accum_out=s_.ap), reads=[py], writes=[junk, s_])
                rstd_of(s_.ap, s_, r_.ap, r_, 1.0 / (512.0 * 512.0 * 512.0))
                f.op("dve", lambda e: e.tensor_scalar(out=y_.ap, in0=py.ap, scalar1=r_.ap[:, 0:1], scalar2=1.0 / 512.0, op0=ALU.mult,
                                                      op1=ALU.mult), reads=[py, r_], writes=[y_])
                f.mm([lambda e, k=k: e.transpose(out=pt.ap[:, k, :], in_=y_.ap[:, k * 128:(k + 1) * 128], identity=ident.ap)
                      for k in range(4)], reads=[y_, ident], writes=[pt])
                f.op("act", lambda e: e.activation(out=yTv[:, 0:4, cp], in_=pt.ap.rearrange("p k (q d) -> p k q d", q=4),
                                                   func=ACTF.Copy), reads=[pt], writes=[yT])

    def phase3():
        with Scope(f) as es:
            hcur = [sb(es, "fh%d" % i, [64, L], F32) for i in range(2)]
            wo = sb(es, "fwo", [64, 2, 2, 512], F32)
            wsd = sb(es, "fwsd", [64, 2, 2, 512], F32)
            HF = sb(es, "HF3", [128, 32, 2, 128], BF16)
            SEL = sb(es, "SEL", [128, 3, 128], BF16)
            dl = sb(es, "dl", [1, 2, 512], F32)
            f.dma("sp", wo.ap.rearrange("p a b n -> p (a b n)"), D["f_wo"].ap, writes=[wo])
            f.dma("sp", HF.ap, D["HF"].ap, writes=[HF])
            f.dma("sp", SEL.ap, D["SEL"].ap, writes=[SEL])
            f.dma("sp", dl.ap, D["long_d"].ap, writes=[dl])
            for o in range(2):
                f.op("dve", lambda e, o=o: e.tensor_tensor(out=wsd.ap[:, o, 0, :], in0=wo.ap[:, o, 0, :], in1=wo.ap[:, o, 1, :], op=ALU.add),
                     reads=[wo], writes=[wsd])
                f.op("dve", lambda e, o=o: e.tensor_tensor(out=wsd.ap[:, o, 1, :], in0=wo.ap[:, o, 0, :], in1=wo.ap[:, o, 1, :], op=ALU.subtract),
                     reads=[wo], writes=[wsd])
            with Scope(f) as em:
                zT = sb(em, "zT", [33, L], F32)
                w0 = sb(em, "fw0", [33, 64], F32)
                fb = sb(em, "fb", [64, 3], F32)
                wi = sb(em, "fwi", [64, 2, 64], F32)
                fr = sb(em, "ffr", [64, 1], F32)
                s1 = sb(em, "fs1", [64, 1], F32)
                s2 = sb(em, "fs2", [64, 3], F32)
                f.dma("sp", zT.ap, D["zT"].ap, writes=[zT])
                f.dma("sp", w0.ap, D["f_w0"].ap, writes=[w0])
                f.dma("sp", fb.ap, D["f_b"].ap, writes=[fb])
                f.dma("sp", wi.ap, D["f_wi"].ap, writes=[wi])
                f.dma("sp", fr.ap, D["f_fr"].ap, writes=[fr])
                inv2pi = float(1.0 / (2 * np.pi))
                f.op("dve", lambda e: e.tensor_scalar(out=s1.ap, in0=fr.ap, scalar1=inv2pi, scalar2=None, op0=ALU.mult), reads=[fr], writes=[s1])
                f.op("dve", lambda e: e.tensor_scalar(out=s2.ap, in0=fb.ap, scalar1=s1.ap[:, 0:1], scalar2=16.0, op0=ALU.mult, op1=ALU.add),
                     reads=[fb, s1], writes=[s2])
                pm = [ps(em, "fpm%d" % i, [64, 512]) for i in range(2)]
                rr = [sb(em, "frr%d" % i, [64, 512], F32) for i in range(2)]
                ki = [sb(em, "fki%d" % i, [64, 512], I32) for i in range(2)]
                kf = [sb(em, "fkf%d" % i, [64, 512], F32) for i in range(2)]
                for layer in range(3):
                    src = zT if layer == 0 else hcur[(layer - 1) % 2]
                    dst = hcur[layer % 2]
                    lw = w0.ap if layer == 0 else wi.ap[:, layer - 1, :]
                    lwt = w0 if layer == 0 else wi
                    kdim = 33 if layer == 0 else 64
                    for c8 in range(8):
                        i = c8 % 2
                        cs = slice(c8 * 512, (c8 + 1) * 512)
                        f.mm([lambda e: e.matmul(pm[i].ap, lhsT=lw, rhs=src.ap[0:kdim, cs], start=True, stop=True)], reads=[src, lwt], writes=[pm[i]])
                        f.op("dve", lambda e: e.tensor_scalar(out=rr[i].ap, in0=pm[i].ap, scalar1=s1.ap[:, 0:1], scalar2=s2.ap[:, layer:layer + 1],
                                                              op0=ALU.mult, op1=ALU.add), reads=[pm[i], s1, s2], writes=[rr[i]])
                        f.op("dve", lambda e: e.tensor_copy(out=ki[i].ap, in_=rr[i].ap), reads=[rr[i]], writes=[ki[i]])
                        f.op("dve", lambda e: e.tensor_copy(out=kf[i].ap, in_=ki[i].ap), reads=[ki[i]], writes=[kf[i]])
                        f.op("dve", lambda e: e.tensor_tensor(out=rr[i].ap, in0=rr[i].ap, in1=kf[i].ap, op=ALU.subtract), reads=[rr[i], kf[i]], writes=[rr[i]])
                        f.op("dve", lambda e: e.tensor_single_scalar(out=kf[i].ap, in_=rr[i].ap, scalar=0.5, op=ALU.is_gt), reads=[rr[i]], writes=[kf[i]])
                        f.op("dve", lambda e: e.tensor_tensor(out=rr[i].ap, in0=rr[i].ap, in1=kf[i].ap, op=ALU.subtract), reads=[rr[i], kf[i]], writes=[rr[i]])
                        f.op("act", lambda e: e.activation(out=dst.ap[:, cs], in_=rr[i].ap, func=ACTF.Sin, scale=float(2 * np.pi)),
                             reads=[rr[i]], writes=[dst])
            h3 = sb(es, "fh3b", [64, L], BF16)
            wsdb = sb(es, "fwsdb", [64, 2, 2, 512], BF16)
            f.op("dve", lambda e: e.tensor_copy(out=h3.ap[:, 0:2048], in_=hcur[0].ap[:, 0:2048]), reads=[hcur[0]], writes=[h3])
            f.op("act", lambda e: e.activation(out=h3.ap[:, 2048:4096], in_=hcur[0].ap[:, 2048:4096], func=ACTF.Copy), reads=[hcur[0]], writes=[h3])
            f.op("dve", lambda e: e.tensor_copy(out=wsdb.ap, in_=wsd.ap), reads=[wsd], writes=[wsdb])
            for o in range(2):
                with Scope(f) as eo:
                    sd = [sb(eo, "fsd%d" % i, [128, 32, 512], BF16) for i in range(2)]
                    sq = [sb(eo, "fsq%d" % i, [128, 512], BF16) for i in range(4)]
                    dec = [sb(eo, "fdec%d" % i, [128, 512], F32) for i in range(2)]
                    pk = [ps(eo, "fpk%d" % i, [128, 512]) for i in range(2)]
                    pn = ps(eo, "fpn", [128, 512])
                    sqt = sb(eo, "fsqt", [128, 512], F32)
                    tmp0 = sb(eo, "ftmp0", [1, 512], F32)
                    pend = []

                    def norm_mm(j, v, q_):
                        first = (j == 0 and v == 0)
                        last = (j == 31 and v == 1)
                        fns = [lambda e: e.matmul(pn.ap, lhsT=SEL.ap[:, 0, :], rhs=q_.ap, start=first, stop=last)]
                        if j == 0:
                            fns.append(lambda e: e.matmul(pn.ap, lhsT=SEL.ap[:, 1 + v, :], rhs=q_.ap, start=False, stop=False))
                        f.mm(fns, reads=[q_, SEL], writes=[pn])
                    for j in range(32):
                        dc = dec[j % 2]
                        f.dma("sp", dc.ap, D["decay"].ap[:, j, :], writes=[dc])
                        cur = []
                        for v in range(2):
                            f.mm([lambda e, v=v: e.matmul(pk[v].ap, lhsT=h3.ap[:, j * 128:(j + 1) * 128], rhs=wsdb.ap[:, o, v, :], start=True, stop=True)],
                                 reads=[h3, wsdb], writes=[pk[v]])
                            f.op("dve", lambda e, v=v: e.tensor_tensor(out=sd[v].ap[:, j, :], in0=pk[v].ap, in1=dc.ap, op=ALU.mult),
                                 reads=[pk[v], dc], writes=[sd[v]])
                            q_ = sq[(2 * j + v) % 4]
                            f.op("act", lambda e, v=v: e.activation(out=q_.ap, in_=sd[v].ap[:, j, :], func=ACTF.Square), reads=[sd[v]], writes=[q_])
                            cur.append((j, v, q_))
                        for a in pend:
                            norm_mm(*a)
                        pend = cur
                    for a in pend:
                        norm_mm(*a)
                    f.op("act", lambda e: e.activation(out=sqt.ap, in_=pn.ap, func=ACTF.Sqrt, scale=0.5, bias=epsT.ap[:, 0:1]),
                         reads=[pn, epsT], writes=[sqt])
                    f.op("dve", lambda e, o=o: e.reciprocal(out=scl.ap[:, o, :], in_=sqt.ap), reads=[sqt], writes=[scl])
                    f.op("dve", lambda e, o=o: e.tensor_tensor(out=tmp0.ap, in0=dl.ap[0:1, o, :], in1=sqt.ap[0:1, :], op=ALU.mult),
                         reads=[dl, sqt], writes=[tmp0])
                    f.op("dve", lambda e: e.tensor_tensor(out=sd[0].ap[0:1, 0, :], in0=sd[0].ap[0:1, 0, :], in1=tmp0.ap, op=ALU.add),
                         reads=[sd[0], tmp0], writes=[sd[0]])
                    f.op("dve", lambda e, o=o: e.tensor_scalar(out=scl.ap[:, o, :], in0=scl.ap[:, o, :], scalar1=2.0 / NFFT, scalar2=None, op0=ALU.mult),
                         reads=[scl], writes=[scl])
                    with Scope(f) as es2:
                        stage1(es2, sd[0], HF, [[(0, lambda j: sd[0].ap[:, j, :])], [(1, lambda j: sd[0].ap[:, j, :])]], [0, 1], "k%da" % o)
                    with Scope(f) as es2:
                        stage1(es2, sd[1], HF, [[(0, lambda j: sd[1].ap[:, j, :])], [(1, lambda j: sd[1].ap[:, j, :])]], [2, 3], "k%db" % o)
                with Scope(f) as es2:
                    A2q = [[sb(es2, "kA2%d_%d" % (i, b), [128, 8, 512], BF16) for i in range(4)] for b in range(2)]
                    kst = [sb(es2, "kst%d" % i, [128, 2, 4, 512], BF16) for i in range(2)]
                    pkk = [ps(es2, "kpk%d" % i, [128, 512]) for i in range(2)]

                    def load_q(q):
                        for i in range(4):
                            f.dma("sp", A2q[q % 2][i].ap, relayout_view(FA.ap[i])[:, q * 8:(q + 1) * 8, :], reads=[FA], writes=[A2q[q % 2][i]])
                    load_q(0)
                    for q in range(4):
                        if q + 1 < 4:
                            load_q(q + 1)
                        A2 = A2q[q % 2]
                        for c16 in range(8):
                            cp = q * 8 + c16
                            st_ = kst[(cp // 4) % 2]
                            f.mm([lambda e: e.matmul(pkk[0].ap, lhsT=CS.ap[:, 0, :], rhs=A2[0].ap[:, c16, :], start=True, stop=False),
                                  lambda e: e.matmul(pkk[0].ap, lhsT=CS.ap[:, 1, :], rhs=A2[1].ap[:, c16, :], start=False, stop=True)],
                                 reads=[CS, A2[0], A2[1]], writes=[pkk[0]])
                            f.mm([lambda e: e.matmul(pkk[1].ap, lhsT=CS.ap[:, 0, :], rhs=A2[3].ap[:, c16, :], start=True, stop=False),
                                  lambda e: e.matmul(pkk[1].ap, lhsT=CS.ap[:, 2, :], rhs=A2[2].ap[:, c16, :], start=False, stop=True)],
                                 reads=[CS, A2[2], A2[3]], writes=[pkk[1]])
                            f.op("act", lambda e: e.activation(out=st_.ap[:, 0, cp % 4, :], in_=pkk[0].ap, func=ACTF.Copy), reads=[pkk[0]], writes=[st_])
                            f.op("dve", lambda e: e.tensor_copy(out=st_.ap[:, 1, cp % 4, :], in_=pkk[1].ap), reads=[pkk[1]], writes=[st_])
                            if cp % 4 == 3:
                                for r in range(2):
                                    f.dma("pool", KH.ap[o, r, :, cp - 3:cp + 1, :], st_.ap[:, r], reads=[st_], writes=[KH], sem=st_, accumulate=True)

    def phase_h(o):
        with Scope(f) as es:
            HF = sb(es, "HFh", [128, 32, 2, 128], BF16)
            z = sb(es, "hz", [128, 32, 512], BF16)
            f.dma("sp", HF.ap, D["HF"].ap, writes=[HF])
            if o == 0:
                load_split("sp", z, z.ap, XG.ap[2], [XG], 2)
            else:
                load_split("sp", z, z.ap, Z1.ap, [Z1], 2)
            stage1(es, z, HF, [[(0, lambda j: z.ap[:, j, :])], [(1, lambda j: z.ap[:, j, :])]], [0, 1], "h%d" % o)
        with Scope(f) as es:
            A2 = [sb(es, "hA2%d" % i, [128, 32, 512], BF16) for i in range(2)]
            for i in range(2):
                load_split("sp", A2[i], A2[i].ap, relayout_view(FA.ap[i]), [FA], 4)
            kk = [[sb(es, "hk%d%d" % (i, r), [128, 4, 512], BF16) for r in range(2)] for i in range(2)]
            pXX = [[ps(es, "hpX%d_%d" % (i, b), [128, 512]) for i in range(2)] for b in range(2)]
            pBB = [[ps(es, "hpB%d_%d" % (i, b), [128, 512]) for i in range(2)] for b in range(2)]
            ttt = [[sb(es, "htt%d_%d" % (i, b), [128, 512], F32) for i in range(4)] for b in range(2)]
            Y = [[sb(es, "hY%d%d" % (i, r), [128, 512], BF16) for r in range(2)] for i in range(2)]
            bst = [sb(es, "hbst%d" % i, [128, 2, 4, 512], BF16) for i in range(2)]
            FBv = [relayout_view(FB.ap[r]) for r in range(2)]
            def emit_x(cp):
                pX = pXX[cp % 2]
                f.mm([lambda e: e.matmul(pX[0].ap, lhsT=CS.ap[:, 0, :], rhs=A2[0].ap[:, cp, :], start=True, stop=False),
                      lambda e: e.matmul(pX[0].ap, lhsT=CS.ap[:, 1, :], rhs=A2[1].ap[:, cp, :], start=False, stop=True)],
                     reads=[CS, A2[0], A2[1]], writes=[pX[0]])
                f.mm([lambda e: e.matmul(pX[1].ap, lhsT=CS.ap[:, 0, :], rhs=A2[1].ap[:, cp, :], start=True, stop=False),
                      lambda e: e.matmul(pX[1].ap, lhsT=CS.ap[:, 2, :], rhs=A2[0].ap[:, cp, :], start=False, stop=True)],
                     reads=[CS, A2[0], A2[1]], writes=[pX[1]])

            emit_x(0)
            for cp in range(32):
                g = cp // 4
                pX, pB, tt = pXX[cp % 2], pBB[cp % 2], ttt[cp % 2]
                kb = kk[g % 2]
                if cp % 4 == 0:
                    for r in range(2):
                        f.dma("sp", kb[r].ap, KH.ap[o, r, :, cp:cp + 4, :], reads=[KH], writes=[kb[r]])
                kr = kb[0].ap[:, cp % 4, :]
                kim = kb[1].ap[:, cp % 4, :]
                f.op("dve", lambda e: e.tensor_tensor(out=tt[0].ap, in0=pX[0].ap, in1=kr, op=ALU.mult), reads=[pX[0], kb[0]], writes=[tt[0]])
                f.op("dve", lambda e: e.tensor_tensor(out=tt[1].ap, in0=pX[1].ap, in1=kim, op=ALU.mult), reads=[pX[1], kb[1]], writes=[tt[1]])
                yy = Y[cp % 2]
                f.op("pool", lambda e: e.tensor_tensor(out=yy[0].ap, in0=tt[0].ap, in1=tt[1].ap, op=ALU.subtract), reads=[tt[0], tt[1]], writes=[yy[0]])
                f.op("dve", lambda e: e.tensor_tensor(out=tt[2].ap, in0=pX[0].ap, in1=kim, op=ALU.mult), reads=[pX[0], kb[1]], writes=[tt[2]])
                f.op("dve", lambda e: e.tensor_tensor(out=tt[3].ap, in0=pX[1].ap, in1=kr, op=ALU.mult), reads=[pX[1], kb[0]], writes=[tt[3]])
                f.op("pool", lambda e: e.tensor_tensor(out=yy[1].ap, in0=tt[2].ap, in1=tt[3].ap, op=ALU.add), reads=[tt[2], tt[3]], writes=[yy[1]])
                if cp + 1 < 32:
                    emit_x(cp + 1)
                f.mm([lambda e: e.matmul(pB[0].ap, lhsT=CS.ap[:, 0, :], rhs=yy[0].ap, start=True, stop=False),
                      lambda e: e.matmul(pB[0].ap, lhsT=CS.ap[:, 2, :], rhs=yy[1].ap, start=False, stop=True)],
                     reads=[CS, yy[0], yy[1]], writes=[pB[0]])
                f.mm([lambda e: e.matmul(pB[1].ap, lhsT=CS.ap[:, 1, :], rhs=yy[0].ap, start=True, stop=False),
                      lambda e: e.matmul(pB[1].ap, lhsT=CS.ap[:, 0, :], rhs=yy[1].ap, start=False, stop=True)],
                     reads=[CS, yy[0], yy[1]], writes=[pB[1]])
                bs = bst[g % 2]
                f.op("act", lambda e: e.activation(out=bs.ap[:, 0, cp % 4, :], in_=pB[0].ap, func=ACTF.Copy), reads=[pB[0]], writes=[bs])
                f.op("act", lambda e: e.activation(out=bs.ap[:, 1, cp % 4, :], in_=pB[1].ap, func=ACTF.Copy), reads=[pB[1]], writes=[bs])
                if cp % 4 == 3:
                    for r in range(2):
                        f.dma("sp", FBv[r][:, cp - 3:cp + 1, :], bs.ap[:, r], reads=[bs], writes=[FB], sem=bs, accumulate=True)
        with Scope(f) as es:
            HG = sb(es, "HGh", [128, 32, 2, 128], BF16)
            Bc = [sb(es, "hBc%d" % i, [128, 16, 512], BF16) for i in range(2)]
            gt = sb(es, "hgt", [128, 32, 512], BF16)
            f.dma("sp", HG.ap, D["HG"].ap, writes=[HG])
            load_split("sp", gt, gt.ap, XG.ap[o], [XG], 2)
            for h in range(4):
                eng = "dve" if h % 2 == 0 else "pool"
                f.op(eng, lambda e, h=h: e.tensor_tensor(out=gt.ap[:, h * 8:(h + 1) * 8, :], in0=gt.ap[:, h * 8:(h + 1) * 8, :],
                                                         in1=scl.ap[:, o:o + 1, :].broadcast_to([128, 8, 512]), op=ALU.mult),
                     reads=[gt, scl], writes=[gt])
            gs = gt
            pY = [ps(es, "hpY%d" % i, [128, 512]) for i in range(2)]
            if o == 0:
                zst = [sb(es, "hzst%d" % i, [128, 4, 512], BF16) for i in range(2)]
            else:
                yf = [sb(es, "hyf%d" % i, [128, 512], F32) for i in range(2)]
                junk = sb(es, "hjunk", [128, 512], BF16)
                ssq = [sb(es, "hss%d" % i, [128, 1], F32) for i in range(2)]
                rsq = [sb(es, "hrs%d" % i, [128, 1], F32) for i in range(2)]
                yn = [sb(es, "hyn%d" % i, [128, 512], BF16) for i in range(2)]
                pT2 = [ps(es, "hpT%d" % i, [128, 4, 128], BF16) for i in range(2)]
                yT = yTbox[0]
                yTv = yT.ap.rearrange("p k (pp j) -> p k j pp", j=32)
            for j in range(32):
                if j % 16 == 0:
                    for r in range(2):
                        load_split("sp", Bc[r], Bc[r].ap, FB.ap[r][:, j:j + 16, :], [FB], 2)
                jl = j % 16
                py = pY[j % 2]
                f.mm([lambda e: e.matmul(py.ap, lhsT=HG.ap[:, j, 0, :], rhs=Bc[0].ap[:, jl, :], start=True, stop=False),
                      lambda e: e.matmul(py.ap, lhsT=HG.ap[:, j, 1, :], rhs=Bc[1].ap[:, jl, :], start=False, stop=True)],
                     reads=[HG, Bc[0], Bc[1]], writes=[py])
                if o == 0:
                    zs = zst[(j // 4) % 2]
                    f.op("dve", lambda e: e.tensor_tensor(out=zs.ap[:, j % 4, :], in0=py.ap, in1=gs.ap[:, j, :], op=ALU.mult),
                         reads=[py, gs], writes=[zs])
                    if j % 4 == 3:
                        f.dma("pool", Z1.ap[:, j - 3:j + 1, :], zs.ap, reads=[zs], writes=[Z1], sem=zs, accumulate=True)
                else:
                    y_, s_, r_, n_, pt = yf[j % 2], ssq[j % 2], rsq[j % 2], yn[j % 2], pT2[j % 2]
                    f.op("dve", lambda e: e.tensor_tensor(out=y_.ap, in0=py.ap, in1=gs.ap[:, j, :], op=ALU.mult), reads=[py, gs], writes=[y_])
                    f.op("act", lambda e: e.activation(out=junk.ap, in_=y_.ap, func=ACTF.Square, accum_out=s_.ap), reads=[y_], writes=[junk, s_])
                    rstd_of(s_.ap, s_, r_.ap, r_, 1.0 / 512.0)
                    f.op("act", lambda e: e.activation(out=n_.ap, in_=y_.ap, func=ACTF.Copy, scale=r_.ap[:, 0:1]), reads=[y_, r_], writes=[n_])
                    f.mm([lambda e, k=k: e.transpose(out=pt.ap[:, k, :], in_=n_.ap[:, k * 128:(k + 1) * 128], identity=ident.ap)
                          for k in range(4)], reads=[n_, ident], writes=[pt])
                    f.op("dve", lambda e: e.tensor_copy(out=yTv[:, 4:8, j, :], in_=pt.ap), reads=[pt], writes=[yT])

    def phase6():
        with Scope(f) as es:
            yT = yTbox[0]
            Wo = sb(es, "Wo", [128, 8, 1024], BF16)
            gfh = sb(es, "gfh", [128, 8], F32)
            gfin = sb(es, "gfin", [128, 1024], F32)
            f.dma("sp", gfh.ap, D["gfh"].ap, writes=[gfh])
            f.dma("sp", gfin.ap, D["gfin"].ap.partition_broadcast(128).rearrange("p a n -> p (a n)"), writes=[gfin])
            with Scope(f) as es2:
                wst = [sb(es2, "wost%d" % i, [128, 1024], F32) for i in range(2)]
                wv = D["w_out"].ap.rearrange("(k p) e -> p k e", p=128)
                for k in range(8):
                    s = wst[k % 2]
                    f.dma("sp", s.ap, wv[:, k, :], writes=[s])
                    f.op("dve", lambda e, k=k, s=s: e.tensor_scalar(out=Wo.ap[:, k, :], in0=s.ap, scalar1=gfh.ap[:, k:k + 1], scalar2=None,
                                                                    op0=ALU.mult), reads=[s, gfh], writes=[Wo])
            aT = sb(es, "aT", [128, 32, 512], BF16)
            hmT = [sb(es, "hmT%d" % i, [128, 8, 512], BF16) for i in range(2)]
            x1s = [sb(es, "x1s%d" % i, [128, 4, 1024], F32) for i in range(2)]
            W1s = [sb(es, "W1s%d" % i, [128, 8, 256], BF16) for i in range(2)]
            W2s = [sb(es, "W2s%d" % i, [128, 4, 512], BF16) for i in range(2)]
            rl = [sb(es, "rl%d" % i, [128, 512], BF16) for i in range(2)]
            hm = [sb(es, "hm%d" % i, [128, 1024], BF16) for i in range(2)]
            junk = sb(es, "mjunk", [128, 1024], BF16)
            ssq = [sb(es, "mss%d" % i, [128, 1], F32) for i in range(4)]
            rsq = [sb(es, "mrs%d" % i, [128, 1], F32) for i in range(4)]
            pw = [ps(es, "mpw%d" % i, [128, 512]) for i in range(2)]
            po = ps(es, "mpo", [128, 512])
            pT = ps(es, "mpT", [128, 8, 128], BF16)
            p2 = [ps(es, "mp2%d" % i, [128, 512]) for i in range(4)]
            xls = [[T(None, "xl") for _ in range(4)] for _ in range(2)]
            xss = [[T(None, "xs") for _ in range(4)] for _ in range(2)]
            xrows = D["x"].ap.rearrange("(n p) d -> n p d", p=128)
            orows = out_d.ap.rearrange("(n p) d -> n p d", p=128)
            cnt = {"n1": 0, "n2": 0}

            xq = [[T(x1s[b].ap[:, s, :], "xq%d%d" % (b, s)) for s in range(4)] for b in range(2)]

            def pro_a(tb, s):
                xb = xq[tb % 2][s]
                tt_ = tb * 4 + s
                f.dma("sp", xb.ap, xrows[tt_], writes=[xb], sem=xls[tb % 2][s])
                for dh in range(2):
                    f.mm([lambda e, k=k, dh=dh: e.matmul(po.ap, lhsT=yT.ap[:, k, tt_ * 128:(tt_ + 1) * 128],
                                                         rhs=Wo.ap[:, k, dh * 512:(dh + 1) * 512], start=(k == 0), stop=(k == 7))
                          for k in range(8)], reads=[yT, Wo], writes=[po])
                    f.op("dve", lambda e, dh=dh: e.tensor_tensor(out=xb.ap[:, dh * 512:(dh + 1) * 512], in0=po.ap,
                                                                 in1=xb.ap[:, dh * 512:(dh + 1) * 512], op=ALU.add),
                         reads=[po, xb], writes=[xb])

            def pro_b1(tb, s):
                xb = xq[tb % 2][s]
                s_, r_, h_ = ssq[s % 2], rsq[s % 2], hm[s % 2]
                f.op("act", lambda e: e.activation(out=junk.ap, in_=xb.ap, func=ACTF.Square, accum_out=s_.ap), reads=[xb], writes=[junk, s_])
                rstd_of(s_.ap, s_, r_.ap, r_, 1.0 / 1024)
                f.op("act", lambda e: e.activation(out=h_.ap, in_=xb.ap, func=ACTF.Copy, scale=r_.ap[:, 0:1]), reads=[xb, r_], writes=[h_])

            def pro_b2(tb, s):
                h_ = hm[s % 2]
                f.mm([lambda e, k=k: e.transpose(out=pT.ap[:, k, :], in_=h_.ap[:, k * 128:(k + 1) * 128], identity=ident.ap)
                      for k in range(8)], reads=[h_, ident], writes=[pT])
                f.op("dve", lambda e: e.tensor_copy(out=hmT[tb % 2].ap[:, :, s * 128:(s + 1) * 128], in_=pT.ap), reads=[pT], writes=[hmT[tb % 2]])

            def fc1_group(tb, fg):
                ws = W1s[cnt["n1"] % 2]
                cnt["n1"] += 1
                f.dma("sp", ws.ap, W1B.ap[:, :, fg * 256:(fg + 1) * 256], reads=[W1B], writes=[ws])
                for fl in range(2):
                    fc = fg * 2 + fl
                    pp = pw[fc % 2]
                    f.mm([lambda e, k=k: e.matmul(pp.ap, lhsT=ws.ap[:, k, fl * 128:(fl + 1) * 128], rhs=hmT[tb % 2].ap[:, k, :],
                                                  start=(k == 0), stop=(k == 7)) for k in range(8)], reads=[ws, hmT[tb % 2]], writes=[pp])
                    rr_ = rl[fc % 2]
                    f.op("act", lambda e: e.activation(out=rr_.ap, in_=pp.ap, func=ACTF.Relu), reads=[pp], writes=[rr_])
                    f.op("pool", lambda e: e.tensor_tensor(out=aT.ap[:, fc, :], in0=rr_.ap, in1=rr_.ap, op=ALU.mult), reads=[rr_], writes=[aT])

            def fc2_all(tb):
                for dh in range(2):
                    for fg in range(8):
                        ws = W2s[cnt["n2"] % 2]
                        cnt["n2"] += 1
                        f.dma("sp", ws.ap, W2B.ap[:, fg * 4:(fg + 1) * 4, dh * 512:(dh + 1) * 512], reads=[W2B], writes=[ws])
                        for s in range(4):
                            f.mm([lambda e, fl=fl: e.matmul(p2[s].ap, lhsT=aT.ap[:, fg * 4 + fl, s * 128:(s + 1) * 128], rhs=ws.ap[:, fl, :],
                                                            start=(fg == 0 and fl == 0), stop=(fg == 7 and fl == 3)) for fl in range(4)],
                                 reads=[aT, ws], writes=[p2[s]])
                    for s in range(4):
                        xb = xq[tb % 2][s]
                        f.op("dve", lambda e: e.tensor_tensor(out=xb.ap[:, dh * 512:(dh + 1) * 512], in0=p2[s].ap,
                                                              in1=xb.ap[:, dh * 512:(dh + 1) * 512], op=ALU.add), reads=[p2[s], xb], writes=[xb])

            def epilogue(tb, s):
                xb = xq[tb % 2][s]
                s_, r_ = ssq[2 + s % 2], rsq[2 + s % 2]
                f.op("act", lambda e: e.activation(out=junk.ap, in_=xb.ap, func=ACTF.Square, accum_out=s_.ap), reads=[xb], writes=[junk, s_])
                rstd_of(s_.ap, s_, r_.ap, r_, 1.0 / 1024)
                f.op("dve", lambda e: e.scalar_tensor_tensor(out=xb.ap, in0=xb.ap, scalar=r_.ap[:, 0:1], in1=gfin.ap, op0=ALU.mult, op1=ALU.mult),
                     reads=[xb, r_, gfin], writes=[xb])
                f.dma("pool", orows[tb * 4 + s], xb.ap, reads=[xb], writes=[out_d], sem=xss[tb % 2][s], accumulate=True)

            for s in range(4):
                pro_a(0, s)
                pro_b1(0, s)
                pro_b2(0, s)
            for tb in range(8):
                for fg in range(16):
                    fc1_group(tb, fg)
                    if tb >= 1 and fg < 4:
                        epilogue(tb - 1, fg)
                    if tb + 1 < 8 and fg >= 4:
                        q, r = divmod(fg - 4, 3)
                        if r == 0:
                            pro_a(tb + 1, q)
                        elif r == 1:
                            pro_b1(tb + 1, q)
                        else:
                            pro_b2(tb + 1, q)
                fc2_all(tb)
            for s in range(4):
                epilogue(7, s)

    def alloc_yT():
        yTbox.append(sb(top, "yT", [128, 8, L], BF16))
    phases = [("p0", phase0), ("p1", phase1), ("p3", phase3), ("yT", alloc_yT), ("p2", phase2),
              ("p4", lambda: phase_h(0)), ("p5", lambda: phase_h(1)), ("p6", phase6)]
    for i, (nm, fn) in enumerate(phases):
        if i <= upto:
            fn()
    f.wait_all("pool", [out_d, RAW, U, XG, FA, FB, KH, Z1, W1B, W2B])
    top.close()
    return nc


def make_in_maps(inputs):
    C = host_consts()
    g = {k: np.asarray(v, dtype=np.float32) for k, v in inputs.items()}

    def pk(v):
        return np.ascontiguousarray(v.reshape(8, 128).T)
    shared = {
        "gm": pk(g["g_mix"][0]), "w_in": g["w_in"][0], "cw": np.concatenate([g["conv_w"][0], g["conv_b"]], 0),
        "f_w0": g["filt_w0"][0],
        "f_b": np.ascontiguousarray(np.stack([g["filt_b0"][0], g["filt_b_inner"][0, 0], g["filt_b_inner"][0, 1]], 1)),
        "f_wi": np.ascontiguousarray(g["filt_w_inner"][0].transpose(1, 0, 2)),
        "f_fr": np.ascontiguousarray(g["filt_freq"][0][:, None]),
        "f_wo": g["filt_w_out"][0], "long_d": g["long_d"],
        "gfh": pk(np.concatenate([g["g_fourier"][0], g["g_hyena"][0]])),
        "w_out": g["w_out"][0], "gml": pk(g["g_mlp"][0]), "w_fc1": g["w_fc1"][0], "w_fc2": g["w_fc2"][0],
        "gfin": g["g_final"][None, :],
    }
    shared.update({k: C[k] for k, _, _ in CONST_SPECS})
    shared = {k: np.ascontiguousarray(v) for k, v in shared.items()}
    maps = []
    for b in range(8):
        m = dict(shared)
        m["x"] = np.ascontiguousarray(g["x"][b])
        maps.append(m)
    return maps


def kernel(**inputs):
    nc = build()
    in_maps = make_in_maps(inputs)
    res = run_bass_kernel_spmd(nc, in_maps, core_ids=list(range(8)))
    return np.stack([np.asarray(r["out"], dtype=np.float32) for r in res.results], 0)
```

```python
import numpy as np
import ml_dtypes
from contextlib import ExitStack
import concourse.bass as bass
import concourse.mybir as mybir
from concourse.bass_utils import run_bass_kernel_spmd

F32 = mybir.dt.float32
BF16 = mybir.dt.bfloat16
I32 = mybir.dt.int32
ALU = mybir.AluOpType
ACTF = mybir.ActivationFunctionType
EPOCH = 8000
L = 4096
NFFT = 8192
EPS = 1e-5
NB = ml_dtypes.bfloat16


class Res:
    __slots__ = ("name", "w", "rd")

    def __init__(self, name):
        self.name = name
        self.w = {}
        self.rd = []


class T:
    __slots__ = ("ap", "r")

    def __init__(self, ap, name):
        self.ap = ap
        self.r = Res(name)


class FW:
    def __init__(self, nc):
        self.nc = nc
        self.eng = {"pe": nc.tensor, "act": nc.scalar, "dve": nc.vector, "pool": nc.gpsimd, "sp": nc.sync}
        self.sems = {}
        self.cnt = {k: 0 for k in ("pe", "act", "dve", "pool")}
        self.waited = {k: {} for k in self.eng}
        self.nsem = 0
        self.recent = {}
        self.gen = 0
        self.free_dma = []
        self._dcnt = {}

    def _sem(self, key):
        s = self.sems.get(key)
        if s is None:
            s = self.nc.alloc_semaphore("s%d" % self.nsem)
            self.nsem += 1
            self.sems[key] = s
        return s

    def _wait(self, engine, deps):
        w = self.waited[engine]
        best = {}
        for key, val in deps:
            if engine == "pe" and key[0] == "pe":
                continue
            if key[0] == "dma" and key[2] < self.gen:
                continue
            if w.get(key, 0) < val and best.get(key, 0) < val:
                best[key] = val
        for key, val in best.items():
            self.eng[engine].wait_ge(self._sem(key), val)
            w[key] = val

    @staticmethod
    def _deps(reads, writes):
        deps = []
        for r in reads:
            deps.extend(r.w.items())
        for r in writes:
            deps.extend(r.w.items())
            deps.extend(r.rd)
        return deps

    @staticmethod
    def _mark(tok, reads, writes, accumulate=False):
        for r in writes:
            if r.w.get(tok[0], 0) < tok[1]:
                r.w[tok[0]] = tok[1]
            r.rd = []
        for r in reads:
            r.rd.append(tok)
            if len(r.rd) > 48:
                d = {}
                for k, v in r.rd:
                    if d.get(k, 0) < v:
                        d[k] = v
                r.rd = list(d.items())

    def _tick(self, engine, inst):
        c = self.cnt[engine]
        self.cnt[engine] = c + 1
        key = (engine, c // EPOCH)
        inst.then_inc(self._sem(key), 1)
        self.recent[key] = c % EPOCH + 1
        return (key, c % EPOCH + 1)

    def barrier(self):
        toks = list(self.recent.items())
        for eng in self.eng:
            self._wait(eng, toks)
        self.recent = {}
        for key in [k for k in self.sems if k[0] == "dma"]:
            self.free_dma.append((self.sems.pop(key), self._dcnt.pop(key)))
        self.gen += 1

    def op(self, engine, fn, reads=(), writes=()):
        reads = [t.r for t in reads]
        writes = [t.r for t in writes]
        self._wait(engine, self._deps(reads, writes))
        tok = self._tick(engine, fn(self.eng[engine]))
        self._mark(tok, reads, writes)

    def mm(self, fns, reads=(), writes=()):
        reads = [t.r for t in reads]
        writes = [t.r for t in writes]
        self._wait("pe", self._deps(reads, writes))
        pe = self.eng["pe"]
        for fn in fns[:-1]:
            fn(pe)
        tok = self._tick("pe", fns[-1](pe))
        self._mark(tok, reads, writes)

    def dma(self, queue, out_ap, in_ap, reads=(), writes=(), sem=None, accumulate=False):
        reads = [t.r for t in reads]
        writes_r = [t.r for t in writes]
        self._wait(queue, self._deps(reads, writes_r))
        s = sem if sem is not None else writes[0]
        key = ("dma", id(s.r), self.gen)
        cnts = self._dcnt
        if key not in self.sems and self.free_dma:
            self.sems[key], cnts[key] = self.free_dma.pop()
        cnts[key] = cnts.get(key, 0) + 16
        self.eng[queue].dma_start(out=out_ap, in_=in_ap).then_inc(self._sem(key), 16)
        tok = (key, cnts[key])
        self.recent[key] = cnts[key]
        self._mark(tok, reads, writes_r, accumulate=accumulate)

    def wait_all(self, engine, ts):
        deps = []
        for t in ts:
            deps.extend(t.r.w.items())
            deps.extend(t.r.rd)
        self._wait(engine, deps)


class Scope:
    def __init__(self, f):
        self.f = f
        self.es = ExitStack()

    def __enter__(self):
        self.es.__enter__()
        return self.es

    def __exit__(self, *a):
        self.f.barrier()
        return self.es.__exit__(*a)


_CONST = None


def host_consts():
    global _CONST
    if _CONST is not None:
        return _CONST
    c = {}
    p = np.arange(128)[:, None, None].astype(np.float64)
    j = np.arange(32)[None, :, None].astype(np.float64)
    cc = np.arange(128)[None, None, :].astype(np.float64)
    ph = 2 * np.pi * (32 * p + j) * (cc + 0.5) / NFFT
    c["HF"] = np.stack([np.cos(ph), -np.sin(ph)], 2).astype(NB)
    pht = ph.transpose(2, 1, 0)
    c["HG"] = np.stack([np.cos(pht), -np.sin(pht)], 2).astype(NB)
    pf = 2 * np.pi * (32 * p + j) * cc / L
    c["FF"] = np.stack([np.cos(pf), np.sin(pf), -np.sin(pf)], 2).astype(NB)
    jd = 2 * np.pi * np.outer(np.arange(32), np.arange(32)) / 32
    C4 = np.kron(np.eye(4), np.cos(jd))
    S4 = np.kron(np.eye(4), np.sin(jd))
    c["CS"] = np.stack([C4, S4, -S4], 1).astype(NB)
    cd = 2 * np.pi * np.outer(np.arange(64), np.arange(64)) / 64
    c["BD"] = np.concatenate([np.kron(np.eye(2), np.cos(cd)), np.kron(np.eye(2), np.sin(cd))], 1).astype(NB)
    c["ident"] = np.eye(128).astype(NB)
    sel = np.zeros((128, 3, 128))
    sel[:, 0, :] = 1.0
    sel[0, 1, :] = 1.0
    sel[0, 2, :] = -1.0
    c["SEL"] = sel.astype(NB)
    pos = np.arange(L, dtype=np.float64)
    t = pos / (L - 1)
    bands = np.linspace(1e-4, 15, 16)
    ang = (2 * np.pi * pos / L)[:, None] * bands[None]
    z = np.concatenate([t[:, None], np.cos(ang), -np.sin(ang)], -1)
    zT = z.T.reshape(33, 128, 32).transpose(0, 2, 1).reshape(33, L)
    c["zT"] = np.ascontiguousarray(zT).astype(np.float32)
    deltas = np.linspace(np.log(1e-2) / 1.5, np.log(1e-2) / 0.3, 512)
    decay = np.exp(-t[:, None] * np.abs(deltas)[None])
    c["decay"] = decay.reshape(128, 32, 512).astype(np.float32)
    _CONST = c
    return c


CONST_SPECS = [("HF", [128, 32, 2, 128], BF16), ("HG", [128, 32, 2, 128], BF16), ("FF", [128, 32, 3, 128], BF16),
               ("CS", [128, 3, 128], BF16), ("BD", [128, 256], BF16), ("ident", [128, 128], BF16),
               ("SEL", [128, 3, 128], BF16), ("zT", [33, L], F32), ("decay", [128, 32, 512], F32)]

IN_SPECS = [("x", [L, 1024], F32), ("gm", [128, 8], F32), ("w_in", [1024, 2048], F32), ("cw", [4, 1536], F32),
            ("f_w0", [33, 64], F32), ("f_b", [64, 3], F32), ("f_wi", [64, 2, 64], F32), ("f_fr", [64, 1], F32),
            ("f_wo", [64, 2048], F32), ("long_d", [1, 2, 512], F32), ("gfh", [128, 8], F32),
            ("w_out", [1024, 1024], F32), ("gml", [128, 8], F32), ("w_fc1", [1024, 4096], F32),
            ("w_fc2", [4096, 1024], F32), ("gfin", [1, 1024], F32)]


def build(upto=99, debug=False):
    nc = bass.Bass("TRN2", target_bir_lowering=False)
    f = FW(nc)
    D = {}
    for name, shape, dt in IN_SPECS + CONST_SPECS:
        D[name] = T(nc.dram_tensor(name, shape, dt, kind="ExternalInput").ap(), name)
    out_d = T(nc.dram_tensor("out", [L, 1024], F32, kind="ExternalOutput").ap(), "out")
    def scratch(name, shape, dt):
        k = "ExternalOutput" if (debug is True or (debug and name in debug)) else "Internal"
        return T(nc.dram_tensor(name, shape, dt, kind=k).ap(), name)

    RAW = scratch("RAW", [128, 34, 1536], BF16)
    U = scratch("U", [128, 32, 1024], BF16)
    XG = scratch("XG", [3, 128, 32, 512], BF16)
    FA = scratch("FA", [4, 128, 32, 512], BF16)
    FB = scratch("FB", [2, 128, 32, 512], BF16)
    KH = scratch("KH", [2, 2, 128, 32, 512], BF16)
    Z1 = scratch("Z1", [128, 32, 512], BF16)
    W1B = scratch("W1B", [128, 8, 4096], BF16)
    W2B = scratch("W2B", [128, 32, 1024], BF16)

    uid = [0]

    def sb(es, name, shape, dt):
        uid[0] += 1
        return T(es.enter_context(nc.sbuf_tensor("sb%d_%s" % (uid[0], name), shape, dt)).ap(), name)

    def ps(es, name, shape, dt=F32):
        uid[0] += 1
        return T(es.enter_context(nc.psum_tensor("ps%d_%s" % (uid[0], name), shape, dt)).ap(), name)

    top = ExitStack()
    ident = sb(top, "ident", [128, 128], BF16)
    CS = sb(top, "CS", [128, 3, 128], BF16)
    epsT = sb(top, "epsT", [128, 1], F32)
    yTbox = []
    scl = sb(top, "scl", [128, 2, 512], F32)
    f.dma("sp", ident.ap, D["ident"].ap, writes=[ident])
    f.dma("sp", CS.ap, D["CS"].ap, writes=[CS])
    f.op("pool", lambda e: e.memset(epsT.ap, EPS), writes=[epsT])

    def rstd_of(ss_ap, ss_t, out_ap, out_t, scale):
        f.op("act", lambda e: e.activation(out=out_ap, in_=ss_ap, func=ACTF.Sqrt, scale=scale, bias=epsT.ap[:, 0:1]),
             reads=[ss_t, epsT], writes=[out_t])
        f.op("dve", lambda e: e.reciprocal(out=out_ap, in_=out_ap), reads=[out_t], writes=[out_t])

    def phase0():
        with Scope(f) as es:
            gml = sb(es, "gml", [128, 8], F32)
            f.dma("sp", gml.ap, D["gml"].ap, writes=[gml])
            st = [sb(es, "p0s%d" % i, [128, 8, 512], F32) for i in range(2)]
            sbf = [sb(es, "p0b%d" % i, [128, 8, 512], BF16) for i in range(2)]
            w1v = D["w_fc1"].ap.rearrange("(k p) f -> p k f", p=128)
            for i in range(8):
                s, b = st[i % 2], sbf[i % 2]
                f.dma("sp", s.ap, w1v[:, :, i * 512:(i + 1) * 512], writes=[s])
                for k in range(8):
                    if k % 2 == 0:
                        f.op("dve", lambda e, k=k: e.tensor_scalar(out=b.ap[:, k, :], in0=s.ap[:, k, :], scalar1=gml.ap[:, k:k + 1],
                                                                   scalar2=None, op0=ALU.mult), reads=[s, gml], writes=[b])
                    else:
                        f.op("act", lambda e, k=k: e.activation(out=b.ap[:, k, :], in_=s.ap[:, k, :], func=ACTF.Copy,
                                                                scale=gml.ap[:, k:k + 1]), reads=[s, gml], writes=[b])
                f.dma("pool", W1B.ap[:, :, i * 512:(i + 1) * 512], b.ap, reads=[b], writes=[W1B], sem=b, accumulate=True)
            w2v = D["w_fc2"].ap.rearrange("(c p) d -> p c d", p=128)
            for i in range(8):
                s, b = st[i % 2], sbf[i % 2]
                s4 = s.ap.rearrange("p (a b) n -> p a (b n)", a=4)
                b4 = b.ap.rearrange("p (a b) n -> p a (b n)", a=4)
                f.dma("sp", s4, w2v[:, i * 4:(i + 1) * 4, :], writes=[s])
                f.op("dve", lambda e: e.tensor_copy(out=b4[:, 0:2], in_=s4[:, 0:2]), reads=[s], writes=[b])
                f.op("act", lambda e: e.activation(out=b4[:, 2:4], in_=s4[:, 2:4], func=ACTF.Copy), reads=[s], writes=[b])
                f.dma("pool", W2B.ap[:, i * 4:(i + 1) * 4, :], b4, reads=[b], writes=[W2B], sem=b, accumulate=True)

    def phase1():
        with Scope(f) as es:
            W1 = sb(es, "W1", [128, 8, 2048], BF16)
            gm = sb(es, "gm", [128, 8], F32)
            BD = sb(es, "BD", [128, 256], BF16)
            cw = sb(es, "cw", [128, 4, 1536], F32)
            f.dma("sp", gm.ap, D["gm"].ap, writes=[gm])
            f.dma("sp", BD.ap, D["BD"].ap, writes=[BD])
            f.dma("sp", cw.ap, D["cw"].ap.partition_broadcast(128), writes=[cw])
            with Scope(f) as es0:
                wst = [sb(es0, "wst%d" % i, [128, 2048], F32) for i in range(2)]
                wv = D["w_in"].ap.rearrange("(k p) e -> p k e", p=128)
                for k in range(8):
                    s = wst[k % 2]
                    f.dma("sp", s.ap, wv[:, k, :], writes=[s])
                    if k % 2 == 0:
                        f.op("dve", lambda e, k=k, s=s: e.tensor_scalar(out=W1.ap[:, k, :], in0=s.ap, scalar1=gm.ap[:, k:k + 1], scalar2=None,
                                                                        op0=ALU.mult), reads=[s, gm], writes=[W1])
                    else:
                        f.op("act", lambda e, k=k, s=s: e.activation(out=W1.ap[:, k, :], in_=s.ap, func=ACTF.Copy, scale=gm.ap[:, k:k + 1]),
                             reads=[s, gm], writes=[W1])
            xts = [sb(es, "xt%d" % i, [128, 1024], F32) for i in range(3)]
            junk = sb(es, "junk", [128, 1024], BF16)
            ss = [sb(es, "ss%d" % i, [128, 1], F32) for i in range(2)]
            rs = [sb(es, "rs%d" % i, [128, 1], F32) for i in range(2)]
            xs = [sb(es, "xs%d" % i, [128, 1024], BF16) for i in range(2)]
            hT = [sb(es, "hT%d" % i, [128, 8, 128], BF16) for i in range(2)]
            rring = [sb(es, "rawr%d" % i, [128, 1536], BF16) for i in range(4)]
            rded = {j: sb(es, "rawd%d" % j, [128, 1536], BF16) for j in (0, 1, 30, 31)}
            hlo = sb(es, "hlo", [128, 1536], BF16)
            hhi = sb(es, "hhi", [128, 1536], BF16)
            acc = sb(es, "cacc", [128, 1536], F32)
            t1 = sb(es, "ct1", [128, 1536], F32)
            t2 = sb(es, "ct2", [128, 1536], F32)
            t3 = sb(es, "ct3", [128, 1536], F32)
            og = [sb(es, "og%d" % i, [128, 1536], BF16) for i in range(2)]
            ufT = [sb(es, "ufT%d" % i, [128, 4, 128], BF16) for i in range(2)]
            Ut = [sb(es, "Ut%d" % i, [128, 4, 256], BF16) for i in range(2)]
            pT = ps(es, "pT", [128, 8, 128], BF16)
            pH = [ps(es, "pH%d" % i, [128, 512]) for i in range(3)]
            pF = ps(es, "pF", [128, 4, 128])
            pU = ps(es, "pU", [128, 4, 256])
            xv = D["x"].ap.rearrange("(p j) d -> p j d", j=32)
            f.op("pool", lambda e: e.memset(hlo.ap, 0.0), writes=[hlo])
            f.op("pool", lambda e: e.memset(hhi.ap, 0.0), writes=[hhi])

            gml = sb(es, "gml", [128, 8], F32)
            f.dma("sp", gml.ap, D["gml"].ap, writes=[gml])
            pst = [sb(es, "p0s%d" % i, [128, 8, 256], F32) for i in range(2)]
            pbf = [sb(es, "p0b%d" % i, [128, 8, 256], BF16) for i in range(2)]
            w1v = D["w_fc1"].ap.rearrange("(k p) f -> p k f", p=128)
            w2v = D["w_fc2"].ap.rearrange("(c p) d -> p c d", p=128)

            def prep_load(i):
                st_ = pst[i % 2]
                if i < 16:
                    f.dma("sp", st_.ap, w1v[:, :, i * 256:(i + 1) * 256], writes=[st_])
                else:
                    c0 = (i - 16) * 2
                    f.dma("sp", st_.ap.rearrange("p (a b) n -> p a (b n)", a=2), w2v[:, c0:c0 + 2, :], writes=[st_])

            def prep_cast(i):
                st_, b_ = pst[i % 2], pbf[i % 2]
                if i < 16:
                    for k in range(8):
                        f.op("act", lambda e, k=k: e.activation(out=b_.ap[:, k, :], in_=st_.ap[:, k, :], func=ACTF.Copy,
                                                                scale=gml.ap[:, k:k + 1]), reads=[st_, gml], writes=[b_])
                    f.dma("pool", W1B.ap[:, :, i * 256:(i + 1) * 256], b_.ap, reads=[b_], writes=[W1B], sem=b_, accumulate=True)
                else:
                    c0 = (i - 16) * 2
                    f.op("act", lambda e: e.activation(out=b_.ap, in_=st_.ap, func=ACTF.Copy), reads=[st_], writes=[b_])
                    f.dma("pool", W2B.ap[:, c0:c0 + 2, :], b_.ap.rearrange("p (a b) n -> p a (b n)", a=2), reads=[b_], writes=[W2B],
                          sem=b_, accumulate=True)

            def rawbuf(j):
                return rded[j] if j in rded else rring[(j - 2) % 4]
            nconv = [0]

            def conv(jo, left, center, right):
                o = og[nconv[0] % 2]
                eng = "dve"
                a_, t_ = (acc, t1) if eng == "dve" else (t2, t3)
                nconv[0] += 1
                f.op(eng, lambda e: e.tensor_tensor(out=a_.ap, in0=center.ap, in1=cw.ap[:, 1, :], op=ALU.mult), reads=[center, cw], writes=[a_])
                f.op(eng, lambda e: e.tensor_tensor(out=t_.ap, in0=left.ap, in1=cw.ap[:, 0, :], op=ALU.mult), reads=[left, cw], writes=[t_])
                f.op(eng, lambda e: e.tensor_tensor(out=a_.ap, in0=a_.ap, in1=t_.ap, op=ALU.add), reads=[a_, t_], writes=[a_])
                f.op(eng, lambda e: e.tensor_tensor(out=t_.ap, in0=right.ap, in1=cw.ap[:, 2, :], op=ALU.mult), reads=[right, cw], writes=[t_])
                f.op(eng, lambda e: e.tensor_tensor(out=a_.ap, in0=a_.ap, in1=t_.ap, op=ALU.add), reads=[a_, t_], writes=[a_])
                f.op(eng, lambda e: e.tensor_tensor(out=o.ap, in0=a_.ap, in1=cw.ap[:, 3, :], op=ALU.add), reads=[a_, cw], writes=[o])
                for t in range(3):
                    f.dma("pool", XG.ap[t, :, jo, :], o.ap[:, t * 512:(t + 1) * 512], reads=[o], writes=[XG], sem=o, accumulate=True)

            for j in range(2):
                f.dma("sp", xts[j % 3].ap, xv[:, j, :], writes=[xts[j % 3]])
            prep_load(0)
            for j in range(32):
                if j + 2 < 32:
                    f.dma("sp", xts[(j + 2) % 3].ap, xv[:, j + 2, :], writes=[xts[(j + 2) % 3]])
                if j + 1 < 32:
                    prep_load(j + 1)
                xt = xts[j % 3]
                s_, r_, x_, h_, uf, ut = ss[j % 2], rs[j % 2], xs[j % 2], hT[j % 2], ufT[j % 2], Ut[j % 2]
                rw = rawbuf(j)
                f.op("act", lambda e: e.activation(out=junk.ap, in_=xt.ap, func=ACTF.Square, accum_out=s_.ap),
                     reads=[xt], writes=[junk, s_])
                rstd_of(s_.ap, s_, r_.ap, r_, 1.0 / 1024)
                f.op("act", lambda e: e.activation(out=x_.ap, in_=xt.ap, func=ACTF.Copy, scale=r_.ap[:, 0:1]),
                     reads=[xt, r_], writes=[x_])
                f.mm([lambda e, k=k: e.transpose(out=pT.ap[:, k, :], in_=x_.ap[:, k * 128:(k + 1) * 128], identity=ident.ap)
                      for k in range(8)], reads=[x_, ident], writes=[pT])
                f.op("dve", lambda e: e.tensor_copy(out=h_.ap, in_=pT.ap), reads=[pT], writes=[h_])
                for n in range(3):
                    f.mm([lambda e, k=k, n=n: e.matmul(pH[n].ap, lhsT=h_.ap[:, k, :], rhs=W1.ap[:, k, 512 + n * 512:1024 + n * 512],
                                                       start=(k == 0), stop=(k == 7)) for k in range(8)],
                         reads=[h_, W1], writes=[pH[n]])
                    f.op("act", lambda e, n=n: e.activation(out=rw.ap[:, n * 512:(n + 1) * 512], in_=pH[n].ap, func=ACTF.Copy),
                         reads=[pH[n]], writes=[rw])
                fns = []
                for ec in range(4):
                    for k in range(8):
                        fns.append(lambda e, k=k, ec=ec: e.matmul(pF.ap[:, ec, :], lhsT=W1.ap[:, k, ec * 128:(ec + 1) * 128],
                                                                  rhs=h_.ap[:, k, :], start=(k == 0), stop=(k == 7)))
                f.mm(fns, reads=[h_, W1], writes=[pF])
                f.op("dve", lambda e: e.tensor_copy(out=uf.ap, in_=pF.ap), reads=[pF], writes=[uf])
                f.mm([lambda e, ec=ec: e.matmul(pU.ap[:, ec, :], lhsT=uf.ap[:, ec, :], rhs=BD.ap, start=True, stop=True)
                      for ec in range(4)], reads=[uf, BD], writes=[pU])
                f.op("dve", lambda e: e.tensor_copy(out=ut.ap, in_=pU.ap), reads=[pU], writes=[ut])
                f.dma("pool", U.ap[:, j, :], ut.ap.rearrange("p a b -> p (a b)"), reads=[ut], writes=[U], sem=ut, accumulate=True)
                prep_cast(j)
                if j >= 2:
                    conv(j - 1, rawbuf(j - 2), rawbuf(j - 1), rawbuf(j))
            f.dma("sp", hlo.ap[1:128, :], rded[31].ap[0:127, :], reads=[rded[31]], writes=[hlo])
            f.dma("sp", hhi.ap[0:127, :], rded[0].ap[1:128, :], reads=[rded[0]], writes=[hhi])
            conv(0, hlo, rded[0], rded[1])
            conv(31, rded[30], rded[31], hhi)

    def relayout_view(dram_ap):
        return dram_ap.rearrange("(cp q) j n -> (q j) cp n", q=4)

    def load_split(queue, dst, dst_ap, src_ap, reads, n):
        tot = dst_ap.shape[1]
        step = tot // n
        for i in range(n):
            f.dma(queue, dst_ap[:, i * step:(i + 1) * step], src_ap[:, i * step:(i + 1) * step], reads=reads, writes=[dst],
                  sem=T(None, "ls"))

    def stage1(es, src, mats, combos, dst_slots, tagp):
        nslot = len(combos)
        pAA = [[ps(es, "%spA%d_%d" % (tagp, i, b), [128, 512]) for i in range(nslot)] for b in range(2)]
        stg = [sb(es, "%sstg%d" % (tagp, i), [128, nslot, 4, 512], BF16) for i in range(2)]
        for j in range(32):
            sg = stg[(j // 4) % 2]
            pA = pAA[j % 2]
            for si, terms in enumerate(combos):
                f.mm([lambda e, mi=mi, rf=rf, ti=ti, si=si: e.matmul(pA[si].ap, lhsT=mats.ap[:, j, mi, :], rhs=rf(j),
                                                                       start=(ti == 0), stop=(ti == len(terms) - 1))
                      for ti, (mi, rf) in enumerate(terms)], reads=[src, mats], writes=[pA[si]])
                eng = "act" if (si + j) % 2 == 0 else "dve"
                if eng == "act":
                    f.op("act", lambda e, si=si: e.activation(out=sg.ap[:, si, j % 4, :], in_=pA[si].ap, func=ACTF.Copy),
                         reads=[pA[si]], writes=[sg])
                else:
                    f.op("dve", lambda e, si=si: e.tensor_copy(out=sg.ap[:, si, j % 4, :], in_=pA[si].ap), reads=[pA[si]], writes=[sg])
            if j % 4 == 3:
                for si in range(nslot):
                    f.dma("sp", FA.ap[dst_slots[si], :, j - 3:j + 1, :], sg.ap[:, si], reads=[sg], writes=[FA], sem=sg, accumulate=True)

    def phase2():
        with Scope(f) as es:
            FF = sb(es, "FF", [128, 32, 3, 128], BF16)
            f.dma("sp", FF.ap, D["FF"].ap, writes=[FF])
            Us = sb(es, "Us", [128, 32, 4, 2, 128], BF16)
            load_split("sp", Us, Us.ap.rearrange("p j a b n -> p j (a b n)"), U.ap, [U], 4)
            uc = lambda j: Us.ap[:, j, :, 0, :]
            us = lambda j: Us.ap[:, j, :, 1, :]
            with Scope(f) as es2:
                stage1(es2, Us, FF, [[(0, uc), (2, us)], [(1, uc), (0, us)]], [0, 1], "f")
        with Scope(f) as es:
            A2 = [sb(es, "fA2%d" % i, [128, 32, 512], BF16) for i in range(2)]
            for i in range(2):
                load_split("sp", A2[i], A2[i].ap, relayout_view(FA.ap[i]), [FA], 4)
            pY = [ps(es, "fpY%d" % i, [128, 512]) for i in range(2)]
            pT2 = [ps(es, "fpT%d" % i, [128, 4, 128], BF16) for i in range(2)]
            junk = sb(es, "fjunk", [128, 512], BF16)
            ssq = [sb(es, "fss%d" % i, [128, 1], F32) for i in range(2)]
            rsq = [sb(es, "frs%d" % i, [128, 1], F32) for i in range(2)]
            yn = [sb(es, "fyn%d" % i, [128, 512], BF16) for i in range(2)]
            yT = yTbox[0]
            yTv = yT.ap.rearrange("p k (d cp q) -> p k cp q d", d=32, cp=32, q=4)
            for cp in range(32):
                py, pt, s_, r_, y_ = pY[cp % 2], pT2[cp % 2], ssq[cp % 2], rsq[cp % 2], yn[cp % 2]
                f.mm([lambda e: e.matmul(py.ap, lhsT=CS.ap[:, 0, :], rhs=A2[0].ap[:, cp, :], start=True, stop=False),
                      lambda e: e.matmul(py.ap, lhsT=CS.ap[:, 2, :], rhs=A2[1].ap[:, cp, :], start=False, stop=True)],
                     reads=[CS, A2[0], A2[1]], writes=[py])
                f.op("act", lambda e: e.activation(out=junk.ap, in_=py.ap, func=ACTF.Square, accum_out=s_.ap), reads=[py], writes=[junk, s_])
                rstd_of(s_.ap, s_, r_.ap, r_, 1.0 / (512.0 * 512.0 * 512.0))
                f.op("dve", lambda e: e.tensor_scalar(out=y_.ap, in0=py.ap, scalar1=r_.ap[:, 0:1], scalar2=1.0 / 512.0, op0=ALU.mult,
                                                      op1=ALU.mult), reads=[py, r_], writes=[y_])
                f.mm([lambda e, k=k: e.transpose(out=pt.ap[:, k, :], in_=y_.ap[:, k * 128:(k + 1) * 128], identity=ident.ap)
                      for k in range(4)], reads=[y_, ident], writes=[pt])
                f.op("act", lambda e: e.activation(out=yTv[:, 0:4, cp], in_=pt.ap.rearrange("p k (q d) -> p k q d", q=4),
                                                   func=ACTF.Copy), reads=[pt], writes=[yT])

    def phase3():
        with Scope(f) as es:
            hcur = [sb(es, "fh%d" % i, [64, L], F32) for i in range(2)]
            wo = sb(es, "fwo", [64, 2, 2, 512], F32)
            wsd = sb(es, "fwsd", [64, 2, 2, 512], F32)
            HF = sb(es, "HF3", [128, 32, 2, 128], BF16)
            SEL = sb(es, "SEL", [128, 3, 128], BF16)
            dl = sb(es, "dl", [1, 2, 512], F32)
            f.dma("sp", wo.ap.rearrange("p a b n -> p (a b n)"), D["f_wo"].ap, writes=[wo])
            f.dma("sp", HF.ap, D["HF"].ap, writes=[HF])
            f.dma("sp", SEL.ap, D["SEL"].ap, writes=[SEL])
            f.dma("sp", dl.ap, D["long_d"].ap, writes=[dl])
            for o in range(2):
                f.op("dve", lambda e, o=o: e.tensor_tensor(out=wsd.ap[:, o, 0, :], in0=wo.ap[:, o, 0, :], in1=wo.ap[:, o, 1, :], op=ALU.add),
                     reads=[wo], writes=[wsd])
                f.op("dve", lambda e, o=o: e.tensor_tensor(out=wsd.ap[:, o, 1, :], in0=wo.ap[:, o, 0, :], in1=wo.ap[:, o, 1, :], op=ALU.subtract),
                     reads=[wo], writes=[wsd])
            with Scope(f) as em:
                zT = sb(em, "zT", [33, L], F32)
                w0 = sb(em, "fw0", [33, 64], F32)
                fb = sb(em, "fb", [64, 3], F32)
                wi = sb(em, "fwi", [64, 2, 64], F32)
                fr = sb(em, "ffr", [64, 1], F32)
                s1 = sb(em, "fs1", [64, 1], F32)
                s2 = sb(em, "fs2", [64, 3], F32)
                f.dma("sp", zT.ap, D["zT"].ap, writes=[zT])
                f.dma("sp", w0.ap, D["f_w0"].ap, writes=[w0])
                f.dma("sp", fb.ap, D["f_b"].ap, writes=[fb])
                f.dma("sp", wi.ap, D["f_wi"].ap, writes=[wi])
                f.dma("sp", fr.ap, D["f_fr"].ap, writes=[fr])
                inv2pi = float(1.0 / (2 * np.pi))
                f.op("dve", lambda e: e.tensor_scalar(out=s1.ap, in0=fr.ap, scalar1=inv2pi, scalar2=None, op0=ALU.mult), reads=[fr], writes=[s1])
                f.op("dve", lambda e: e.tensor_scalar(out=s2.ap, in0=fb.ap, scalar1=s1.ap[:, 0:1], scalar2=16.0, op0=ALU.mult, op1=ALU.add),
                     reads=[fb, s1], writes=[s2])
                pm = [ps(em, "fpm%d" % i, [64, 512]) for i in range(2)]
                rr = [sb(em, "frr%d" % i, [64, 512], F32) for i in range(2)]
                ki = [sb(em, "fki%d" % i, [64, 512], I32) for i in range(2)]
                kf = [sb(em, "fkf%d" % i, [64, 512], F32) for i in range(2)]
                for layer in range(3):
                    src = zT if layer == 0 else hcur[(layer - 1) % 2]
                    dst = hcur[layer % 2]
                    lw = w0.ap if layer == 0 else wi.ap[:, layer - 1, :]
                    lwt = w0 if layer == 0 else wi
                    kdim = 33 if layer == 0 else 64
                    for c8 in range(8):
                        i = c8 % 2
                        cs = slice(c8 * 512, (c8 + 1) * 512)
                        f.mm([lambda e: e.matmul(pm[i].ap, lhsT=lw, rhs=src.ap[0:kdim, cs], start=True, stop=True)], reads=[src, lwt], writes=[pm[i]])
                        f.op("dve", lambda e: e.tensor_scalar(out=rr[i].ap, in0=pm[i].ap, scalar1=s1.ap[:, 0:1], scalar2=s2.ap[:, layer:layer + 1],
                                                              op0=ALU.mult, op1=ALU.add), reads=[pm[i], s1, s2], writes=[rr[i]])
                        f.op("dve", lambda e: e.tensor_copy(out=ki[i].ap, in_=rr[i].ap), reads=[rr[i]], writes=[ki[i]])
                        f.op("dve", lambda e: e.tensor_copy(out=kf[i].ap, in_=ki[i].ap), reads=[ki[i]], writes=[kf[i]])
                        f.op("dve", lambda e: e.tensor_tensor(out=rr[i].ap, in0=rr[i].ap, in1=kf[i].ap, op=ALU.subtract), reads=[rr[i], kf[i]], writes=[rr[i]])
                        f.op("dve", lambda e: e.tensor_single_scalar(out=kf[i].ap, in_=rr[i].ap, scalar=0.5, op=ALU.is_gt), reads=[rr[i]], writes=[kf[i]])
                        f.op("dve", lambda e: e.tensor_tensor(out=rr[i].ap, in0=rr[i].ap, in1=kf[i].ap, op=ALU.subtract), reads=[rr[i], kf[i]], writes=[rr[i]])
                        f.op("act", lambda e: e.activation(out=dst.ap[:, cs], in_=rr[i].ap, func=ACTF.Sin, scale=float(2 * np.pi)),
                             reads=[rr[i]], writes=[dst])
            h3 = sb(es, "fh3b", [64, L], BF16)
            wsdb = sb(es, "fwsdb", [64, 2, 2, 512], BF16)
            f.op("dve", lambda e: e.tensor_copy(out=h3.ap[:, 0:2048], in_=hcur[0].ap[:, 0:2048]), reads=[hcur[0]], writes=[h3])
            f.op("act", lambda e: e.activation(out=h3.ap[:, 2048:4096], in_=hcur[0].ap[:, 2048:4096], func=ACTF.Copy), reads=[hcur[0]], writes=[h3])
            f.op("dve", lambda e: e.tensor_copy(out=wsdb.ap, in_=wsd.ap), reads=[wsd], writes=[wsdb])
            for o in range(2):
                with Scope(f) as eo:
                    sd = [sb(eo, "fsd%d" % i, [128, 32, 512], BF16) for i in range(2)]
                    sq = [sb(eo, "fsq%d" % i, [128, 512], BF16) for i in range(4)]
                    dec = [sb(eo, "fdec%d" % i, [128, 512], F32) for i in range(2)]
                    pk = [ps(eo, "fpk%d" % i, [128, 512]) for i in range(2)]
                    pn = ps(eo, "fpn", [128, 512])
                    sqt = sb(eo, "fsqt", [128, 512], F32)
                    tmp0 = sb(eo, "ftmp0", [1, 512], F32)
                    pend = []

                    def norm_mm(j, v, q_):
                        first = (j == 0 and v == 0)
                        last = (j == 31 and v == 1)
                        fns = [lambda e: e.matmul(pn.ap, lhsT=SEL.ap[:, 0, :], rhs=q_.ap, start=first, stop=last)]
                        if j == 0:
                            fns.append(lambda e: e.matmul(pn.ap, lhsT=SEL.ap[:, 1 + v, :], rhs=q_.ap, start=False, stop=False))
                        f.mm(fns, reads=[q_, SEL], writes=[pn])
                    for j in range(32):
                        dc = dec[j % 2]
                        f.dma("sp", dc.ap, D["decay"].ap[:, j, :], writes=[dc])
                        cur = []
                        for v in range(2):
                            f.mm([lambda e, v=v: e.matmul(pk[v].ap, lhsT=h3.ap[:, j * 128:(j + 1) * 128], rhs=wsdb.ap[:, o, v, :], start=True, stop=True)],
                                 reads=[h3, wsdb], writes=[pk[v]])
                            f.op("dve", lambda e, v=v: e.tensor_tensor(out=sd[v].ap[:, j, :], in0=pk[v].ap, in1=dc.ap, op=ALU.mult),
                                 reads=[pk[v], dc], writes=[sd[v]])
                            q_ = sq[(2 * j + v) % 4]
                            f.op("act", lambda e, v=v: e.activation(out=q_.ap, in_=sd[v].ap[:, j, :], func=ACTF.Square), reads=[sd[v]], writes=[q_])
                            cur.append((j, v, q_))
                        for a in pend:
                            norm_mm(*a)
                        pend = cur
                    for a in pend:
                        norm_mm(*a)
                    f.op("act", lambda e: e.activation(out=sqt.ap, in_=pn.ap, func=ACTF.Sqrt, scale=0.5, bias=epsT.ap[:, 0:1]),
                         reads=[pn, epsT], writes=[sqt])
                    f.op("dve", lambda e, o=o: e.reciprocal(out=scl.ap[:, o, :], in_=sqt.ap), reads=[sqt], writes=[scl])
                    f.op("dve", lambda e, o=o: e.tensor_tensor(out=tmp0.ap, in0=dl.ap[0:1, o, :], in1=sqt.ap[0:1, :], op=ALU.mult),
                         reads=[dl, sqt], writes=[tmp0])
                    f.op("dve", lambda e: e.tensor_tensor(out=sd[0].ap[0:1, 0, :], in0=sd[0].ap[0:1, 0, :], in1=tmp0.ap, op=ALU.add),
                         reads=[sd[0], tmp0], writes=[sd[0]])
                    f.op("dve", lambda e, o=o: e.tensor_scalar(out=scl.ap[:, o, :], in0=scl.ap[:, o, :], scalar1=2.0 / NFFT, scalar2=None, op0=ALU.mult),
                         reads=[scl], writes=[scl])
                    with Scope(f) as es2:
                        stage1(es2, sd[0], HF, [[(0, lambda j: sd[0].ap[:, j, :])], [(1, lambda j: sd[0].ap[:, j, :])]], [0, 1], "k%da" % o)
                    with Scope(f) as es2:
                        stage1(es2, sd[1], HF, [[(0, lambda j: sd[1].ap[:, j, :])], [(1, lambda j: sd[1].ap[:, j, :])]], [2, 3], "k%db" % o)
                with Scope(f) as es2:
                    A2q = [[sb(es2, "kA2%d_%d" % (i, b), [128, 8, 512], BF16) for i in range(4)] for b in range(2)]
                    kst = [sb(es2, "kst%d" % i, [128, 2, 4, 512], BF16) for i in range(2)]
                    pkk = [ps(es2, "kpk%d" % i, [128, 512]) for i in range(2)]

                    def load_q(q):
                        for i in range(4):
                            f.dma("sp", A2q[q % 2][i].ap, relayout_view(FA.ap[i])[:, q * 8:(q + 1) * 8, :], reads=[FA], writes=[A2q[q % 2][i]])
                    load_q(0)
                    for q in range(4):
                        if q + 1 < 4:
                            load_q(q + 1)
                        A2 = A2q[q % 2]
                        for c16 in range(8):
                            cp = q * 8 + c16
                            st_ = kst[(cp // 4) % 2]
                            f.mm([lambda e: e.matmul(pkk[0].ap, lhsT=CS.ap[:, 0, :], rhs=A2[0].ap[:, c16, :], start=True, stop=False),
                                  lambda e: e.matmul(pkk[0].ap, lhsT=CS.ap[:, 1, :], rhs=A2[1].ap[:, c16, :], start=False, stop=True)],
                                 reads=[CS, A2[0], A2[1]], writes=[pkk[0]])
                            f.mm([lambda e: e.matmul(pkk[1].ap, lhsT=CS.ap[:, 0, :], rhs=A2[3].ap[:, c16, :], start=True, stop=False),
                                  lambda e: e.matmul(pkk[1].ap, lhsT=CS.ap[:, 2, :], rhs=A2[2].ap[:, c16, :], start=False, stop=True)],
                                 reads=[CS, A2[2], A2[3]], writes=[pkk[1]])
                            f.op("act", lambda e: e.activation(out=st_.ap[:, 0, cp % 4, :], in_=pkk[0].ap, func=ACTF.Copy), reads=[pkk[0]], writes=[st_])
                            f.op("dve", lambda e: e.tensor_copy(out=st_.ap[:, 1, cp % 4, :], in_=pkk[1].ap), reads=[pkk[1]], writes=[st_])
                            if cp % 4 == 3:
                                for r in range(2):
                                    f.dma("pool", KH.ap[o, r, :, cp - 3:cp + 1, :], st_.ap[:, r], reads=[st_], writes=[KH], sem=st_, accumulate=True)

    def phase_h(o):
        with Scope(f) as es:
            HF = sb(es, "HFh", [128, 32, 2, 128], BF16)
            z = sb(es, "hz", [128, 32, 512], BF16)
            f.dma("sp", HF.ap, D["HF"].ap, writes=[HF])
            if o == 0:
                load_split("sp", z, z.ap, XG.ap[2], [XG], 2)
            else:
                load_split("sp", z, z.ap, Z1.ap, [Z1], 2)
            stage1(es, z, HF, [[(0, lambda j: z.ap[:, j, :])], [(1, lambda j: z.ap[:, j, :])]], [0, 1], "h%d" % o)
        with Scope(f) as es:
            A2 = [sb(es, "hA2%d" % i, [128, 32, 512], BF16) for i in range(2)]
            for i in range(2):
                load_split("sp", A2[i], A2[i].ap, relayout_view(FA.ap[i]), [FA], 4)
            kk = [[sb(es, "hk%d%d" % (i, r), [128, 4, 512], BF16) for r in range(2)] for i in range(2)]
            pXX = [[ps(es, "hpX%d_%d" % (i, b), [128, 512]) for i in range(2)] for b in range(2)]
            pBB = [[ps(es, "hpB%d_%d" % (i, b), [128, 512]) for i in range(2)] for b in range(2)]
            ttt = [[sb(es, "htt%d_%d" % (i, b), [128, 512], F32) for i in range(4)] for b in range(2)]
            Y = [[sb(es, "hY%d%d" % (i, r), [128, 512], BF16) for r in range(2)] for i in range(2)]
            bst = [sb(es, "hbst%d" % i, [128, 2, 4, 512], BF16) for i in range(2)]
            FBv = [relayout_view(FB.ap[r]) for r in range(2)]
            def emit_x(cp):
                pX = pXX[cp % 2]
                f.mm([lambda e: e.matmul(pX[0].ap, lhsT=CS.ap[:, 0, :], rhs=A2[0].ap[:, cp, :], start=True, stop=False),
                      lambda e: e.matmul(pX[0].ap, lhsT=CS.ap[:, 1, :], rhs=A2[1].ap[:, cp, :], start=False, stop=True)],
                     reads=[CS, A2[0], A2[1]], writes=[pX[0]])
                f.mm([lambda e: e.matmul(pX[1].ap, lhsT=CS.ap[:, 0, :], rhs=A2[1].ap[:, cp, :], start=True, stop=False),
                      lambda e: e.matmul(pX[1].ap, lhsT=CS.ap[:, 2, :], rhs=A2[0].ap[:, cp, :], start=False, stop=True)],
                     reads=[CS, A2[0], A2[1]], writes=[pX[1]])

            emit_x(0)
            for cp in range(32):
                g = cp // 4
                pX, pB, tt = pXX[cp % 2], pBB[cp % 2], ttt[cp % 2]
                kb = kk[g % 2]
                if cp % 4 == 0:
                    for r in range(2):
                        f.dma("sp", kb[r].ap, KH.ap[o, r, :, cp:cp + 4, :], reads=[KH], writes=[kb[r]])
                kr = kb[0].ap[:, cp % 4, :]
                kim = kb[1].ap[:, cp % 4, :]
                f.op("dve", lambda e: e.tensor_tensor(out=tt[0].ap, in0=pX[0].ap, in1=kr, op=ALU.mult), reads=[pX[0], kb[0]], writes=[tt[0]])
                f.op("dve", lambda e: e.tensor_tensor(out=tt[1].ap, in0=pX[1].ap, in1=kim, op=ALU.mult), reads=[pX[1], kb[1]], writes=[tt[1]])
                yy = Y[cp % 2]
                f.op("pool", lambda e: e.tensor_tensor(out=yy[0].ap, in0=tt[0].ap, in1=tt[1].ap, op=ALU.subtract), reads=[tt[0], tt[1]], writes=[yy[0]])
                f.op("dve", lambda e: e.tensor_tensor(out=tt[2].ap, in0=pX[0].ap, in1=kim, op=ALU.mult), reads=[pX[0], kb[1]], writes=[tt[2]])
                f.op("dve", lambda e: e.tensor_tensor(out=tt[3].ap, in0=pX[1].ap, in1=kr, op=ALU.mult), reads=[pX[1], kb[0]], writes=[tt[3]])
                f.op("pool", lambda e: e.tensor_tensor(out=yy[1].ap, in0=tt[2].ap, in1=tt[3].ap, op=ALU.add), reads=[tt[2], tt[3]], writes=[yy[1]])
                if cp + 1 < 32:
                    emit_x(cp + 1)
                f.mm([lambda e: e.matmul(pB[0].ap, lhsT=CS.ap[:, 0, :], rhs=yy[0].ap, start=True, stop=False),
                      lambda e: e.matmul(pB[0].ap, lhsT=CS.ap[:, 2, :], rhs=yy[1].ap, start=False, stop=True)],
                     reads=[CS, yy[0], yy[1]], writes=[pB[0]])
                f.mm([lambda e: e.matmul(pB[1].ap, lhsT=CS.ap[:, 1, :], rhs=yy[0].ap, start=True, stop=False),
                      lambda e: e.matmul(pB[1].ap, lhsT=CS.ap[:, 0, :], rhs=yy[1].ap, start=False, stop=True)],
                     reads=[CS, yy[0], yy[1]], writes=[pB[1]])
                bs = bst[g % 2]
                f.op("act", lambda e: e.activation(out=bs.ap[:, 0, cp % 4, :], in_=pB[0].ap, func=ACTF.Copy), reads=[pB[0]], writes=[bs])
                f.op("act", lambda e: e.activation(out=bs.ap[:, 1, cp % 4, :], in_=pB[1].ap, func=ACTF.Copy), reads=[pB[1]], writes=[bs])
                if cp % 4 == 3:
                    for r in range(2):
                        f.dma("sp", FBv[r][:, cp - 3:cp + 1, :], bs.ap[:, r], reads=[bs], writes=[FB], sem=bs, accumulate=True)
        with Scope(f) as es:
            HG = sb(es, "HGh", [128, 32, 2, 128], BF16)
            Bc = [sb(es, "hBc%d" % i, [128, 16, 512], BF16) for i in range(2)]
            gt = sb(es, "hgt", [128, 32, 512], BF16)
            f.dma("sp", HG.ap, D["HG"].ap, writes=[HG])
            load_split("sp", gt, gt.ap, XG.ap[o], [XG], 2)
            for h in range(4):
                eng = "dve" if h % 2 == 0 else "pool"
                f.op(eng, lambda e, h=h: e.tensor_tensor(out=gt.ap[:, h * 8:(h + 1) * 8, :], in0=gt.ap[:, h * 8:(h + 1) * 8, :],
                                                         in1=scl.ap[:, o:o + 1, :].broadcast_to([128, 8, 512]), op=ALU.mult),
                     reads=[gt, scl], writes=[gt])
            gs = gt
            pY = [ps(es, "hpY%d" % i, [128, 512]) for i in range(2)]
            if o == 0:
                zst = [sb(es, "hzst%d" % i, [128, 4, 512], BF16) for i in range(2)]
            else:
                yf = [sb(es, "hyf%d" % i, [128, 512], F32) for i in range(2)]
                junk = sb(es, "hjunk", [128, 512], BF16)
                ssq = [sb(es, "hss%d" % i, [128, 1], F32) for i in range(2)]
                rsq = [sb(es, "hrs%d" % i, [128, 1], F32) for i in range(2)]
                yn = [sb(es, "hyn%d" % i, [128, 512], BF16) for i in range(2)]
                pT2 = [ps(es, "hpT%d" % i, [128, 4, 128], BF16) for i in range(2)]
                yT = yTbox[0]
                yTv = yT.ap.rearrange("p k (pp j) -> p k j pp", j=32)
            for j in range(32):
                if j % 16 == 0:
                    for r in range(2):
                        load_split("sp", Bc[r], Bc[r].ap, FB.ap[r][:, j:j + 16, :], [FB], 2)
                jl = j % 16
                py = pY[j % 2]
                f.mm([lambda e: e.matmul(py.ap, lhsT=HG.ap[:, j, 0, :], rhs=Bc[0].ap[:, jl, :], start=True, stop=False),
                      lambda e: e.matmul(py.ap, lhsT=HG.ap[:, j, 1, :], rhs=Bc[1].ap[:, jl, :], start=False, stop=True)],
                     reads=[HG, Bc[0], Bc[1]], writes=[py])
                if o == 0:
                    zs = zst[(j // 4) % 2]
                    f.op("dve", lambda e: e.tensor_tensor(out=zs.ap[:, j % 4, :], in0=py.ap, in1=gs.ap[:, j, :], op=ALU.mult),
                         reads=[py, gs], writes=[zs])
                    if j % 4 == 3:
                        f.dma("pool", Z1.ap[:, j - 3:j + 1, :], zs.ap, reads=[zs], writes=[Z1], sem=zs, accumulate=True)
                else:
                    y_, s_, r_, n_, pt = yf[j % 2], ssq[j % 2], rsq[j % 2], yn[j % 2], pT2[j % 2]
                    f.op("dve", lambda e: e.tensor_tensor(out=y_.ap, in0=py.ap, in1=gs.ap[:, j, :], op=ALU.mult), reads=[py, gs], writes=[y_])
                    f.op("act", lambda e: e.activation(out=junk.ap, in_=y_.ap, func=ACTF.Square, accum_out=s_.ap), reads=[y_], writes=[junk, s_])
                    rstd_of(s_.ap, s_, r_.ap, r_, 1.0 / 512.0)
                    f.op("act", lambda e: e.activation(out=n_.ap, in_=y_.ap, func=ACTF.Copy, scale=r_.ap[:, 0:1]), reads=[y_, r_], writes=[n_])
                    f.mm([lambda e, k=k: e.transpose(out=pt.ap[:, k, :], in_=n_.ap[:, k * 128:(k + 1) * 128], identity=ident.ap)
                          for k in range(4)], reads=[n_, ident], writes=[pt])
                    f.op("dve", lambda e: e.tensor_copy(out=yTv[:, 4:8, j, :], in_=pt.ap), reads=[pt], writes=[yT])

    def phase6():
        with Scope(f) as es:
            yT = yTbox[0]
            Wo = sb(es, "Wo", [128, 8, 1024], BF16)
            gfh = sb(es, "gfh", [128, 8], F32)
            gfin = sb(es, "gfin", [128, 1024], F32)
            f.dma("sp", gfh.ap, D["gfh"].ap, writes=[gfh])
            f.dma("sp", gfin.ap, D["gfin"].ap.partition_broadcast(128).rearrange("p a n -> p (a n)"), writes=[gfin])
            with Scope(f) as es2:
                wst = [sb(es2, "wost%d" % i, [128, 1024], F32) for i in range(2)]
                wv = D["w_out"].ap.rearrange("(k p) e -> p k e", p=128)
                for k in range(8):
                    s = wst[k % 2]
                    f.dma("sp", s.ap, wv[:, k, :], writes=[s])
                    f.op("dve", lambda e, k=k, s=s: e.tensor_scalar(out=Wo.ap[:, k, :], in0=s.ap, scalar1=gfh.ap[:, k:k + 1], scalar2=None,
                                                                    op0=ALU.mult), reads=[s, gfh], writes=[Wo])
            aT = sb(es, "aT", [128, 32, 512], BF16)
            hmT = [sb(es, "hmT%d" % i, [128, 8, 512], BF16) for i in range(2)]
            x1s = [sb(es, "x1s%d" % i, [128, 4, 1024], F32) for i in range(2)]
            W1s = [sb(es, "W1s%d" % i, [128, 8, 256], BF16) for i in range(2)]
            W2s = [sb(es, "W2s%d" % i, [128, 4, 512], BF16) for i in range(2)]
            rl = [sb(es, "rl%d" % i, [128, 512], BF16) for i in range(2)]
            hm = [sb(es, "hm%d" % i, [128, 1024], BF16) for i in range(2)]
            junk = sb(es, "mjunk", [128, 1024], BF16)
            ssq = [sb(es, "mss%d" % i, [128, 1], F32) for i in range(4)]
            rsq = [sb(es, "mrs%d" % i, [128, 1], F32) for i in range(4)]
            pw = [ps(es, "mpw%d" % i, [128, 512]) for i in range(2)]
            po = ps(es, "mpo", [128, 512])
            pT = ps(es, "mpT", [128, 8, 128], BF16)
            p2 = [ps(es, "mp2%d" % i, [128, 512]) for i in range(4)]
            xls = [[T(None, "xl") for _ in range(4)] for _ in range(2)]
            xss = [[T(None, "xs") for _ in range(4)] for _ in range(2)]
            xrows = D["x"].ap.rearrange("(n p) d -> n p d", p=128)
            orows = out_d.ap.rearrange("(n p) d -> n p d", p=128)
            cnt = {"n1": 0, "n2": 0}

            xq = [[T(x1s[b].ap[:, s, :], "xq%d%d" % (b, s)) for s in range(4)] for b in range(2)]

            def pro_a(tb, s):
                xb = xq[tb % 2][s]
                tt_ = tb * 4 + s
                f.dma("sp", xb.ap, xrows[tt_], writes=[xb], sem=xls[tb % 2][s])
                for dh in range(2):
                    f.mm([lambda e, k=k, dh=dh: e.matmul(po.ap, lhsT=yT.ap[:, k, tt_ * 128:(tt_ + 1) * 128],
                                                         rhs=Wo.ap[:, k, dh * 512:(dh + 1) * 512], start=(k == 0), stop=(k == 7))
                          for k in range(8)], reads=[yT, Wo], writes=[po])
                    f.op("dve", lambda e, dh=dh: e.tensor_tensor(out=xb.ap[:, dh * 512:(dh + 1) * 512], in0=po.ap,
                                                                 in1=xb.ap[:, dh * 512:(dh + 1) * 512], op=ALU.add),
                         reads=[po, xb], writes=[xb])

            def pro_b1(tb, s):
                xb = xq[tb % 2][s]
                s_, r_, h_ = ssq[s % 2], rsq[s % 2], hm[s % 2]
                f.op("act", lambda e: e.activation(out=junk.ap, in_=xb.ap, func=ACTF.Square, accum_out=s_.ap), reads=[xb], writes=[junk, s_])
                rstd_of(s_.ap, s_, r_.ap, r_, 1.0 / 1024)
                f.op("act", lambda e: e.activation(out=h_.ap, in_=xb.ap, func=ACTF.Copy, scale=r_.ap[:, 0:1]), reads=[xb, r_], writes=[h_])

            def pro_b2(tb, s):
                h_ = hm[s % 2]
                f.mm([lambda e, k=k: e.transpose(out=pT.ap[:, k, :], in_=h_.ap[:, k * 128:(k + 1) * 128], identity=ident.ap)
                      for k in range(8)], reads=[h_, ident], writes=[pT])
                f.op("dve", lambda e: e.tensor_copy(out=hmT[tb % 2].ap[:, :, s * 128:(s + 1) * 128], in_=pT.ap), reads=[pT], writes=[hmT[tb % 2]])

            def fc1_group(tb, fg):
                ws = W1s[cnt["n1"] % 2]
                cnt["n1"] += 1
                f.dma("sp", ws.ap, W1B.ap[:, :, fg * 256:(fg + 1) * 256], reads=[W1B], writes=[ws])
                for fl in range(2):
                    fc = fg * 2 + fl
                    pp = pw[fc % 2]
                    f.mm([lambda e, k=k: e.matmul(pp.ap, lhsT=ws.ap[:, k, fl * 128:(fl + 1) * 128], rhs=hmT[tb % 2].ap[:, k, :],
                                                  start=(k == 0), stop=(k == 7)) for k in range(8)], reads=[ws, hmT[tb % 2]], writes=[pp])
                    rr_ = rl[fc % 2]
                    f.op("act", lambda e: e.activation(out=rr_.ap, in_=pp.ap, func=ACTF.Relu), reads=[pp], writes=[rr_])
                    f.op("pool", lambda e: e.tensor_tensor(out=aT.ap[:, fc, :], in0=rr_.ap, in1=rr_.ap, op=ALU.mult), reads=[rr_], writes=[aT])

            def fc2_all(tb):
                for dh in range(2):
                    for fg in range(8):
                        ws = W2s[cnt["n2"] % 2]
                        cnt["n2"] += 1
                        f.dma("sp", ws.ap, W2B.ap[:, fg * 4:(fg + 1) * 4, dh * 512:(dh + 1) * 512], reads=[W2B], writes=[ws])
                        for s in range(4):
                            f.mm([lambda e, fl=fl: e.matmul(p2[s].ap, lhsT=aT.ap[:, fg * 4 + fl, s * 128:(s + 1) * 128], rhs=ws.ap[:, fl, :],
                                                            start=(fg == 0 and fl == 0), stop=(fg == 7 and fl == 3)) for fl in range(4)],
                                 reads=[aT, ws], writes=[p2[s]])
                    for s in range(4):
                        xb = xq[tb % 2][s]
                        f.op("dve", lambda e: e.tensor_tensor(out=xb.ap[:, dh * 512:(dh + 1) * 512], in0=p2[s].ap,
                                                              in1=xb.ap[:, dh * 512:(dh + 1) * 512], op=ALU.add), reads=[p2[s], xb], writes=[xb])

            def epilogue(tb, s):
                xb = xq[tb % 2][s]
                s_, r_ = ssq[2 + s % 2], rsq[2 + s % 2]
                f.op("act", lambda e: e.activation(out=junk.ap, in_=xb.ap, func=ACTF.Square, accum_out=s_.ap), reads=[xb], writes=[junk, s_])
                rstd_of(s_.ap, s_, r_.ap, r_, 1.0 / 1024)
                f.op("dve", lambda e: e.scalar_tensor_tensor(out=xb.ap, in0=xb.ap, scalar=r_.ap[:, 0:1], in1=gfin.ap, op0=ALU.mult, op1=ALU.mult),
                     reads=[xb, r_, gfin], writes=[xb])
                f.dma("pool", orows[tb * 4 + s], xb.ap, reads=[xb], writes=[out_d], sem=xss[tb % 2][s], accumulate=True)

            for s in range(4):
                pro_a(0, s)
                pro_b1(0, s)
                pro_b2(0, s)
            for tb in range(8):
                for fg in range(16):
                    fc1_group(tb, fg)
                    if tb >= 1 and fg < 4:
                        epilogue(tb - 1, fg)
                    if tb + 1 < 8 and fg >= 4:
                        q, r = divmod(fg - 4, 3)
                        if r == 0:
                            pro_a(tb + 1, q)
                        elif r == 1:
                            pro_b1(tb + 1, q)
                        else:
                            pro_b2(tb + 1, q)
                fc2_all(tb)
            for s in range(4):
                epilogue(7, s)

    def alloc_yT():
        yTbox.append(sb(top, "yT", [128, 8, L], BF16))
    phases = [("p1", phase1), ("p3", phase3), ("yT", alloc_yT), ("p2", phase2),
              ("p4", lambda: phase_h(0)), ("p5", lambda: phase_h(1)), ("p6", phase6)]
    for i, (nm, fn) in enumerate(phases):
        if i <= upto:
            fn()
    f.wait_all("pool", [out_d, RAW, U, XG, FA, FB, KH, Z1, W1B, W2B])
    top.close()
    return nc


def make_in_maps(inputs):
    C = host_consts()
    g = {k: np.asarray(v, dtype=np.float32) for k, v in inputs.items()}

    def pk(v):
        return np.ascontiguousarray(v.reshape(8, 128).T)
    shared = {
        "gm": pk(g["g_mix"][0]), "w_in": g["w_in"][0], "cw": np.concatenate([g["conv_w"][0], g["conv_b"]], 0),
        "f_w0": g["filt_w0"][0],
        "f_b": np.ascontiguousarray(np.stack([g["filt_b0"][0], g["filt_b_inner"][0, 0], g["filt_b_inner"][0, 1]], 1)),
        "f_wi": np.ascontiguousarray(g["filt_w_inner"][0].transpose(1, 0, 2)),
        "f_fr": np.ascontiguousarray(g["filt_freq"][0][:, None]),
        "f_wo": g["filt_w_out"][0], "long_d": g["long_d"],
        "gfh": pk(np.concatenate([g["g_fourier"][0], g["g_hyena"][0]])),
        "w_out": g["w_out"][0], "gml": pk(g["g_mlp"][0]), "w_fc1": g["w_fc1"][0], "w_fc2": g["w_fc2"][0],
        "gfin": g["g_final"][None, :],
    }
    shared.update({k: C[k] for k, _, _ in CONST_SPECS})
    shared = {k: np.ascontiguousarray(v) for k, v in shared.items()}
    maps = []
    for b in range(8):
        m = dict(shared)
        m["x"] = np.ascontiguousarray(g["x"][b])
        maps.append(m)
    return maps


def kernel(**inputs):
    nc = build()
    in_maps = make_in_maps(inputs)
    res = run_bass_kernel_spmd(nc, in_maps, core_ids=list(range(8)))
    return np.stack([np.asarray(r["out"], dtype=np.float32) for r in res.results], 0)
```
